# Optimizing a Trainium2 kernel written in Bass

```python
import math
import jax, jax.numpy as jnp
from jax import lax
import numpy as np

D_MODEL = 1024
BATCH = 2
SEQ = 8192
DEPTH = 2
DEC_BATCH = 16
DEC_SEQ = 2048
PAST_LEN = 128

HEAD_DIM = 64
D_MIX = D_MODEL
CHUNK = 128
A_WIDTH = D_MIX // 4
A_GROUPS = A_WIDTH // HEAD_DIM
A_GROUP_DIM = A_WIDTH // A_GROUPS
B_WIDTH = 3 * D_MIX // 8
B_HEADS = B_WIDTH // HEAD_DIM
C_WIDTH = 3 * D_MIX // 8
C_HEADS = C_WIDTH // HEAD_DIM
C_KV_HEADS = C_HEADS // 3
KV_WIDTH = C_KV_HEADS * HEAD_DIM
IN_DIM = 2 * A_WIDTH + 4 * B_WIDTH + C_WIDTH + 2 * KV_WIDTH
WINDOW = 128
ROT_DIM = HEAD_DIM // 4
ROPE_THETA = 500000.0
RET_THETA = 10000.0
D_FF = 2816
CONV_WIDTH = 3
EPS = 1e-6

kernel_name = 'hybrid_bidir_encoder_two_groups'


def _rmsnorm(x, g):
    xf = x.astype(jnp.float32)
    y = xf * lax.rsqrt(jnp.mean(xf * xf, axis=-1, keepdims=True) + EPS)
    return (y * g.astype(jnp.float32)).astype(x.dtype)


def _rope(x, rot_dim, theta):
    s = x.shape[1]
    half = rot_dim // 2
    freqs = jnp.exp(-math.log(theta) * jnp.arange(half, dtype=jnp.float32) * 2.0 / rot_dim)
    ang = jnp.arange(s, dtype=jnp.float32)[:, None] * freqs[None, :]
    cos = jnp.cos(ang)[:, None, :]
    sin = jnp.sin(ang)[:, None, :]
    x1 = x[..., :half].astype(jnp.float32)
    x2 = x[..., half:rot_dim].astype(jnp.float32)
    rest = x[..., rot_dim:].astype(jnp.float32)
    return jnp.concatenate([x1 * cos - x2 * sin, x1 * sin + x2 * cos, rest], axis=-1)


def _sgu(u, v, norm_g, w_s, b_s):
    bsz, s, _ = u.shape
    n = s // CHUNK
    u = jax.nn.gelu(u)
    v = jax.nn.gelu(v).reshape(bsz, s, A_GROUPS, A_GROUP_DIM)
    v = _rmsnorm(v, norm_g.reshape(A_GROUPS, A_GROUP_DIM))
    vc = v.reshape(bsz, n, CHUNK, A_GROUPS, A_GROUP_DIM)
    gate = jnp.einsum('gij,bnjgd->bnigd', w_s, vc) + b_s.T[None, None, :, :, None]
    return u * gate.reshape(bsz, s, A_WIDTH)


def _retention_dir(q, k, v, log_gamma, strict):
    bsz, _, c, h, dk = q.shape
    dv = v.shape[-1]
    pos = jnp.arange(c, dtype=jnp.float32)
    diff = pos[:, None] - pos[None, :]
    mask = (diff > 0) if strict else (diff >= 0)
    decay_in = jnp.where(mask[None], jnp.exp(jnp.maximum(diff, 0.0)[None] * log_gamma[:, None, None]), 0.0)
    scores = jnp.einsum('bnihd,bnjhd->bnhij', q, k) * decay_in
    y_inner = jnp.einsum('bnhij,bnjhe->bnihe', scores, v)
    k_dec = jnp.exp((c - 1 - pos)[:, None] * log_gamma[None, :])
    kv = jnp.einsum('bnjhd,bnjhe->bnhde', k * k_dec[None, None, :, :, None], v)
    chunk_decay = jnp.exp(c * log_gamma)[:, None, None]

    def step(state, kv_n):
        return chunk_decay * state + kv_n, state

    s0 = jnp.zeros((bsz, h, dk, dv), jnp.float32)
    _, s_prev = lax.scan(step, s0, jnp.moveaxis(kv, 1, 0))
    s_prev = jnp.moveaxis(s_prev, 0, 1)
    q_dec = jnp.exp((pos + 1.0)[:, None] * log_gamma[None, :])
    y_cross = jnp.einsum('bnihd,bnhde->bnihe', q * q_dec[None, None, :, :, None], s_prev)
    return y_inner + y_cross


def _retention(q, k, v, g, dec_f, dec_b):
    bsz, s, h, d = q.shape
    n = s // CHUNK
    q = _rope(q, HEAD_DIM, RET_THETA)
    k = _rope(k, HEAD_DIM, RET_THETA) * (HEAD_DIM ** -0.5)
    v = v.astype(jnp.float32)
    chunk = lambda t: t.reshape(bsz, n, CHUNK, h, d)
    flip = lambda t: chunk(t[:, ::-1])
    lg_f = jax.nn.log_sigmoid(dec_f.astype(jnp.float32))
    lg_b = jax.nn.log_sigmoid(dec_b.astype(jnp.float32))
    y_f = _retention_dir(chunk(q), chunk(k), chunk(v), lg_f, False).reshape(bsz, s, h, d)
    y_b = _retention_dir(flip(q), flip(k), flip(v), lg_b, True).reshape(bsz, s, h, d)[:, ::-1]
    y = y_f + y_b
    mu = jnp.mean(y, axis=-1, keepdims=True)
    var = jnp.mean(jnp.square(y - mu), axis=-1, keepdims=True)
    y = (y - mu) * lax.rsqrt(var + EPS)
    out = jax.nn.silu(g.astype(jnp.float32)) * y
    return out.reshape(bsz, s, B_WIDTH).astype(g.dtype)


def _window_gqa(q, k, v, sink):
    bsz, s, _, _ = q.shape
    n = s // CHUNK
    grp = C_HEADS // C_KV_HEADS
    q = _rope(q, ROT_DIM, ROPE_THETA)
    k = _rope(k, ROT_DIM, ROPE_THETA)
    qb = q.reshape(bsz, n, CHUNK, C_KV_HEADS, grp, HEAD_DIM)

    def band(t):
        tp = jnp.pad(t.astype(jnp.float32), ((0, 0), (CHUNK, CHUNK), (0, 0), (0, 0)))
        tp = tp.reshape(bsz, n + 2, CHUNK, C_KV_HEADS, HEAD_DIM)
        return jnp.concatenate([tp[:, :-2], tp[:, 1:-1], tp[:, 2:]], axis=2)

    kb = band(k)
    vb = band(v)
    sc = jnp.einsum('bnikgd,bnjkd->bnkgij', qb, kb) * (HEAD_DIM ** -0.5)
    blk = jnp.arange(n)[:, None, None]
    qpos = blk * CHUNK + jnp.arange(CHUNK)[None, :, None]
    kpos = (blk - 1) * CHUNK + jnp.arange(3 * CHUNK)[None, None, :]
    mask = (jnp.abs(kpos - qpos) <= WINDOW) & (kpos >= 0) & (kpos < s)
    sc = jnp.where(mask[None, :, None, None], sc, -jnp.inf)
    sk = sink.astype(jnp.float32).reshape(C_KV_HEADS, grp)[:, :, None, None]
    m = jnp.maximum(jnp.max(sc, axis=-1, keepdims=True), sk)
    p = jnp.exp(sc - m)
    denom = jnp.sum(p, axis=-1, keepdims=True) + jnp.exp(sk - m)
    o = jnp.einsum('bnkgij,bnjkd->bnikgd', p / denom, vb)
    return o.reshape(bsz, s, C_WIDTH)


def _token_mixers(h, w_in, sgu_norm, sgu_w, sgu_b, dec_f, dec_b, sink, w_out):
    bsz, s, _ = h.shape
    proj = h @ w_in
    sizes = [A_WIDTH, A_WIDTH, B_WIDTH, B_WIDTH, B_WIDTH, B_WIDTH, C_WIDTH, KV_WIDTH]
    points = list(np.cumsum(sizes))
    a_u, a_v, b_q, b_k, b_v, b_g, c_q, c_k, c_v = jnp.split(proj, points, axis=-1)
    y_a = _sgu(a_u, a_v, sgu_norm, sgu_w, sgu_b)
    hb = lambda t: t.reshape(bsz, s, B_HEADS, HEAD_DIM)
    y_b = _retention(hb(b_q), hb(b_k), hb(b_v), hb(b_g), dec_f, dec_b)
    y_c = _window_gqa(c_q.reshape(bsz, s, C_HEADS, HEAD_DIM),
                      c_k.reshape(bsz, s, C_KV_HEADS, HEAD_DIM),
                      c_v.reshape(bsz, s, C_KV_HEADS, HEAD_DIM), sink)
    y = jnp.concatenate([y_a, y_b, y_c], axis=-1).astype(h.dtype)
    return y @ w_out


def _conv_ffn(h, w_up, conv_w, conv_b, w_down):
    s = h.shape[1]
    up = h @ w_up
    pad = CONV_WIDTH // 2
    upp = jnp.pad(up, ((0, 0), (pad, pad), (0, 0)))
    conv = conv_b[None, None, :]
    for t in range(CONV_WIDTH):
        conv = conv + conv_w[t][None, None, :] * upp[:, t:t + s]
    gate, val = jnp.split(conv, 2, axis=-1)
    return (jax.nn.gelu(gate) * val) @ w_down


def _layer(x, c, w_ada, b_ada, g_pre_mix, g_post_mix, g_pre_ffn, g_post_ffn,
           w_in, sgu_norm, sgu_w, sgu_b, dec_f, dec_b, sink, w_out,
           w_up, conv_w, conv_b, w_down):
    mod = jax.nn.silu(c) @ w_ada + b_ada
    sh1, sc1, gt1, sh2, sc2, gt2 = [m[:, None, :] for m in jnp.split(mod, 6, axis=-1)]
    h = _rmsnorm(x, g_pre_mix) * (1.0 + sc1) + sh1
    y = _token_mixers(h, w_in, sgu_norm, sgu_w, sgu_b, dec_f, dec_b, sink, w_out)
    x = x + gt1 * _rmsnorm(y, g_post_mix)
    h = _rmsnorm(x, g_pre_ffn) * (1.0 + sc2) + sh2
    y = _conv_ffn(h, w_up, conv_w, conv_b, w_down)
    x = x + gt2 * _rmsnorm(y, g_post_ffn)
    return x


def setup_inputs(seed: int = 0) -> dict:
    key = jax.random.key(seed)
    ks = jax.random.split(key, 24)
    nrm = lambda k, shape, scale: scale * jax.random.normal(k, shape, jnp.float32)
    z0 = jnp.asarray(np.log(2.0 ** (5 + np.arange(B_HEADS)) - 1.0).astype(np.float32))
    return {
        'x_prompt': nrm(ks[0], (BATCH, SEQ, D_MODEL), 1.0),
        'x_sample': nrm(ks[1], (DEC_BATCH, DEC_SEQ, D_MODEL), 1.0),
        'c_prompt': nrm(ks[2], (BATCH, D_MODEL), 1.0),
        'c_sample': nrm(ks[3], (DEC_BATCH, D_MODEL), 1.0),
        'w_ada': nrm(ks[4], (DEPTH, D_MODEL, 6 * D_MODEL), D_MODEL ** -0.5),
        'b_ada': nrm(ks[5], (DEPTH, 6 * D_MODEL), 0.02),
        'norm_pre_mix': 1.0 + nrm(ks[6], (DEPTH, D_MODEL), 0.05),
        'norm_post_mix': 1.0 + nrm(ks[7], (DEPTH, D_MODEL), 0.05),
        'norm_pre_ffn': 1.0 + nrm(ks[8], (DEPTH, D_MODEL), 0.05),
        'norm_post_ffn': 1.0 + nrm(ks[9], (DEPTH, D_MODEL), 0.05),
        'w_in': nrm(ks[10], (DEPTH, D_MODEL, IN_DIM), D_MODEL ** -0.5),
        'sgu_norm': 1.0 + nrm(ks[11], (DEPTH, A_WIDTH), 0.05),
        'sgu_w': nrm(ks[12], (DEPTH, A_GROUPS, CHUNK, CHUNK), CHUNK ** -0.5),
        'sgu_b': 1.0 + nrm(ks[13], (DEPTH, A_GROUPS, CHUNK), 0.1),
        'ret_decay_fwd': z0[None, :] + nrm(ks[14], (DEPTH, B_HEADS), 0.05),
        'ret_decay_bwd': z0[None, :] + nrm(ks[15], (DEPTH, B_HEADS), 0.05),
        'attn_sink': nrm(ks[16], (DEPTH, C_HEADS), 0.5),
        'w_out': nrm(ks[17], (DEPTH, D_MIX, D_MODEL), D_MIX ** -0.5),
        'w_up': nrm(ks[18], (DEPTH, D_MODEL, 2 * D_FF), D_MODEL ** -0.5),
        'conv_w': nrm(ks[19], (DEPTH, CONV_WIDTH, 2 * D_FF), CONV_WIDTH ** -0.5),
        'conv_b': nrm(ks[20], (DEPTH, 2 * D_FF), 0.02),
        'w_down': nrm(ks[21], (DEPTH, D_FF, D_MODEL), D_FF ** -0.5),
    }


def reference(x_prompt, x_sample, c_prompt, c_sample, w_ada, b_ada,
              norm_pre_mix, norm_post_mix, norm_pre_ffn, norm_post_ffn,
              w_in, sgu_norm, sgu_w, sgu_b, ret_decay_fwd, ret_decay_bwd, attn_sink,
              w_out, w_up, conv_w, conv_b, w_down):
    y_prompt = x_prompt
    y_sample = x_sample
    for l in range(DEPTH):
        layer_params = (w_ada[l], b_ada[l], norm_pre_mix[l], norm_post_mix[l],
                        norm_pre_ffn[l], norm_post_ffn[l], w_in[l], sgu_norm[l],
                        sgu_w[l], sgu_b[l], ret_decay_fwd[l], ret_decay_bwd[l],
                        attn_sink[l], w_out[l], w_up[l], conv_w[l], conv_b[l], w_down[l])
        y_prompt = _layer(y_prompt, c_prompt, *layer_params)
        y_sample = _layer(y_sample, c_sample, *layer_params)
    return (y_prompt, y_sample)
```

```python
import math
import os
import numpy as np
DBG = os.environ.get("KDBG", "sgu,ret,ap,attn,out").split(",")
DBG2 = os.environ.get("KDBG2", "r1,r2,r3,r4,r5").split(",")
from contextlib import ExitStack
import concourse.bass as bass
import concourse.mybir as mybir
from concourse.bass_utils import run_bass_kernel_spmd

F32 = mybir.dt.float32
BF16 = mybir.dt.bfloat16
AF = mybir.ActivationFunctionType
ALU = mybir.AluOpType
AX = mybir.AxisListType

SAME_ENGINE_SYNC = bool(int(os.environ.get("KSES", "1")))
EPS = 1e-6
D = 1024
IN_DIM = 2688
DFF = 2816
NFC = 22
DEPTH = 2


class Tile:
    __slots__ = ("name", "w", "r", "rd", "excl")

    def __init__(self, name="", excl=False):
        self.name = name
        self.w = None
        self.r = {}
        self.rd = []
        self.excl = excl


class Op:
    __slots__ = ("eng", "fn", "deps", "inc", "val", "dsem")

    def __init__(self, eng, fn, dsem):
        self.eng = eng
        self.fn = fn
        self.dsem = dsem
        self.deps = set()
        self.inc = dsem is not None
        self.val = 0


class DSem:
    __slots__ = ("h", "cnt")

    def __init__(self, h):
        self.h = h
        self.cnt = 0


class Prog:
    ENGS = ("pe", "act", "dve", "pool", "sp")

    def __init__(self, nc, es):
        self.nc = nc
        self.es = es
        self.ops = {e: [] for e in self.ENGS}
        self.esem = {e: es.enter_context(nc.semaphore("s_" + e)) for e in self.ENGS}
        self.n = 0
        self.bar_idx = {}
        self.region = None
        self.cur = None
        self.streams = {}

    def sb(self, shape, dt, name=None):
        self.n += 1
        return self.es.enter_context(self.nc.sbuf_tensor(f"sb{self.n}_{name or ''}", list(shape), dt))

    def ps(self, shape, dt=F32, name=None):
        self.n += 1
        return self.es.enter_context(self.nc.psum_tensor(name or f"ps{self.n}", list(shape), dt))

    def dsem(self):
        self.n += 1
        return DSem(self.es.enter_context(self.nc.semaphore(f"ds{self.n}")))

    def set_stream(self, name):
        self.cur = None if name is None else self.streams.setdefault(name, [])

    def flush(self):
        self.cur = None
        lists = [v for v in self.streams.values() if v]
        if os.environ.get("KSEQ"):
            for li in lists:
                for r_ in li:
                    self._op(*r_)
            self.streams = {}
            return
        idx = [0] * len(lists)
        while True:
            best = None
            for i, li in enumerate(lists):
                if idx[i] < len(li):
                    f = (idx[i] + 1) / len(li)
                    if best is None or f < best[0]:
                        best = (f, i)
            if best is None:
                break
            i = best[1]
            self._op(*lists[i][idx[i]])
            idx[i] += 1
        self.streams = {}

    def op(self, eng, fn, reads=(), writes=(), dsem=None):
        if self.cur is not None:
            self.cur.append((eng, fn, list(reads), list(writes), dsem))
            return None
        return self._op(eng, fn, reads, writes, dsem)

    def _op(self, eng, fn, reads=(), writes=(), dsem=None):
        o = Op(eng, fn, dsem)
        deps = o.deps
        for t in reads:
            if t.w is not None:
                deps.add(t.w)
            if t.excl:
                for e_, o_ in t.r.items():
                    if e_ != eng:
                        deps.add(o_)
        for t in writes:
            if t.w is not None:
                deps.add(t.w)
            deps.update(t.r.values())
            deps.update(t.rd)
        for t in reads:
            if dsem is not None:
                t.rd.append(o)
            else:
                t.r[eng] = o
        for t in writes:
            t.w = o
            t.r = {}
            t.rd = []
        if dsem is not None:
            dsem.cnt += 16
            o.val = dsem.cnt
        self.ops[eng].append(o)
        return o

    def I(self, eng, meth, reads, writes, *a, **k):
        return self.op(eng, (meth, a, k), reads, writes)

    def dma(self, eng, out, in_, reads, writes, dsem=None, slow=False):
        if dsem is None:
            dsem = self.dsem()
        k = dict(out=out, in_=in_)
        if slow:
            k["allow_slow_non_contiguous"] = True
        return self.op(eng, ("dma_start", (), k), reads, writes, dsem)

    def barrier(self):
        lasts = []
        for e in self.ENGS:
            comp = [o for o in self.ops[e] if o.dsem is None and o.fn is not None]
            if comp:
                lasts.append(comp[-1])
        dmas = [o for e in self.ENGS for o in self.ops[e][self.bar_idx.get(e, 0):] if o.dsem is not None]
        for e in self.ENGS:
            self.bar_idx[e] = len(self.ops[e])
        for e in self.ENGS:
            o = Op(e, None, None)
            o.deps.update(lasts)
            o.deps.update(dmas)
            self.ops[e].append(o)

    def wait_all(self, eng, tiles):
        o = Op(eng, None, None)
        for t in tiles:
            if t.w is not None:
                o.deps.add(t.w)
        self.ops[eng].append(o)

    def emit(self):
        nc = self.nc
        for e in self.ENGS:
            for o in self.ops[e]:
                for d in o.deps:
                    if d.dsem is not None:
                        continue
                    if d.eng != o.eng or o.dsem is not None:
                        d.inc = True
                    elif SAME_ENGINE_SYNC and d.eng != "pe":
                        d.inc = True
        for e in self.ENGS:
            c = 0
            for o in self.ops[e]:
                if o.dsem is None:
                    if o.inc:
                        c += 1
                    o.val = c
        counts = {}
        with nc.Block() as block:
            def run(e):
                def body(h):
                    waited = {}
                    nw = 0
                    for o in self.ops[e]:
                        need = {}
                        for d in o.deps:
                            if d.dsem is not None:
                                key = d.dsem
                                sem = d.dsem.h
                            else:
                                if d.eng == e and o.dsem is None:
                                    if e == "pe" or not SAME_ENGINE_SYNC:
                                        continue
                                key = d.eng
                                sem = self.esem[d.eng]
                            if need.get(key, (None, 0))[1] < d.val:
                                need[key] = (sem, d.val)
                        for key, (sem, val) in need.items():
                            if waited.get(key, 0) < val:
                                h.wait_ge(sem, val)
                                waited[key] = val
                                nw += 1
                        if o.fn is None:
                            continue
                        meth, a, k = o.fn
                        inst = getattr(h, meth)(*a, **k)
                        if o.dsem is not None:
                            inst.then_inc(o.dsem.h, 16)
                        elif o.inc:
                            inst.then_inc(self.esem[e], 1)
                    counts[e] = (len(self.ops[e]), nw)
                return body
            block.tensor(run("pe"))
            block.scalar(run("act"))
            block.vector(run("dve"))
            block.gpsimd(run("pool"))
            block.sync(run("sp"))
        return counts


class Region:
    def __init__(self, big, start, limit):
        self.big = big
        self.off = start
        self.limit = limit

    def alloc(self, shape, dt):
        n = 1
        for s in shape[1:]:
            n *= s
        n16 = n * (2 if dt == F32 else 1)
        self.off = (self.off + 15) // 16 * 16
        ap = self.big[0:shape[0], self.off:self.off + n16]
        self.off += n16
        assert self.off <= self.limit, ("region overflow", self.off, self.limit)
        if dt == F32:
            ap = ap.bitcast(F32)
        if len(shape) == 3:
            ap = ap.rearrange("p (a b) -> p a b", a=shape[1])
        elif len(shape) == 4:
            ap = ap.rearrange("p (a b c) -> p a b c", a=shape[1], b=shape[2])
        return ap


class Buf:
    def __init__(self, P, shape, dt, name=None):
        if getattr(P, "region", None) is not None:
            self.t = P.region.alloc(shape, dt)
        else:
            self.t = P.sb(shape, dt, name)
        self.T = Tile(name or "")

    def __getitem__(self, k):
        return self.t[k]


class Ring:
    def __init__(self, P, n, shape, dt, name):
        self.b = [Buf(P, shape, dt, f"{name}{i}") for i in range(n)]
        self.n = n
        self.i = -1

    def next(self):
        self.i += 1
        return self.b[self.i % self.n]

    def at(self, k):
        return self.b[k % self.n]


def build(NCH, SEGC, depth=DEPTH, stop_after=None):
    NSEG = NCH // SEGC
    NT = NCH * 128
    NB = NCH // 2
    nc = bass.Bass("TRN2", target_bir_lowering=False)

    def din(name, shape):
        return nc.dram_tensor(name, list(shape), F32, kind="ExternalInput").ap()

    x_in = din("x", [NT, D])
    cT_in = din("cT", [128, 8, NSEG])
    flag_in = din("flag", [128, 1])
    rope_in = din("rope", [NCH, 128, 120])
    ident_in = din("ident", [128, 128])
    cm_in = din("cmats", [128, 6, 128])
    tri_in = din("tri", [128, 2, 128])
    jv_in = din("jv", [128, 2])
    w_ada = din("w_ada", [depth, D, 6 * D]); b_ada = din("b_ada", [depth, 6 * D])
    w_in = din("w_in", [depth, D, IN_DIM]); w_out = din("w_out", [depth, D, D])
    w_up = din("w_up", [depth, D, 2 * DFF]); w_down = din("w_down", [depth, DFF, D])
    cw_in = din("cw", [depth, 128, 2 * NFC, 4])
    gfm_in = din("gfm", [depth, 128, 2, 8])
    gpost_in = din("gpost", [depth, 2 * D])
    sguw_in = din("sgu_wT", [depth, 128, 4, 128]); sgub_in = din("sgu_bT", [depth, 128, 4])
    sgun_in = din("sgu_n", [depth, 256])
    dec18_in = din("dec18", [depth, 128, 18])
    sink_in = din("sink6", [depth, 128, 6])
    y_out = nc.dram_tensor("y", [NT, D], F32, kind="ExternalOutput").ap()
    xa = nc.dram_tensor("xa", [NT, D], F32, kind="Internal").ap()
    xb = nc.dram_tensor("xb", [NT, D], F32, kind="Internal").ap()
    modD = nc.dram_tensor("modD", [depth, NSEG, 6 * D], F32, kind="Internal").ap()
    gD = nc.dram_tensor("gD", [depth, 2, NSEG, D], F32, kind="Internal").ap()

    es = ExitStack()
    P = Prog(nc, es)
    Txa = [Tile(f"xa{i}") for i in range(NCH)]
    Txb = [Tile(f"xb{i}") for i in range(NCH)]
    Ty = [Tile(f"y{i}") for i in range(NCH)]
    TmodD = [None] * depth
    TgD = [None] * depth

    PS = P.ps([128, 4096], F32, "PSALL")
    PSb = PS.bitcast(BF16)
    Tbank = [Tile(f"bank{i}", excl=True) for i in range(8)]

    def bank(i):
        return PS[:, 512 * i:512 * (i + 1)]

    def bankb(i):
        return PSb[:, 1024 * i:1024 * (i + 1)]

    class Gen:
        def __init__(self, ids):
            self.ids = ids
            self.i = -1

        def next(self):
            self.i += 1
            b = self.ids[self.i % len(self.ids)]
            return bank(b), Tbank[b]

    ident = Buf(P, [128, 128], BF16, "ident")
    tri = Buf(P, [128, 2, 128], BF16, "tri")
    trif = Buf(P, [128, 2, 128], BF16, "trif")
    jv = Buf(P, [128, 2], F32, "jv")
    flag = Buf(P, [128, 1], F32, "flag")
    ropeR = Ring(P, 3, [128, 120], F32, "rope")
    drope = [P.dsem() for _ in range(3)]
    cmh = Buf(P, [128, 8], F32, "cmh")
    siluT = Buf(P, [128, 8, NSEG], BF16, "siluT")
    cTf = Buf(P, [128, 8, NSEG], F32, "cTf")

    dsems = []

    def DS():
        d = P.dsem()
        dsems.append(d)
        return d

    _nd = {}

    def ND(name):
        if name not in _nd:
            _nd[name] = DS()
        return _nd[name]
    for (b, src, eng) in [(ident, ident_in, "pool"), (tri, tri_in, "pool"), (jv, jv_in, "sp"),
                          (flag, flag_in, "sp"), (cTf, cT_in, "sp")]:
        P.dma(eng, b[:], src, [], [b.T], ND("c_" + b.T.name))
    P.I("pool", "memset", [], [cmh.T], cmh[:], -0.5)
    P.I("act", "activation", [cTf.T], [siluT.T], out=siluT[:], in_=cTf[:], func=AF.Silu)
    P.I("dve", "tensor_scalar", [tri.T, flag.T], [trif.T], out=trif[:], in0=tri[:], scalar1=flag[:, 0:1], scalar2=None, op0=ALU.mult)

    ARENA = 8 * 2 * DFF + NFC * D
    BIGN = ARENA + 21120
    W = P.sb([128, BIGN], BF16, "arena")
    win_v = W[:, 0:8 * IN_DIM].rearrange("p (k n) -> p k n", k=8)
    wout_v = W[:, 8 * IN_DIM:8 * IN_DIM + 8 * D].rearrange("p (k n) -> p k n", k=8)
    SB0 = 8 * IN_DIM + 8 * D
    sbst_v = W[:, SB0:SB0 + NCH * 192].rearrange("p (c q e) -> p c q e", c=NCH, q=3)
    wup_v = W[:, 0:8 * 2 * DFF].rearrange("p (k n) -> p k n", k=8)
    wdn_v = W[:, 8 * 2 * DFF:ARENA].rearrange("p (k n) -> p k n", k=NFC)
    Twin = [Tile(f"win{k}") for k in range(8)]
    Twout = [Tile(f"wout{k}") for k in range(8)]
    Tsb = [Tile(f"sbst{c}") for c in range(NCH)]
    Twup = [Tile(f"wup{k}") for k in range(8)]
    Twdn = [Tile(f"wdn{k}") for k in range(NFC)]
    A_tiles = Twin + Twout + Tsb
    B_tiles = Twup + Twdn
    dWin = [DS() for _ in range(8)]; dWout = [DS() for _ in range(8)]; dWup = [DS() for _ in range(8)]; dWdn = [DS() for _ in range(NFC)]

    regA = Region(W, SB0 + NCH * 192, BIGN)
    P.region = regA
    d18 = Buf(P, [128, 18], F32, "d18"); e18 = Buf(P, [128, 18], F32, "e18"); lg18 = Buf(P, [128, 18], F32, "lg18")
    DT = Buf(P, [128, 6, 128], F32, "DT")
    QDF = Buf(P, [128, 3, 128], F32, "QDF"); QDB = Buf(P, [128, 3, 128], F32, "QDB")
    KDF = Buf(P, [128, 6], F32, "KDF"); KDB = Buf(P, [128, 6], F32, "KDB")
    CDF = Buf(P, [128, 3], F32, "CDF"); CDB = Buf(P, [128, 3], F32, "CDB")
    ESK = Buf(P, [128, 6], F32, "ESK")
    SN = Buf(P, [128, 256], F32, "SN")
    WsT = Buf(P, [128, 4, 128], BF16, "WsT"); SGB = Buf(P, [128, 4], F32, "SGB")
    P.region = None
    CW = Buf(P, [128, 2 * NFC, 4], F32, "CW")
    gfm = Buf(P, [128, 2, 8], F32, "gfm")
    fm = [Buf(P, [128, NSEG, 8], F32, f"fm{i}") for i in range(4)]
    A1f = Buf(P, [128, NSEG, 8], F32, "A1f"); A2f = Buf(P, [128, NSEG, 8], F32, "A2f")
    Gt = Ring(P, 1, [128, D], F32, "Gt")
    dl = [DS() for _ in range(4)]
    dmb = [DS() for _ in range(2)]; dgp = [DS() for _ in range(2)]; dgb = [DS() for _ in range(2)]

    xcr = Ring(P, 3, [128, D], F32, "xc")
    st = Ring(P, 16, [128, 8], F32, "st")
    xn = Ring(P, 1, [128, D], BF16, "xn")
    xnew = Ring(P, 2, [128, D], F32, "xnew")
    P.region = regA
    hTr = Ring(P, 2, [128, 8, 128], BF16, "hT")
    gA = Gen([2, 3, 5, 6])
    st_rings = {None: st, "SC": Ring(P, 8, [128, 8], F32, "stSC"), "R": Ring(P, 8, [128, 8], F32, "stR"), "L": Ring(P, 8, [128, 8], F32, "stL"),
                "N": Ring(P, 8, [128, 8], F32, "stN"), "U": st}
    gens = {None: gA, "SC": Gen([2]), "R": Gen([3, 5]), "L": Gen([6]), "N": Gen([0]), "U": gA}
    gLpost = Gen([6, 7])
    _rg = P.region
    P.region = None
    st_rings["K"] = st
    gens["K"] = gA
    st_rings["NB"] = Ring(P, 8, [128, 8], F32, "stNB")
    gens["NB"] = Gen([0])
    P.region = _rg
    CUR = {"st": st, "gen": gA}
    Tb1a = Tbank[1]; Tb1b = Tbank[1]

    def use(name):
        P.set_stream(name)
        CUR["st"] = st_rings[name]
        CUR["gen"] = gens[name]

    u_b = Buf(P, [128, 256], BF16, "u"); vg = Buf(P, [128, 256], F32, "vg"); sq = Buf(P, [128, 256], F32, "sq")
    vn = Buf(P, [128, 256], F32, "vn"); vn2 = Buf(P, [128, 256], BF16, "vn2"); gt_ = Buf(P, [128, 256], F32, "gt")
    r1 = Buf(P, [128, 384], F32, "r1"); r2 = Buf(P, [128, 384], F32, "r2"); rsum = Buf(P, [128, 384], F32, "rsum")
    qr = Buf(P, [128, 384], BF16, "qr"); kr = Buf(P, [128, 384], BF16, "kr")
    kdf = Buf(P, [128, 384], BF16, "kdf"); kdb = Buf(P, [128, 384], BF16, "kdb")
    vt = Buf(P, [128, 384], BF16, "vt"); sg = Buf(P, [128, 384], F32, "sg")
    qkT = Buf(P, [128, 6, 128], BF16, "qkT")
    qdf = Buf(P, [128, 3, 128], BF16, "qdf"); qdb = Buf(P, [128, 3, 128], BF16, "qdb")
    PT = Buf(P, [128, 6, 128], BF16, "PT")
    Rf = Buf(P, [128, 3, 64], F32, "Rf"); Rb = Buf(P, [128, 3, 64], F32, "Rb"); Rt = Buf(P, [128, 3, 64], F32, "Rt")
    Sfb = Buf(P, [128, 3, 64], BF16, "Sfb")
    ysq = Buf(P, [128, 384], F32, "ysq"); yc_ = Buf(P, [128, 384], F32, "yc"); yn_ = Buf(P, [128, 384], F32, "yn")
    cqb = Buf(P, [128, 384], BF16, "cqb"); ckb = Buf(P, [128, 128], BF16, "ckb")
    a1 = Buf(P, [128, 6, 16], F32, "a1"); a2 = Buf(P, [128, 6, 16], F32, "a2")
    cqT = Ring(P, 3, [128, 3, 128], BF16, "cqT"); ckT = Ring(P, 4, [128, 128], BF16, "ckT")
    cvb = Ring(P, 4, [128, 2, 65], BF16, "cvb")
    Eb = Ring(P, 6, [128, 3, 128], BF16, "Eb")
    mix = Ring(P, 3, [128, D], BF16, "mix")
    mixT = Buf(P, [128, 8, 128], BF16, "mixT")
    cm = Buf(P, [128, 6, 128], F32, "cmats")
    ta = Buf(P, [128, 128], F32, "ta"); tb = Buf(P, [128, 128], F32, "tb")
    wad = Ring(P, 2, [128, 8, 256], BF16, "wad")
    badb = Ring(P, 2, [NSEG, 256], F32, "badb")
    mblk = Ring(P, 2, [NSEG, 256], F32, "mblk")
    gpb = Ring(P, 2, [NSEG, 256], F32, "gpb")
    gblk = Ring(P, 2, [NSEG, 256], F32, "gblk")
    print("regA end", regA.off, BIGN)
    regB = Region(W, ARENA, BIGN)
    P.region = regB
    HB = Ring(P, 3, [128, 8, 258], BF16, "HB")
    T0 = Ring(P, 2, [128, 256], F32, "T0"); T1 = Ring(P, 2, [128, 256], F32, "T1"); T2 = Ring(P, 2, [128, 256], F32, "T2")
    gg = Ring(P, 1, [128, 256], F32, "gg")
    actT = Ring(P, 2, [128, NFC, 256], BF16, "actT")
    print("regB end", regB.off, BIGN)
    P.region = None
    dx = [DS() for _ in range(4)]
    dst_ = [DS() for _ in range(2)]
    dG = [DS() for _ in range(2)]
    dcp = [DS() for _ in range(4)]


    def rstd_from(ssb, n_inv, out_col):
        sbuf, c = ssb
        obuf, oc = out_col
        t = CUR["st"].next()
        P.I("dve", "tensor_scalar", [sbuf.T], [t.T], out=t[:, 0:1], in0=sbuf[:, c:c + 1], scalar1=n_inv, scalar2=EPS, op0=ALU.mult, op1=ALU.add)
        P.I("pool", "tensor_tensor", [t.T, cmh.T], [obuf.T], out=obuf[:, oc:oc + 1], in0=t[:, 0:1], in1=cmh[:, 0:1], op=ALU.pow)

    def layer_setup(l):
        P.dma("sp", cm[:], cm_in, [], [cm.T], ND("cm"))
        TmD = [Tile(f"modD{l}_{i}") for i in range(24)]
        TgDl = [Tile(f"gD{l}_{i}") for i in range(8)]
        for cb in range(24):
            c0 = cb * 256
            wb = wad.next(); bb = badb.next(); mb = mblk.next()
            P.dma("pool", wb[:], w_ada[l, :, c0:c0 + 256].rearrange("(k p) n -> p k n", p=128), [], [wb.T], dl[cb % 2])
            P.dma("sp", bb[:], b_ada[l:l + 1, c0:c0 + 256].partition_broadcast(NSEG), [], [bb.T], dl[2 + cb % 2])
            pb, Tpb = CUR["gen"].next()
            for kc in range(8):
                P.I("pe", "matmul", [siluT.T, wb.T], [Tpb], pb[0:NSEG, 0:256], lhsT=siluT[:, kc, :], rhs=wb[:, kc, :], start=(kc == 0), stop=(kc == 7))
            P.I("dve", "tensor_tensor", [Tpb, bb.T], [mb.T], out=mb[:], in0=pb[0:NSEG, 0:256], in1=bb[:], op=ALU.add)
            P.dma("sp", modD[l, :, c0:c0 + 256], mb[:], [mb.T], [TmD[cb]], dmb[cb % 2])
            part = cb // 4
            if part in (2, 5):
                gi = 0 if part == 2 else 1
                j = cb % 4
                gp = gpb.next(); gb_ = gblk.next()
                P.dma("sp", gp[:], gpost_in[l:l + 1, gi * D + j * 256:gi * D + (j + 1) * 256].partition_broadcast(NSEG), [], [gp.T], dgp[gpb.i % 2])
                P.I("dve", "tensor_tensor", [mb.T, gp.T], [gb_.T], out=gb_[:], in0=mb[:], in1=gp[:], op=ALU.mult)
                P.dma("sp", gD[l, gi, :, j * 256:(j + 1) * 256], gb_[:], [gb_.T], [TgDl[gi * 4 + j]], dgb[gblk.i % 2])
        TmodD[l] = TmD
        TgD[l] = TgDl
        for i, part in enumerate([0, 1, 3, 4]):
            for s_ in range(NSEG):
                P.dma("sp", fm[i][:, s_, :], modD[l, s_, part * D:(part + 1) * D].rearrange("(k p) -> p k", p=128), TmodD[l][part * 4:part * 4 + 4], [fm[i].T], ND(f"fm{i}"), slow=True)
        P.dma("sp", gfm[:], gfm_in[l], [], [gfm.T], ND("gfm"))
        for (Af, scb, gi) in [(A1f, fm[1], 0), (A2f, fm[3], 1)]:
            P.I("dve", "scalar_tensor_tensor", [scb.T, gfm.T], [Af.T],
                out=Af[:], in0=scb[:], scalar=1.0, in1=gfm[:, gi, :].unsqueeze(1).to_broadcast([128, NSEG, 8]),
                op0=ALU.add, op1=ALU.mult)
        P.dma("sp", d18[:], dec18_in[l], [], [d18.T], ND("d18"))
        P.dma("sp", ESK[:], sink_in[l], [], [ESK.T], ND("ESK"))
        P.dma("sp", SN[:], sgun_in[l:l + 1, :].partition_broadcast(128), [], [SN.T], ND("SN"))
        P.dma("pool", WsT[:], sguw_in[l], [], [WsT.T], ND("WsT"))
        P.dma("sp", SGB[:], sgub_in[l], [], [SGB.T], ND("SGB"))
        P.I("act", "activation", [d18.T], [e18.T], out=e18[:], in_=d18[:], func=AF.Exp, scale=-1.0)
        P.I("dve", "tensor_scalar", [e18.T], [e18.T], out=e18[:], in0=e18[:], scalar1=1.0, scalar2=None, op0=ALU.add)
        P.I("act", "activation", [e18.T], [lg18.T], out=lg18[:], in_=e18[:], func=AF.Ln)
        P.I("dve", "tensor_scalar", [lg18.T], [lg18.T], out=lg18[:], in0=lg18[:], scalar1=-1.0, scalar2=None, op0=ALU.mult)
        P.I("act", "activation", [ESK.T], [ESK.T], out=ESK[:], in_=ESK[:], func=AF.Exp)
        for h in range(6):
            P.I("act", "activation", [cm.T, lg18.T], [ta.T], out=ta[:], in_=cm[:, 0, :], func=AF.Exp, scale=lg18[:, h:h + 1])
            P.I("act", "activation", [cm.T, lg18.T], [tb.T], out=tb[:], in_=cm[:, 1, :], func=AF.Exp, scale=lg18[:, 6 + h:7 + h])
            P.I("dve", "scalar_tensor_tensor", [ta.T, cm.T], [ta.T], out=ta[:], in0=ta[:], scalar=0.125, in1=cm[:, 2, :], op0=ALU.mult, op1=ALU.mult)
            P.I("dve", "scalar_tensor_tensor", [tb.T, cm.T], [tb.T], out=tb[:], in0=tb[:], scalar=0.125, in1=cm[:, 3, :], op0=ALU.mult, op1=ALU.mult)
            P.I("dve", "tensor_tensor", [ta.T, tb.T], [DT.T], out=DT[:, (h % 2) * 3 + h // 2, :], in0=ta[:], in1=tb[:], op=ALU.add)
        for q in range(3):
            P.I("act", "activation", [cm.T, lg18.T], [QDF.T], out=QDF[:, q, :], in_=cm[:, 4, :], func=AF.Exp, scale=lg18[:, 12 + q:13 + q])
            P.I("act", "activation", [cm.T, lg18.T], [QDB.T], out=QDB[:, q, :], in_=cm[:, 5, :], func=AF.Exp, scale=lg18[:, 15 + q:16 + q])
        P.I("act", "activation", [lg18.T, jv.T], [KDF.T], out=KDF[:], in_=lg18[:, 0:6], func=AF.Exp, scale=jv[:, 0:1])
        P.I("act", "activation", [lg18.T, jv.T], [KDB.T], out=KDB[:], in_=lg18[:, 6:12], func=AF.Exp, scale=jv[:, 1:2])
        P.I("dve", "tensor_scalar", [KDF.T], [KDF.T], out=KDF[:], in0=KDF[:], scalar1=0.125, scalar2=None, op0=ALU.mult)
        P.I("dve", "tensor_scalar", [KDB.T], [KDB.T], out=KDB[:], in0=KDB[:], scalar1=0.125, scalar2=None, op0=ALU.mult)
        P.I("act", "activation", [lg18.T], [CDF.T], out=CDF[:], in_=lg18[:, 12:15], func=AF.Exp, scale=128.0)
        P.I("act", "activation", [lg18.T], [CDB.T], out=CDB[:], in_=lg18[:, 15:18], func=AF.Exp, scale=128.0)

    def load_A_weights(l):
        for k in range(8):
            P.dma("pool", win_v[:, k, :], w_in[l, k * 128:(k + 1) * 128, :], [], [Twin[k]] + (B_tiles if k == 0 else []), dWin[k])
        for k in range(8):
            P.dma("pool", wout_v[:, k, :], w_out[l, k * 128:(k + 1) * 128, :], [], [Twout[k]], dWout[k])

    def load_B_weights(l):
        for k in range(8):
            P.dma("pool", wup_v[:, k, :], w_up[l, k * 128:(k + 1) * 128, :], [], [Twup[k]] + (A_tiles if k == 0 else []), dWup[k])
        for k in range(NFC):
            P.dma("pool", wdn_v[:, k, :], w_down[l, k * 128:(k + 1) * 128, :], [], [Twdn[k]], dWdn[k])

    def load_x(src, Tsrc, n, ring, dlist):
        xc = ring.next()
        P.dma("sp", xc[:], src[n * 128:(n + 1) * 128, :], [Tsrc[n]] if Tsrc is not None else [], [xc.T], dlist[ring.i % ring.n])
        return xc

    def norm_T(xc, seg, Af, Bf, dst_fn, dstT):
        s = CUR["st"].next()
        xb_ = xn.next()
        P.I("act", "activation", [xc.T], [xb_.T, s.T], out=xb_[:], in_=xc[:], func=AF.Square, accum_out=s[:, 0:1])
        rstd_from((s, 0), 1.0 / D, (s, 1))
        P.I("dve", "tensor_scalar", [xc.T, s.T], [xb_.T], out=xb_[:], in0=xc[:], scalar1=s[:, 1:2], scalar2=None, op0=ALU.mult)
        for kc in range(8):
            P.I("pe", "transpose", [xb_.T, ident.T], [Tbank[0]], out=bankb(0)[:, kc * 128:(kc + 1) * 128], in_=xb_[:, kc * 128:(kc + 1) * 128], identity=ident[:])
        for kc in range(8):
            P.I("act", "activation", [Tbank[0], Af.T, Bf.T], [dstT], out=dst_fn(kc), in_=bankb(0)[:, kc * 128:(kc + 1) * 128], func=AF.Identity,
                                                              scale=Af[:, seg, kc:kc + 1], bias=Bf[:, seg, kc:kc + 1])

    def proj(hT, wv, Tw, c0, ncol, gen):
        pb, Tpb = gen.next()
        for kc in range(8):
            P.I("pe", "matmul", [hT.T, Tw[kc]], [Tpb], pb[:, 0:ncol], lhsT=hT[:, kc, :], rhs=wv[:, kc, c0:c0 + ncol], start=(kc == 0), stop=(kc == 7))
        return pb, Tpb

    def load_rope(n):
        rp = ropeR.next()
        P.dma("sp", rp[:], rope_in[n], [], [rp.T], drope[ropeR.i % 3])
        return rp

    def rope64(pb, Tpb, rp, out_f32):
        x3 = pb[:, 0:384].rearrange("p (a d) -> p a d", d=32)
        x4 = pb[:, 0:384].rearrange("p (h t d) -> p h t d", h=6, t=2)
        P.I("dve", "tensor_tensor", [Tpb, rp.T], [r1.T], out=r1[:].rearrange("p (a d) -> p a d", d=32), in0=x3,
                                              in1=rp[:, 0:32].unsqueeze(1).to_broadcast([128, 12, 32]), op=ALU.mult)
        r2v = r2[:].rearrange("p (h t d) -> p h t d", h=6, t=2)
        P.I("dve", "tensor_tensor", [Tpb, rp.T], [r2.T], out=r2v[:, :, 0, :], in0=x4[:, :, 1, :],
                                              in1=rp[:, 64:96].unsqueeze(1).to_broadcast([128, 6, 32]), op=ALU.mult)
        P.I("dve", "tensor_tensor", [Tpb, rp.T], [r2.T], out=r2v[:, :, 1, :], in0=x4[:, :, 0, :],
                                              in1=rp[:, 32:64].unsqueeze(1).to_broadcast([128, 6, 32]), op=ALU.mult)
        P.I("pool", "tensor_tensor", [r1.T, r2.T], [out_f32.T], out=out_f32[:], in0=r1[:], in1=r2[:], op=ALU.add)

    def kv_update(kd, R, CD, n, store_fn, store_T, gen, boundary):
        pb, Tpb = gen.next()
        for q in range(3):
            P.I("pe", "matmul", [kd.T, vt.T], [Tpb], pb[:, q * 128:(q + 1) * 128], lhsT=kd[:, q * 128:(q + 1) * 128], rhs=vt[:, q * 128:(q + 1) * 128],
                                                       start=True, stop=True)
        P.I("dve", "tensor_tensor", [R.T, CD.T], [Rt.T], out=Rt[:], in0=R[:], in1=CD[:].unsqueeze(2).to_broadcast([128, 3, 64]), op=ALU.mult)
        kv3 = pb[:, 0:384].rearrange("p (q e) -> p q e", q=3)
        P.I("dve", "tensor_tensor", [Rt.T, Tpb], [R.T], out=R[0:64], in0=Rt[0:64], in1=kv3[0:64, :, 0:64], op=ALU.add)
        P.I("dve", "tensor_tensor", [Rt.T, Tpb], [R.T], out=R[64:128], in0=Rt[64:128], in1=kv3[64:128, :, 64:128], op=ALU.add)
        if boundary:
            P.I("dve", "tensor_scalar", [R.T, flag.T], [R.T], out=R[:], in0=R[:], scalar1=flag[:, 0:1], scalar2=None, op0=ALU.mult)
        if store_fn is not None:
            P.I("act", "activation", [R.T], [store_T], out=store_fn, in_=R[:], func=AF.Copy)

    def post_stage(lhs_fn, K, wv, Tw, xc, Gb, n, dst, Tdst, gen):
        pbs = []
        for half in range(2):
            pb, Tpb = gen.next()
            for kc in range(K):
                lh, Tl = lhs_fn(kc)
                P.I("pe", "matmul", [Tl, Tw[kc]], [Tpb], pb[:, :], lhsT=lh, rhs=wv[:, kc, half * 512:(half + 1) * 512],
                                                                                      start=(kc == 0), stop=(kc == K - 1))
            pbs.append((pb, Tpb))
        s = CUR["st"].next()
        xo = xnew.next()
        for half in range(2):
            pb, Tpb = pbs[half]
            P.I("act", "activation", [Tpb], [xo.T, s.T], out=xo[:, 0:512], in_=pb[:, :], func=AF.Square, accum_out=s[:, half:half + 1])
        P.I("dve", "tensor_tensor", [s.T], [s.T], out=s[:, 2:3], in0=s[:, 0:1], in1=s[:, 1:2], op=ALU.add)
        rstd_from((s, 2), 1.0 / D, (s, 3))
        for half in range(2):
            pb, Tpb = pbs[half]
            P.I("dve", "scalar_tensor_tensor", [Tpb, s.T, Gb.T], [xo.T], out=xo[:, half * 512:(half + 1) * 512], in0=pb[:, :], scalar=s[:, 3:4],
                                                                                   in1=Gb[:, half * 512:(half + 1) * 512], op0=ALU.mult, op1=ALU.mult)
        P.I("pool", "tensor_tensor", [xc.T, xo.T], [xo.T], out=xo[:], in0=xo[:], in1=xc[:], op=ALU.add)
        P.dma("sp", dst[n * 128:(n + 1) * 128, :], xo[:], [xo.T], [Tdst[n]], dst_[xnew.i % 2])

    def load_G(l, gi, seg):
        Gb = Gt.next()
        P.dma("sp", Gb[:], gD[l, gi, seg:seg + 1, :].partition_broadcast(128), TgD[l][gi * 4:gi * 4 + 4], [Gb.T], dG[0])
        return Gb

    def phase_A0(l, src, Tsrc):
        P.I("pool", "memset", [], [Rb.T], Rb[:], 0.0)
        P.I("pool", "memset", [], [Tsb[NCH - 1]], sbst_v[:, NCH - 1], 0.0)
        hTs, rps = {}, {}

        def front(c):
            xc = load_x(src, Tsrc, c, xcr, dx)
            rps[c] = load_rope(c)
            hT = hTr.next()
            hTs[c] = hT
            norm_T(xc, c // SEGC, A1f, fm[0], lambda kc: hT[:, kc, :], hT.T)

        front(NCH - 1)
        for n in range(NCH - 1, 0, -1):
            if n - 1 >= 1:
                use("N")
                front(n - 1)
            use("K")
            hT = hTs.pop(n)
            rp = rps.pop(n)
            pk, Tpk = proj(hT, win_v, Twin, 896, 384, CUR["gen"])
            pv, Tpv = proj(hT, win_v, Twin, 1280, 384, CUR["gen"])
            rope64(pk, Tpk, rp, rsum)
            P.I("dve", "tensor_tensor", [rsum.T, KDB.T], [kdb.T], out=kdb[:].rearrange("p (h d) -> p h d", h=6), in0=rsum[:].rearrange("p (h d) -> p h d", h=6),
                in1=KDB[:].unsqueeze(2).to_broadcast([128, 6, 64]), op=ALU.mult)
            P.I("act", "activation", [Tpv], [vt.T], out=vt[:], in_=pv[:, 0:384], func=AF.Copy)
            kv_update(kdb, Rb, CDB, n, sbst_v[:, n - 1], Tsb[n - 1], CUR["gen"], boundary=(n % SEGC == 0))
            use(None)
            P.flush()

    def attention(l, m):
        mx = mix.at(m)
        pO, TpO = bank(7), Tbank[7]
        O3 = pO[:, 0:390].rearrange("p (h e) -> p h e", h=6)
        blks = [b for b in (-1, 0, 1) if 0 <= m + b < NCH]
        for kvh in range(2):
            Es = []
            for bi, b in enumerate(blks):
                pS, TpS = CUR["gen"].next()
                kT = ckT.at(m + b); qT = cqT.at(m)
                P.I("pe", "matmul", [kT.T, qT.T], [TpS], pS[:, 0:384], lhsT=kT[kvh * 64:(kvh + 1) * 64, :],
                    rhs=qT[kvh * 64:(kvh + 1) * 64, :, :], start=True, stop=True)
                E = Eb.next()
                P.I("act", "activation", [TpS], [E.T], out=E[:].rearrange("p g i -> p (g i)"), in_=pS[:, 0:384], func=AF.Exp, scale=0.125)
                if b != 0:
                    bnd = (b == -1 and m % SEGC == 0) or (b == 1 and m % SEGC == SEGC - 1)
                    mk = trif if bnd else tri
                    mi = 0 if b == -1 else 1
                    P.I("pool", "tensor_tensor", [E.T, mk.T], [E.T], out=E[:], in0=E[:], in1=mk[:, mi, :].unsqueeze(1).to_broadcast([128, 3, 128]), op=ALU.mult)
                Es.append(E)
            for g in range(3):
                for bi, b in enumerate(blks):
                    cv = cvb.at(m + b)
                    P.I("pe", "matmul", [Es[bi].T, cv.T], [TpO], O3[:, kvh * 3 + g, :], lhsT=Es[bi][:, g, :], rhs=cv[:, kvh, :],
                        start=(bi == 0), stop=(bi == len(blks) - 1))
        s = CUR["st"].next()
        P.I("dve", "tensor_tensor", [TpO, ESK.T], [s.T], out=s[:, 0:6], in0=O3[:, :, 64], in1=ESK[:], op=ALU.add)
        P.I("dve", "reciprocal", [s.T], [s.T], out=s[:, 0:6], in_=s[:, 0:6])
        P.I("dve", "tensor_tensor", [TpO, s.T], [mx.T], out=mx[:, 640:1024].rearrange("p (h d) -> p h d", h=6), in0=O3[:, :, 0:64],
                                              in1=s[:, 0:6].unsqueeze(2).to_broadcast([128, 6, 64]), op=ALU.mult)

    def out_stage(l, m, xc, Gb, dst, Tdst):
        mx = mix.at(m)
        for half in range(2):
            for k4 in range(4):
                kc = half * 4 + k4
                P.I("pe", "transpose", [mx.T, ident.T], [Tb1a], out=bankb(1)[:, k4 * 128:(k4 + 1) * 128], in_=mx[:, kc * 128:(kc + 1) * 128], identity=ident[:])
            P.I("act", "activation", [Tb1a], [mixT.T], out=mixT[:, half * 4:(half + 1) * 4, :].rearrange("p k i -> p (k i)"), in_=bankb(1)[:, 0:512], func=AF.Copy)
        post_stage(lambda kc: (mixT[:, kc, :], mixT.T), 8, wout_v, Twout, xc, Gb, m, dst, Tdst, gLpost)

    def phase_A1(l, src, Tsrc, dst, Tdst):
        P.I("pool", "memset", [], [Rf.T], Rf[:], 0.0)
        P.I("pool", "memset", [], [Sfb.T], Sfb[:], 0.0)
        for b_ in cvb.b:
            P.I("pool", "memset", [], [b_.T], b_[:], 1.0)
        xcs = {}
        Gb = None
        Gseg = {}
        LAG = 1
        hTs, rps = {}, {}

        def attn_proj(n, hT, rp):
            pcq, Tpcq = proj(hT, win_v, Twin, 2048, 384, CUR["gen"])
            P.I("act", "activation", [Tpcq], [cqb.T], out=cqb[:], in_=pcq[:, 0:384], func=AF.Copy)
            c3 = pcq[:, 0:384].rearrange("p (h d) -> p h d", h=6)
            P.I("dve", "tensor_tensor", [Tpcq, rp.T], [a1.T], out=a1[:].rearrange("p h (t d) -> p h t d", t=2), in0=c3[:, :, 0:16].rearrange("p h (t d) -> p h t d", t=2),
                                                  in1=rp[:, 96:104].unsqueeze(1).unsqueeze(1).to_broadcast([128, 6, 2, 8]), op=ALU.mult)
            P.I("dve", "tensor_tensor", [Tpcq, rp.T], [a2.T], out=a2[:, :, 0:8], in0=c3[:, :, 8:16], in1=rp[:, 112:120].unsqueeze(1).to_broadcast([128, 6, 8]), op=ALU.mult)
            P.I("dve", "tensor_tensor", [Tpcq, rp.T], [a2.T], out=a2[:, :, 8:16], in0=c3[:, :, 0:8], in1=rp[:, 104:112].unsqueeze(1).to_broadcast([128, 6, 8]), op=ALU.mult)
            P.I("pool", "tensor_tensor", [a1.T, a2.T], [cqb.T], out=cqb[:].rearrange("p (h d) -> p h d", h=6)[:, :, 0:16], in0=a1[:], in1=a2[:], op=ALU.add)
            pck, Tpck = proj(hT, win_v, Twin, 2432, 256, CUR["gen"])
            cv = cvb.at(n)
            P.I("act", "activation", [Tpck], [ckb.T], out=ckb[:], in_=pck[:, 0:128], func=AF.Copy)
            P.I("act", "activation", [Tpck], [cv.T], out=cv[:, :, 0:64], in_=pck[:, 128:256].rearrange("p (h d) -> p h d", h=2), func=AF.Copy)
            k3 = pck[:, 0:128].rearrange("p (h d) -> p h d", h=2)
            P.I("dve", "tensor_tensor", [Tpck, rp.T], [a1.T], out=a1[:, 0:2, :].rearrange("p h (t d) -> p h t d", t=2), in0=k3[:, :, 0:16].rearrange("p h (t d) -> p h t d", t=2),
                                                  in1=rp[:, 96:104].unsqueeze(1).unsqueeze(1).to_broadcast([128, 2, 2, 8]), op=ALU.mult)
            P.I("dve", "tensor_tensor", [Tpck, rp.T], [a2.T], out=a2[:, 0:2, 0:8], in0=k3[:, :, 8:16], in1=rp[:, 112:120].unsqueeze(1).to_broadcast([128, 2, 8]), op=ALU.mult)
            P.I("dve", "tensor_tensor", [Tpck, rp.T], [a2.T], out=a2[:, 0:2, 8:16], in0=k3[:, :, 0:8], in1=rp[:, 104:112].unsqueeze(1).to_broadcast([128, 2, 8]), op=ALU.mult)
            P.I("pool", "tensor_tensor", [a1.T, a2.T], [ckb.T], out=ckb[:].rearrange("p (h d) -> p h d", h=2)[:, :, 0:16], in0=a1[:, 0:2, :], in1=a2[:, 0:2, :], op=ALU.add)
            for q in range(3):
                P.I("pe", "transpose", [cqb.T, ident.T], [Tb1b], out=bankb(1)[:, 512 + q * 128:512 + (q + 1) * 128], in_=cqb[:, q * 128:(q + 1) * 128], identity=ident[:])
            P.I("pe", "transpose", [ckb.T, ident.T], [Tb1b], out=bankb(1)[:, 896:1024], in_=ckb[:], identity=ident[:])
            cq_t = cqT.at(n); ck_t = ckT.at(n)
            P.I("act", "activation", [Tb1b], [cq_t.T], out=cq_t[:].rearrange("p a i -> p (a i)"), in_=bankb(1)[:, 512:896], func=AF.Copy)
            P.I("act", "activation", [Tb1b], [ck_t.T], out=ck_t[:], in_=bankb(1)[:, 896:1024], func=AF.Copy)


        def front(c):
            xc = load_x(src, Tsrc, c, xcr, dx)
            rp = load_rope(c)
            xcs[c] = xc
            rps[c] = rp
            hT = hTr.next()
            hTs[c] = hT
            norm_T(xc, c // SEGC, A1f, fm[0], lambda kc: hT[:, kc, :], hT.T)
            attn_proj(c, hT, rp)

        front(0)
        for n in range(NCH + LAG):
            if n + 1 < NCH:
                use("N")
                front(n + 1)
            if n < NCH:
                hT = hTs.pop(n)
                rp = rps.pop(n)
                mx = mix.at(n)
                use("SC")
                if 'sgu' in DBG:
                    pb, Tpb = proj(hT, win_v, Twin, 0, 512, CUR["gen"])
                    P.I("act", "activation", [Tpb], [u_b.T], out=u_b[:], in_=pb[:, 0:256], func=AF.Gelu_apprx_tanh)
                    P.I("act", "activation", [Tpb], [vg.T], out=vg[:], in_=pb[:, 256:512], func=AF.Gelu_apprx_tanh)
                    P.I("dve", "tensor_tensor", [vg.T], [sq.T], out=sq[:], in0=vg[:], in1=vg[:], op=ALU.mult)
                    s = CUR["st"].next()
                    P.I("dve", "tensor_reduce", [sq.T], [s.T], out=s[:, 0:4], in_=sq[:].rearrange("p (g d) -> p g d", g=4), axis=AX.X, op=ALU.add)
                    P.I("dve", "tensor_scalar", [s.T], [s.T], out=s[:, 0:4], in0=s[:, 0:4], scalar1=1.0 / 64, scalar2=EPS, op0=ALU.mult, op1=ALU.add)
                    P.I("pool", "tensor_tensor", [s.T, cmh.T], [s.T], out=s[:, 4:8], in0=s[:, 0:4], in1=cmh[:, 0:4], op=ALU.pow)
                    P.I("dve", "tensor_tensor", [vg.T, s.T], [vn.T], out=vn[:].rearrange("p (g d) -> p g d", g=4), in0=vg[:].rearrange("p (g d) -> p g d", g=4),
                                                          in1=s[:, 4:8].unsqueeze(2).to_broadcast([128, 4, 64]), op=ALU.mult)
                    P.I("pool", "tensor_tensor", [vn.T, SN.T], [vn2.T], out=vn2[:], in0=vn[:], in1=SN[:], op=ALU.mult)
                    pg, Tpg = CUR["gen"].next()
                    for g in range(4):
                        P.I("pe", "matmul", [WsT.T, vn2.T], [Tpg], pg[:, g * 64:(g + 1) * 64], lhsT=WsT[:, g, :], rhs=vn2[:, g * 64:(g + 1) * 64], start=True, stop=True)
                    P.I("dve", "tensor_tensor", [Tpg, SGB.T], [gt_.T], out=gt_[:].rearrange("p (g d) -> p g d", g=4), in0=pg[:, 0:256].rearrange("p (g d) -> p g d", g=4),
                                                          in1=SGB[:].unsqueeze(2).to_broadcast([128, 4, 64]), op=ALU.add)
                    P.I("pool", "tensor_tensor", [gt_.T, u_b.T], [mx.T], out=mx[:, 0:256], in0=gt_[:], in1=u_b[:], op=ALU.mult)
                use("R")
                if 'ret' in DBG:
                    pq, Tpq = proj(hT, win_v, Twin, 512, 384, CUR["gen"])
                    rope64(pq, Tpq, rp, rsum)
                    P.I("act", "activation", [rsum.T], [qr.T], out=qr[:], in_=rsum[:], func=AF.Copy)
                    pk, Tpk = proj(hT, win_v, Twin, 896, 384, CUR["gen"])
                    rope64(pk, Tpk, rp, rsum)
                    P.I("act", "activation", [rsum.T], [kr.T], out=kr[:], in_=rsum[:], func=AF.Copy)
                    P.I("dve", "tensor_tensor", [rsum.T, KDF.T], [kdf.T], out=kdf[:].rearrange("p (h d) -> p h d", h=6), in0=rsum[:].rearrange("p (h d) -> p h d", h=6),
                                                          in1=KDF[:].unsqueeze(2).to_broadcast([128, 6, 64]), op=ALU.mult)
                    pv, Tpv = proj(hT, win_v, Twin, 1280, 384, CUR["gen"])
                    P.I("act", "activation", [Tpv], [vt.T], out=vt[:], in_=pv[:, 0:384], func=AF.Copy)
                    pgg, Tpgg = proj(hT, win_v, Twin, 1664, 384, CUR["gen"])
                    P.I("act", "activation", [Tpgg], [sg.T], out=sg[:], in_=pgg[:, 0:384], func=AF.Silu)
                    if 'r1' in DBG2:
                        for i, srcb in enumerate([qr, kr]):
                            for q in range(3):
                                P.I("pe", "transpose", [srcb.T, ident.T], [Tbank[4]], out=bankb(4)[:, (i * 3 + q) * 128:(i * 3 + q + 1) * 128],
                                                                                               in_=srcb[:, q * 128:(q + 1) * 128], identity=ident[:])
                        P.I("act", "activation", [Tbank[4]], [qkT.T], out=qkT[:].rearrange("p a i -> p (a i)"), in_=bankb(4)[:, 0:768], func=AF.Copy)
                        P.I("pool", "tensor_tensor", [qkT.T, QDF.T], [qdf.T], out=qdf[:], in0=qkT[:, 0:3, :], in1=QDF[:], op=ALU.mult)
                        P.I("pool", "tensor_tensor", [qkT.T, QDB.T], [qdb.T], out=qdb[:], in0=qkT[:, 0:3, :], in1=QDB[:], op=ALU.mult)
                    if 'r2' in DBG2:
                        for par in range(2):
                            pS, TpS = CUR["gen"].next()
                            for q_ in range(3):
                                P.I("pe", "matmul", [qkT.T], [TpS], pS[:, q_ * 128:(q_ + 1) * 128], lhsT=qkT[par * 64:(par + 1) * 64, 3 + q_, :],
                                    rhs=qkT[par * 64:(par + 1) * 64, q_, :], start=True, stop=True)
                            P.I("dve", "tensor_tensor", [TpS, DT.T], [PT.T], out=PT[:, par * 3:(par + 1) * 3, :], in0=pS[:, 0:384].rearrange("p (h i) -> p h i", h=3),
                                in1=DT[:, par * 3:(par + 1) * 3, :], op=ALU.mult)
                    if 'r3' in DBG2:
                        pY, TpY = CUR["gen"].next()
                        for h in range(6):
                            q_, par = h // 2, h % 2
                            sl = slice(par * 64, (par + 1) * 64)
                            P.I("pe", "matmul", [PT.T, vt.T], [TpY], pY[:, h * 64:(h + 1) * 64], lhsT=PT[:, par * 3 + q_, :], rhs=vt[:, h * 64:(h + 1) * 64], start=True, stop=False)
                            P.I("pe", "matmul", [qdf.T, Sfb.T], [TpY], pY[:, h * 64:(h + 1) * 64], lhsT=qdf[sl, q_, :], rhs=Sfb[sl, q_, :], start=False, stop=False)
                            P.I("pe", "matmul", [qdb.T, Tsb[n]], [TpY], pY[:, h * 64:(h + 1) * 64], lhsT=qdb[sl, q_, :], rhs=sbst_v[sl, n, q_, :], start=False, stop=True)
                    if 'r4' in DBG2:
                        kv_update(kdf, Rf, CDF, n, Sfb[:], Sfb.T, CUR["gen"], boundary=((n + 1) % SEGC == 0))
                    if 'r5' in DBG2:
                        s2 = CUR["st"].next()
                        Y3 = pY[:, 0:384].rearrange("p (h d) -> p h d", h=6)
                        P.I("dve", "tensor_reduce", [TpY], [s2.T], out=s2[:, 0:6], in_=Y3, axis=AX.X, op=ALU.add)
                        P.I("act", "activation", [TpY], [ysq.T], out=ysq[:], in_=pY[:, 0:384], func=AF.Square)
                        s3 = CUR["st"].next()
                        P.I("dve", "tensor_reduce", [ysq.T], [s3.T], out=s3[:, 0:6], in_=ysq[:].rearrange("p (h d) -> p h d", h=6), axis=AX.X, op=ALU.add)
                        P.I("dve", "tensor_scalar", [s2.T], [s2.T], out=s2[:, 0:6], in0=s2[:, 0:6], scalar1=1.0 / 64, scalar2=None, op0=ALU.mult)
                        s4 = CUR["st"].next()
                        P.I("dve", "tensor_tensor", [s2.T], [s4.T], out=s4[:, 0:6], in0=s2[:, 0:6], in1=s2[:, 0:6], op=ALU.mult)
                        P.I("dve", "scalar_tensor_tensor", [s3.T, s4.T], [s3.T], out=s3[:, 0:6], in0=s3[:, 0:6], scalar=1.0 / 64, in1=s4[:, 0:6], op0=ALU.mult, op1=ALU.subtract)
                        P.I("dve", "tensor_scalar", [s3.T], [s3.T], out=s3[:, 0:6], in0=s3[:, 0:6], scalar1=EPS, scalar2=None, op0=ALU.add)
                        P.I("pool", "tensor_tensor", [s3.T, cmh.T], [s4.T], out=s4[:, 0:6], in0=s3[:, 0:6], in1=cmh[:, 0:6], op=ALU.pow)
                        P.I("dve", "tensor_tensor", [TpY, s2.T], [yc_.T], out=yc_[:].rearrange("p (h d) -> p h d", h=6), in0=Y3, in1=s2[:, 0:6].unsqueeze(2).to_broadcast([128, 6, 64]),
                                                              op=ALU.subtract)
                        P.I("dve", "tensor_tensor", [yc_.T, s4.T], [yn_.T], out=yn_[:].rearrange("p (h d) -> p h d", h=6), in0=yc_[:].rearrange("p (h d) -> p h d", h=6),
                                                              in1=s4[:, 0:6].unsqueeze(2).to_broadcast([128, 6, 64]), op=ALU.mult)
                        P.I("pool", "tensor_tensor", [yn_.T, sg.T], [mx.T], out=mx[:, 256:640], in0=yn_[:], in1=sg[:], op=ALU.mult)
            m = n - LAG
            if m >= 0:
                use("L")
                if 'attn' in DBG:
                    attention(l, m)
                if 'out' in DBG:
                    out_stage(l, m, xcs.pop(m), Gseg[m // SEGC], dst, Tdst)
            use(None)
            P.flush()
            mf = n - (LAG - 1)
            if 0 <= mf < NCH and mf % SEGC == 0:
                Gseg[mf // SEGC] = load_G(l, 0, mf // SEGC)

    class SubRing:
        def __init__(self, bufs):
            self.b = bufs; self.n = len(bufs); self.i = -1

        def next(self):
            self.i += 1
            return self.b[self.i % self.n]

    def phase_B(l, src, Tsrc, dst, Tdst):
        P.dma("sp", CW[:], cw_in[l], [], [CW.T], ND("CW"))
        xN = SubRing(xcr.b[0:2]); xD = SubRing(xcr.b[2:3])
        dxN = dx[0:2]; dxD = dx[2:3]
        gB = Gen([1, 2, 3, 4, 5, 6, 7])
        hbs = {}
        normed = [-1]
        Gseg = {}

        def do_norm(c):
            b, half = c // 2, c % 2
            seg = c // SEGC
            if half == 0:
                hb = HB.next()
                hbs[b] = hb
                if b == 0:
                    P.I("pool", "memset", [], [hb.T], hb[:, :, 0:1], 0.0)
                if b == NB - 1:
                    P.I("pool", "memset", [], [hb.T], hb[:, :, 257:258], 0.0)
            hb = hbs[b]
            xc = load_x(src, Tsrc, c, xN, dxN)
            o = 1 + half * 128
            norm_T(xc, seg, A2f, fm[2], lambda kc: hb[:, kc, o:o + 128], hb.T)
            if half == 0 and b > 0:
                pv_ = hbs[b - 1]
                if c % SEGC == 0:
                    P.I("dve", "tensor_scalar", [hb.T, flag.T], [pv_.T], out=pv_[:, :, 257:258], in0=hb[:, :, 1:2], scalar1=flag[:, 0:1], scalar2=None, op0=ALU.mult)
                else:
                    P.I("pool", "tensor_copy", [hb.T], [pv_.T], out=pv_[:, :, 257:258], in_=hb[:, :, 1:2])
            normed[0] = c

        def halo_fwd(b):
            hb = hbs[b]; nx = hbs[b + 1]
            c = 2 * b + 1
            if (c + 1) % SEGC == 0:
                P.I("dve", "tensor_scalar", [hb.T, flag.T], [nx.T], out=nx[:, :, 0:1], in0=hb[:, :, 256:257], scalar1=flag[:, 0:1], scalar2=None, op0=ALU.mult)
            else:
                P.I("pool", "tensor_copy", [hb.T], [nx.T], out=nx[:, :, 0:1], in_=hb[:, :, 256:257])

        def up_pairs(b, f0, f1):
            hb = hbs[b]
            aT = actT.at(b)
            for fc in range(f0, f1):
                tl = []
                for which in range(2):
                    fcc = fc + which * NFC
                    pb, Tpb = gB.next()
                    for kc in range(8):
                        P.I("pe", "matmul", [Twup[kc], hb.T], [Tpb], pb[:, 0:258], lhsT=wup_v[:, kc, fcc * 128:(fcc + 1) * 128], rhs=hb[:, kc, :],
                            start=(kc == 0), stop=(kc == 7))
                    tl.append((fcc, pb, Tpb, T0.next(), T1.next(), T2.next()))
                for (fcc, pb, Tpb, t0, t1, t2) in tl:
                    P.I("act", "activation", [Tpb, CW.T], [t0.T], out=t0[:], in_=pb[:, 0:256], func=AF.Identity, scale=CW[:, fcc, 0:1], bias=CW[:, fcc, 3:4])
                for (fcc, pb, Tpb, t0, t1, t2) in tl:
                    P.I("dve", "scalar_tensor_tensor", [Tpb, CW.T, t0.T], [t1.T], out=t1[:], in0=pb[:, 1:257], scalar=CW[:, fcc, 1:2], in1=t0[:],
                        op0=ALU.mult, op1=ALU.add)
                for (fcc, pb, Tpb, t0, t1, t2) in tl:
                    P.I("dve", "scalar_tensor_tensor", [Tpb, CW.T, t1.T], [t2.T], out=t2[:], in0=pb[:, 2:258], scalar=CW[:, fcc, 2:3], in1=t1[:],
                        op0=ALU.mult, op1=ALU.add)
                g_ = gg.next()
                tg, tv = tl[0][5], tl[1][5]
                P.I("act", "activation", [tg.T], [g_.T], out=g_[:], in_=tg[:], func=AF.Gelu_apprx_tanh)
                P.I("pool", "tensor_tensor", [tv.T, g_.T], [aT.T], out=aT[:, fc, :], in0=tv[:], in1=g_[:], op=ALU.mult)

        def down(b):
            aT = actT.at(b)
            for half in range(2):
                c = 2 * b + half
                seg = c // SEGC
                if seg not in Gseg:
                    Gseg[seg] = load_G(l, 1, seg)
                xr = load_x(src, Tsrc, c, xD, dxD)
                post_stage(lambda kc: (aT[:, kc, half * 128:(half + 1) * 128], aT.T), NFC, wdn_v, Twdn, xr, Gseg[seg], c, dst, Tdst, gB)

        do_norm(0); do_norm(1)
        if NCH > 2:
            do_norm(2)
            halo_fwd(0)
        for b in range(NB):
            use("NB")
            if 2 * b + 3 < NCH:
                do_norm(2 * b + 3)
            if 2 * b + 4 < NCH:
                do_norm(2 * b + 4)
                halo_fwd(b + 1)
            use("U")
            up_pairs(b, 0, 4)
            if b > 0:
                down(b - 1)
            up_pairs(b, 4, NFC)
            use(None)
            P.flush()
        down(NB - 1)

    cur, Tcur = x_in, None
    for l in range(depth):
        P.barrier()
        load_A_weights(l)
        layer_setup(l)
        P.barrier()
        if stop_after == ("setup", l):
            break
        phase_A0(l, cur, Tcur)
        if stop_after == ("A0", l):
            break
        phase_A1(l, cur, Tcur, xa, Txa)
        if stop_after == ("A1", l):
            cur, Tcur = xa, Txa
            break
        P.barrier()
        load_B_weights(l)
        last = (l == depth - 1)
        dst, Tdst = (y_out, Ty) if last else (xb, Txb)
        phase_B(l, xa, Txa, dst, Tdst)
        cur, Tcur = dst, Tdst
    if cur is not y_out:
        for n in range(NCH):
            xc = load_x(cur, Tcur, n, xcr, dx)
            P.dma("sp", y_out[n * 128:(n + 1) * 128, :], xc[:], [xc.T], [Ty[n]], dcp[xcr.i % xcr.n])
    P.wait_all("sp", Ty)
    counts = P.emit()
    es.close()
    return nc, counts


def _rope_tables(pos, rot_dim, theta):
    half = rot_dim // 2
    freqs = np.exp(-math.log(theta) * np.arange(half, dtype=np.float32) * np.float32(2.0) / np.float32(rot_dim)).astype(np.float32)
    ang = pos.astype(np.float32)[:, None] * freqs[None, :]
    return np.cos(ang).astype(np.float32), np.sin(ang).astype(np.float32)


def core_tables(NCH, SEGC, is_prompt):
    n = np.arange(NCH)
    if is_prompt:
        base = n * 128
    else:
        base = (n % SEGC) * 128
    pos = (base[None, :] + np.arange(128)[:, None]).reshape(-1)
    rc_, rs_ = _rope_tables(pos, 64, 10000.0)
    ac_, as__ = _rope_tables(pos, 16, 500000.0)
    f = lambda a, d: a.reshape(128, NCH, d)
    rope = np.concatenate([f(rc_, 32), f(rs_, 32), f(-rs_, 32), f(ac_, 8), f(as__, 8), f(-as__, 8)], 2)
    return dict(rope=np.ascontiguousarray(rope.transpose(1, 0, 2)).astype(np.float32),
                flag=np.full((128, 1), 1.0 if is_prompt else 0.0, np.float32))


def const_inputs():
    j = np.arange(128, dtype=np.float32)[:, None]
    i = np.arange(128, dtype=np.float32)[None, :]
    dpos = np.maximum(i - j, 0); dneg = np.maximum(j - i, 0)
    mge = (i >= j).astype(np.float32); mlt = (i < j).astype(np.float32)
    io1 = np.broadcast_to(i + 1, (128, 128)); io2 = np.broadcast_to(128 - i, (128, 128))
    cm = np.stack([dpos, dneg, mge, mlt, io1, io2], 1).astype(np.float32)
    tri = np.stack([(j >= i).astype(np.float32) * np.ones((128, 128), np.float32), (j <= i).astype(np.float32) * np.ones((128, 128), np.float32)], 1)
    jv = np.concatenate([127 - j, j], 1).astype(np.float32)
    return dict(ident=np.eye(128, dtype=np.float32), cmats=np.ascontiguousarray(cm), tri=np.ascontiguousarray(tri), jv=jv)


def weight_inputs(w_ada, b_ada, norm_pre_mix, norm_post_mix, norm_pre_ffn, norm_post_ffn, w_in, sgu_norm, sgu_w, sgu_b,
                  ret_decay_fwd, ret_decay_bwd, attn_sink, w_out, w_up, conv_w, conv_b, w_down):
    L = w_in.shape[0]
    perm = np.arange(IN_DIM)
    hq = [0, 3, 1, 4, 2, 5]
    perm[2048:2432] = np.concatenate([2048 + h * 64 + np.arange(64) for h in hq])
    w_in_p = np.ascontiguousarray(w_in[:, :, perm])
    cw = np.concatenate([conv_w, conv_b[:, None, :]], 1)
    cw = np.ascontiguousarray(cw.reshape(L, 4, 2 * NFC, 128).transpose(0, 3, 2, 1))
    gfm = np.stack([norm_pre_mix, norm_pre_ffn], 1).reshape(L, 2, 8, 128).transpose(0, 3, 1, 2)
    gpost = np.concatenate([norm_post_mix, norm_post_ffn], 1)
    sgu_wT = np.ascontiguousarray(sgu_w.transpose(0, 3, 1, 2))
    sgu_bT = np.ascontiguousarray(sgu_b.transpose(0, 2, 1))
    dec6 = np.concatenate([ret_decay_fwd, ret_decay_bwd], 1)
    dec6 = np.broadcast_to(dec6[:, None, :], (L, 128, 12))
    decP = np.zeros((L, 128, 6), np.float32)
    decP[:, 0:64, 0:3] = ret_decay_fwd[:, None, 0::2]; decP[:, 64:128, 0:3] = ret_decay_fwd[:, None, 1::2]
    decP[:, 0:64, 3:6] = ret_decay_bwd[:, None, 0::2]; decP[:, 64:128, 3:6] = ret_decay_bwd[:, None, 1::2]
    sink6 = np.ascontiguousarray(np.broadcast_to(attn_sink[:, None, :], (L, 128, 6)))
    c = np.ascontiguousarray
    return dict(w_ada=c(w_ada), b_ada=c(b_ada), w_in=w_in_p, w_out=c(w_out), w_up=c(w_up), w_down=c(w_down), cw=cw,
                gfm=c(gfm.astype(np.float32)), gpost=c(gpost.astype(np.float32)), sgu_wT=sgu_wT, sgu_bT=sgu_bT, sgu_n=c(sgu_norm),
                dec18=c(np.concatenate([dec6, decP], 2).astype(np.float32)), sink6=sink6.astype(np.float32))


def core_c(c_rows):
    ns = c_rows.shape[0]
    return np.ascontiguousarray(c_rows.reshape(ns, 8, 128).transpose(2, 1, 0)).astype(np.float32)


_CACHE = {}


def kernel(x_prompt, x_sample, c_prompt, c_sample, w_ada, b_ada, norm_pre_mix, norm_post_mix, norm_pre_ffn, norm_post_ffn,
           w_in, sgu_norm, sgu_w, sgu_b, ret_decay_fwd, ret_decay_bwd, attn_sink, w_out, w_up, conv_w, conv_b, w_down):
    NCH, SEGC = 64, 16
    f = lambda a: np.asarray(a, dtype=np.float32)
    x_prompt, x_sample, c_prompt, c_sample = f(x_prompt), f(x_sample), f(c_prompt), f(c_sample)
    wts = weight_inputs(*[f(a) for a in (w_ada, b_ada, norm_pre_mix, norm_post_mix, norm_pre_ffn, norm_post_ffn, w_in, sgu_norm, sgu_w, sgu_b,
                                           ret_decay_fwd, ret_decay_bwd, attn_sink, w_out, w_up, conv_w, conv_b, w_down)])
    consts = const_inputs()
    if "nc" not in _CACHE:
        _CACHE["nc"] = build(NCH, SEGC)[0]
    nc = _CACHE["nc"]
    in_maps = []
    for core in range(8):
        if core < 2:
            xs = x_prompt[core]
            cr = np.repeat(c_prompt[core:core + 1], 4, 0)
            tb = core_tables(NCH, SEGC, True)
        else:
            k = min(core - 2, 3)
            xs = x_sample[4 * k:4 * k + 4].reshape(NCH * 128, D)
            cr = c_sample[4 * k:4 * k + 4]
            tb = core_tables(NCH, SEGC, False)
        m = dict(x=np.ascontiguousarray(xs), cT=core_c(cr))
        m.update(tb); m.update(consts); m.update(wts)
        in_maps.append(m)
    res = run_bass_kernel_spmd(nc, in_maps, core_ids=list(range(8)))
    r = res.results
    y_prompt = np.stack([r[0]["y"], r[1]["y"]], 0).astype(np.float32)
    y_sample = np.concatenate([r[2 + k]["y"].reshape(4, 2048, D) for k in range(4)], 0).astype(np.float32)
    return (y_prompt, y_sample)
```

```python
import math
import os
import numpy as np
DBG = os.environ.get("KDBG", "sgu,ret,ap,attn,out").split(",")
DBG2 = os.environ.get("KDBG2", "r1,r2,r3,r4,r5").split(",")
from contextlib import ExitStack
import concourse.bass as bass
import concourse.mybir as mybir
from concourse.bass_utils import run_bass_kernel_spmd

F32 = mybir.dt.float32
BF16 = mybir.dt.bfloat16
AF = mybir.ActivationFunctionType
ALU = mybir.AluOpType
AX = mybir.AxisListType

SAME_ENGINE_SYNC = bool(int(os.environ.get("KSES", "1")))
EPS = 1e-6
D = 1024
IN_DIM = 2688
DFF = 2816
NFC = 22
DEPTH = 2


class Tile:
    __slots__ = ("name", "w", "r", "rd", "excl")

    def __init__(self, name="", excl=False):
        self.name = name
        self.w = None
        self.r = {}
        self.rd = []
        self.excl = excl


class Op:
    __slots__ = ("eng", "fn", "deps", "inc", "val", "dsem")

    def __init__(self, eng, fn, dsem):
        self.eng = eng
        self.fn = fn
        self.dsem = dsem
        self.deps = set()
        self.inc = dsem is not None
        self.val = 0


class DSem:
    __slots__ = ("h", "cnt")

    def __init__(self, h):
        self.h = h
        self.cnt = 0


class Prog:
    ENGS = ("pe", "act", "dve", "pool", "sp")

    def __init__(self, nc, es):
        self.nc = nc
        self.es = es
        self.ops = {e: [] for e in self.ENGS}
        self.esem = {e: es.enter_context(nc.semaphore("s_" + e)) for e in self.ENGS}
        self.n = 0
        self.bar_idx = {}
        self.region = None
        self.cur = None
        self.streams = {}

    def sb(self, shape, dt, name=None):
        self.n += 1
        return self.es.enter_context(self.nc.sbuf_tensor(f"sb{self.n}_{name or ''}", list(shape), dt))

    def ps(self, shape, dt=F32, name=None):
        self.n += 1
        return self.es.enter_context(self.nc.psum_tensor(name or f"ps{self.n}", list(shape), dt))

    def dsem(self):
        self.n += 1
        return DSem(self.es.enter_context(self.nc.semaphore(f"ds{self.n}")))

    def set_stream(self, name):
        self.cur = None if name is None else self.streams.setdefault(name, [])

    def flush(self):
        self.cur = None
        lists = [v for v in self.streams.values() if v]
        if os.environ.get("KSEQ"):
            for li in lists:
                for r_ in li:
                    self._op(*r_)
            self.streams = {}
            return
        idx = [0] * len(lists)
        while True:
            best = None
            for i, li in enumerate(lists):
                if idx[i] < len(li):
                    f = (idx[i] + 1) / len(li)
                    if best is None or f < best[0]:
                        best = (f, i)
            if best is None:
                break
            i = best[1]
            self._op(*lists[i][idx[i]])
            idx[i] += 1
        self.streams = {}

    def op(self, eng, fn, reads=(), writes=(), dsem=None):
        if self.cur is not None:
            self.cur.append((eng, fn, list(reads), list(writes), dsem))
            return None
        return self._op(eng, fn, reads, writes, dsem)

    def _op(self, eng, fn, reads=(), writes=(), dsem=None):
        o = Op(eng, fn, dsem)
        deps = o.deps
        for t in reads:
            if t.w is not None:
                deps.add(t.w)
            if t.excl:
                for e_, o_ in t.r.items():
                    if e_ != eng:
                        deps.add(o_)
        for t in writes:
            if t.w is not None:
                deps.add(t.w)
            deps.update(t.r.values())
            deps.update(t.rd)
        for t in reads:
            if dsem is not None:
                t.rd.append(o)
            else:
                t.r[eng] = o
        for t in writes:
            t.w = o
            t.r = {}
            t.rd = []
        if dsem is not None:
            dsem.cnt += 16
            o.val = dsem.cnt
        self.ops[eng].append(o)
        return o

    def I(self, eng, meth, reads, writes, *a, **k):
        return self.op(eng, (meth, a, k), reads, writes)

    def dma(self, eng, out, in_, reads, writes, dsem=None, slow=False):
        if dsem is None:
            dsem = self.dsem()
        k = dict(out=out, in_=in_)
        if slow:
            k["allow_slow_non_contiguous"] = True
        return self.op(eng, ("dma_start", (), k), reads, writes, dsem)

    def barrier(self):
        lasts = []
        for e in self.ENGS:
            comp = [o for o in self.ops[e] if o.dsem is None and o.fn is not None]
            if comp:
                lasts.append(comp[-1])
        dmas = [o for e in self.ENGS for o in self.ops[e][self.bar_idx.get(e, 0):] if o.dsem is not None]
        for e in self.ENGS:
            self.bar_idx[e] = len(self.ops[e])
        for e in self.ENGS:
            o = Op(e, None, None)
            o.deps.update(lasts)
            o.deps.update(dmas)
            self.ops[e].append(o)

    def wait_all(self, eng, tiles):
        o = Op(eng, None, None)
        for t in tiles:
            if t.w is not None:
                o.deps.add(t.w)
        self.ops[eng].append(o)

    def emit(self):
        nc = self.nc
        for e in self.ENGS:
            for o in self.ops[e]:
                for d in o.deps:
                    if d.dsem is not None:
                        continue
                    if d.eng != o.eng or o.dsem is not None:
                        d.inc = True
                    elif SAME_ENGINE_SYNC and d.eng != "pe":
                        d.inc = True
        for e in self.ENGS:
            c = 0
            for o in self.ops[e]:
                if o.dsem is None:
                    if o.inc:
                        c += 1
                    o.val = c
        counts = {}
        with nc.Block() as block:
            def run(e):
                def body(h):
                    waited = {}
                    nw = 0
                    for o in self.ops[e]:
                        need = {}
                        for d in o.deps:
                            if d.dsem is not None:
                                key = d.dsem
                                sem = d.dsem.h
                            else:
                                if d.eng == e and o.dsem is None:
                                    if e == "pe" or not SAME_ENGINE_SYNC:
                                        continue
                                key = d.eng
                                sem = self.esem[d.eng]
                            if need.get(key, (None, 0))[1] < d.val:
                                need[key] = (sem, d.val)
                        for key, (sem, val) in need.items():
                            if waited.get(key, 0) < val:
                                h.wait_ge(sem, val)
                                waited[key] = val
                                nw += 1
                        if o.fn is None:
                            continue
                        meth, a, k = o.fn
                        inst = getattr(h, meth)(*a, **k)
                        if o.dsem is not None:
                            inst.then_inc(o.dsem.h, 16)
                        elif o.inc:
                            inst.then_inc(self.esem[e], 1)
                    counts[e] = (len(self.ops[e]), nw)
                return body
            block.tensor(run("pe"))
            block.scalar(run("act"))
            block.vector(run("dve"))
            block.gpsimd(run("pool"))
            block.sync(run("sp"))
        return counts


class Region:
    def __init__(self, big, start, limit):
        self.big = big
        self.off = start
        self.limit = limit

    def alloc(self, shape, dt):
        n = 1
        for s in shape[1:]:
            n *= s
        n16 = n * (2 if dt == F32 else 1)
        self.off = (self.off + 15) // 16 * 16
        ap = self.big[0:shape[0], self.off:self.off + n16]
        self.off += n16
        assert self.off <= self.limit, ("region overflow", self.off, self.limit)
        if dt == F32:
            ap = ap.bitcast(F32)
        if len(shape) == 3:
            ap = ap.rearrange("p (a b) -> p a b", a=shape[1])
        elif len(shape) == 4:
            ap = ap.rearrange("p (a b c) -> p a b c", a=shape[1], b=shape[2])
        return ap


class Buf:
    def __init__(self, P, shape, dt, name=None):
        if getattr(P, "region", None) is not None:
            self.t = P.region.alloc(shape, dt)
        else:
            self.t = P.sb(shape, dt, name)
        self.T = Tile(name or "")

    def __getitem__(self, k):
        return self.t[k]


class Ring:
    def __init__(self, P, n, shape, dt, name):
        self.b = [Buf(P, shape, dt, f"{name}{i}") for i in range(n)]
        self.n = n
        self.i = -1

    def next(self):
        self.i += 1
        return self.b[self.i % self.n]

    def at(self, k):
        return self.b[k % self.n]


def build(NCH, SEGC, depth=DEPTH, stop_after=None):
    NSEG = NCH // SEGC
    NT = NCH * 128
    NB = NCH // 2
    nc = bass.Bass("TRN2", target_bir_lowering=False)

    def din(name, shape):
        return nc.dram_tensor(name, list(shape), F32, kind="ExternalInput").ap()

    x_in = din("x", [NT, D])
    cT_in = din("cT", [128, 8, NSEG])
    flag_in = din("flag", [128, 1])
    rope_in = din("rope", [NCH, 128, 120])
    ident_in = din("ident", [128, 128])
    cm_in = din("cmats", [128, 6, 128])
    tri_in = din("tri", [128, 2, 128])
    jv_in = din("jv", [128, 2])
    w_ada = din("w_ada", [depth, D, 6 * D]); b_ada = din("b_ada", [depth, 6 * D])
    w_in = din("w_in", [depth, D, IN_DIM]); w_out = din("w_out", [depth, D, D])
    w_up = din("w_up", [depth, D, 2 * DFF]); w_down = din("w_down", [depth, DFF, D])
    cw_in = din("cw", [depth, 128, 2 * NFC, 4])
    gfm_in = din("gfm", [depth, 128, 2, 8])
    gpost_in = din("gpost", [depth, 2 * D])
    sguw_in = din("sgu_wT", [depth, 128, 4, 128]); sgub_in = din("sgu_bT", [depth, 128, 4])
    sgun_in = din("sgu_n", [depth, 256])
    dec18_in = din("dec18", [depth, 128, 18])
    sink_in = din("sink6", [depth, 128, 6])
    y_out = nc.dram_tensor("y", [NT, D], F32, kind="ExternalOutput").ap()
    xa = nc.dram_tensor("xa", [NT, D], F32, kind="Internal").ap()
    xb = nc.dram_tensor("xb", [NT, D], F32, kind="Internal").ap()
    modD = nc.dram_tensor("modD", [depth, NSEG, 6 * D], F32, kind="Internal").ap()
    gD = nc.dram_tensor("gD", [depth, 2, NSEG, D], F32, kind="Internal").ap()

    es = ExitStack()
    P = Prog(nc, es)
    Txa = [Tile(f"xa{i}") for i in range(NCH)]
    Txb = [Tile(f"xb{i}") for i in range(NCH)]
    Ty = [Tile(f"y{i}") for i in range(NCH)]
    TmodD = [None] * depth
    TgD = [None] * depth

    PS = P.ps([128, 4096], F32, "PSALL")
    PSb = PS.bitcast(BF16)
    Tbank = [Tile(f"bank{i}", excl=True) for i in range(8)]

    def bank(i):
        return PS[:, 512 * i:512 * (i + 1)]

    def bankb(i):
        return PSb[:, 1024 * i:1024 * (i + 1)]

    class Gen:
        def __init__(self, ids):
            self.ids = ids
            self.i = -1

        def next(self):
            self.i += 1
            b = self.ids[self.i % len(self.ids)]
            return bank(b), Tbank[b]

    ident = Buf(P, [128, 128], BF16, "ident")
    tri = Buf(P, [128, 2, 128], BF16, "tri")
    trif = Buf(P, [128, 2, 128], BF16, "trif")
    jv = Buf(P, [128, 2], F32, "jv")
    flag = Buf(P, [128, 1], F32, "flag")
    ropeR = Ring(P, 3, [128, 120], F32, "rope")
    drope = [P.dsem() for _ in range(3)]
    cmh = Buf(P, [128, 8], F32, "cmh")
    siluT = Buf(P, [128, 8, NSEG], BF16, "siluT")
    cTf = Buf(P, [128, 8, NSEG], F32, "cTf")

    dsems = []

    def DS():
        d = P.dsem()
        dsems.append(d)
        return d

    _nd = {}

    def ND(name):
        if name not in _nd:
            _nd[name] = DS()
        return _nd[name]
    for (b, src, eng) in [(ident, ident_in, "pool"), (tri, tri_in, "pool"), (jv, jv_in, "sp"),
                          (flag, flag_in, "sp"), (cTf, cT_in, "sp")]:
        P.dma(eng, b[:], src, [], [b.T], ND("c_" + b.T.name))
    P.I("pool", "memset", [], [cmh.T], cmh[:], -0.5)
    P.I("act", "activation", [cTf.T], [siluT.T], out=siluT[:], in_=cTf[:], func=AF.Silu)
    P.I("dve", "tensor_scalar", [tri.T, flag.T], [trif.T], out=trif[:], in0=tri[:], scalar1=flag[:, 0:1], scalar2=None, op0=ALU.mult)

    ARENA = 8 * 2 * DFF + NFC * D
    BIGN = ARENA + 21120
    W = P.sb([128, BIGN], BF16, "arena")
    win_v = W[:, 0:8 * IN_DIM].rearrange("p (k n) -> p k n", k=8)
    wout_v = W[:, 8 * IN_DIM:8 * IN_DIM + 8 * D].rearrange("p (k n) -> p k n", k=8)
    SB0 = 8 * IN_DIM + 8 * D
    sbst_v = W[:, SB0:SB0 + NCH * 192].rearrange("p (c q e) -> p c q e", c=NCH, q=3)
    wup_v = W[:, 0:8 * 2 * DFF].rearrange("p (k n) -> p k n", k=8)
    wdn_v = W[:, 8 * 2 * DFF:ARENA].rearrange("p (k n) -> p k n", k=NFC)
    Twin = [Tile(f"win{k}") for k in range(8)]
    Twout = [Tile(f"wout{k}") for k in range(8)]
    Tsb = [Tile(f"sbst{c}") for c in range(NCH)]
    Twup = [Tile(f"wup{k}") for k in range(8)]
    Twdn = [Tile(f"wdn{k}") for k in range(NFC)]
    A_tiles = Twin + Twout + Tsb
    B_tiles = Twup + Twdn
    dWin = [DS() for _ in range(8)]; dWout = [DS() for _ in range(8)]; dWup = [DS() for _ in range(8)]; dWdn = [DS() for _ in range(NFC)]

    regA = Region(W, SB0 + NCH * 192, BIGN)
    P.region = regA
    d18 = Buf(P, [128, 18], F32, "d18"); e18 = Buf(P, [128, 18], F32, "e18"); lg18 = Buf(P, [128, 18], F32, "lg18")
    DT = Buf(P, [128, 6, 128], F32, "DT")
    QDF = Buf(P, [128, 3, 128], F32, "QDF"); QDB = Buf(P, [128, 3, 128], F32, "QDB")
    KDF = Buf(P, [128, 6], F32, "KDF"); KDB = Buf(P, [128, 6], F32, "KDB")
    CDF = Buf(P, [128, 3], F32, "CDF"); CDB = Buf(P, [128, 3], F32, "CDB")
    ESK = Buf(P, [128, 6], F32, "ESK")
    SN = Buf(P, [128, 256], F32, "SN")
    WsT = Buf(P, [128, 4, 128], BF16, "WsT"); SGB = Buf(P, [128, 4], F32, "SGB")
    P.region = None
    CW = Buf(P, [128, 2 * NFC, 4], F32, "CW")
    gfm = Buf(P, [128, 2, 8], F32, "gfm")
    fm = [Buf(P, [128, NSEG, 8], F32, f"fm{i}") for i in range(4)]
    A1f = Buf(P, [128, NSEG, 8], F32, "A1f"); A2f = Buf(P, [128, NSEG, 8], F32, "A2f")
    Gt = Ring(P, 1, [128, D], F32, "Gt")
    dl = [DS() for _ in range(4)]
    dmb = [DS() for _ in range(2)]; dgp = [DS() for _ in range(2)]; dgb = [DS() for _ in range(2)]

    xcr = Ring(P, 3, [128, D], F32, "xc")
    st = Ring(P, 16, [128, 8], F32, "st")
    xn = Ring(P, 1, [128, D], BF16, "xn")
    xnew = Ring(P, 2, [128, D], F32, "xnew")
    P.region = regA
    hTr = Ring(P, 2, [128, 8, 128], BF16, "hT")
    gA = Gen([2, 3, 5, 6])
    st_rings = {None: st, "SC": Ring(P, 8, [128, 8], F32, "stSC"), "R": Ring(P, 8, [128, 8], F32, "stR"), "L": Ring(P, 8, [128, 8], F32, "stL"),
                "N": Ring(P, 8, [128, 8], F32, "stN"), "U": st}
    gens = {None: gA, "SC": Gen([2]), "R": Gen([3, 5]), "L": Gen([6]), "N": Gen([0]), "U": gA}
    gLpost = Gen([6, 7])
    _rg = P.region
    P.region = None
    st_rings["K"] = st
    gens["K"] = gA
    st_rings["NB"] = Ring(P, 8, [128, 8], F32, "stNB")
    gens["NB"] = Gen([0])
    P.region = _rg
    CUR = {"st": st, "gen": gA}
    Tb1a = Tbank[1]; Tb1b = Tbank[1]

    def use(name):
        P.set_stream(name)
        CUR["st"] = st_rings[name]
        CUR["gen"] = gens[name]

    u_b = Buf(P, [128, 256], BF16, "u"); vg = Buf(P, [128, 256], F32, "vg"); sq = Buf(P, [128, 256], F32, "sq")
    vn = Buf(P, [128, 256], F32, "vn"); vn2 = Buf(P, [128, 256], BF16, "vn2"); gt_ = Buf(P, [128, 256], F32, "gt")
    r1 = Buf(P, [128, 384], F32, "r1"); r2 = Buf(P, [128, 384], F32, "r2"); rsum = Buf(P, [128, 384], F32, "rsum")
    qr = Buf(P, [128, 384], BF16, "qr"); kr = Buf(P, [128, 384], BF16, "kr")
    kdf = Buf(P, [128, 384], BF16, "kdf"); kdb = Buf(P, [128, 384], BF16, "kdb")
    vt = Buf(P, [128, 384], BF16, "vt"); sg = Buf(P, [128, 384], F32, "sg")
    qkT = Buf(P, [128, 6, 128], BF16, "qkT")
    qdf = Buf(P, [128, 3, 128], BF16, "qdf"); qdb = Buf(P, [128, 3, 128], BF16, "qdb")
    PT = Buf(P, [128, 6, 128], BF16, "PT")
    Rf = Buf(P, [128, 3, 64], F32, "Rf"); Rb = Buf(P, [128, 3, 64], F32, "Rb"); Rt = Buf(P, [128, 3, 64], F32, "Rt")
    Sfb = Buf(P, [128, 3, 64], BF16, "Sfb")
    ysq = Buf(P, [128, 384], F32, "ysq"); yc_ = Buf(P, [128, 384], F32, "yc"); yn_ = Buf(P, [128, 384], F32, "yn")
    cqb = Buf(P, [128, 384], BF16, "cqb"); ckb = Buf(P, [128, 128], BF16, "ckb")
    a1 = Buf(P, [128, 6, 16], F32, "a1"); a2 = Buf(P, [128, 6, 16], F32, "a2")
    cqT = Ring(P, 3, [128, 3, 128], BF16, "cqT"); ckT = Ring(P, 4, [128, 128], BF16, "ckT")
    cvb = Ring(P, 4, [128, 2, 65], BF16, "cvb")
    Eb = Ring(P, 6, [128, 3, 128], BF16, "Eb")
    mix = Ring(P, 3, [128, D], BF16, "mix")
    mixT = Buf(P, [128, 8, 128], BF16, "mixT")
    cm = Buf(P, [128, 6, 128], F32, "cmats")
    ta = Buf(P, [128, 128], F32, "ta"); tb = Buf(P, [128, 128], F32, "tb")
    wad = Ring(P, 2, [128, 8, 256], BF16, "wad")
    badb = Ring(P, 2, [NSEG, 256], F32, "badb")
    mblk = Ring(P, 2, [NSEG, 256], F32, "mblk")
    gpb = Ring(P, 2, [NSEG, 256], F32, "gpb")
    gblk = Ring(P, 2, [NSEG, 256], F32, "gblk")
    print("regA end", regA.off, BIGN)
    regB = Region(W, ARENA, BIGN)
    P.region = regB
    HB = Ring(P, 3, [128, 8, 258], BF16, "HB")
    T0 = Ring(P, 2, [128, 256], F32, "T0"); T1 = Ring(P, 2, [128, 256], F32, "T1"); T2 = Ring(P, 2, [128, 256], F32, "T2")
    gg = Ring(P, 1, [128, 256], F32, "gg")
    actT = Ring(P, 2, [128, NFC, 256], BF16, "actT")
    print("regB end", regB.off, BIGN)
    P.region = None
    dx = [DS() for _ in range(4)]
    dst_ = [DS() for _ in range(2)]
    dG = [DS() for _ in range(2)]
    dcp = [DS() for _ in range(4)]


    def rstd_from(ssb, n_inv, out_col):
        sbuf, c = ssb
        obuf, oc = out_col
        t = CUR["st"].next()
        P.I("dve", "tensor_scalar", [sbuf.T], [t.T], out=t[:, 0:1], in0=sbuf[:, c:c + 1], scalar1=n_inv, scalar2=EPS, op0=ALU.mult, op1=ALU.add)
        P.I("pool", "tensor_tensor", [t.T, cmh.T], [obuf.T], out=obuf[:, oc:oc + 1], in0=t[:, 0:1], in1=cmh[:, 0:1], op=ALU.pow)

    def layer_setup(l):
        P.dma("sp", cm[:], cm_in, [], [cm.T], ND("cm"))
        TmD = [Tile(f"modD{l}_{i}") for i in range(24)]
        TgDl = [Tile(f"gD{l}_{i}") for i in range(8)]
        for cb in range(24):
            c0 = cb * 256
            wb = wad.next(); bb = badb.next(); mb = mblk.next()
            P.dma("pool", wb[:], w_ada[l, :, c0:c0 + 256].rearrange("(k p) n -> p k n", p=128), [], [wb.T], dl[cb % 2])
            P.dma("sp", bb[:], b_ada[l:l + 1, c0:c0 + 256].partition_broadcast(NSEG), [], [bb.T], dl[2 + cb % 2])
            pb, Tpb = CUR["gen"].next()
            for kc in range(8):
                P.I("pe", "matmul", [siluT.T, wb.T], [Tpb], pb[0:NSEG, 0:256], lhsT=siluT[:, kc, :], rhs=wb[:, kc, :], start=(kc == 0), stop=(kc == 7))
            P.I("dve", "tensor_tensor", [Tpb, bb.T], [mb.T], out=mb[:], in0=pb[0:NSEG, 0:256], in1=bb[:], op=ALU.add)
            P.dma("sp", modD[l, :, c0:c0 + 256], mb[:], [mb.T], [TmD[cb]], dmb[cb % 2])
            part = cb // 4
            if part in (2, 5):
                gi = 0 if part == 2 else 1
                j = cb % 4
                gp = gpb.next(); gb_ = gblk.next()
                P.dma("sp", gp[:], gpost_in[l:l + 1, gi * D + j * 256:gi * D + (j + 1) * 256].partition_broadcast(NSEG), [], [gp.T], dgp[gpb.i % 2])
                P.I("dve", "tensor_tensor", [mb.T, gp.T], [gb_.T], out=gb_[:], in0=mb[:], in1=gp[:], op=ALU.mult)
                P.dma("sp", gD[l, gi, :, j * 256:(j + 1) * 256], gb_[:], [gb_.T], [TgDl[gi * 4 + j]], dgb[gblk.i % 2])
        TmodD[l] = TmD
        TgD[l] = TgDl
        for i, part in enumerate([0, 1, 3, 4]):
            for s_ in range(NSEG):
                P.dma("sp", fm[i][:, s_, :], modD[l, s_, part * D:(part + 1) * D].rearrange("(k p) -> p k", p=128), TmodD[l][part * 4:part * 4 + 4], [fm[i].T], ND(f"fm{i}"), slow=True)
        P.dma("sp", gfm[:], gfm_in[l], [], [gfm.T], ND("gfm"))
        for (Af, scb, gi) in [(A1f, fm[1], 0), (A2f, fm[3], 1)]:
            P.I("dve", "scalar_tensor_tensor", [scb.T, gfm.T], [Af.T],
                out=Af[:], in0=scb[:], scalar=1.0, in1=gfm[:, gi, :].unsqueeze(1).to_broadcast([128, NSEG, 8]),
                op0=ALU.add, op1=ALU.mult)
        P.dma("sp", d18[:], dec18_in[l], [], [d18.T], ND("d18"))
        P.dma("sp", ESK[:], sink_in[l], [], [ESK.T], ND("ESK"))
        P.dma("sp", SN[:], sgun_in[l:l + 1, :].partition_broadcast(128), [], [SN.T], ND("SN"))
        P.dma("pool", WsT[:], sguw_in[l], [], [WsT.T], ND("WsT"))
        P.dma("sp", SGB[:], sgub_in[l], [], [SGB.T], ND("SGB"))
        P.I("act", "activation", [d18.T], [e18.T], out=e18[:], in_=d18[:], func=AF.Exp, scale=-1.0)
        P.I("dve", "tensor_scalar", [e18.T], [e18.T], out=e18[:], in0=e18[:], scalar1=1.0, scalar2=None, op0=ALU.add)
        P.I("act", "activation", [e18.T], [lg18.T], out=lg18[:], in_=e18[:], func=AF.Ln)
        P.I("dve", "tensor_scalar", [lg18.T], [lg18.T], out=lg18[:], in0=lg18[:], scalar1=-1.0, scalar2=None, op0=ALU.mult)
        P.I("act", "activation", [ESK.T], [ESK.T], out=ESK[:], in_=ESK[:], func=AF.Exp)
        for h in range(6):
            P.I("act", "activation", [cm.T, lg18.T], [ta.T], out=ta[:], in_=cm[:, 0, :], func=AF.Exp, scale=lg18[:, h:h + 1])
            P.I("act", "activation", [cm.T, lg18.T], [tb.T], out=tb[:], in_=cm[:, 1, :], func=AF.Exp, scale=lg18[:, 6 + h:7 + h])
            P.I("dve", "scalar_tensor_tensor", [ta.T, cm.T], [ta.T], out=ta[:], in0=ta[:], scalar=0.125, in1=cm[:, 2, :], op0=ALU.mult, op1=ALU.mult)
            P.I("dve", "scalar_tensor_tensor", [tb.T, cm.T], [tb.T], out=tb[:], in0=tb[:], scalar=0.125, in1=cm[:, 3, :], op0=ALU.mult, op1=ALU.mult)
            P.I("dve", "tensor_tensor", [ta.T, tb.T], [DT.T], out=DT[:, (h % 2) * 3 + h // 2, :], in0=ta[:], in1=tb[:], op=ALU.add)
        for q in range(3):
            P.I("act", "activation", [cm.T, lg18.T], [QDF.T], out=QDF[:, q, :], in_=cm[:, 4, :], func=AF.Exp, scale=lg18[:, 12 + q:13 + q])
            P.I("act", "activation", [cm.T, lg18.T], [QDB.T], out=QDB[:, q, :], in_=cm[:, 5, :], func=AF.Exp, scale=lg18[:, 15 + q:16 + q])
        P.I("act", "activation", [lg18.T, jv.T], [KDF.T], out=KDF[:], in_=lg18[:, 0:6], func=AF.Exp, scale=jv[:, 0:1])
        P.I("act", "activation", [lg18.T, jv.T], [KDB.T], out=KDB[:], in_=lg18[:, 6:12], func=AF.Exp, scale=jv[:, 1:2])
        P.I("dve", "tensor_scalar", [KDF.T], [KDF.T], out=KDF[:], in0=KDF[:], scalar1=0.125, scalar2=None, op0=ALU.mult)
        P.I("dve", "tensor_scalar", [KDB.T], [KDB.T], out=KDB[:], in0=KDB[:], scalar1=0.125, scalar2=None, op0=ALU.mult)
        P.I("act", "activation", [lg18.T], [CDF.T], out=CDF[:], in_=lg18[:, 12:15], func=AF.Exp, scale=128.0)
        P.I("act", "activation", [lg18.T], [CDB.T], out=CDB[:], in_=lg18[:, 15:18], func=AF.Exp, scale=128.0)

    def load_A_weights(l):
        for k in range(8):
            P.dma("pool", win_v[:, k, :], w_in[l, k * 128:(k + 1) * 128, :], [], [Twin[k]] + (B_tiles if k == 0 else []), dWin[k])
        for k in range(8):
            P.dma("pool", wout_v[:, k, :], w_out[l, k * 128:(k + 1) * 128, :], [], [Twout[k]], dWout[k])

    def load_B_weights(l):
        for k in range(8):
            P.dma("pool", wup_v[:, k, :], w_up[l, k * 128:(k + 1) * 128, :], [], [Twup[k]] + (A_tiles if k == 0 else []), dWup[k])
        for k in range(NFC):
            P.dma("pool", wdn_v[:, k, :], w_down[l, k * 128:(k + 1) * 128, :], [], [Twdn[k]], dWdn[k])

    def load_x(src, Tsrc, n, ring, dlist):
        xc = ring.next()
        P.dma("sp", xc[:], src[n * 128:(n + 1) * 128, :], [Tsrc[n]] if Tsrc is not None else [], [xc.T], dlist[ring.i % ring.n])
        return xc

    def norm_front(xc):
        s = CUR["st"].next()
        xb_ = xn.next()
        P.I("act", "activation", [xc.T], [xb_.T, s.T], out=xb_[:], in_=xc[:], func=AF.Square, accum_out=s[:, 0:1])
        rstd_from((s, 0), 1.0 / D, (s, 1))
        P.I("dve", "tensor_scalar", [xc.T, s.T], [xb_.T], out=xb_[:], in0=xc[:], scalar1=s[:, 1:2], scalar2=None, op0=ALU.mult)
        return xb_

    def norm_T(xc, seg, Af, Bf, dst_fn, dstT):
        norm_back(norm_front(xc), seg, Af, Bf, dst_fn, dstT)

    def norm_back(xb_, seg, Af, Bf, dst_fn, dstT):
        for kc in range(8):
            P.I("pe", "transpose", [xb_.T, ident.T], [Tbank[0]], out=bankb(0)[:, kc * 128:(kc + 1) * 128], in_=xb_[:, kc * 128:(kc + 1) * 128], identity=ident[:])
        for kc in range(8):
            P.I("act", "activation", [Tbank[0], Af.T, Bf.T], [dstT], out=dst_fn(kc), in_=bankb(0)[:, kc * 128:(kc + 1) * 128], func=AF.Identity,
                                                              scale=Af[:, seg, kc:kc + 1], bias=Bf[:, seg, kc:kc + 1])

    def proj(hT, wv, Tw, c0, ncol, gen):
        pb, Tpb = gen.next()
        for kc in range(8):
            P.I("pe", "matmul", [hT.T, Tw[kc]], [Tpb], pb[:, 0:ncol], lhsT=hT[:, kc, :], rhs=wv[:, kc, c0:c0 + ncol], start=(kc == 0), stop=(kc == 7))
        return pb, Tpb

    def load_rope(n):
        rp = ropeR.next()
        P.dma("sp", rp[:], rope_in[n], [], [rp.T], drope[ropeR.i % 3])
        return rp

    def rope64(pb, Tpb, rp, out_f32):
        x3 = pb[:, 0:384].rearrange("p (a d) -> p a d", d=32)
        x4 = pb[:, 0:384].rearrange("p (h t d) -> p h t d", h=6, t=2)
        P.I("dve", "tensor_tensor", [Tpb, rp.T], [r1.T], out=r1[:].rearrange("p (a d) -> p a d", d=32), in0=x3,
                                              in1=rp[:, 0:32].unsqueeze(1).to_broadcast([128, 12, 32]), op=ALU.mult)
        r2v = r2[:].rearrange("p (h t d) -> p h t d", h=6, t=2)
        P.I("dve", "tensor_tensor", [Tpb, rp.T], [r2.T], out=r2v[:, :, 0, :], in0=x4[:, :, 1, :],
                                              in1=rp[:, 64:96].unsqueeze(1).to_broadcast([128, 6, 32]), op=ALU.mult)
        P.I("dve", "tensor_tensor", [Tpb, rp.T], [r2.T], out=r2v[:, :, 1, :], in0=x4[:, :, 0, :],
                                              in1=rp[:, 32:64].unsqueeze(1).to_broadcast([128, 6, 32]), op=ALU.mult)
        P.I("pool", "tensor_tensor", [r1.T, r2.T], [out_f32.T], out=out_f32[:], in0=r1[:], in1=r2[:], op=ALU.add)

    def kv_update(kd, R, CD, n, store_fn, store_T, gen, boundary):
        pb, Tpb = gen.next()
        for q in range(3):
            P.I("pe", "matmul", [kd.T, vt.T], [Tpb], pb[:, q * 128:(q + 1) * 128], lhsT=kd[:, q * 128:(q + 1) * 128], rhs=vt[:, q * 128:(q + 1) * 128],
                                                       start=True, stop=True)
        P.I("dve", "tensor_tensor", [R.T, CD.T], [Rt.T], out=Rt[:], in0=R[:], in1=CD[:].unsqueeze(2).to_broadcast([128, 3, 64]), op=ALU.mult)
        kv3 = pb[:, 0:384].rearrange("p (q e) -> p q e", q=3)
        P.I("dve", "tensor_tensor", [Rt.T, Tpb], [R.T], out=R[0:64], in0=Rt[0:64], in1=kv3[0:64, :, 0:64], op=ALU.add)
        P.I("dve", "tensor_tensor", [Rt.T, Tpb], [R.T], out=R[64:128], in0=Rt[64:128], in1=kv3[64:128, :, 64:128], op=ALU.add)
        if boundary:
            P.I("dve", "tensor_scalar", [R.T, flag.T], [R.T], out=R[:], in0=R[:], scalar1=flag[:, 0:1], scalar2=None, op0=ALU.mult)
        if store_fn is not None:
            P.I("act", "activation", [R.T], [store_T], out=store_fn, in_=R[:], func=AF.Copy)

    def post_stage(lhs_fn, K, wv, Tw, xc, Gb, n, dst, Tdst, gen):
        pbs = []
        for half in range(2):
            pb, Tpb = gen.next()
            for kc in range(K):
                lh, Tl = lhs_fn(kc)
                P.I("pe", "matmul", [Tl, Tw[kc]], [Tpb], pb[:, :], lhsT=lh, rhs=wv[:, kc, half * 512:(half + 1) * 512],
                                                                                      start=(kc == 0), stop=(kc == K - 1))
            pbs.append((pb, Tpb))
        s = CUR["st"].next()
        xo = xnew.next()
        for half in range(2):
            pb, Tpb = pbs[half]
            P.I("act", "activation", [Tpb], [xo.T, s.T], out=xo[:, 0:512], in_=pb[:, :], func=AF.Square, accum_out=s[:, half:half + 1])
        P.I("dve", "tensor_tensor", [s.T], [s.T], out=s[:, 2:3], in0=s[:, 0:1], in1=s[:, 1:2], op=ALU.add)
        rstd_from((s, 2), 1.0 / D, (s, 3))
        for half in range(2):
            pb, Tpb = pbs[half]
            P.I("dve", "scalar_tensor_tensor", [Tpb, s.T, Gb.T], [xo.T], out=xo[:, half * 512:(half + 1) * 512], in0=pb[:, :], scalar=s[:, 3:4],
                                                                                   in1=Gb[:, half * 512:(half + 1) * 512], op0=ALU.mult, op1=ALU.mult)
        P.I("pool", "tensor_tensor", [xc.T, xo.T], [xo.T], out=xo[:], in0=xo[:], in1=xc[:], op=ALU.add)
        P.dma("sp", dst[n * 128:(n + 1) * 128, :], xo[:], [xo.T], [Tdst[n]], dst_[xnew.i % 2])

    def load_G(l, gi, seg):
        Gb = Gt.next()
        P.dma("sp", Gb[:], gD[l, gi, seg:seg + 1, :].partition_broadcast(128), TgD[l][gi * 4:gi * 4 + 4], [Gb.T], dG[0])
        return Gb

    def phase_A0(l, src, Tsrc):
        P.I("pool", "memset", [], [Rb.T], Rb[:], 0.0)
        P.I("pool", "memset", [], [Tsb[NCH - 1]], sbst_v[:, NCH - 1], 0.0)
        hTs, rps = {}, {}

        xbs = {}

        def front_a(c):
            xc = load_x(src, Tsrc, c, xcr, dx)
            rps[c] = load_rope(c)
            xbs[c] = norm_front(xc)

        def front_b(c):
            hT = hTr.next()
            hTs[c] = hT
            norm_back(xbs.pop(c), c // SEGC, A1f, fm[0], lambda kc: hT[:, kc, :], hT.T)

        front_a(NCH - 1)
        front_b(NCH - 1)
        for n in range(NCH - 1, 0, -1):
            if n - 1 >= 1:
                use("N")
                front_a(n - 1)
                front_b(n - 1)
            use("K")
            hT = hTs.pop(n)
            rp = rps.pop(n)
            pk, Tpk = proj(hT, win_v, Twin, 896, 384, CUR["gen"])
            pv, Tpv = proj(hT, win_v, Twin, 1280, 384, CUR["gen"])
            rope64(pk, Tpk, rp, rsum)
            P.I("dve", "tensor_tensor", [rsum.T, KDB.T], [kdb.T], out=kdb[:].rearrange("p (h d) -> p h d", h=6), in0=rsum[:].rearrange("p (h d) -> p h d", h=6),
                in1=KDB[:].unsqueeze(2).to_broadcast([128, 6, 64]), op=ALU.mult)
            P.I("act", "activation", [Tpv], [vt.T], out=vt[:], in_=pv[:, 0:384], func=AF.Copy)
            kv_update(kdb, Rb, CDB, n, sbst_v[:, n - 1], Tsb[n - 1], CUR["gen"], boundary=(n % SEGC == 0))
            use(None)
            P.flush()

    def attention(l, m):
        mx = mix.at(m)
        pO, TpO = bank(7), Tbank[7]
        O3 = pO[:, 0:390].rearrange("p (h e) -> p h e", h=6)
        blks = [b for b in (-1, 0, 1) if 0 <= m + b < NCH]
        for kvh in range(2):
            Es = []
            for bi, b in enumerate(blks):
                pS, TpS = CUR["gen"].next()
                kT = ckT.at(m + b); qT = cqT.at(m)
                P.I("pe", "matmul", [kT.T, qT.T], [TpS], pS[:, 0:384], lhsT=kT[kvh * 64:(kvh + 1) * 64, :],
                    rhs=qT[kvh * 64:(kvh + 1) * 64, :, :], start=True, stop=True)
                E = Eb.next()
                P.I("act", "activation", [TpS], [E.T], out=E[:].rearrange("p g i -> p (g i)"), in_=pS[:, 0:384], func=AF.Exp, scale=0.125)
                if b != 0:
                    bnd = (b == -1 and m % SEGC == 0) or (b == 1 and m % SEGC == SEGC - 1)
                    mk = trif if bnd else tri
                    mi = 0 if b == -1 else 1
                    P.I("pool", "tensor_tensor", [E.T, mk.T], [E.T], out=E[:], in0=E[:], in1=mk[:, mi, :].unsqueeze(1).to_broadcast([128, 3, 128]), op=ALU.mult)
                Es.append(E)
            for g in range(3):
                for bi, b in enumerate(blks):
                    cv = cvb.at(m + b)
                    P.I("pe", "matmul", [Es[bi].T, cv.T], [TpO], O3[:, kvh * 3 + g, :], lhsT=Es[bi][:, g, :], rhs=cv[:, kvh, :],
                        start=(bi == 0), stop=(bi == len(blks) - 1))
        s = CUR["st"].next()
        P.I("dve", "tensor_tensor", [TpO, ESK.T], [s.T], out=s[:, 0:6], in0=O3[:, :, 64], in1=ESK[:], op=ALU.add)
        P.I("dve", "reciprocal", [s.T], [s.T], out=s[:, 0:6], in_=s[:, 0:6])
        P.I("dve", "tensor_tensor", [TpO, s.T], [mx.T], out=mx[:, 640:1024].rearrange("p (h d) -> p h d", h=6), in0=O3[:, :, 0:64],
                                              in1=s[:, 0:6].unsqueeze(2).to_broadcast([128, 6, 64]), op=ALU.mult)

    def out_stage(l, m, xc, Gb, dst, Tdst):
        mx = mix.at(m)
        for half in range(2):
            for k4 in range(4):
                kc = half * 4 + k4
                P.I("pe", "transpose", [mx.T, ident.T], [Tb1a], out=bankb(1)[:, k4 * 128:(k4 + 1) * 128], in_=mx[:, kc * 128:(kc + 1) * 128], identity=ident[:])
            P.I("act", "activation", [Tb1a], [mixT.T], out=mixT[:, half * 4:(half + 1) * 4, :].rearrange("p k i -> p (k i)"), in_=bankb(1)[:, 0:512], func=AF.Copy)
        post_stage(lambda kc: (mixT[:, kc, :], mixT.T), 8, wout_v, Twout, xc, Gb, m, dst, Tdst, gLpost)

    def phase_A1(l, src, Tsrc, dst, Tdst):
        P.I("pool", "memset", [], [Rf.T], Rf[:], 0.0)
        P.I("pool", "memset", [], [Sfb.T], Sfb[:], 0.0)
        for b_ in cvb.b:
            P.I("pool", "memset", [], [b_.T], b_[:], 1.0)
        xcs = {}
        Gb = None
        Gseg = {}
        LAG = 2
        hTs, rps = {}, {}

        class _SR:
            def __init__(self, bufs):
                self.b = bufs; self.n = len(bufs); self.i = -1

            def next(self):
                self.i += 1
                return self.b[self.i % self.n]

        xNr = _SR(xcr.b[0:2]); xLr = _SR(xcr.b[2:3])

        def attn_proj(n, hT, rp):
            pcq, Tpcq = proj(hT, win_v, Twin, 2048, 384, CUR["gen"])
            P.I("act", "activation", [Tpcq], [cqb.T], out=cqb[:], in_=pcq[:, 0:384], func=AF.Copy)
            c3 = pcq[:, 0:384].rearrange("p (h d) -> p h d", h=6)
            P.I("dve", "tensor_tensor", [Tpcq, rp.T], [a1.T], out=a1[:].rearrange("p h (t d) -> p h t d", t=2), in0=c3[:, :, 0:16].rearrange("p h (t d) -> p h t d", t=2),
                                                  in1=rp[:, 96:104].unsqueeze(1).unsqueeze(1).to_broadcast([128, 6, 2, 8]), op=ALU.mult)
            P.I("dve", "tensor_tensor", [Tpcq, rp.T], [a2.T], out=a2[:, :, 0:8], in0=c3[:, :, 8:16], in1=rp[:, 112:120].unsqueeze(1).to_broadcast([128, 6, 8]), op=ALU.mult)
            P.I("dve", "tensor_tensor", [Tpcq, rp.T], [a2.T], out=a2[:, :, 8:16], in0=c3[:, :, 0:8], in1=rp[:, 104:112].unsqueeze(1).to_broadcast([128, 6, 8]), op=ALU.mult)
            P.I("pool", "tensor_tensor", [a1.T, a2.T], [cqb.T], out=cqb[:].rearrange("p (h d) -> p h d", h=6)[:, :, 0:16], in0=a1[:], in1=a2[:], op=ALU.add)
            pck, Tpck = proj(hT, win_v, Twin, 2432, 256, CUR["gen"])
            cv = cvb.at(n)
            P.I("act", "activation", [Tpck], [ckb.T], out=ckb[:], in_=pck[:, 0:128], func=AF.Copy)
            P.I("act", "activation", [Tpck], [cv.T], out=cv[:, :, 0:64], in_=pck[:, 128:256].rearrange("p (h d) -> p h d", h=2), func=AF.Copy)
            k3 = pck[:, 0:128].rearrange("p (h d) -> p h d", h=2)
            P.I("dve", "tensor_tensor", [Tpck, rp.T], [a1.T], out=a1[:, 0:2, :].rearrange("p h (t d) -> p h t d", t=2), in0=k3[:, :, 0:16].rearrange("p h (t d) -> p h t d", t=2),
                                                  in1=rp[:, 96:104].unsqueeze(1).unsqueeze(1).to_broadcast([128, 2, 2, 8]), op=ALU.mult)
            P.I("dve", "tensor_tensor", [Tpck, rp.T], [a2.T], out=a2[:, 0:2, 0:8], in0=k3[:, :, 8:16], in1=rp[:, 112:120].unsqueeze(1).to_broadcast([128, 2, 8]), op=ALU.mult)
            P.I("dve", "tensor_tensor", [Tpck, rp.T], [a2.T], out=a2[:, 0:2, 8:16], in0=k3[:, :, 0:8], in1=rp[:, 104:112].unsqueeze(1).to_broadcast([128, 2, 8]), op=ALU.mult)
            P.I("pool", "tensor_tensor", [a1.T, a2.T], [ckb.T], out=ckb[:].rearrange("p (h d) -> p h d", h=2)[:, :, 0:16], in0=a1[:, 0:2, :], in1=a2[:, 0:2, :], op=ALU.add)
            for q in range(3):
                P.I("pe", "transpose", [cqb.T, ident.T], [Tb1b], out=bankb(1)[:, 512 + q * 128:512 + (q + 1) * 128], in_=cqb[:, q * 128:(q + 1) * 128], identity=ident[:])
            P.I("pe", "transpose", [ckb.T, ident.T], [Tb1b], out=bankb(1)[:, 896:1024], in_=ckb[:], identity=ident[:])
            cq_t = cqT.at(n); ck_t = ckT.at(n)
            P.I("act", "activation", [Tb1b], [cq_t.T], out=cq_t[:].rearrange("p a i -> p (a i)"), in_=bankb(1)[:, 512:896], func=AF.Copy)
            P.I("act", "activation", [Tb1b], [ck_t.T], out=ck_t[:], in_=bankb(1)[:, 896:1024], func=AF.Copy)


        xbs = {}

        def front_a(c):
            xc = load_x(src, Tsrc, c, xNr, dx[0:2])
            rps[c] = load_rope(c)
            xbs[c] = norm_front(xc)

        def front_b(c):
            hT = hTr.next()
            hTs[c] = hT
            norm_back(xbs.pop(c), c // SEGC, A1f, fm[0], lambda kc: hT[:, kc, :], hT.T)

        front_a(0)
        front_b(0)
        for n in range(NCH + LAG):
            if n < NCH:
                hT = hTs.pop(n)
                rp = rps.pop(n)
                mx = mix.at(n)
            use("N")
            if n < NCH:
                attn_proj(n, hT, rp)
            if n + 1 < NCH:
                front_a(n + 1)
                front_b(n + 1)
            if n < NCH:
                use("SC")
                if 'sgu' in DBG:
                    pb, Tpb = proj(hT, win_v, Twin, 0, 512, CUR["gen"])
                    P.I("act", "activation", [Tpb], [u_b.T], out=u_b[:], in_=pb[:, 0:256], func=AF.Gelu_apprx_tanh)
                    P.I("act", "activation", [Tpb], [vg.T], out=vg[:], in_=pb[:, 256:512], func=AF.Gelu_apprx_tanh)
                    P.I("dve", "tensor_tensor", [vg.T], [sq.T], out=sq[:], in0=vg[:], in1=vg[:], op=ALU.mult)
                    s = CUR["st"].next()
                    P.I("dve", "tensor_reduce", [sq.T], [s.T], out=s[:, 0:4], in_=sq[:].rearrange("p (g d) -> p g d", g=4), axis=AX.X, op=ALU.add)
                    P.I("dve", "tensor_scalar", [s.T], [s.T], out=s[:, 0:4], in0=s[:, 0:4], scalar1=1.0 / 64, scalar2=EPS, op0=ALU.mult, op1=ALU.add)
                    P.I("pool", "tensor_tensor", [s.T, cmh.T], [s.T], out=s[:, 4:8], in0=s[:, 0:4], in1=cmh[:, 0:4], op=ALU.pow)
                    P.I("dve", "tensor_tensor", [vg.T, s.T], [vn.T], out=vn[:].rearrange("p (g d) -> p g d", g=4), in0=vg[:].rearrange("p (g d) -> p g d", g=4),
                                                          in1=s[:, 4:8].unsqueeze(2).to_broadcast([128, 4, 64]), op=ALU.mult)
                    P.I("pool", "tensor_tensor", [vn.T, SN.T], [vn2.T], out=vn2[:], in0=vn[:], in1=SN[:], op=ALU.mult)
                    pg, Tpg = CUR["gen"].next()
                    for g in range(4):
                        P.I("pe", "matmul", [WsT.T, vn2.T], [Tpg], pg[:, g * 64:(g + 1) * 64], lhsT=WsT[:, g, :], rhs=vn2[:, g * 64:(g + 1) * 64], start=True, stop=True)
                    P.I("dve", "tensor_tensor", [Tpg, SGB.T], [gt_.T], out=gt_[:].rearrange("p (g d) -> p g d", g=4), in0=pg[:, 0:256].rearrange("p (g d) -> p g d", g=4),
                                                          in1=SGB[:].unsqueeze(2).to_broadcast([128, 4, 64]), op=ALU.add)
                    P.I("pool", "tensor_tensor", [gt_.T, u_b.T], [mx.T], out=mx[:, 0:256], in0=gt_[:], in1=u_b[:], op=ALU.mult)
                use("R")
                if 'ret' in DBG:
                    pq, Tpq = proj(hT, win_v, Twin, 512, 384, CUR["gen"])
                    rope64(pq, Tpq, rp, rsum)
                    P.I("act", "activation", [rsum.T], [qr.T], out=qr[:], in_=rsum[:], func=AF.Copy)
                    pk, Tpk = proj(hT, win_v, Twin, 896, 384, CUR["gen"])
                    rope64(pk, Tpk, rp, rsum)
                    P.I("act", "activation", [rsum.T], [kr.T], out=kr[:], in_=rsum[:], func=AF.Copy)
                    P.I("dve", "tensor_tensor", [rsum.T, KDF.T], [kdf.T], out=kdf[:].rearrange("p (h d) -> p h d", h=6), in0=rsum[:].rearrange("p (h d) -> p h d", h=6),
                                                          in1=KDF[:].unsqueeze(2).to_broadcast([128, 6, 64]), op=ALU.mult)
                    pv, Tpv = proj(hT, win_v, Twin, 1280, 384, CUR["gen"])
                    P.I("act", "activation", [Tpv], [vt.T], out=vt[:], in_=pv[:, 0:384], func=AF.Copy)
                    pgg, Tpgg = proj(hT, win_v, Twin, 1664, 384, CUR["gen"])
                    P.I("act", "activation", [Tpgg], [sg.T], out=sg[:], in_=pgg[:, 0:384], func=AF.Silu)
                    if 'r1' in DBG2:
                        for i, srcb in enumerate([qr, kr]):
                            for q in range(3):
                                P.I("pe", "transpose", [srcb.T, ident.T], [Tbank[4]], out=bankb(4)[:, (i * 3 + q) * 128:(i * 3 + q + 1) * 128],
                                                                                               in_=srcb[:, q * 128:(q + 1) * 128], identity=ident[:])
                        P.I("act", "activation", [Tbank[4]], [qkT.T], out=qkT[:].rearrange("p a i -> p (a i)"), in_=bankb(4)[:, 0:768], func=AF.Copy)
                        P.I("pool", "tensor_tensor", [qkT.T, QDF.T], [qdf.T], out=qdf[:], in0=qkT[:, 0:3, :], in1=QDF[:], op=ALU.mult)
                        P.I("pool", "tensor_tensor", [qkT.T, QDB.T], [qdb.T], out=qdb[:], in0=qkT[:, 0:3, :], in1=QDB[:], op=ALU.mult)
                    if 'r2' in DBG2:
                        for par in range(2):
                            pS, TpS = CUR["gen"].next()
                            for q_ in range(3):
                                P.I("pe", "matmul", [qkT.T], [TpS], pS[:, q_ * 128:(q_ + 1) * 128], lhsT=qkT[par * 64:(par + 1) * 64, 3 + q_, :],
                                    rhs=qkT[par * 64:(par + 1) * 64, q_, :], start=True, stop=True)
                            P.I("dve", "tensor_tensor", [TpS, DT.T], [PT.T], out=PT[:, par * 3:(par + 1) * 3, :], in0=pS[:, 0:384].rearrange("p (h i) -> p h i", h=3),
                                in1=DT[:, par * 3:(par + 1) * 3, :], op=ALU.mult)
                    if 'r3' in DBG2:
                        pY, TpY = CUR["gen"].next()
                        for h in range(6):
                            q_, par = h // 2, h % 2
                            sl = slice(par * 64, (par + 1) * 64)
                            P.I("pe", "matmul", [PT.T, vt.T], [TpY], pY[:, h * 64:(h + 1) * 64], lhsT=PT[:, par * 3 + q_, :], rhs=vt[:, h * 64:(h + 1) * 64], start=True, stop=False)
                            P.I("pe", "matmul", [qdf.T, Sfb.T], [TpY], pY[:, h * 64:(h + 1) * 64], lhsT=qdf[sl, q_, :], rhs=Sfb[sl, q_, :], start=False, stop=False)
                            P.I("pe", "matmul", [qdb.T, Tsb[n]], [TpY], pY[:, h * 64:(h + 1) * 64], lhsT=qdb[sl, q_, :], rhs=sbst_v[sl, n, q_, :], start=False, stop=True)
                    if 'r4' in DBG2:
                        kv_update(kdf, Rf, CDF, n, Sfb[:], Sfb.T, CUR["gen"], boundary=((n + 1) % SEGC == 0))
                    if 'r5' in DBG2:
                        s2 = CUR["st"].next()
                        Y3 = pY[:, 0:384].rearrange("p (h d) -> p h d", h=6)
                        P.I("dve", "tensor_reduce", [TpY], [s2.T], out=s2[:, 0:6], in_=Y3, axis=AX.X, op=ALU.add)
                        P.I("act", "activation", [TpY], [ysq.T], out=ysq[:], in_=pY[:, 0:384], func=AF.Square)
                        s3 = CUR["st"].next()
                        P.I("dve", "tensor_reduce", [ysq.T], [s3.T], out=s3[:, 0:6], in_=ysq[:].rearrange("p (h d) -> p h d", h=6), axis=AX.X, op=ALU.add)
                        P.I("dve", "tensor_scalar", [s2.T], [s2.T], out=s2[:, 0:6], in0=s2[:, 0:6], scalar1=1.0 / 64, scalar2=None, op0=ALU.mult)
                        s4 = CUR["st"].next()
                        P.I("dve", "tensor_tensor", [s2.T], [s4.T], out=s4[:, 0:6], in0=s2[:, 0:6], in1=s2[:, 0:6], op=ALU.mult)
                        P.I("dve", "scalar_tensor_tensor", [s3.T, s4.T], [s3.T], out=s3[:, 0:6], in0=s3[:, 0:6], scalar=1.0 / 64, in1=s4[:, 0:6], op0=ALU.mult, op1=ALU.subtract)
                        P.I("dve", "tensor_scalar", [s3.T], [s3.T], out=s3[:, 0:6], in0=s3[:, 0:6], scalar1=EPS, scalar2=None, op0=ALU.add)
                        P.I("pool", "tensor_tensor", [s3.T, cmh.T], [s4.T], out=s4[:, 0:6], in0=s3[:, 0:6], in1=cmh[:, 0:6], op=ALU.pow)
                        P.I("dve", "tensor_tensor", [TpY, s2.T], [yc_.T], out=yc_[:].rearrange("p (h d) -> p h d", h=6), in0=Y3, in1=s2[:, 0:6].unsqueeze(2).to_broadcast([128, 6, 64]),
                                                              op=ALU.subtract)
                        P.I("dve", "tensor_tensor", [yc_.T, s4.T], [yn_.T], out=yn_[:].rearrange("p (h d) -> p h d", h=6), in0=yc_[:].rearrange("p (h d) -> p h d", h=6),
                                                              in1=s4[:, 0:6].unsqueeze(2).to_broadcast([128, 6, 64]), op=ALU.mult)
                        P.I("pool", "tensor_tensor", [yn_.T, sg.T], [mx.T], out=mx[:, 256:640], in0=yn_[:], in1=sg[:], op=ALU.mult)
            m = n - LAG
            if m >= 0:
                use("L")
                if 'attn' in DBG:
                    attention(l, m)
                if 'out' in DBG:
                    xr = load_x(src, Tsrc, m, xLr, dx[2:3])
                    out_stage(l, m, xr, Gseg[m // SEGC], dst, Tdst)
            use(None)
            P.flush()
            mf = n - (LAG - 1)
            if 0 <= mf < NCH and mf % SEGC == 0:
                Gseg[mf // SEGC] = load_G(l, 0, mf // SEGC)

    class SubRing:
        def __init__(self, bufs):
            self.b = bufs; self.n = len(bufs); self.i = -1

        def next(self):
            self.i += 1
            return self.b[self.i % self.n]

    def phase_B(l, src, Tsrc, dst, Tdst):
        P.dma("sp", CW[:], cw_in[l], [], [CW.T], ND("CW"))
        xN = SubRing(xcr.b[0:2]); xD = SubRing(xcr.b[2:3])
        dxN = dx[0:2]; dxD = dx[2:3]
        gB = Gen([1, 2, 3, 4, 5, 6, 7])
        hbs = {}
        normed = [-1]
        Gseg = {}

        def do_norm(c):
            hb_alloc(c)
            nback(c, nfront(c))

        def hb_alloc(c):
            b, half = c // 2, c % 2
            if half == 0:
                hb = HB.next()
                hbs[b] = hb
                if b == 0:
                    P.I("pool", "memset", [], [hb.T], hb[:, :, 0:1], 0.0)
                if b == NB - 1:
                    P.I("pool", "memset", [], [hb.T], hb[:, :, 257:258], 0.0)

        def nfront(c):
            xc = load_x(src, Tsrc, c, xN, dxN)
            return norm_front(xc)

        def nfront_pre(c):
            hb_alloc(c)
            return nfront(c)

        def nback(c, xb_):
            b, half = c // 2, c % 2
            seg = c // SEGC
            hb = hbs[b]
            o = 1 + half * 128
            norm_back(xb_, seg, A2f, fm[2], lambda kc: hb[:, kc, o:o + 128], hb.T)
            if half == 0 and b > 0:
                pv_ = hbs[b - 1]
                if c % SEGC == 0:
                    P.I("dve", "tensor_scalar", [hb.T, flag.T], [pv_.T], out=pv_[:, :, 257:258], in0=hb[:, :, 1:2], scalar1=flag[:, 0:1], scalar2=None, op0=ALU.mult)
                else:
                    P.I("pool", "tensor_copy", [hb.T], [pv_.T], out=pv_[:, :, 257:258], in_=hb[:, :, 1:2])
            normed[0] = c

        def halo_fwd(b):
            hb = hbs[b]; nx = hbs[b + 1]
            c = 2 * b + 1
            if (c + 1) % SEGC == 0:
                P.I("dve", "tensor_scalar", [hb.T, flag.T], [nx.T], out=nx[:, :, 0:1], in0=hb[:, :, 256:257], scalar1=flag[:, 0:1], scalar2=None, op0=ALU.mult)
            else:
                P.I("pool", "tensor_copy", [hb.T], [nx.T], out=nx[:, :, 0:1], in_=hb[:, :, 256:257])

        def up_pairs(b, f0, f1):
            hb = hbs[b]
            aT = actT.at(b)
            for fc in range(f0, f1):
                tl = []
                for which in range(2):
                    fcc = fc + which * NFC
                    pb, Tpb = gB.next()
                    for kc in range(8):
                        P.I("pe", "matmul", [Twup[kc], hb.T], [Tpb], pb[:, 0:258], lhsT=wup_v[:, kc, fcc * 128:(fcc + 1) * 128], rhs=hb[:, kc, :],
                            start=(kc == 0), stop=(kc == 7))
                    tl.append((fcc, pb, Tpb, T0.next(), T1.next(), T2.next()))
                for (fcc, pb, Tpb, t0, t1, t2) in tl:
                    P.I("act", "activation", [Tpb, CW.T], [t0.T], out=t0[:], in_=pb[:, 0:256], func=AF.Identity, scale=CW[:, fcc, 0:1], bias=CW[:, fcc, 3:4])
                for (fcc, pb, Tpb, t0, t1, t2) in tl:
                    P.I("dve", "scalar_tensor_tensor", [Tpb, CW.T, t0.T], [t1.T], out=t1[:], in0=pb[:, 1:257], scalar=CW[:, fcc, 1:2], in1=t0[:],
                        op0=ALU.mult, op1=ALU.add)
                for (fcc, pb, Tpb, t0, t1, t2) in tl:
                    P.I("dve", "scalar_tensor_tensor", [Tpb, CW.T, t1.T], [t2.T], out=t2[:], in0=pb[:, 2:258], scalar=CW[:, fcc, 2:3], in1=t1[:],
                        op0=ALU.mult, op1=ALU.add)
                g_ = gg.next()
                tg, tv = tl[0][5], tl[1][5]
                P.I("act", "activation", [tg.T], [g_.T], out=g_[:], in_=tg[:], func=AF.Gelu_apprx_tanh)
                P.I("pool", "tensor_tensor", [tv.T, g_.T], [aT.T], out=aT[:, fc, :], in0=tv[:], in1=g_[:], op=ALU.mult)

        def down(b):
            aT = actT.at(b)
            for half in range(2):
                c = 2 * b + half
                seg = c // SEGC
                if seg not in Gseg:
                    Gseg[seg] = load_G(l, 1, seg)
                xr = load_x(src, Tsrc, c, xD, dxD)
                post_stage(lambda kc: (aT[:, kc, half * 128:(half + 1) * 128], aT.T), NFC, wdn_v, Twdn, xr, Gseg[seg], c, dst, Tdst, gB)

        do_norm(0); do_norm(1)
        if NCH > 2:
            do_norm(2)
            halo_fwd(0)
        for b in range(NB):
            c1, c2 = 2 * b + 3, 2 * b + 4
            f1 = nfront_pre(c1) if c1 < NCH else None
            up_pairs(b, 0, 4)
            if b > 0:
                down(b - 1)
            up_pairs(b, 4, 8)
            if f1 is not None:
                nback(c1, f1)
            f2 = nfront_pre(c2) if c2 < NCH else None
            up_pairs(b, 8, 16)
            if f2 is not None:
                nback(c2, f2)
                halo_fwd(b + 1)
            up_pairs(b, 16, NFC)
        down(NB - 1)

    cur, Tcur = x_in, None
    for l in range(depth):
        P.barrier()
        load_A_weights(l)
        layer_setup(l)
        P.barrier()
        if stop_after == ("setup", l):
            break
        phase_A0(l, cur, Tcur)
        if stop_after == ("A0", l):
            break
        phase_A1(l, cur, Tcur, xa, Txa)
        if stop_after == ("A1", l):
            cur, Tcur = xa, Txa
            break
        P.barrier()
        load_B_weights(l)
        last = (l == depth - 1)
        dst, Tdst = (y_out, Ty) if last else (xb, Txb)
        phase_B(l, xa, Txa, dst, Tdst)
        cur, Tcur = dst, Tdst
    if cur is not y_out:
        for n in range(NCH):
            xc = load_x(cur, Tcur, n, xcr, dx)
            P.dma("sp", y_out[n * 128:(n + 1) * 128, :], xc[:], [xc.T], [Ty[n]], dcp[xcr.i % xcr.n])
    P.wait_all("sp", Ty)
    counts = P.emit()
    es.close()
    return nc, counts


def _rope_tables(pos, rot_dim, theta):
    half = rot_dim // 2
    freqs = np.exp(-math.log(theta) * np.arange(half, dtype=np.float32) * np.float32(2.0) / np.float32(rot_dim)).astype(np.float32)
    ang = pos.astype(np.float32)[:, None] * freqs[None, :]
    return np.cos(ang).astype(np.float32), np.sin(ang).astype(np.float32)


def core_tables(NCH, SEGC, is_prompt):
    n = np.arange(NCH)
    if is_prompt:
        base = n * 128
    else:
        base = (n % SEGC) * 128
    pos = (base[None, :] + np.arange(128)[:, None]).reshape(-1)
    rc_, rs_ = _rope_tables(pos, 64, 10000.0)
    ac_, as__ = _rope_tables(pos, 16, 500000.0)
    f = lambda a, d: a.reshape(128, NCH, d)
    rope = np.concatenate([f(rc_, 32), f(rs_, 32), f(-rs_, 32), f(ac_, 8), f(as__, 8), f(-as__, 8)], 2)
    return dict(rope=np.ascontiguousarray(rope.transpose(1, 0, 2)).astype(np.float32),
                flag=np.full((128, 1), 1.0 if is_prompt else 0.0, np.float32))


def const_inputs():
    j = np.arange(128, dtype=np.float32)[:, None]
    i = np.arange(128, dtype=np.float32)[None, :]
    dpos = np.maximum(i - j, 0); dneg = np.maximum(j - i, 0)
    mge = (i >= j).astype(np.float32); mlt = (i < j).astype(np.float32)
    io1 = np.broadcast_to(i + 1, (128, 128)); io2 = np.broadcast_to(128 - i, (128, 128))
    cm = np.stack([dpos, dneg, mge, mlt, io1, io2], 1).astype(np.float32)
    tri = np.stack([(j >= i).astype(np.float32) * np.ones((128, 128), np.float32), (j <= i).astype(np.float32) * np.ones((128, 128), np.float32)], 1)
    jv = np.concatenate([127 - j, j], 1).astype(np.float32)
    return dict(ident=np.eye(128, dtype=np.float32), cmats=np.ascontiguousarray(cm), tri=np.ascontiguousarray(tri), jv=jv)


def weight_inputs(w_ada, b_ada, norm_pre_mix, norm_post_mix, norm_pre_ffn, norm_post_ffn, w_in, sgu_norm, sgu_w, sgu_b,
                  ret_decay_fwd, ret_decay_bwd, attn_sink, w_out, w_up, conv_w, conv_b, w_down):
    L = w_in.shape[0]
    perm = np.arange(IN_DIM)
    hq = [0, 3, 1, 4, 2, 5]
    perm[2048:2432] = np.concatenate([2048 + h * 64 + np.arange(64) for h in hq])
    w_in_p = np.ascontiguousarray(w_in[:, :, perm])
    cw = np.concatenate([conv_w, conv_b[:, None, :]], 1)
    cw = np.ascontiguousarray(cw.reshape(L, 4, 2 * NFC, 128).transpose(0, 3, 2, 1))
    gfm = np.stack([norm_pre_mix, norm_pre_ffn], 1).reshape(L, 2, 8, 128).transpose(0, 3, 1, 2)
    gpost = np.concatenate([norm_post_mix, norm_post_ffn], 1)
    sgu_wT = np.ascontiguousarray(sgu_w.transpose(0, 3, 1, 2))
    sgu_bT = np.ascontiguousarray(sgu_b.transpose(0, 2, 1))
    dec6 = np.concatenate([ret_decay_fwd, ret_decay_bwd], 1)
    dec6 = np.broadcast_to(dec6[:, None, :], (L, 128, 12))
    decP = np.zeros((L, 128, 6), np.float32)
    decP[:, 0:64, 0:3] = ret_decay_fwd[:, None, 0::2]; decP[:, 64:128, 0:3] = ret_decay_fwd[:, None, 1::2]
    decP[:, 0:64, 3:6] = ret_decay_bwd[:, None, 0::2]; decP[:, 64:128, 3:6] = ret_decay_bwd[:, None, 1::2]
    sink6 = np.ascontiguousarray(np.broadcast_to(attn_sink[:, None, :], (L, 128, 6)))
    c = np.ascontiguousarray
    return dict(w_ada=c(w_ada), b_ada=c(b_ada), w_in=w_in_p, w_out=c(w_out), w_up=c(w_up), w_down=c(w_down), cw=cw,
                gfm=c(gfm.astype(np.float32)), gpost=c(gpost.astype(np.float32)), sgu_wT=sgu_wT, sgu_bT=sgu_bT, sgu_n=c(sgu_norm),
                dec18=c(np.concatenate([dec6, decP], 2).astype(np.float32)), sink6=sink6.astype(np.float32))


def core_c(c_rows):
    ns = c_rows.shape[0]
    return np.ascontiguousarray(c_rows.reshape(ns, 8, 128).transpose(2, 1, 0)).astype(np.float32)


_CACHE = {}


def kernel(x_prompt, x_sample, c_prompt, c_sample, w_ada, b_ada, norm_pre_mix, norm_post_mix, norm_pre_ffn, norm_post_ffn,
           w_in, sgu_norm, sgu_w, sgu_b, ret_decay_fwd, ret_decay_bwd, attn_sink, w_out, w_up, conv_w, conv_b, w_down):
    NCH, SEGC = 64, 16
    f = lambda a: np.asarray(a, dtype=np.float32)
    x_prompt, x_sample, c_prompt, c_sample = f(x_prompt), f(x_sample), f(c_prompt), f(c_sample)
    wts = weight_inputs(*[f(a) for a in (w_ada, b_ada, norm_pre_mix, norm_post_mix, norm_pre_ffn, norm_post_ffn, w_in, sgu_norm, sgu_w, sgu_b,
                                           ret_decay_fwd, ret_decay_bwd, attn_sink, w_out, w_up, conv_w, conv_b, w_down)])
    consts = const_inputs()
    if "nc" not in _CACHE:
        _CACHE["nc"] = build(NCH, SEGC)[0]
    nc = _CACHE["nc"]
    in_maps = []
    for core in range(8):
        if core < 2:
            xs = x_prompt[core]
            cr = np.repeat(c_prompt[core:core + 1], 4, 0)
            tb = core_tables(NCH, SEGC, True)
        else:
            k = min(core - 2, 3)
            xs = x_sample[4 * k:4 * k + 4].reshape(NCH * 128, D)
            cr = c_sample[4 * k:4 * k + 4]
            tb = core_tables(NCH, SEGC, False)
        m = dict(x=np.ascontiguousarray(xs), cT=core_c(cr))
        m.update(tb); m.update(consts); m.update(wts)
        in_maps.append(m)
    res = run_bass_kernel_spmd(nc, in_maps, core_ids=list(range(8)))
    r = res.results
    y_prompt = np.stack([r[0]["y"], r[1]["y"]], 0).astype(np.float32)
    y_sample = np.concatenate([r[2 + k]["y"].reshape(4, 2048, D) for k in range(4)], 0).astype(np.float32)
    return (y_prompt, y_sample)
```

```python
import math
import os
import numpy as np
DBG = os.environ.get("KDBG", "sgu,ret,ap,attn,out").split(",")
DBG2 = os.environ.get("KDBG2", "r1,r2,r3,r4,r5").split(",")
from contextlib import ExitStack
import concourse.bass as bass
import concourse.mybir as mybir
from concourse.bass_utils import run_bass_kernel_spmd

F32 = mybir.dt.float32
BF16 = mybir.dt.bfloat16
AF = mybir.ActivationFunctionType
ALU = mybir.AluOpType
AX = mybir.AxisListType

SAME_ENGINE_SYNC = bool(int(os.environ.get("KSES", "1")))
EPS = 1e-6
D = 1024
IN_DIM = 2688
DFF = 2816
NFC = 22
DEPTH = 2


class Tile:
    __slots__ = ("name", "w", "r", "rd", "excl")

    def __init__(self, name="", excl=False):
        self.name = name
        self.w = None
        self.r = {}
        self.rd = []
        self.excl = excl


class Op:
    __slots__ = ("eng", "fn", "deps", "inc", "val", "dsem")

    def __init__(self, eng, fn, dsem):
        self.eng = eng
        self.fn = fn
        self.dsem = dsem
        self.deps = set()
        self.inc = dsem is not None
        self.val = 0


class DSem:
    __slots__ = ("h", "cnt")

    def __init__(self, h):
        self.h = h
        self.cnt = 0


class Prog:
    ENGS = ("pe", "act", "dve", "pool", "sp")

    def __init__(self, nc, es):
        self.nc = nc
        self.es = es
        self.ops = {e: [] for e in self.ENGS}
        self.esem = {e: es.enter_context(nc.semaphore("s_" + e)) for e in self.ENGS}
        self.n = 0
        self.bar_idx = {}
        self.region = None
        self.cur = None
        self.streams = {}

    def sb(self, shape, dt, name=None):
        self.n += 1
        return self.es.enter_context(self.nc.sbuf_tensor(f"sb{self.n}_{name or ''}", list(shape), dt))

    def ps(self, shape, dt=F32, name=None):
        self.n += 1
        return self.es.enter_context(self.nc.psum_tensor(name or f"ps{self.n}", list(shape), dt))

    def dsem(self):
        self.n += 1
        return DSem(self.es.enter_context(self.nc.semaphore(f"ds{self.n}")))

    def set_stream(self, name):
        self.cur = None if name is None else self.streams.setdefault(name, [])

    def flush(self):
        self.cur = None
        lists = [v for v in self.streams.values() if v]
        if os.environ.get("KSEQ"):
            for li in lists:
                for r_ in li:
                    self._op(*r_)
            self.streams = {}
            return
        idx = [0] * len(lists)
        while True:
            best = None
            for i, li in enumerate(lists):
                if idx[i] < len(li):
                    f = (idx[i] + 1) / len(li)
                    if best is None or f < best[0]:
                        best = (f, i)
            if best is None:
                break
            i = best[1]
            self._op(*lists[i][idx[i]])
            idx[i] += 1
        self.streams = {}

    def op(self, eng, fn, reads=(), writes=(), dsem=None):
        if self.cur is not None:
            self.cur.append((eng, fn, list(reads), list(writes), dsem))
            return None
        return self._op(eng, fn, reads, writes, dsem)

    def _op(self, eng, fn, reads=(), writes=(), dsem=None):
        o = Op(eng, fn, dsem)
        deps = o.deps
        for t in reads:
            if t.w is not None:
                deps.add(t.w)
            if t.excl:
                for e_, o_ in t.r.items():
                    if e_ != eng:
                        deps.add(o_)
        for t in writes:
            if t.w is not None:
                deps.add(t.w)
            deps.update(t.r.values())
            deps.update(t.rd)
        for t in reads:
            if dsem is not None:
                t.rd.append(o)
            else:
                t.r[eng] = o
        for t in writes:
            t.w = o
            t.r = {}
            t.rd = []
        if dsem is not None:
            dsem.cnt += 16
            o.val = dsem.cnt
        self.ops[eng].append(o)
        return o

    def I(self, eng, meth, reads, writes, *a, **k):
        return self.op(eng, (meth, a, k), reads, writes)

    def dma(self, eng, out, in_, reads, writes, dsem=None, slow=False):
        if dsem is None:
            dsem = self.dsem()
        k = dict(out=out, in_=in_)
        if slow:
            k["allow_slow_non_contiguous"] = True
        return self.op(eng, ("dma_start", (), k), reads, writes, dsem)

    def barrier(self):
        lasts = []
        for e in self.ENGS:
            comp = [o for o in self.ops[e] if o.dsem is None and o.fn is not None]
            if comp:
                lasts.append(comp[-1])
        dmas = [o for e in self.ENGS for o in self.ops[e][self.bar_idx.get(e, 0):] if o.dsem is not None]
        for e in self.ENGS:
            self.bar_idx[e] = len(self.ops[e])
        for e in self.ENGS:
            o = Op(e, None, None)
            o.deps.update(lasts)
            o.deps.update(dmas)
            self.ops[e].append(o)

    def wait_all(self, eng, tiles):
        o = Op(eng, None, None)
        for t in tiles:
            if t.w is not None:
                o.deps.add(t.w)
        self.ops[eng].append(o)

    def emit(self):
        nc = self.nc
        for e in self.ENGS:
            for o in self.ops[e]:
                for d in o.deps:
                    if d.dsem is not None:
                        continue
                    if d.eng != o.eng or o.dsem is not None:
                        d.inc = True
                    elif SAME_ENGINE_SYNC and d.eng != "pe":
                        d.inc = True
        for e in self.ENGS:
            c = 0
            for o in self.ops[e]:
                if o.dsem is None:
                    if o.inc:
                        c += 1
                    o.val = c
        counts = {}
        with nc.Block() as block:
            def run(e):
                def body(h):
                    waited = {}
                    nw = 0
                    for o in self.ops[e]:
                        need = {}
                        for d in o.deps:
                            if d.dsem is not None:
                                key = d.dsem
                                sem = d.dsem.h
                            else:
                                if d.eng == e and o.dsem is None:
                                    if e == "pe" or not SAME_ENGINE_SYNC:
                                        continue
                                key = d.eng
                                sem = self.esem[d.eng]
                            if need.get(key, (None, 0))[1] < d.val:
                                need[key] = (sem, d.val)
                        for key, (sem, val) in need.items():
                            if waited.get(key, 0) < val:
                                h.wait_ge(sem, val)
                                waited[key] = val
                                nw += 1
                        if o.fn is None:
                            continue
                        meth, a, k = o.fn
                        inst = getattr(h, meth)(*a, **k)
                        if o.dsem is not None:
                            inst.then_inc(o.dsem.h, 16)
                        elif o.inc:
                            inst.then_inc(self.esem[e], 1)
                    counts[e] = (len(self.ops[e]), nw)
                return body
            block.tensor(run("pe"))
            block.scalar(run("act"))
            block.vector(run("dve"))
            block.gpsimd(run("pool"))
            block.sync(run("sp"))
        return counts


class Region:
    def __init__(self, big, start, limit):
        self.big = big
        self.off = start
        self.limit = limit

    def alloc(self, shape, dt):
        n = 1
        for s in shape[1:]:
            n *= s
        n16 = n * (2 if dt == F32 else 1)
        self.off = (self.off + 15) // 16 * 16
        ap = self.big[0:shape[0], self.off:self.off + n16]
        self.off += n16
        assert self.off <= self.limit, ("region overflow", self.off, self.limit)
        if dt == F32:
            ap = ap.bitcast(F32)
        if len(shape) == 3:
            ap = ap.rearrange("p (a b) -> p a b", a=shape[1])
        elif len(shape) == 4:
            ap = ap.rearrange("p (a b c) -> p a b c", a=shape[1], b=shape[2])
        return ap


class Buf:
    def __init__(self, P, shape, dt, name=None):
        if getattr(P, "region", None) is not None:
            self.t = P.region.alloc(shape, dt)
        else:
            self.t = P.sb(shape, dt, name)
        self.T = Tile(name or "")

    def __getitem__(self, k):
        return self.t[k]


class Ring:
    def __init__(self, P, n, shape, dt, name):
        self.b = [Buf(P, shape, dt, f"{name}{i}") for i in range(n)]
        self.n = n
        self.i = -1

    def next(self):
        self.i += 1
        return self.b[self.i % self.n]

    def at(self, k):
        return self.b[k % self.n]


def build(NCH, SEGC, depth=DEPTH, stop_after=None):
    NSEG = NCH // SEGC
    NT = NCH * 128
    NB = NCH // 2
    nc = bass.Bass("TRN2", target_bir_lowering=False)

    def din(name, shape):
        return nc.dram_tensor(name, list(shape), F32, kind="ExternalInput").ap()

    x_in = din("x", [NT, D])
    cT_in = din("cT", [128, 8, NSEG])
    flag_in = din("flag", [128, 1])
    rope_in = din("rope", [NCH, 128, 120])
    ident_in = din("ident", [128, 128])
    cm_in = din("cmats", [128, 6, 128])
    tri_in = din("tri", [128, 2, 128])
    jv_in = din("jv", [128, 2])
    w_ada = din("w_ada", [depth, D, 6 * D]); b_ada = din("b_ada", [depth, 6 * D])
    w_in = din("w_in", [depth, D, IN_DIM]); w_out = din("w_out", [depth, D, D])
    w_up = din("w_up", [depth, D, 2 * DFF]); w_down = din("w_down", [depth, DFF, D])
    cw_in = din("cw", [depth, 128, 2 * NFC, 4])
    gfm_in = din("gfm", [depth, 128, 2, 8])
    gpost_in = din("gpost", [depth, 2 * D])
    sguw_in = din("sgu_wT", [depth, 128, 4, 128]); sgub_in = din("sgu_bT", [depth, 128, 4])
    sgun_in = din("sgu_n", [depth, 256])
    dec18_in = din("dec18", [depth, 128, 18])
    sink_in = din("sink6", [depth, 128, 6])
    y_out = nc.dram_tensor("y", [NT, D], F32, kind="ExternalOutput").ap()
    xa = nc.dram_tensor("xa", [NT, D], F32, kind="Internal").ap()
    xb = nc.dram_tensor("xb", [NT, D], F32, kind="Internal").ap()
    modD = nc.dram_tensor("modD", [depth, NSEG, 6 * D], F32, kind="Internal").ap()
    gD = nc.dram_tensor("gD", [depth, 2, NSEG, D], F32, kind="Internal").ap()

    es = ExitStack()
    P = Prog(nc, es)
    Txa = [Tile(f"xa{i}") for i in range(NCH)]
    Txb = [Tile(f"xb{i}") for i in range(NCH)]
    Ty = [Tile(f"y{i}") for i in range(NCH)]
    TmodD = [None] * depth
    TgD = [None] * depth

    PS = P.ps([128, 4096], F32, "PSALL")
    PSb = PS.bitcast(BF16)
    Tbank = [Tile(f"bank{i}", excl=True) for i in range(8)]

    def bank(i):
        return PS[:, 512 * i:512 * (i + 1)]

    def bankb(i):
        return PSb[:, 1024 * i:1024 * (i + 1)]

    class Gen:
        def __init__(self, ids):
            self.ids = ids
            self.i = -1

        def next(self):
            self.i += 1
            b = self.ids[self.i % len(self.ids)]
            return bank(b), Tbank[b]

    ident = Buf(P, [128, 128], BF16, "ident")
    tri = Buf(P, [128, 2, 128], BF16, "tri")
    trif = Buf(P, [128, 2, 128], BF16, "trif")
    jv = Buf(P, [128, 2], F32, "jv")
    flag = Buf(P, [128, 1], F32, "flag")
    ropeR = Ring(P, 3, [128, 120], F32, "rope")
    drope = [P.dsem() for _ in range(3)]
    cmh = Buf(P, [128, 8], F32, "cmh")
    siluT = Buf(P, [128, 8, NSEG], BF16, "siluT")
    cTf = Buf(P, [128, 8, NSEG], F32, "cTf")

    dsems = []

    def DS():
        d = P.dsem()
        dsems.append(d)
        return d

    _nd = {}

    def ND(name):
        if name not in _nd:
            _nd[name] = DS()
        return _nd[name]
    for (b, src, eng) in [(ident, ident_in, "pool"), (tri, tri_in, "pool"), (jv, jv_in, "sp"),
                          (flag, flag_in, "sp"), (cTf, cT_in, "sp")]:
        P.dma(eng, b[:], src, [], [b.T], ND("c_" + b.T.name))
    P.I("pool", "memset", [], [cmh.T], cmh[:], -0.5)
    P.I("act", "activation", [cTf.T], [siluT.T], out=siluT[:], in_=cTf[:], func=AF.Silu)
    P.I("dve", "tensor_scalar", [tri.T, flag.T], [trif.T], out=trif[:], in0=tri[:], scalar1=flag[:, 0:1], scalar2=None, op0=ALU.mult)

    ARENA = 8 * 2 * DFF + NFC * D
    BIGN = ARENA + 21120
    W = P.sb([128, BIGN], BF16, "arena")
    win_v = W[:, 0:8 * IN_DIM].rearrange("p (k n) -> p k n", k=8)
    wout_v = W[:, 8 * IN_DIM:8 * IN_DIM + 8 * D].rearrange("p (k n) -> p k n", k=8)
    SB0 = 8 * IN_DIM + 8 * D
    sbst_v = W[:, SB0:SB0 + NCH * 192].rearrange("p (c q e) -> p c q e", c=NCH, q=3)
    wup_v = W[:, 0:8 * 2 * DFF].rearrange("p (k n) -> p k n", k=8)
    wdn_v = W[:, 8 * 2 * DFF:ARENA].rearrange("p (k n) -> p k n", k=NFC)
    Twin = [Tile(f"win{k}") for k in range(8)]
    Twout = [Tile(f"wout{k}") for k in range(8)]
    Tsb = [Tile(f"sbst{c}") for c in range(NCH)]
    Twup = [Tile(f"wup{k}") for k in range(8)]
    Twdn = [Tile(f"wdn{k}") for k in range(NFC)]
    A_tiles = Twin + Twout + Tsb
    B_tiles = Twup + Twdn
    dWin = [DS() for _ in range(8)]; dWout = [DS() for _ in range(8)]; dWup = [DS() for _ in range(8)]; dWdn = [DS() for _ in range(NFC)]

    regA = Region(W, SB0 + NCH * 192, BIGN)
    P.region = regA
    d18 = Buf(P, [128, 18], F32, "d18"); e18 = Buf(P, [128, 18], F32, "e18"); lg18 = Buf(P, [128, 18], F32, "lg18")
    DT = Buf(P, [128, 6, 128], F32, "DT")
    QDF = Buf(P, [128, 3, 128], F32, "QDF"); QDB = Buf(P, [128, 3, 128], F32, "QDB")
    KDF = Buf(P, [128, 6], F32, "KDF"); KDB = Buf(P, [128, 6], F32, "KDB")
    CDF = Buf(P, [128, 3], F32, "CDF"); CDB = Buf(P, [128, 3], F32, "CDB")
    ESK = Buf(P, [128, 6], F32, "ESK")
    SN = Buf(P, [128, 256], F32, "SN")
    WsT = Buf(P, [128, 4, 128], BF16, "WsT"); SGB = Buf(P, [128, 4], F32, "SGB")
    P.region = None
    CW = Buf(P, [128, 2 * NFC, 4], F32, "CW")
    gfm = Buf(P, [128, 2, 8], F32, "gfm")
    fm = [Buf(P, [128, NSEG, 8], F32, f"fm{i}") for i in range(4)]
    A1f = Buf(P, [128, NSEG, 8], F32, "A1f"); A2f = Buf(P, [128, NSEG, 8], F32, "A2f")
    Gt = Ring(P, 1, [128, D], F32, "Gt")
    dl = [DS() for _ in range(4)]
    dmb = [DS() for _ in range(2)]; dgp = [DS() for _ in range(2)]; dgb = [DS() for _ in range(2)]

    xcr = Ring(P, 3, [128, D], F32, "xc")
    st = Ring(P, 16, [128, 8], F32, "st")
    xn = Ring(P, 1, [128, D], BF16, "xn")
    xnew = Ring(P, 2, [128, D], F32, "xnew")
    P.region = regA
    hTr = Ring(P, 2, [128, 8, 128], BF16, "hT")
    gA = Gen([2, 3, 5, 6])
    st_rings = {None: st, "SC": Ring(P, 8, [128, 8], F32, "stSC"), "R": Ring(P, 8, [128, 8], F32, "stR"), "L": Ring(P, 8, [128, 8], F32, "stL"),
                "N": Ring(P, 8, [128, 8], F32, "stN"), "U": st}
    gens = {None: gA, "SC": Gen([2]), "R": Gen([3, 5]), "L": Gen([6]), "N": Gen([0]), "U": gA}
    gLpost = Gen([6, 7])
    _rg = P.region
    P.region = None
    st_rings["K"] = st
    gens["K"] = gA
    st_rings["NB"] = Ring(P, 8, [128, 8], F32, "stNB")
    gens["NB"] = Gen([0])
    P.region = _rg
    CUR = {"st": st, "gen": gA}
    Tb1a = Tbank[1]; Tb1b = Tbank[1]

    def use(name):
        P.set_stream(name)
        CUR["st"] = st_rings[name]
        CUR["gen"] = gens[name]

    u_b = Buf(P, [128, 256], BF16, "u"); vg = Buf(P, [128, 256], F32, "vg"); sq = Buf(P, [128, 256], F32, "sq")
    vn = Buf(P, [128, 256], F32, "vn"); vn2 = Buf(P, [128, 256], BF16, "vn2"); gt_ = Buf(P, [128, 256], F32, "gt")
    r1 = Buf(P, [128, 384], F32, "r1"); r2 = Buf(P, [128, 384], F32, "r2"); rsum = Buf(P, [128, 384], F32, "rsum")
    qr = Buf(P, [128, 384], BF16, "qr"); kr = Buf(P, [128, 384], BF16, "kr")
    kdf = Buf(P, [128, 384], BF16, "kdf"); kdb = Buf(P, [128, 384], BF16, "kdb")
    vt = Buf(P, [128, 384], BF16, "vt"); sg = Buf(P, [128, 384], F32, "sg")
    qkT = Buf(P, [128, 6, 128], BF16, "qkT")
    qdf = Buf(P, [128, 3, 128], BF16, "qdf"); qdb = Buf(P, [128, 3, 128], BF16, "qdb")
    PT = Buf(P, [128, 6, 128], BF16, "PT")
    Rf = Buf(P, [128, 3, 64], F32, "Rf"); Rb = Buf(P, [128, 3, 64], F32, "Rb"); Rt = Buf(P, [128, 3, 64], F32, "Rt")
    Sfb = Buf(P, [128, 3, 64], BF16, "Sfb")
    ysq = Buf(P, [128, 384], F32, "ysq"); yc_ = Buf(P, [128, 384], F32, "yc"); yn_ = Buf(P, [128, 384], F32, "yn")
    cqb = Buf(P, [128, 384], BF16, "cqb"); ckb = Buf(P, [128, 128], BF16, "ckb")
    a1 = Buf(P, [128, 6, 16], F32, "a1"); a2 = Buf(P, [128, 6, 16], F32, "a2")
    cqT = Ring(P, 3, [128, 3, 128], BF16, "cqT"); ckT = Ring(P, 4, [128, 128], BF16, "ckT")
    cvb = Ring(P, 4, [128, 2, 65], BF16, "cvb")
    Eb = Ring(P, 6, [128, 3, 128], BF16, "Eb")
    mix = Ring(P, 3, [128, D], BF16, "mix")
    mixT = Buf(P, [128, 8, 128], BF16, "mixT")
    cm = Buf(P, [128, 6, 128], F32, "cmats")
    ta = Buf(P, [128, 128], F32, "ta"); tb = Buf(P, [128, 128], F32, "tb")
    wad = Ring(P, 2, [128, 8, 256], BF16, "wad")
    badb = Ring(P, 2, [NSEG, 256], F32, "badb")
    mblk = Ring(P, 2, [NSEG, 256], F32, "mblk")
    gpb = Ring(P, 2, [NSEG, 256], F32, "gpb")
    gblk = Ring(P, 2, [NSEG, 256], F32, "gblk")
    print("regA end", regA.off, BIGN)
    regB = Region(W, ARENA, BIGN)
    P.region = regB
    HB = Ring(P, 3, [128, 8, 258], BF16, "HB")
    T0 = Ring(P, 2, [128, 256], F32, "T0"); T1 = Ring(P, 2, [128, 256], F32, "T1"); T2 = Ring(P, 2, [128, 256], F32, "T2")
    gg = Ring(P, 1, [128, 256], F32, "gg")
    actT = Ring(P, 2, [128, NFC, 256], BF16, "actT")
    print("regB end", regB.off, BIGN)
    P.region = None
    dx = [DS() for _ in range(4)]
    dst_ = [DS() for _ in range(2)]
    dG = [DS() for _ in range(2)]
    dcp = [DS() for _ in range(4)]


    def rstd_from(ssb, n_inv, out_col):
        sbuf, c = ssb
        obuf, oc = out_col
        t = CUR["st"].next()
        P.I("dve", "tensor_scalar", [sbuf.T], [t.T], out=t[:, 0:1], in0=sbuf[:, c:c + 1], scalar1=n_inv, scalar2=EPS, op0=ALU.mult, op1=ALU.add)
        P.I("pool", "tensor_tensor", [t.T, cmh.T], [obuf.T], out=obuf[:, oc:oc + 1], in0=t[:, 0:1], in1=cmh[:, 0:1], op=ALU.pow)

    def layer_setup(l):
        P.dma("sp", cm[:], cm_in, [], [cm.T], ND("cm"))
        TmD = [Tile(f"modD{l}_{i}") for i in range(24)]
        TgDl = [Tile(f"gD{l}_{i}") for i in range(8)]
        for cb in range(24):
            c0 = cb * 256
            wb = wad.next(); bb = badb.next(); mb = mblk.next()
            P.dma("pool", wb[:], w_ada[l, :, c0:c0 + 256].rearrange("(k p) n -> p k n", p=128), [], [wb.T], dl[cb % 2])
            P.dma("sp", bb[:], b_ada[l:l + 1, c0:c0 + 256].partition_broadcast(NSEG), [], [bb.T], dl[2 + cb % 2])
            pb, Tpb = CUR["gen"].next()
            for kc in range(8):
                P.I("pe", "matmul", [siluT.T, wb.T], [Tpb], pb[0:NSEG, 0:256], lhsT=siluT[:, kc, :], rhs=wb[:, kc, :], start=(kc == 0), stop=(kc == 7))
            P.I("dve", "tensor_tensor", [Tpb, bb.T], [mb.T], out=mb[:], in0=pb[0:NSEG, 0:256], in1=bb[:], op=ALU.add)
            P.dma("sp", modD[l, :, c0:c0 + 256], mb[:], [mb.T], [TmD[cb]], dmb[cb % 2])
            part = cb // 4
            if part in (2, 5):
                gi = 0 if part == 2 else 1
                j = cb % 4
                gp = gpb.next(); gb_ = gblk.next()
                P.dma("sp", gp[:], gpost_in[l:l + 1, gi * D + j * 256:gi * D + (j + 1) * 256].partition_broadcast(NSEG), [], [gp.T], dgp[gpb.i % 2])
                P.I("dve", "tensor_tensor", [mb.T, gp.T], [gb_.T], out=gb_[:], in0=mb[:], in1=gp[:], op=ALU.mult)
                P.dma("sp", gD[l, gi, :, j * 256:(j + 1) * 256], gb_[:], [gb_.T], [TgDl[gi * 4 + j]], dgb[gblk.i % 2])
        TmodD[l] = TmD
        TgD[l] = TgDl
        for i, part in enumerate([0, 1, 3, 4]):
            for s_ in range(NSEG):
                P.dma("sp", fm[i][:, s_, :], modD[l, s_, part * D:(part + 1) * D].rearrange("(k p) -> p k", p=128), TmodD[l][part * 4:part * 4 + 4], [fm[i].T], ND(f"fm{i}"), slow=True)
        P.dma("sp", gfm[:], gfm_in[l], [], [gfm.T], ND("gfm"))
        for (Af, scb, gi) in [(A1f, fm[1], 0), (A2f, fm[3], 1)]:
            P.I("dve", "scalar_tensor_tensor", [scb.T, gfm.T], [Af.T],
                out=Af[:], in0=scb[:], scalar=1.0, in1=gfm[:, gi, :].unsqueeze(1).to_broadcast([128, NSEG, 8]),
                op0=ALU.add, op1=ALU.mult)
        P.dma("sp", d18[:], dec18_in[l], [], [d18.T], ND("d18"))
        P.dma("sp", ESK[:], sink_in[l], [], [ESK.T], ND("ESK"))
        P.dma("sp", SN[:], sgun_in[l:l + 1, :].partition_broadcast(128), [], [SN.T], ND("SN"))
        P.dma("pool", WsT[:], sguw_in[l], [], [WsT.T], ND("WsT"))
        P.dma("sp", SGB[:], sgub_in[l], [], [SGB.T], ND("SGB"))
        P.I("act", "activation", [d18.T], [e18.T], out=e18[:], in_=d18[:], func=AF.Exp, scale=-1.0)
        P.I("dve", "tensor_scalar", [e18.T], [e18.T], out=e18[:], in0=e18[:], scalar1=1.0, scalar2=None, op0=ALU.add)
        P.I("act", "activation", [e18.T], [lg18.T], out=lg18[:], in_=e18[:], func=AF.Ln)
        P.I("dve", "tensor_scalar", [lg18.T], [lg18.T], out=lg18[:], in0=lg18[:], scalar1=-1.0, scalar2=None, op0=ALU.mult)
        P.I("act", "activation", [ESK.T], [ESK.T], out=ESK[:], in_=ESK[:], func=AF.Exp)
        for h in range(6):
            P.I("act", "activation", [cm.T, lg18.T], [ta.T], out=ta[:], in_=cm[:, 0, :], func=AF.Exp, scale=lg18[:, h:h + 1])
            P.I("act", "activation", [cm.T, lg18.T], [tb.T], out=tb[:], in_=cm[:, 1, :], func=AF.Exp, scale=lg18[:, 6 + h:7 + h])
            P.I("dve", "scalar_tensor_tensor", [ta.T, cm.T], [ta.T], out=ta[:], in0=ta[:], scalar=0.125, in1=cm[:, 2, :], op0=ALU.mult, op1=ALU.mult)
            P.I("dve", "scalar_tensor_tensor", [tb.T, cm.T], [tb.T], out=tb[:], in0=tb[:], scalar=0.125, in1=cm[:, 3, :], op0=ALU.mult, op1=ALU.mult)
            P.I("dve", "tensor_tensor", [ta.T, tb.T], [DT.T], out=DT[:, (h % 2) * 3 + h // 2, :], in0=ta[:], in1=tb[:], op=ALU.add)
        for q in range(3):
            P.I("act", "activation", [cm.T, lg18.T], [QDF.T], out=QDF[:, q, :], in_=cm[:, 4, :], func=AF.Exp, scale=lg18[:, 12 + q:13 + q])
            P.I("act", "activation", [cm.T, lg18.T], [QDB.T], out=QDB[:, q, :], in_=cm[:, 5, :], func=AF.Exp, scale=lg18[:, 15 + q:16 + q])
        P.I("act", "activation", [lg18.T, jv.T], [KDF.T], out=KDF[:], in_=lg18[:, 0:6], func=AF.Exp, scale=jv[:, 0:1])
        P.I("act", "activation", [lg18.T, jv.T], [KDB.T], out=KDB[:], in_=lg18[:, 6:12], func=AF.Exp, scale=jv[:, 1:2])
        P.I("dve", "tensor_scalar", [KDF.T], [KDF.T], out=KDF[:], in0=KDF[:], scalar1=0.125, scalar2=None, op0=ALU.mult)
        P.I("dve", "tensor_scalar", [KDB.T], [KDB.T], out=KDB[:], in0=KDB[:], scalar1=0.125, scalar2=None, op0=ALU.mult)
        P.I("act", "activation", [lg18.T], [CDF.T], out=CDF[:], in_=lg18[:, 12:15], func=AF.Exp, scale=128.0)
        P.I("act", "activation", [lg18.T], [CDB.T], out=CDB[:], in_=lg18[:, 15:18], func=AF.Exp, scale=128.0)

    def load_A_weights(l):
        for k in range(8):
            P.dma("pool", win_v[:, k, :], w_in[l, k * 128:(k + 1) * 128, :], [], [Twin[k]] + (B_tiles if k == 0 else []), dWin[k])
        for k in range(8):
            P.dma("pool", wout_v[:, k, :], w_out[l, k * 128:(k + 1) * 128, :], [], [Twout[k]], dWout[k])

    def load_B_weights(l):
        for k in range(8):
            P.dma("pool", wup_v[:, k, :], w_up[l, k * 128:(k + 1) * 128, :], [], [Twup[k]] + (A_tiles if k == 0 else []), dWup[k])
        for k in range(NFC):
            P.dma("pool", wdn_v[:, k, :], w_down[l, k * 128:(k + 1) * 128, :], [], [Twdn[k]], dWdn[k])

    def load_x(src, Tsrc, n, ring, dlist):
        xc = ring.next()
        P.dma("sp", xc[:], src[n * 128:(n + 1) * 128, :], [Tsrc[n]] if Tsrc is not None else [], [xc.T], dlist[ring.i % ring.n])
        return xc

    def norm_front(xc):
        s = CUR["st"].next()
        xb_ = xn.next()
        P.I("act", "activation", [xc.T], [xb_.T, s.T], out=xb_[:], in_=xc[:], func=AF.Square, accum_out=s[:, 0:1])
        rstd_from((s, 0), 1.0 / D, (s, 1))
        P.I("dve", "tensor_scalar", [xc.T, s.T], [xb_.T], out=xb_[:], in0=xc[:], scalar1=s[:, 1:2], scalar2=None, op0=ALU.mult)
        return xb_

    def norm_T(xc, seg, Af, Bf, dst_fn, dstT):
        norm_back(norm_front(xc), seg, Af, Bf, dst_fn, dstT)

    def norm_back(xb_, seg, Af, Bf, dst_fn, dstT):
        for kc in range(8):
            P.I("pe", "transpose", [xb_.T, ident.T], [Tbank[0]], out=bankb(0)[:, kc * 128:(kc + 1) * 128], in_=xb_[:, kc * 128:(kc + 1) * 128], identity=ident[:])
        for kc in range(8):
            P.I("act", "activation", [Tbank[0], Af.T, Bf.T], [dstT], out=dst_fn(kc), in_=bankb(0)[:, kc * 128:(kc + 1) * 128], func=AF.Identity,
                                                              scale=Af[:, seg, kc:kc + 1], bias=Bf[:, seg, kc:kc + 1])

    def proj(hT, wv, Tw, c0, ncol, gen):
        pb, Tpb = gen.next()
        for kc in range(8):
            P.I("pe", "matmul", [hT.T, Tw[kc]], [Tpb], pb[:, 0:ncol], lhsT=hT[:, kc, :], rhs=wv[:, kc, c0:c0 + ncol], start=(kc == 0), stop=(kc == 7))
        return pb, Tpb

    def load_rope(n):
        rp = ropeR.next()
        P.dma("sp", rp[:], rope_in[n], [], [rp.T], drope[ropeR.i % 3])
        return rp

    def rope64(pb, Tpb, rp, out_f32):
        x3 = pb[:, 0:384].rearrange("p (a d) -> p a d", d=32)
        x4 = pb[:, 0:384].rearrange("p (h t d) -> p h t d", h=6, t=2)
        P.I("dve", "tensor_tensor", [Tpb, rp.T], [r1.T], out=r1[:].rearrange("p (a d) -> p a d", d=32), in0=x3,
                                              in1=rp[:, 0:32].unsqueeze(1).to_broadcast([128, 12, 32]), op=ALU.mult)
        r2v = r2[:].rearrange("p (h t d) -> p h t d", h=6, t=2)
        P.I("dve", "tensor_tensor", [Tpb, rp.T], [r2.T], out=r2v[:, :, 0, :], in0=x4[:, :, 1, :],
                                              in1=rp[:, 64:96].unsqueeze(1).to_broadcast([128, 6, 32]), op=ALU.mult)
        P.I("dve", "tensor_tensor", [Tpb, rp.T], [r2.T], out=r2v[:, :, 1, :], in0=x4[:, :, 0, :],
                                              in1=rp[:, 32:64].unsqueeze(1).to_broadcast([128, 6, 32]), op=ALU.mult)
        P.I("dve", "tensor_tensor", [r1.T, r2.T], [out_f32.T], out=out_f32[:], in0=r1[:], in1=r2[:], op=ALU.add)

    def kv_update(kd, R, CD, n, store_fn, store_T, gen, boundary):
        pb, Tpb = gen.next()
        for q in range(3):
            P.I("pe", "matmul", [kd.T, vt.T], [Tpb], pb[:, q * 128:(q + 1) * 128], lhsT=kd[:, q * 128:(q + 1) * 128], rhs=vt[:, q * 128:(q + 1) * 128],
                                                       start=True, stop=True)
        P.I("dve", "tensor_tensor", [R.T, CD.T], [Rt.T], out=Rt[:], in0=R[:], in1=CD[:].unsqueeze(2).to_broadcast([128, 3, 64]), op=ALU.mult)
        kv3 = pb[:, 0:384].rearrange("p (q e) -> p q e", q=3)
        P.I("dve", "tensor_tensor", [Rt.T, Tpb], [R.T], out=R[0:64], in0=Rt[0:64], in1=kv3[0:64, :, 0:64], op=ALU.add)
        P.I("dve", "tensor_tensor", [Rt.T, Tpb], [R.T], out=R[64:128], in0=Rt[64:128], in1=kv3[64:128, :, 64:128], op=ALU.add)
        if boundary:
            P.I("dve", "tensor_scalar", [R.T, flag.T], [R.T], out=R[:], in0=R[:], scalar1=flag[:, 0:1], scalar2=None, op0=ALU.mult)
        if store_fn is not None:
            P.I("act", "activation", [R.T], [store_T], out=store_fn, in_=R[:], func=AF.Copy)

    def post_stage(lhs_fn, K, wv, Tw, xc, Gb, n, dst, Tdst, gen):
        pbs = []
        for half in range(2):
            pb, Tpb = gen.next()
            for kc in range(K):
                lh, Tl = lhs_fn(kc)
                P.I("pe", "matmul", [Tl, Tw[kc]], [Tpb], pb[:, :], lhsT=lh, rhs=wv[:, kc, half * 512:(half + 1) * 512],
                                                                                      start=(kc == 0), stop=(kc == K - 1))
            pbs.append((pb, Tpb))
        s = CUR["st"].next()
        xo = xnew.next()
        for half in range(2):
            pb, Tpb = pbs[half]
            P.I("act", "activation", [Tpb], [xo.T, s.T], out=xo[:, 0:512], in_=pb[:, :], func=AF.Square, accum_out=s[:, half:half + 1])
        P.I("dve", "tensor_tensor", [s.T], [s.T], out=s[:, 2:3], in0=s[:, 0:1], in1=s[:, 1:2], op=ALU.add)
        rstd_from((s, 2), 1.0 / D, (s, 3))
        for half in range(2):
            pb, Tpb = pbs[half]
            P.I("dve", "scalar_tensor_tensor", [Tpb, s.T, Gb.T], [xo.T], out=xo[:, half * 512:(half + 1) * 512], in0=pb[:, :], scalar=s[:, 3:4],
                                                                                   in1=Gb[:, half * 512:(half + 1) * 512], op0=ALU.mult, op1=ALU.mult)
        P.I("pool", "tensor_tensor", [xc.T, xo.T], [xo.T], out=xo[:], in0=xo[:], in1=xc[:], op=ALU.add)
        P.dma("sp", dst[n * 128:(n + 1) * 128, :], xo[:], [xo.T], [Tdst[n]], dst_[xnew.i % 2])

    def load_G(l, gi, seg):
        Gb = Gt.next()
        P.dma("sp", Gb[:], gD[l, gi, seg:seg + 1, :].partition_broadcast(128), TgD[l][gi * 4:gi * 4 + 4], [Gb.T], dG[0])
        return Gb

    def phase_A0(l, src, Tsrc):
        P.I("pool", "memset", [], [Rb.T], Rb[:], 0.0)
        P.I("pool", "memset", [], [Tsb[NCH - 1]], sbst_v[:, NCH - 1], 0.0)
        hTs, rps = {}, {}

        xbs = {}

        def front_a(c):
            xc = load_x(src, Tsrc, c, xcr, dx)
            rps[c] = load_rope(c)
            xbs[c] = norm_front(xc)

        def front_b(c):
            hT = hTr.next()
            hTs[c] = hT
            norm_back(xbs.pop(c), c // SEGC, A1f, fm[0], lambda kc: hT[:, kc, :], hT.T)

        front_a(NCH - 1)
        front_b(NCH - 1)
        for n in range(NCH - 1, 0, -1):
            if n - 1 >= 1:
                use("N")
                front_a(n - 1)
                front_b(n - 1)
            use("K")
            hT = hTs.pop(n)
            rp = rps.pop(n)
            pk, Tpk = proj(hT, win_v, Twin, 896, 384, CUR["gen"])
            pv, Tpv = proj(hT, win_v, Twin, 1280, 384, CUR["gen"])
            rope64(pk, Tpk, rp, rsum)
            P.I("dve", "tensor_tensor", [rsum.T, KDB.T], [kdb.T], out=kdb[:].rearrange("p (h d) -> p h d", h=6), in0=rsum[:].rearrange("p (h d) -> p h d", h=6),
                in1=KDB[:].unsqueeze(2).to_broadcast([128, 6, 64]), op=ALU.mult)
            P.I("act", "activation", [Tpv], [vt.T], out=vt[:], in_=pv[:, 0:384], func=AF.Copy)
            kv_update(kdb, Rb, CDB, n, sbst_v[:, n - 1], Tsb[n - 1], CUR["gen"], boundary=(n % SEGC == 0))
            use(None)
            P.flush()

    def attention(l, m):
        mx = mix.at(m)
        pO, TpO = bank(7), Tbank[7]
        O3 = pO[:, 0:390].rearrange("p (h e) -> p h e", h=6)
        blks = [b for b in (-1, 0, 1) if 0 <= m + b < NCH]
        for kvh in range(2):
            Es = []
            for bi, b in enumerate(blks):
                pS, TpS = CUR["gen"].next()
                kT = ckT.at(m + b); qT = cqT.at(m)
                P.I("pe", "matmul", [kT.T, qT.T], [TpS], pS[:, 0:384], lhsT=kT[kvh * 64:(kvh + 1) * 64, :],
                    rhs=qT[kvh * 64:(kvh + 1) * 64, :, :], start=True, stop=True)
                E = Eb.next()
                P.I("act", "activation", [TpS], [E.T], out=E[:].rearrange("p g i -> p (g i)"), in_=pS[:, 0:384], func=AF.Exp, scale=0.125)
                if b != 0:
                    bnd = (b == -1 and m % SEGC == 0) or (b == 1 and m % SEGC == SEGC - 1)
                    mk = trif if bnd else tri
                    mi = 0 if b == -1 else 1
                    P.I("pool", "tensor_tensor", [E.T, mk.T], [E.T], out=E[:], in0=E[:], in1=mk[:, mi, :].unsqueeze(1).to_broadcast([128, 3, 128]), op=ALU.mult)
                Es.append(E)
            for g in range(3):
                for bi, b in enumerate(blks):
                    cv = cvb.at(m + b)
                    P.I("pe", "matmul", [Es[bi].T, cv.T], [TpO], O3[:, kvh * 3 + g, :], lhsT=Es[bi][:, g, :], rhs=cv[:, kvh, :],
                        start=(bi == 0), stop=(bi == len(blks) - 1))
        s = CUR["st"].next()
        P.I("dve", "tensor_tensor", [TpO, ESK.T], [s.T], out=s[:, 0:6], in0=O3[:, :, 64], in1=ESK[:], op=ALU.add)
        P.I("dve", "reciprocal", [s.T], [s.T], out=s[:, 0:6], in_=s[:, 0:6])
        P.I("dve", "tensor_tensor", [TpO, s.T], [mx.T], out=mx[:, 640:1024].rearrange("p (h d) -> p h d", h=6), in0=O3[:, :, 0:64],
                                              in1=s[:, 0:6].unsqueeze(2).to_broadcast([128, 6, 64]), op=ALU.mult)

    def out_stage(l, m, xc, Gb, dst, Tdst):
        mx = mix.at(m)
        for half in range(2):
            for k4 in range(4):
                kc = half * 4 + k4
                P.I("pe", "transpose", [mx.T, ident.T], [Tb1a], out=bankb(1)[:, k4 * 128:(k4 + 1) * 128], in_=mx[:, kc * 128:(kc + 1) * 128], identity=ident[:])
            P.I("act", "activation", [Tb1a], [mixT.T], out=mixT[:, half * 4:(half + 1) * 4, :].rearrange("p k i -> p (k i)"), in_=bankb(1)[:, 0:512], func=AF.Copy)
        post_stage(lambda kc: (mixT[:, kc, :], mixT.T), 8, wout_v, Twout, xc, Gb, m, dst, Tdst, gLpost)

    def phase_A1(l, src, Tsrc, dst, Tdst):
        P.I("pool", "memset", [], [Rf.T], Rf[:], 0.0)
        P.I("pool", "memset", [], [Sfb.T], Sfb[:], 0.0)
        for b_ in cvb.b:
            P.I("pool", "memset", [], [b_.T], b_[:], 1.0)
        xcs = {}
        Gb = None
        Gseg = {}
        LAG = 2
        hTs, rps = {}, {}

        class _SR:
            def __init__(self, bufs):
                self.b = bufs; self.n = len(bufs); self.i = -1

            def next(self):
                self.i += 1
                return self.b[self.i % self.n]

        xNr = _SR(xcr.b[0:2]); xLr = _SR(xcr.b[2:3])

        def attn_proj(n, hT, rp):
            pcq, Tpcq = proj(hT, win_v, Twin, 2048, 384, CUR["gen"])
            P.I("act", "activation", [Tpcq], [cqb.T], out=cqb[:], in_=pcq[:, 0:384], func=AF.Copy)
            c3 = pcq[:, 0:384].rearrange("p (h d) -> p h d", h=6)
            P.I("dve", "tensor_tensor", [Tpcq, rp.T], [a1.T], out=a1[:].rearrange("p h (t d) -> p h t d", t=2), in0=c3[:, :, 0:16].rearrange("p h (t d) -> p h t d", t=2),
                                                  in1=rp[:, 96:104].unsqueeze(1).unsqueeze(1).to_broadcast([128, 6, 2, 8]), op=ALU.mult)
            P.I("dve", "tensor_tensor", [Tpcq, rp.T], [a2.T], out=a2[:, :, 0:8], in0=c3[:, :, 8:16], in1=rp[:, 112:120].unsqueeze(1).to_broadcast([128, 6, 8]), op=ALU.mult)
            P.I("dve", "tensor_tensor", [Tpcq, rp.T], [a2.T], out=a2[:, :, 8:16], in0=c3[:, :, 0:8], in1=rp[:, 104:112].unsqueeze(1).to_broadcast([128, 6, 8]), op=ALU.mult)
            P.I("pool", "tensor_tensor", [a1.T, a2.T], [cqb.T], out=cqb[:].rearrange("p (h d) -> p h d", h=6)[:, :, 0:16], in0=a1[:], in1=a2[:], op=ALU.add)
            pck, Tpck = proj(hT, win_v, Twin, 2432, 256, CUR["gen"])
            cv = cvb.at(n)
            P.I("act", "activation", [Tpck], [ckb.T], out=ckb[:], in_=pck[:, 0:128], func=AF.Copy)
            P.I("act", "activation", [Tpck], [cv.T], out=cv[:, :, 0:64], in_=pck[:, 128:256].rearrange("p (h d) -> p h d", h=2), func=AF.Copy)
            k3 = pck[:, 0:128].rearrange("p (h d) -> p h d", h=2)
            P.I("dve", "tensor_tensor", [Tpck, rp.T], [a1.T], out=a1[:, 0:2, :].rearrange("p h (t d) -> p h t d", t=2), in0=k3[:, :, 0:16].rearrange("p h (t d) -> p h t d", t=2),
                                                  in1=rp[:, 96:104].unsqueeze(1).unsqueeze(1).to_broadcast([128, 2, 2, 8]), op=ALU.mult)
            P.I("dve", "tensor_tensor", [Tpck, rp.T], [a2.T], out=a2[:, 0:2, 0:8], in0=k3[:, :, 8:16], in1=rp[:, 112:120].unsqueeze(1).to_broadcast([128, 2, 8]), op=ALU.mult)
            P.I("dve", "tensor_tensor", [Tpck, rp.T], [a2.T], out=a2[:, 0:2, 8:16], in0=k3[:, :, 0:8], in1=rp[:, 104:112].unsqueeze(1).to_broadcast([128, 2, 8]), op=ALU.mult)
            P.I("pool", "tensor_tensor", [a1.T, a2.T], [ckb.T], out=ckb[:].rearrange("p (h d) -> p h d", h=2)[:, :, 0:16], in0=a1[:, 0:2, :], in1=a2[:, 0:2, :], op=ALU.add)
            for q in range(3):
                P.I("pe", "transpose", [cqb.T, ident.T], [Tb1b], out=bankb(1)[:, 512 + q * 128:512 + (q + 1) * 128], in_=cqb[:, q * 128:(q + 1) * 128], identity=ident[:])
            P.I("pe", "transpose", [ckb.T, ident.T], [Tb1b], out=bankb(1)[:, 896:1024], in_=ckb[:], identity=ident[:])
            cq_t = cqT.at(n); ck_t = ckT.at(n)
            P.I("act", "activation", [Tb1b], [cq_t.T], out=cq_t[:].rearrange("p a i -> p (a i)"), in_=bankb(1)[:, 512:896], func=AF.Copy)
            P.I("act", "activation", [Tb1b], [ck_t.T], out=ck_t[:], in_=bankb(1)[:, 896:1024], func=AF.Copy)


        xbs = {}

        def front_a(c):
            xc = load_x(src, Tsrc, c, xNr, dx[0:2])
            rps[c] = load_rope(c)
            xbs[c] = norm_front(xc)

        def front_b(c):
            hT = hTr.next()
            hTs[c] = hT
            norm_back(xbs.pop(c), c // SEGC, A1f, fm[0], lambda kc: hT[:, kc, :], hT.T)

        front_a(0)
        front_b(0)
        for n in range(NCH + LAG):
            if n < NCH:
                hT = hTs.pop(n)
                rp = rps.pop(n)
                mx = mix.at(n)
            use("N")
            if n < NCH:
                attn_proj(n, hT, rp)
            if n + 1 < NCH:
                front_a(n + 1)
                front_b(n + 1)
            if n < NCH:
                use("SC")
                if 'sgu' in DBG:
                    pb, Tpb = proj(hT, win_v, Twin, 0, 512, CUR["gen"])
                    P.I("act", "activation", [Tpb], [u_b.T], out=u_b[:], in_=pb[:, 0:256], func=AF.Gelu_apprx_tanh)
                    P.I("act", "activation", [Tpb], [vg.T], out=vg[:], in_=pb[:, 256:512], func=AF.Gelu_apprx_tanh)
                    P.I("dve", "tensor_tensor", [vg.T], [sq.T], out=sq[:], in0=vg[:], in1=vg[:], op=ALU.mult)
                    s = CUR["st"].next()
                    P.I("dve", "tensor_reduce", [sq.T], [s.T], out=s[:, 0:4], in_=sq[:].rearrange("p (g d) -> p g d", g=4), axis=AX.X, op=ALU.add)
                    P.I("dve", "tensor_scalar", [s.T], [s.T], out=s[:, 0:4], in0=s[:, 0:4], scalar1=1.0 / 64, scalar2=EPS, op0=ALU.mult, op1=ALU.add)
                    P.I("pool", "tensor_tensor", [s.T, cmh.T], [s.T], out=s[:, 4:8], in0=s[:, 0:4], in1=cmh[:, 0:4], op=ALU.pow)
                    P.I("dve", "tensor_tensor", [vg.T, s.T], [vn.T], out=vn[:].rearrange("p (g d) -> p g d", g=4), in0=vg[:].rearrange("p (g d) -> p g d", g=4),
                                                          in1=s[:, 4:8].unsqueeze(2).to_broadcast([128, 4, 64]), op=ALU.mult)
                    P.I("pool", "tensor_tensor", [vn.T, SN.T], [vn2.T], out=vn2[:], in0=vn[:], in1=SN[:], op=ALU.mult)
                    pg, Tpg = CUR["gen"].next()
                    for g in range(4):
                        P.I("pe", "matmul", [WsT.T, vn2.T], [Tpg], pg[:, g * 64:(g + 1) * 64], lhsT=WsT[:, g, :], rhs=vn2[:, g * 64:(g + 1) * 64], start=True, stop=True)
                    P.I("dve", "tensor_tensor", [Tpg, SGB.T], [gt_.T], out=gt_[:].rearrange("p (g d) -> p g d", g=4), in0=pg[:, 0:256].rearrange("p (g d) -> p g d", g=4),
                                                          in1=SGB[:].unsqueeze(2).to_broadcast([128, 4, 64]), op=ALU.add)
                    P.I("pool", "tensor_tensor", [gt_.T, u_b.T], [mx.T], out=mx[:, 0:256], in0=gt_[:], in1=u_b[:], op=ALU.mult)
                use("R")
                if 'ret' in DBG:
                    pq, Tpq = proj(hT, win_v, Twin, 512, 384, CUR["gen"])
                    rope64(pq, Tpq, rp, rsum)
                    P.I("act", "activation", [rsum.T], [qr.T], out=qr[:], in_=rsum[:], func=AF.Copy)
                    pk, Tpk = proj(hT, win_v, Twin, 896, 384, CUR["gen"])
                    rope64(pk, Tpk, rp, rsum)
                    P.I("act", "activation", [rsum.T], [kr.T], out=kr[:], in_=rsum[:], func=AF.Copy)
                    P.I("dve", "tensor_tensor", [rsum.T, KDF.T], [kdf.T], out=kdf[:].rearrange("p (h d) -> p h d", h=6), in0=rsum[:].rearrange("p (h d) -> p h d", h=6),
                                                          in1=KDF[:].unsqueeze(2).to_broadcast([128, 6, 64]), op=ALU.mult)
                    pv, Tpv = proj(hT, win_v, Twin, 1280, 384, CUR["gen"])
                    P.I("act", "activation", [Tpv], [vt.T], out=vt[:], in_=pv[:, 0:384], func=AF.Copy)
                    pgg, Tpgg = proj(hT, win_v, Twin, 1664, 384, CUR["gen"])
                    P.I("act", "activation", [Tpgg], [sg.T], out=sg[:], in_=pgg[:, 0:384], func=AF.Silu)
                    if 'r1' in DBG2:
                        for i, srcb in enumerate([qr, kr]):
                            for q in range(3):
                                P.I("pe", "transpose", [srcb.T, ident.T], [Tbank[4]], out=bankb(4)[:, (i * 3 + q) * 128:(i * 3 + q + 1) * 128],
                                                                                               in_=srcb[:, q * 128:(q + 1) * 128], identity=ident[:])
                        P.I("act", "activation", [Tbank[4]], [qkT.T], out=qkT[:].rearrange("p a i -> p (a i)"), in_=bankb(4)[:, 0:768], func=AF.Copy)
                        P.I("dve", "tensor_tensor", [qkT.T, QDF.T], [qdf.T], out=qdf[:], in0=qkT[:, 0:3, :], in1=QDF[:], op=ALU.mult)
                        P.I("pool", "tensor_tensor", [qkT.T, QDB.T], [qdb.T], out=qdb[:], in0=qkT[:, 0:3, :], in1=QDB[:], op=ALU.mult)
                    if 'r2' in DBG2:
                        for par in range(2):
                            pS, TpS = CUR["gen"].next()
                            for q_ in range(3):
                                P.I("pe", "matmul", [qkT.T], [TpS], pS[:, q_ * 128:(q_ + 1) * 128], lhsT=qkT[par * 64:(par + 1) * 64, 3 + q_, :],
                                    rhs=qkT[par * 64:(par + 1) * 64, q_, :], start=True, stop=True)
                            P.I("dve", "tensor_tensor", [TpS, DT.T], [PT.T], out=PT[:, par * 3:(par + 1) * 3, :], in0=pS[:, 0:384].rearrange("p (h i) -> p h i", h=3),
                                in1=DT[:, par * 3:(par + 1) * 3, :], op=ALU.mult)
                    if 'r3' in DBG2:
                        pY, TpY = CUR["gen"].next()
                        for h in range(6):
                            q_, par = h // 2, h % 2
                            sl = slice(par * 64, (par + 1) * 64)
                            P.I("pe", "matmul", [PT.T, vt.T], [TpY], pY[:, h * 64:(h + 1) * 64], lhsT=PT[:, par * 3 + q_, :], rhs=vt[:, h * 64:(h + 1) * 64], start=True, stop=False)
                            P.I("pe", "matmul", [qdf.T, Sfb.T], [TpY], pY[:, h * 64:(h + 1) * 64], lhsT=qdf[sl, q_, :], rhs=Sfb[sl, q_, :], start=False, stop=False)
                            P.I("pe", "matmul", [qdb.T, Tsb[n]], [TpY], pY[:, h * 64:(h + 1) * 64], lhsT=qdb[sl, q_, :], rhs=sbst_v[sl, n, q_, :], start=False, stop=True)
                    if 'r4' in DBG2:
                        kv_update(kdf, Rf, CDF, n, Sfb[:], Sfb.T, CUR["gen"], boundary=((n + 1) % SEGC == 0))
                    if 'r5' in DBG2:
                        s2 = CUR["st"].next()
                        Y3 = pY[:, 0:384].rearrange("p (h d) -> p h d", h=6)
                        P.I("dve", "tensor_reduce", [TpY], [s2.T], out=s2[:, 0:6], in_=Y3, axis=AX.X, op=ALU.add)
                        P.I("act", "activation", [TpY], [ysq.T], out=ysq[:], in_=pY[:, 0:384], func=AF.Square)
                        s3 = CUR["st"].next()
                        P.I("dve", "tensor_reduce", [ysq.T], [s3.T], out=s3[:, 0:6], in_=ysq[:].rearrange("p (h d) -> p h d", h=6), axis=AX.X, op=ALU.add)
                        P.I("dve", "tensor_scalar", [s2.T], [s2.T], out=s2[:, 0:6], in0=s2[:, 0:6], scalar1=1.0 / 64, scalar2=None, op0=ALU.mult)
                        s4 = CUR["st"].next()
                        P.I("dve", "tensor_tensor", [s2.T], [s4.T], out=s4[:, 0:6], in0=s2[:, 0:6], in1=s2[:, 0:6], op=ALU.mult)
                        P.I("dve", "scalar_tensor_tensor", [s3.T, s4.T], [s3.T], out=s3[:, 0:6], in0=s3[:, 0:6], scalar=1.0 / 64, in1=s4[:, 0:6], op0=ALU.mult, op1=ALU.subtract)
                        P.I("dve", "tensor_scalar", [s3.T], [s3.T], out=s3[:, 0:6], in0=s3[:, 0:6], scalar1=EPS, scalar2=None, op0=ALU.add)
                        P.I("pool", "tensor_tensor", [s3.T, cmh.T], [s4.T], out=s4[:, 0:6], in0=s3[:, 0:6], in1=cmh[:, 0:6], op=ALU.pow)
                        P.I("dve", "tensor_tensor", [TpY, s2.T], [yc_.T], out=yc_[:].rearrange("p (h d) -> p h d", h=6), in0=Y3, in1=s2[:, 0:6].unsqueeze(2).to_broadcast([128, 6, 64]),
                                                              op=ALU.subtract)
                        P.I("dve", "tensor_tensor", [yc_.T, s4.T], [yn_.T], out=yn_[:].rearrange("p (h d) -> p h d", h=6), in0=yc_[:].rearrange("p (h d) -> p h d", h=6),
                                                              in1=s4[:, 0:6].unsqueeze(2).to_broadcast([128, 6, 64]), op=ALU.mult)
                        P.I("pool", "tensor_tensor", [yn_.T, sg.T], [mx.T], out=mx[:, 256:640], in0=yn_[:], in1=sg[:], op=ALU.mult)
            m = n - LAG
            if m >= 0:
                use("L")
                if 'attn' in DBG:
                    attention(l, m)
                if 'out' in DBG:
                    xr = load_x(src, Tsrc, m, xLr, dx[2:3])
                    out_stage(l, m, xr, Gseg[m // SEGC], dst, Tdst)
            use(None)
            P.flush()
            mf = n - (LAG - 1)
            if 0 <= mf < NCH and mf % SEGC == 0:
                Gseg[mf // SEGC] = load_G(l, 0, mf // SEGC)

    class SubRing:
        def __init__(self, bufs):
            self.b = bufs; self.n = len(bufs); self.i = -1

        def next(self):
            self.i += 1
            return self.b[self.i % self.n]

    def phase_B(l, src, Tsrc, dst, Tdst):
        P.dma("sp", CW[:], cw_in[l], [], [CW.T], ND("CW"))
        xN = SubRing(xcr.b[0:2]); xD = SubRing(xcr.b[2:3])
        dxN = dx[0:2]; dxD = dx[2:3]
        gB = Gen([1, 2, 3, 4, 5, 6, 7])
        hbs = {}
        normed = [-1]
        Gseg = {}

        def do_norm(c):
            hb_alloc(c)
            nback(c, nfront(c))

        def hb_alloc(c):
            b, half = c // 2, c % 2
            if half == 0:
                hb = HB.next()
                hbs[b] = hb
                if b == 0:
                    P.I("pool", "memset", [], [hb.T], hb[:, :, 0:1], 0.0)
                if b == NB - 1:
                    P.I("pool", "memset", [], [hb.T], hb[:, :, 257:258], 0.0)

        def nfront(c):
            xc = load_x(src, Tsrc, c, xN, dxN)
            return norm_front(xc)

        def nfront_pre(c):
            hb_alloc(c)
            return nfront(c)

        def nback(c, xb_):
            b, half = c // 2, c % 2
            seg = c // SEGC
            hb = hbs[b]
            o = 1 + half * 128
            norm_back(xb_, seg, A2f, fm[2], lambda kc: hb[:, kc, o:o + 128], hb.T)
            if half == 0 and b > 0:
                pv_ = hbs[b - 1]
                if c % SEGC == 0:
                    P.I("dve", "tensor_scalar", [hb.T, flag.T], [pv_.T], out=pv_[:, :, 257:258], in0=hb[:, :, 1:2], scalar1=flag[:, 0:1], scalar2=None, op0=ALU.mult)
                else:
                    P.I("pool", "tensor_copy", [hb.T], [pv_.T], out=pv_[:, :, 257:258], in_=hb[:, :, 1:2])
            normed[0] = c

        def halo_fwd(b):
            hb = hbs[b]; nx = hbs[b + 1]
            c = 2 * b + 1
            if (c + 1) % SEGC == 0:
                P.I("dve", "tensor_scalar", [hb.T, flag.T], [nx.T], out=nx[:, :, 0:1], in0=hb[:, :, 256:257], scalar1=flag[:, 0:1], scalar2=None, op0=ALU.mult)
            else:
                P.I("pool", "tensor_copy", [hb.T], [nx.T], out=nx[:, :, 0:1], in_=hb[:, :, 256:257])

        def up_pairs(b, f0, f1):
            hb = hbs[b]
            aT = actT.at(b)
            for fc in range(f0, f1):
                tl = []
                for which in range(2):
                    fcc = fc + which * NFC
                    pb, Tpb = gB.next()
                    for kc in range(8):
                        P.I("pe", "matmul", [Twup[kc], hb.T], [Tpb], pb[:, 0:258], lhsT=wup_v[:, kc, fcc * 128:(fcc + 1) * 128], rhs=hb[:, kc, :],
                            start=(kc == 0), stop=(kc == 7))
                    tl.append((fcc, pb, Tpb, T0.next(), T1.next(), T2.next()))
                for (fcc, pb, Tpb, t0, t1, t2) in tl:
                    P.I("act", "activation", [Tpb, CW.T], [t0.T], out=t0[:], in_=pb[:, 0:256], func=AF.Identity, scale=CW[:, fcc, 0:1], bias=CW[:, fcc, 3:4])
                for (fcc, pb, Tpb, t0, t1, t2) in tl:
                    P.I("dve", "scalar_tensor_tensor", [Tpb, CW.T, t0.T], [t1.T], out=t1[:], in0=pb[:, 1:257], scalar=CW[:, fcc, 1:2], in1=t0[:],
                        op0=ALU.mult, op1=ALU.add)
                for (fcc, pb, Tpb, t0, t1, t2) in tl:
                    P.I("dve", "scalar_tensor_tensor", [Tpb, CW.T, t1.T], [t2.T], out=t2[:], in0=pb[:, 2:258], scalar=CW[:, fcc, 2:3], in1=t1[:],
                        op0=ALU.mult, op1=ALU.add)
                g_ = gg.next()
                tg, tv = tl[0][5], tl[1][5]
                P.I("act", "activation", [tg.T], [g_.T], out=g_[:], in_=tg[:], func=AF.Gelu_apprx_tanh)
                P.I("pool", "tensor_tensor", [tv.T, g_.T], [aT.T], out=aT[:, fc, :], in0=tv[:], in1=g_[:], op=ALU.mult)

        def down(b):
            aT = actT.at(b)
            for half in range(2):
                c = 2 * b + half
                seg = c // SEGC
                if seg not in Gseg:
                    Gseg[seg] = load_G(l, 1, seg)
                xr = load_x(src, Tsrc, c, xD, dxD)
                post_stage(lambda kc: (aT[:, kc, half * 128:(half + 1) * 128], aT.T), NFC, wdn_v, Twdn, xr, Gseg[seg], c, dst, Tdst, gB)

        do_norm(0); do_norm(1)
        if NCH > 2:
            do_norm(2)
            halo_fwd(0)
        for b in range(NB):
            c1, c2 = 2 * b + 3, 2 * b + 4
            f1 = nfront_pre(c1) if c1 < NCH else None
            up_pairs(b, 0, 4)
            if b > 0:
                down(b - 1)
            up_pairs(b, 4, 8)
            if f1 is not None:
                nback(c1, f1)
            f2 = nfront_pre(c2) if c2 < NCH else None
            up_pairs(b, 8, 16)
            if f2 is not None:
                nback(c2, f2)
                halo_fwd(b + 1)
            up_pairs(b, 16, NFC)
        down(NB - 1)

    cur, Tcur = x_in, None
    for l in range(depth):
        P.barrier()
        load_A_weights(l)
        layer_setup(l)
        P.barrier()
        if stop_after == ("setup", l):
            break
        phase_A0(l, cur, Tcur)
        if stop_after == ("A0", l):
            break
        phase_A1(l, cur, Tcur, xa, Txa)
        if stop_after == ("A1", l):
            cur, Tcur = xa, Txa
            break
        P.barrier()
        load_B_weights(l)
        last = (l == depth - 1)
        dst, Tdst = (y_out, Ty) if last else (xb, Txb)
        phase_B(l, xa, Txa, dst, Tdst)
        cur, Tcur = dst, Tdst
    if cur is not y_out:
        for n in range(NCH):
            xc = load_x(cur, Tcur, n, xcr, dx)
            P.dma("sp", y_out[n * 128:(n + 1) * 128, :], xc[:], [xc.T], [Ty[n]], dcp[xcr.i % xcr.n])
    P.wait_all("sp", Ty)
    counts = P.emit()
    es.close()
    return nc, counts


def _rope_tables(pos, rot_dim, theta):
    half = rot_dim // 2
    freqs = np.exp(-math.log(theta) * np.arange(half, dtype=np.float32) * np.float32(2.0) / np.float32(rot_dim)).astype(np.float32)
    ang = pos.astype(np.float32)[:, None] * freqs[None, :]
    return np.cos(ang).astype(np.float32), np.sin(ang).astype(np.float32)


def core_tables(NCH, SEGC, is_prompt):
    n = np.arange(NCH)
    if is_prompt:
        base = n * 128
    else:
        base = (n % SEGC) * 128
    pos = (base[None, :] + np.arange(128)[:, None]).reshape(-1)
    rc_, rs_ = _rope_tables(pos, 64, 10000.0)
    ac_, as__ = _rope_tables(pos, 16, 500000.0)
    f = lambda a, d: a.reshape(128, NCH, d)
    rope = np.concatenate([f(rc_, 32), f(rs_, 32), f(-rs_, 32), f(ac_, 8), f(as__, 8), f(-as__, 8)], 2)
    return dict(rope=np.ascontiguousarray(rope.transpose(1, 0, 2)).astype(np.float32),
                flag=np.full((128, 1), 1.0 if is_prompt else 0.0, np.float32))


def const_inputs():
    j = np.arange(128, dtype=np.float32)[:, None]
    i = np.arange(128, dtype=np.float32)[None, :]
    dpos = np.maximum(i - j, 0); dneg = np.maximum(j - i, 0)
    mge = (i >= j).astype(np.float32); mlt = (i < j).astype(np.float32)
    io1 = np.broadcast_to(i + 1, (128, 128)); io2 = np.broadcast_to(128 - i, (128, 128))
    cm = np.stack([dpos, dneg, mge, mlt, io1, io2], 1).astype(np.float32)
    tri = np.stack([(j >= i).astype(np.float32) * np.ones((128, 128), np.float32), (j <= i).astype(np.float32) * np.ones((128, 128), np.float32)], 1)
    jv = np.concatenate([127 - j, j], 1).astype(np.float32)
    return dict(ident=np.eye(128, dtype=np.float32), cmats=np.ascontiguousarray(cm), tri=np.ascontiguousarray(tri), jv=jv)


def weight_inputs(w_ada, b_ada, norm_pre_mix, norm_post_mix, norm_pre_ffn, norm_post_ffn, w_in, sgu_norm, sgu_w, sgu_b,
                  ret_decay_fwd, ret_decay_bwd, attn_sink, w_out, w_up, conv_w, conv_b, w_down):
    L = w_in.shape[0]
    perm = np.arange(IN_DIM)
    hq = [0, 3, 1, 4, 2, 5]
    perm[2048:2432] = np.concatenate([2048 + h * 64 + np.arange(64) for h in hq])
    w_in_p = np.ascontiguousarray(w_in[:, :, perm])
    cw = np.concatenate([conv_w, conv_b[:, None, :]], 1)
    cw = np.ascontiguousarray(cw.reshape(L, 4, 2 * NFC, 128).transpose(0, 3, 2, 1))
    gfm = np.stack([norm_pre_mix, norm_pre_ffn], 1).reshape(L, 2, 8, 128).transpose(0, 3, 1, 2)
    gpost = np.concatenate([norm_post_mix, norm_post_ffn], 1)
    sgu_wT = np.ascontiguousarray(sgu_w.transpose(0, 3, 1, 2))
    sgu_bT = np.ascontiguousarray(sgu_b.transpose(0, 2, 1))
    dec6 = np.concatenate([ret_decay_fwd, ret_decay_bwd], 1)
    dec6 = np.broadcast_to(dec6[:, None, :], (L, 128, 12))
    decP = np.zeros((L, 128, 6), np.float32)
    decP[:, 0:64, 0:3] = ret_decay_fwd[:, None, 0::2]; decP[:, 64:128, 0:3] = ret_decay_fwd[:, None, 1::2]
    decP[:, 0:64, 3:6] = ret_decay_bwd[:, None, 0::2]; decP[:, 64:128, 3:6] = ret_decay_bwd[:, None, 1::2]
    sink6 = np.ascontiguousarray(np.broadcast_to(attn_sink[:, None, :], (L, 128, 6)))
    c = np.ascontiguousarray
    return dict(w_ada=c(w_ada), b_ada=c(b_ada), w_in=w_in_p, w_out=c(w_out), w_up=c(w_up), w_down=c(w_down), cw=cw,
                gfm=c(gfm.astype(np.float32)), gpost=c(gpost.astype(np.float32)), sgu_wT=sgu_wT, sgu_bT=sgu_bT, sgu_n=c(sgu_norm),
                dec18=c(np.concatenate([dec6, decP], 2).astype(np.float32)), sink6=sink6.astype(np.float32))


def core_c(c_rows):
    ns = c_rows.shape[0]
    return np.ascontiguousarray(c_rows.reshape(ns, 8, 128).transpose(2, 1, 0)).astype(np.float32)


_CACHE = {}


def kernel(x_prompt, x_sample, c_prompt, c_sample, w_ada, b_ada, norm_pre_mix, norm_post_mix, norm_pre_ffn, norm_post_ffn,
           w_in, sgu_norm, sgu_w, sgu_b, ret_decay_fwd, ret_decay_bwd, attn_sink, w_out, w_up, conv_w, conv_b, w_down):
    NCH, SEGC = 64, 16
    f = lambda a: np.asarray(a, dtype=np.float32)
    x_prompt, x_sample, c_prompt, c_sample = f(x_prompt), f(x_sample), f(c_prompt), f(c_sample)
    wts = weight_inputs(*[f(a) for a in (w_ada, b_ada, norm_pre_mix, norm_post_mix, norm_pre_ffn, norm_post_ffn, w_in, sgu_norm, sgu_w, sgu_b,
                                           ret_decay_fwd, ret_decay_bwd, attn_sink, w_out, w_up, conv_w, conv_b, w_down)])
    consts = const_inputs()
    if "nc" not in _CACHE:
        _CACHE["nc"] = build(NCH, SEGC)[0]
    nc = _CACHE["nc"]
    in_maps = []
    for core in range(8):
        if core < 2:
            xs = x_prompt[core]
            cr = np.repeat(c_prompt[core:core + 1], 4, 0)
            tb = core_tables(NCH, SEGC, True)
        else:
            k = min(core - 2, 3)
            xs = x_sample[4 * k:4 * k + 4].reshape(NCH * 128, D)
            cr = c_sample[4 * k:4 * k + 4]
            tb = core_tables(NCH, SEGC, False)
        m = dict(x=np.ascontiguousarray(xs), cT=core_c(cr))
        m.update(tb); m.update(consts); m.update(wts)
        in_maps.append(m)
    res = run_bass_kernel_spmd(nc, in_maps, core_ids=list(range(8)))
    r = res.results
    y_prompt = np.stack([r[0]["y"], r[1]["y"]], 0).astype(np.float32)
    y_sample = np.concatenate([r[2 + k]["y"].reshape(4, 2048, D) for k in range(4)], 0).astype(np.float32)
    return (y_prompt, y_sample)
```

```python
import math
import os
import numpy as np
DBG = os.environ.get("KDBG", "sgu,ret,ap,attn,out").split(",")
DBG2 = os.environ.get("KDBG2", "r1,r2,r3,r4,r5").split(",")
from contextlib import ExitStack
import concourse.bass as bass
import concourse.mybir as mybir
from concourse.bass_utils import run_bass_kernel_spmd

F32 = mybir.dt.float32
BF16 = mybir.dt.bfloat16
AF = mybir.ActivationFunctionType
ALU = mybir.AluOpType
AX = mybir.AxisListType

SAME_ENGINE_SYNC = bool(int(os.environ.get("KSES", "1")))
EPS = 1e-6
D = 1024
IN_DIM = 2688
DFF = 2816
NFC = 22
DEPTH = 2


class Tile:
    __slots__ = ("name", "w", "r", "rd", "excl")

    def __init__(self, name="", excl=False):
        self.name = name
        self.w = None
        self.r = {}
        self.rd = []
        self.excl = excl


class Op:
    __slots__ = ("eng", "fn", "deps", "inc", "val", "dsem")

    def __init__(self, eng, fn, dsem):
        self.eng = eng
        self.fn = fn
        self.dsem = dsem
        self.deps = set()
        self.inc = dsem is not None
        self.val = 0


class DSem:
    __slots__ = ("h", "cnt")

    def __init__(self, h):
        self.h = h
        self.cnt = 0


class Prog:
    ENGS = ("pe", "act", "dve", "pool", "sp")

    def __init__(self, nc, es):
        self.nc = nc
        self.es = es
        self.ops = {e: [] for e in self.ENGS}
        self.esem = {e: es.enter_context(nc.semaphore("s_" + e)) for e in self.ENGS}
        self.n = 0
        self.bar_idx = {}
        self.region = None
        self.cur = None
        self.streams = {}

    def sb(self, shape, dt, name=None):
        self.n += 1
        return self.es.enter_context(self.nc.sbuf_tensor(f"sb{self.n}_{name or ''}", list(shape), dt))

    def ps(self, shape, dt=F32, name=None):
        self.n += 1
        return self.es.enter_context(self.nc.psum_tensor(name or f"ps{self.n}", list(shape), dt))

    def dsem(self):
        self.n += 1
        return DSem(self.es.enter_context(self.nc.semaphore(f"ds{self.n}")))

    def set_stream(self, name):
        self.cur = None if name is None else self.streams.setdefault(name, [])

    def flush(self):
        self.cur = None
        lists = [v for v in self.streams.values() if v]
        if os.environ.get("KSEQ"):
            for li in lists:
                for r_ in li:
                    self._op(*r_)
            self.streams = {}
            return
        idx = [0] * len(lists)
        while True:
            best = None
            for i, li in enumerate(lists):
                if idx[i] < len(li):
                    f = (idx[i] + 1) / len(li)
                    if best is None or f < best[0]:
                        best = (f, i)
            if best is None:
                break
            i = best[1]
            self._op(*lists[i][idx[i]])
            idx[i] += 1
        self.streams = {}

    def op(self, eng, fn, reads=(), writes=(), dsem=None):
        if self.cur is not None:
            self.cur.append((eng, fn, list(reads), list(writes), dsem))
            return None
        return self._op(eng, fn, reads, writes, dsem)

    def _op(self, eng, fn, reads=(), writes=(), dsem=None):
        o = Op(eng, fn, dsem)
        deps = o.deps
        for t in reads:
            if t.w is not None:
                deps.add(t.w)
            if t.excl:
                for e_, o_ in t.r.items():
                    if e_ != eng:
                        deps.add(o_)
        for t in writes:
            if t.w is not None:
                deps.add(t.w)
            deps.update(t.r.values())
            deps.update(t.rd)
        for t in reads:
            if dsem is not None:
                t.rd.append(o)
            else:
                t.r[eng] = o
        for t in writes:
            t.w = o
            t.r = {}
            t.rd = []
        if dsem is not None:
            dsem.cnt += 16
            o.val = dsem.cnt
        self.ops[eng].append(o)
        return o

    def I(self, eng, meth, reads, writes, *a, **k):
        return self.op(eng, (meth, a, k), reads, writes)

    def dma(self, eng, out, in_, reads, writes, dsem=None, slow=False):
        if dsem is None:
            dsem = self.dsem()
        k = dict(out=out, in_=in_)
        if slow:
            k["allow_slow_non_contiguous"] = True
        return self.op(eng, ("dma_start", (), k), reads, writes, dsem)

    def barrier(self):
        lasts = []
        for e in self.ENGS:
            comp = [o for o in self.ops[e] if o.dsem is None and o.fn is not None]
            if comp:
                lasts.append(comp[-1])
        dmas = [o for e in self.ENGS for o in self.ops[e][self.bar_idx.get(e, 0):] if o.dsem is not None]
        for e in self.ENGS:
            self.bar_idx[e] = len(self.ops[e])
        for e in self.ENGS:
            o = Op(e, None, None)
            o.deps.update(lasts)
            o.deps.update(dmas)
            self.ops[e].append(o)

    def wait_all(self, eng, tiles):
        o = Op(eng, None, None)
        for t in tiles:
            if t.w is not None:
                o.deps.add(t.w)
        self.ops[eng].append(o)

    def emit(self):
        nc = self.nc
        for e in self.ENGS:
            for o in self.ops[e]:
                for d in o.deps:
                    if d.dsem is not None:
                        continue
                    if d.eng != o.eng or o.dsem is not None:
                        d.inc = True
                    elif SAME_ENGINE_SYNC and d.eng != "pe":
                        d.inc = True
        for e in self.ENGS:
            c = 0
            for o in self.ops[e]:
                if o.dsem is None:
                    if o.inc:
                        c += 1
                    o.val = c
        counts = {}
        with nc.Block() as block:
            def run(e):
                def body(h):
                    waited = {}
                    nw = 0
                    for o in self.ops[e]:
                        need = {}
                        for d in o.deps:
                            if d.dsem is not None:
                                key = d.dsem
                                sem = d.dsem.h
                            else:
                                if d.eng == e and o.dsem is None:
                                    if e == "pe" or not SAME_ENGINE_SYNC:
                                        continue
                                key = d.eng
                                sem = self.esem[d.eng]
                            if need.get(key, (None, 0))[1] < d.val:
                                need[key] = (sem, d.val)
                        for key, (sem, val) in need.items():
                            if waited.get(key, 0) < val:
                                h.wait_ge(sem, val)
                                waited[key] = val
                                nw += 1
                        if o.fn is None:
                            continue
                        meth, a, k = o.fn
                        inst = getattr(h, meth)(*a, **k)
                        if o.dsem is not None:
                            inst.then_inc(o.dsem.h, 16)
                        elif o.inc:
                            inst.then_inc(self.esem[e], 1)
                    counts[e] = (len(self.ops[e]), nw)
                return body
            block.tensor(run("pe"))
            block.scalar(run("act"))
            block.vector(run("dve"))
            block.gpsimd(run("pool"))
            block.sync(run("sp"))
        return counts


class Region:
    def __init__(self, big, start, limit):
        self.big = big
        self.off = start
        self.limit = limit

    def alloc(self, shape, dt):
        n = 1
        for s in shape[1:]:
            n *= s
        n16 = n * (2 if dt == F32 else 1)
        self.off = (self.off + 15) // 16 * 16
        ap = self.big[0:shape[0], self.off:self.off + n16]
        self.off += n16
        assert self.off <= self.limit, ("region overflow", self.off, self.limit)
        if dt == F32:
            ap = ap.bitcast(F32)
        if len(shape) == 3:
            ap = ap.rearrange("p (a b) -> p a b", a=shape[1])
        elif len(shape) == 4:
            ap = ap.rearrange("p (a b c) -> p a b c", a=shape[1], b=shape[2])
        return ap


class Buf:
    def __init__(self, P, shape, dt, name=None):
        if getattr(P, "region", None) is not None:
            self.t = P.region.alloc(shape, dt)
        else:
            self.t = P.sb(shape, dt, name)
        self.T = Tile(name or "")

    def __getitem__(self, k):
        return self.t[k]


class Ring:
    def __init__(self, P, n, shape, dt, name):
        self.b = [Buf(P, shape, dt, f"{name}{i}") for i in range(n)]
        self.n = n
        self.i = -1

    def next(self):
        self.i += 1
        return self.b[self.i % self.n]

    def at(self, k):
        return self.b[k % self.n]


def build(NCH, SEGC, depth=DEPTH, stop_after=None):
    NSEG = NCH // SEGC
    NT = NCH * 128
    NB = NCH // 2
    nc = bass.Bass("TRN2", target_bir_lowering=False)

    def din(name, shape):
        return nc.dram_tensor(name, list(shape), F32, kind="ExternalInput").ap()

    x_in = din("x", [NT, D])
    cT_in = din("cT", [128, 8, NSEG])
    flag_in = din("flag", [128, 1])
    rope_in = din("rope", [NCH, 128, 120])
    ident_in = din("ident", [128, 128])
    cm_in = din("cmats", [128, 6, 128])
    tri_in = din("tri", [128, 2, 128])
    jv_in = din("jv", [128, 2])
    w_ada = din("w_ada", [depth, D, 6 * D]); b_ada = din("b_ada", [depth, 6 * D])
    w_in = din("w_in", [depth, D, IN_DIM]); w_out = din("w_out", [depth, D, D])
    w_up = din("w_up", [depth, D, 2 * DFF]); w_down = din("w_down", [depth, DFF, D])
    cw_in = din("cw", [depth, 128, 2 * NFC, 4])
    gfm_in = din("gfm", [depth, 128, 2, 8])
    gpost_in = din("gpost", [depth, 2 * D])
    sguw_in = din("sgu_wT", [depth, 128, 4, 128]); sgub_in = din("sgu_bT", [depth, 128, 4])
    sgun_in = din("sgu_n", [depth, 256])
    dec18_in = din("dec18", [depth, 128, 18])
    sink_in = din("sink6", [depth, 128, 6])
    y_out = nc.dram_tensor("y", [NT, D], F32, kind="ExternalOutput").ap()
    xa = nc.dram_tensor("xa", [NT, D], F32, kind="Internal").ap()
    xb = nc.dram_tensor("xb", [NT, D], F32, kind="Internal").ap()
    modD = nc.dram_tensor("modD", [depth, NSEG, 6 * D], F32, kind="Internal").ap()
    gD = nc.dram_tensor("gD", [depth, 2, NSEG, D], F32, kind="Internal").ap()

    es = ExitStack()
    P = Prog(nc, es)
    Txa = [Tile(f"xa{i}") for i in range(NCH)]
    Txb = [Tile(f"xb{i}") for i in range(NCH)]
    Ty = [Tile(f"y{i}") for i in range(NCH)]
    TmodD = [None] * depth
    TgD = [None] * depth

    PS = P.ps([128, 4096], F32, "PSALL")
    PSb = PS.bitcast(BF16)
    Tbank = [Tile(f"bank{i}", excl=True) for i in range(8)]

    def bank(i):
        return PS[:, 512 * i:512 * (i + 1)]

    def bankb(i):
        return PSb[:, 1024 * i:1024 * (i + 1)]

    class Gen:
        def __init__(self, ids):
            self.ids = ids
            self.i = -1

        def next(self):
            self.i += 1
            b = self.ids[self.i % len(self.ids)]
            return bank(b), Tbank[b]

    ident = Buf(P, [128, 128], BF16, "ident")
    tri = Buf(P, [128, 2, 128], BF16, "tri")
    trif = Buf(P, [128, 2, 128], BF16, "trif")
    jv = Buf(P, [128, 2], F32, "jv")
    flag = Buf(P, [128, 1], F32, "flag")
    ropeR = Ring(P, 3, [128, 120], F32, "rope")
    drope = [P.dsem() for _ in range(3)]
    cmh = Buf(P, [128, 8], F32, "cmh")
    siluT = Buf(P, [128, 8, NSEG], BF16, "siluT")
    cTf = Buf(P, [128, 8, NSEG], F32, "cTf")

    dsems = []

    def DS():
        d = P.dsem()
        dsems.append(d)
        return d

    _nd = {}

    def ND(name):
        if name not in _nd:
            _nd[name] = DS()
        return _nd[name]
    for (b, src, eng) in [(ident, ident_in, "pool"), (tri, tri_in, "pool"), (jv, jv_in, "sp"),
                          (flag, flag_in, "sp"), (cTf, cT_in, "sp")]:
        P.dma(eng, b[:], src, [], [b.T], ND("c_" + b.T.name))
    P.I("pool", "memset", [], [cmh.T], cmh[:], -0.5)
    P.I("act", "activation", [cTf.T], [siluT.T], out=siluT[:], in_=cTf[:], func=AF.Silu)
    P.I("dve", "tensor_scalar", [tri.T, flag.T], [trif.T], out=trif[:], in0=tri[:], scalar1=flag[:, 0:1], scalar2=None, op0=ALU.mult)

    ARENA = 8 * 2 * DFF + NFC * D
    BIGN = ARENA + 21120
    W = P.sb([128, BIGN], BF16, "arena")
    win_v = W[:, 0:8 * IN_DIM].rearrange("p (k n) -> p k n", k=8)
    wout_v = W[:, 8 * IN_DIM:8 * IN_DIM + 8 * D].rearrange("p (k n) -> p k n", k=8)
    SB0 = 8 * IN_DIM + 8 * D
    sbst_v = W[:, SB0:SB0 + NCH * 192].rearrange("p (c q e) -> p c q e", c=NCH, q=3)
    wup_v = W[:, 0:8 * 2 * DFF].rearrange("p (k n) -> p k n", k=8)
    wdn_v = W[:, 8 * 2 * DFF:ARENA].rearrange("p (k n) -> p k n", k=NFC)
    Twin = [Tile(f"win{k}") for k in range(8)]
    Twout = [Tile(f"wout{k}") for k in range(8)]
    Tsb = [Tile(f"sbst{c}") for c in range(NCH)]
    Twup = [Tile(f"wup{k}") for k in range(8)]
    Twdn = [Tile(f"wdn{k}") for k in range(NFC)]
    A_tiles = Twin + Twout + Tsb
    B_tiles = Twup + Twdn
    dWin = [DS() for _ in range(8)]; dWout = [DS() for _ in range(8)]; dWup = [DS() for _ in range(8)]; dWdn = [DS() for _ in range(NFC)]

    regA = Region(W, SB0 + NCH * 192, BIGN)
    P.region = regA
    d18 = Buf(P, [128, 18], F32, "d18"); e18 = Buf(P, [128, 18], F32, "e18"); lg18 = Buf(P, [128, 18], F32, "lg18")
    DT = Buf(P, [128, 6, 128], F32, "DT")
    QDF = Buf(P, [128, 3, 128], F32, "QDF"); QDB = Buf(P, [128, 3, 128], F32, "QDB")
    KDF = Buf(P, [128, 6], F32, "KDF"); KDB = Buf(P, [128, 6], F32, "KDB")
    CDF = Buf(P, [128, 3], F32, "CDF"); CDB = Buf(P, [128, 3], F32, "CDB")
    ESK = Buf(P, [128, 6], F32, "ESK")
    SN = Buf(P, [128, 256], F32, "SN")
    WsT = Buf(P, [128, 4, 128], BF16, "WsT"); SGB = Buf(P, [128, 4], F32, "SGB")
    P.region = None
    CW = Buf(P, [128, 2 * NFC, 4], F32, "CW")
    gfm = Buf(P, [128, 2, 8], F32, "gfm")
    fm = [Buf(P, [128, NSEG, 8], F32, f"fm{i}") for i in range(4)]
    A1f = Buf(P, [128, NSEG, 8], F32, "A1f"); A2f = Buf(P, [128, NSEG, 8], F32, "A2f")
    Gt = Ring(P, 1, [128, D], F32, "Gt")
    dl = [DS() for _ in range(4)]
    dmb = [DS() for _ in range(2)]; dgp = [DS() for _ in range(2)]; dgb = [DS() for _ in range(2)]

    xcr = Ring(P, 3, [128, D], F32, "xc")
    st = Ring(P, 16, [128, 8], F32, "st")
    xn = Ring(P, 1, [128, D], BF16, "xn")
    xnew = Ring(P, 2, [128, D], F32, "xnew")
    P.region = regA
    hTr = Ring(P, 2, [128, 8, 128], BF16, "hT")
    gA = Gen([2, 3, 5, 6])
    st_rings = {None: st, "SC": Ring(P, 8, [128, 8], F32, "stSC"), "R": Ring(P, 8, [128, 8], F32, "stR"), "L": Ring(P, 8, [128, 8], F32, "stL"),
                "N": Ring(P, 8, [128, 8], F32, "stN"), "U": st}
    gens = {None: gA, "SC": Gen([2]), "R": Gen([3, 5]), "L": Gen([6]), "N": Gen([0]), "U": gA}
    gLpost = Gen([6, 7])
    _rg = P.region
    P.region = None
    st_rings["K"] = st
    gens["K"] = gA
    st_rings["NB"] = Ring(P, 8, [128, 8], F32, "stNB")
    gens["NB"] = Gen([0])
    P.region = _rg
    CUR = {"st": st, "gen": gA}
    Tb1a = Tbank[1]; Tb1b = Tbank[1]

    def use(name):
        P.set_stream(name)
        CUR["st"] = st_rings[name]
        CUR["gen"] = gens[name]

    u_b = Buf(P, [128, 256], BF16, "u"); vg = Buf(P, [128, 256], F32, "vg"); sq = Buf(P, [128, 256], F32, "sq")
    vn = Buf(P, [128, 256], F32, "vn"); vn2 = Buf(P, [128, 256], BF16, "vn2"); gt_ = Buf(P, [128, 256], F32, "gt")
    r1 = Buf(P, [128, 384], F32, "r1"); r2 = Buf(P, [128, 384], F32, "r2"); rsum = Buf(P, [128, 384], F32, "rsum")
    qr = Buf(P, [128, 384], BF16, "qr"); kr = Buf(P, [128, 384], BF16, "kr")
    kdf = Buf(P, [128, 384], BF16, "kdf"); kdb = Buf(P, [128, 384], BF16, "kdb")
    vt = Buf(P, [128, 384], BF16, "vt"); sg = Buf(P, [128, 384], F32, "sg")
    qkT = Buf(P, [128, 6, 128], BF16, "qkT")
    qdf = Buf(P, [128, 3, 128], BF16, "qdf"); qdb = Buf(P, [128, 3, 128], BF16, "qdb")
    PT = Buf(P, [128, 6, 128], BF16, "PT")
    Rf = Buf(P, [128, 3, 64], F32, "Rf"); Rb = Buf(P, [128, 3, 64], F32, "Rb"); Rt = Buf(P, [128, 3, 64], F32, "Rt")
    Sfb = Buf(P, [128, 3, 64], BF16, "Sfb")
    ysq = Buf(P, [128, 384], F32, "ysq"); yc_ = Buf(P, [128, 384], F32, "yc"); yn_ = Buf(P, [128, 384], F32, "yn")
    cqb = Buf(P, [128, 384], BF16, "cqb"); ckb = Buf(P, [128, 128], BF16, "ckb")
    a1 = Buf(P, [128, 6, 16], F32, "a1"); a2 = Buf(P, [128, 6, 16], F32, "a2")
    cqT = Ring(P, 3, [128, 3, 128], BF16, "cqT"); ckT = Ring(P, 4, [128, 128], BF16, "ckT")
    cvb = Ring(P, 4, [128, 2, 65], BF16, "cvb")
    Eb = Ring(P, 6, [128, 3, 128], BF16, "Eb")
    mix = Ring(P, 3, [128, D], BF16, "mix")
    mixT = Buf(P, [128, 8, 128], BF16, "mixT")
    cm = Buf(P, [128, 6, 128], F32, "cmats")
    ta = Buf(P, [128, 128], F32, "ta"); tb = Buf(P, [128, 128], F32, "tb")
    wad = Ring(P, 2, [128, 8, 256], BF16, "wad")
    badb = Ring(P, 2, [NSEG, 256], F32, "badb")
    mblk = Ring(P, 2, [NSEG, 256], F32, "mblk")
    gpb = Ring(P, 2, [NSEG, 256], F32, "gpb")
    gblk = Ring(P, 2, [NSEG, 256], F32, "gblk")
    print("regA end", regA.off, BIGN)
    regB = Region(W, ARENA, BIGN)
    P.region = regB
    HB = Ring(P, 3, [128, 8, 258], BF16, "HB")
    T0 = Ring(P, 2, [128, 256], F32, "T0"); T1 = Ring(P, 2, [128, 256], F32, "T1"); T2 = Ring(P, 2, [128, 256], F32, "T2")
    gg = Ring(P, 1, [128, 256], F32, "gg")
    actT = Ring(P, 2, [128, NFC, 256], BF16, "actT")
    print("regB end", regB.off, BIGN)
    P.region = None
    dx = [DS() for _ in range(4)]
    dst_ = [DS() for _ in range(2)]
    dG = [DS() for _ in range(2)]
    dcp = [DS() for _ in range(4)]


    def rstd_from(ssb, n_inv, out_col):
        sbuf, c = ssb
        obuf, oc = out_col
        t = CUR["st"].next()
        P.I("dve", "tensor_scalar", [sbuf.T], [t.T], out=t[:, 0:1], in0=sbuf[:, c:c + 1], scalar1=n_inv, scalar2=EPS, op0=ALU.mult, op1=ALU.add)
        P.I("pool", "tensor_tensor", [t.T, cmh.T], [obuf.T], out=obuf[:, oc:oc + 1], in0=t[:, 0:1], in1=cmh[:, 0:1], op=ALU.pow)

    def layer_setup(l):
        P.dma("sp", cm[:], cm_in, [], [cm.T], ND("cm"))
        TmD = [Tile(f"modD{l}_{i}") for i in range(24)]
        TgDl = [Tile(f"gD{l}_{i}") for i in range(8)]
        for cb in range(24):
            c0 = cb * 256
            wb = wad.next(); bb = badb.next(); mb = mblk.next()
            P.dma("pool", wb[:], w_ada[l, :, c0:c0 + 256].rearrange("(k p) n -> p k n", p=128), [], [wb.T], dl[cb % 2])
            P.dma("sp", bb[:], b_ada[l:l + 1, c0:c0 + 256].partition_broadcast(NSEG), [], [bb.T], dl[2 + cb % 2])
            pb, Tpb = CUR["gen"].next()
            for kc in range(8):
                P.I("pe", "matmul", [siluT.T, wb.T], [Tpb], pb[0:NSEG, 0:256], lhsT=siluT[:, kc, :], rhs=wb[:, kc, :], start=(kc == 0), stop=(kc == 7))
            P.I("dve", "tensor_tensor", [Tpb, bb.T], [mb.T], out=mb[:], in0=pb[0:NSEG, 0:256], in1=bb[:], op=ALU.add)
            P.dma("sp", modD[l, :, c0:c0 + 256], mb[:], [mb.T], [TmD[cb]], dmb[cb % 2])
            part = cb // 4
            if part in (2, 5):
                gi = 0 if part == 2 else 1
                j = cb % 4
                gp = gpb.next(); gb_ = gblk.next()
                P.dma("sp", gp[:], gpost_in[l:l + 1, gi * D + j * 256:gi * D + (j + 1) * 256].partition_broadcast(NSEG), [], [gp.T], dgp[gpb.i % 2])
                P.I("dve", "tensor_tensor", [mb.T, gp.T], [gb_.T], out=gb_[:], in0=mb[:], in1=gp[:], op=ALU.mult)
                P.dma("sp", gD[l, gi, :, j * 256:(j + 1) * 256], gb_[:], [gb_.T], [TgDl[gi * 4 + j]], dgb[gblk.i % 2])
        TmodD[l] = TmD
        TgD[l] = TgDl
        for i, part in enumerate([0, 1, 3, 4]):
            for s_ in range(NSEG):
                P.dma("sp", fm[i][:, s_, :], modD[l, s_, part * D:(part + 1) * D].rearrange("(k p) -> p k", p=128), TmodD[l][part * 4:part * 4 + 4], [fm[i].T], ND(f"fm{i}"), slow=True)
        P.dma("sp", gfm[:], gfm_in[l], [], [gfm.T], ND("gfm"))
        for (Af, scb, gi) in [(A1f, fm[1], 0), (A2f, fm[3], 1)]:
            P.I("dve", "scalar_tensor_tensor", [scb.T, gfm.T], [Af.T],
                out=Af[:], in0=scb[:], scalar=1.0, in1=gfm[:, gi, :].unsqueeze(1).to_broadcast([128, NSEG, 8]),
                op0=ALU.add, op1=ALU.mult)
        P.dma("sp", d18[:], dec18_in[l], [], [d18.T], ND("d18"))
        P.dma("sp", ESK[:], sink_in[l], [], [ESK.T], ND("ESK"))
        P.dma("sp", SN[:], sgun_in[l:l + 1, :].partition_broadcast(128), [], [SN.T], ND("SN"))
        P.dma("pool", WsT[:], sguw_in[l], [], [WsT.T], ND("WsT"))
        P.dma("sp", SGB[:], sgub_in[l], [], [SGB.T], ND("SGB"))
        P.I("act", "activation", [d18.T], [e18.T], out=e18[:], in_=d18[:], func=AF.Exp, scale=-1.0)
        P.I("dve", "tensor_scalar", [e18.T], [e18.T], out=e18[:], in0=e18[:], scalar1=1.0, scalar2=None, op0=ALU.add)
        P.I("act", "activation", [e18.T], [lg18.T], out=lg18[:], in_=e18[:], func=AF.Ln)
        P.I("dve", "tensor_scalar", [lg18.T], [lg18.T], out=lg18[:], in0=lg18[:], scalar1=-1.0, scalar2=None, op0=ALU.mult)
        P.I("act", "activation", [ESK.T], [ESK.T], out=ESK[:], in_=ESK[:], func=AF.Exp)
        for h in range(6):
            P.I("act", "activation", [cm.T, lg18.T], [ta.T], out=ta[:], in_=cm[:, 0, :], func=AF.Exp, scale=lg18[:, h:h + 1])
            P.I("act", "activation", [cm.T, lg18.T], [tb.T], out=tb[:], in_=cm[:, 1, :], func=AF.Exp, scale=lg18[:, 6 + h:7 + h])
            P.I("dve", "scalar_tensor_tensor", [ta.T, cm.T], [ta.T], out=ta[:], in0=ta[:], scalar=0.125, in1=cm[:, 2, :], op0=ALU.mult, op1=ALU.mult)
            P.I("dve", "scalar_tensor_tensor", [tb.T, cm.T], [tb.T], out=tb[:], in0=tb[:], scalar=0.125, in1=cm[:, 3, :], op0=ALU.mult, op1=ALU.mult)
            P.I("dve", "tensor_tensor", [ta.T, tb.T], [DT.T], out=DT[:, (h % 2) * 3 + h // 2, :], in0=ta[:], in1=tb[:], op=ALU.add)
        for q in range(3):
            P.I("act", "activation", [cm.T, lg18.T], [QDF.T], out=QDF[:, q, :], in_=cm[:, 4, :], func=AF.Exp, scale=lg18[:, 12 + q:13 + q])
            P.I("act", "activation", [cm.T, lg18.T], [QDB.T], out=QDB[:, q, :], in_=cm[:, 5, :], func=AF.Exp, scale=lg18[:, 15 + q:16 + q])
        P.I("act", "activation", [lg18.T, jv.T], [KDF.T], out=KDF[:], in_=lg18[:, 0:6], func=AF.Exp, scale=jv[:, 0:1])
        P.I("act", "activation", [lg18.T, jv.T], [KDB.T], out=KDB[:], in_=lg18[:, 6:12], func=AF.Exp, scale=jv[:, 1:2])
        P.I("dve", "tensor_scalar", [KDF.T], [KDF.T], out=KDF[:], in0=KDF[:], scalar1=0.125, scalar2=None, op0=ALU.mult)
        P.I("dve", "tensor_scalar", [KDB.T], [KDB.T], out=KDB[:], in0=KDB[:], scalar1=0.125, scalar2=None, op0=ALU.mult)
        P.I("act", "activation", [lg18.T], [CDF.T], out=CDF[:], in_=lg18[:, 12:15], func=AF.Exp, scale=128.0)
        P.I("act", "activation", [lg18.T], [CDB.T], out=CDB[:], in_=lg18[:, 15:18], func=AF.Exp, scale=128.0)

    def load_A_weights(l):
        for k in range(8):
            P.dma("pool", win_v[:, k, :], w_in[l, k * 128:(k + 1) * 128, :], [], [Twin[k]] + (B_tiles if k == 0 else []), dWin[k])
        for k in range(8):
            P.dma("pool", wout_v[:, k, :], w_out[l, k * 128:(k + 1) * 128, :], [], [Twout[k]], dWout[k])

    def load_B_weights(l):
        for k in range(8):
            P.dma("pool", wup_v[:, k, :], w_up[l, k * 128:(k + 1) * 128, :], [], [Twup[k]] + (A_tiles if k == 0 else []), dWup[k])
        for k in range(NFC):
            P.dma("pool", wdn_v[:, k, :], w_down[l, k * 128:(k + 1) * 128, :], [], [Twdn[k]], dWdn[k])

    def load_x(src, Tsrc, n, ring, dlist):
        xc = ring.next()
        P.dma("sp", xc[:], src[n * 128:(n + 1) * 128, :], [Tsrc[n]] if Tsrc is not None else [], [xc.T], dlist[ring.i % ring.n])
        return xc

    def norm_front(xc):
        return norm_front_2(norm_front_1(xc))

    def norm_front_1(xc):
        s = CUR["st"].next()
        xb_ = xn.next()
        P.I("act", "activation", [xc.T], [xb_.T, s.T], out=xb_[:], in_=xc[:], func=AF.Square, accum_out=s[:, 0:1])
        rstd_from((s, 0), 1.0 / D, (s, 1))
        return (xc, s, xb_)

    def norm_front_2(st3):
        xc, s, xb_ = st3
        P.I("dve", "tensor_scalar", [xc.T, s.T], [xb_.T], out=xb_[:], in0=xc[:], scalar1=s[:, 1:2], scalar2=None, op0=ALU.mult)
        return xb_

    def norm_T(xc, seg, Af, Bf, dst_fn, dstT):
        norm_back(norm_front(xc), seg, Af, Bf, dst_fn, dstT)

    def norm_back(xb_, seg, Af, Bf, dst_fn, dstT):
        for kc in range(8):
            P.I("pe", "transpose", [xb_.T, ident.T], [Tbank[0]], out=bankb(0)[:, kc * 128:(kc + 1) * 128], in_=xb_[:, kc * 128:(kc + 1) * 128], identity=ident[:])
        for kc in range(8):
            P.I("act", "activation", [Tbank[0], Af.T, Bf.T], [dstT], out=dst_fn(kc), in_=bankb(0)[:, kc * 128:(kc + 1) * 128], func=AF.Identity,
                                                              scale=Af[:, seg, kc:kc + 1], bias=Bf[:, seg, kc:kc + 1])

    def proj(hT, wv, Tw, c0, ncol, gen):
        pb, Tpb = gen.next()
        for kc in range(8):
            P.I("pe", "matmul", [hT.T, Tw[kc]], [Tpb], pb[:, 0:ncol], lhsT=hT[:, kc, :], rhs=wv[:, kc, c0:c0 + ncol], start=(kc == 0), stop=(kc == 7))
        return pb, Tpb

    def load_rope(n):
        rp = ropeR.next()
        P.dma("sp", rp[:], rope_in[n], [], [rp.T], drope[ropeR.i % 3])
        return rp

    def rope64(pb, Tpb, rp, out_f32):
        x3 = pb[:, 0:384].rearrange("p (a d) -> p a d", d=32)
        x4 = pb[:, 0:384].rearrange("p (h t d) -> p h t d", h=6, t=2)
        P.I("dve", "tensor_tensor", [Tpb, rp.T], [r1.T], out=r1[:].rearrange("p (a d) -> p a d", d=32), in0=x3,
                                              in1=rp[:, 0:32].unsqueeze(1).to_broadcast([128, 12, 32]), op=ALU.mult)
        r2v = r2[:].rearrange("p (h t d) -> p h t d", h=6, t=2)
        P.I("dve", "tensor_tensor", [Tpb, rp.T], [r2.T], out=r2v[:, :, 0, :], in0=x4[:, :, 1, :],
                                              in1=rp[:, 64:96].unsqueeze(1).to_broadcast([128, 6, 32]), op=ALU.mult)
        P.I("dve", "tensor_tensor", [Tpb, rp.T], [r2.T], out=r2v[:, :, 1, :], in0=x4[:, :, 0, :],
                                              in1=rp[:, 32:64].unsqueeze(1).to_broadcast([128, 6, 32]), op=ALU.mult)
        P.I("dve", "tensor_tensor", [r1.T, r2.T], [out_f32.T], out=out_f32[:], in0=r1[:], in1=r2[:], op=ALU.add)

    def kv_update(kd, R, CD, n, store_fn, store_T, gen, boundary):
        pb, Tpb = gen.next()
        for q in range(3):
            P.I("pe", "matmul", [kd.T, vt.T], [Tpb], pb[:, q * 128:(q + 1) * 128], lhsT=kd[:, q * 128:(q + 1) * 128], rhs=vt[:, q * 128:(q + 1) * 128],
                                                       start=True, stop=True)
        P.I("dve", "tensor_tensor", [R.T, CD.T], [Rt.T], out=Rt[:], in0=R[:], in1=CD[:].unsqueeze(2).to_broadcast([128, 3, 64]), op=ALU.mult)
        kv3 = pb[:, 0:384].rearrange("p (q e) -> p q e", q=3)
        P.I("dve", "tensor_tensor", [Rt.T, Tpb], [R.T], out=R[0:64], in0=Rt[0:64], in1=kv3[0:64, :, 0:64], op=ALU.add)
        P.I("dve", "tensor_tensor", [Rt.T, Tpb], [R.T], out=R[64:128], in0=Rt[64:128], in1=kv3[64:128, :, 64:128], op=ALU.add)
        if boundary:
            P.I("dve", "tensor_scalar", [R.T, flag.T], [R.T], out=R[:], in0=R[:], scalar1=flag[:, 0:1], scalar2=None, op0=ALU.mult)
        if store_fn is not None:
            P.I("act", "activation", [R.T], [store_T], out=store_fn, in_=R[:], func=AF.Copy)

    def post_stage(lhs_fn, K, wv, Tw, xc, Gb, n, dst, Tdst, gen, add_eng="pool"):
        post_stage_2(post_stage_1(lhs_fn, K, wv, Tw, xc, Gb, n, dst, Tdst, gen), add_eng)

    def post_stage_1(lhs_fn, K, wv, Tw, xc, Gb, n, dst, Tdst, gen):
        pbs = []
        for half in range(2):
            pb, Tpb = gen.next()
            for kc in range(K):
                lh, Tl = lhs_fn(kc)
                P.I("pe", "matmul", [Tl, Tw[kc]], [Tpb], pb[:, :], lhsT=lh, rhs=wv[:, kc, half * 512:(half + 1) * 512],
                                                                                      start=(kc == 0), stop=(kc == K - 1))
            pbs.append((pb, Tpb))
        s = CUR["st"].next()
        xo = xnew.next()
        for half in range(2):
            pb, Tpb = pbs[half]
            P.I("act", "activation", [Tpb], [xo.T, s.T], out=xo[:, 0:512], in_=pb[:, :], func=AF.Square, accum_out=s[:, half:half + 1])
        P.I("dve", "tensor_tensor", [s.T], [s.T], out=s[:, 2:3], in0=s[:, 0:1], in1=s[:, 1:2], op=ALU.add)
        rstd_from((s, 2), 1.0 / D, (s, 3))
        return (pbs, s, xo, xc, Gb, n, dst, Tdst, xnew.i % 2)

    def post_stage_2(state, add_eng="pool"):
        pbs, s, xo, xc, Gb, n, dst, Tdst, slot = state
        for half in range(2):
            pb, Tpb = pbs[half]
            P.I("dve", "scalar_tensor_tensor", [Tpb, s.T, Gb.T], [xo.T], out=xo[:, half * 512:(half + 1) * 512], in0=pb[:, :], scalar=s[:, 3:4],
                                                                                   in1=Gb[:, half * 512:(half + 1) * 512], op0=ALU.mult, op1=ALU.mult)
        P.I(add_eng, "tensor_tensor", [xc.T, xo.T], [xo.T], out=xo[:], in0=xo[:], in1=xc[:], op=ALU.add)
        P.dma("sp", dst[n * 128:(n + 1) * 128, :], xo[:], [xo.T], [Tdst[n]], dst_[slot])

    def load_G(l, gi, seg):
        Gb = Gt.next()
        P.dma("sp", Gb[:], gD[l, gi, seg:seg + 1, :].partition_broadcast(128), TgD[l][gi * 4:gi * 4 + 4], [Gb.T], dG[0])
        return Gb

    def phase_A0(l, src, Tsrc):
        P.I("pool", "memset", [], [Rb.T], Rb[:], 0.0)
        P.I("pool", "memset", [], [Tsb[NCH - 1]], sbst_v[:, NCH - 1], 0.0)
        hTs, rps = {}, {}

        xbs = {}

        def front_a(c):
            xc = load_x(src, Tsrc, c, xcr, dx)
            rps[c] = load_rope(c)
            xbs[c] = norm_front(xc)

        def front_b(c):
            hT = hTr.next()
            hTs[c] = hT
            norm_back(xbs.pop(c), c // SEGC, A1f, fm[0], lambda kc: hT[:, kc, :], hT.T)

        front_a(NCH - 1)
        front_b(NCH - 1)
        for n in range(NCH - 1, 0, -1):
            if n - 1 >= 1:
                use("N")
                front_a(n - 1)
                front_b(n - 1)
            use("K")
            hT = hTs.pop(n)
            rp = rps.pop(n)
            pk, Tpk = proj(hT, win_v, Twin, 896, 384, CUR["gen"])
            pv, Tpv = proj(hT, win_v, Twin, 1280, 384, CUR["gen"])
            rope64(pk, Tpk, rp, rsum)
            P.I("dve", "tensor_tensor", [rsum.T, KDB.T], [kdb.T], out=kdb[:].rearrange("p (h d) -> p h d", h=6), in0=rsum[:].rearrange("p (h d) -> p h d", h=6),
                in1=KDB[:].unsqueeze(2).to_broadcast([128, 6, 64]), op=ALU.mult)
            P.I("act", "activation", [Tpv], [vt.T], out=vt[:], in_=pv[:, 0:384], func=AF.Copy)
            kv_update(kdb, Rb, CDB, n, sbst_v[:, n - 1], Tsb[n - 1], CUR["gen"], boundary=(n % SEGC == 0))
            use(None)
            P.flush()

    def attention(l, m):
        mx = mix.at(m)
        pO, TpO = bank(7), Tbank[7]
        O3 = pO[:, 0:390].rearrange("p (h e) -> p h e", h=6)
        blks = [b for b in (-1, 0, 1) if 0 <= m + b < NCH]
        for kvh in range(2):
            Es = []
            for bi, b in enumerate(blks):
                pS, TpS = CUR["gen"].next()
                kT = ckT.at(m + b); qT = cqT.at(m)
                P.I("pe", "matmul", [kT.T, qT.T], [TpS], pS[:, 0:384], lhsT=kT[kvh * 64:(kvh + 1) * 64, :],
                    rhs=qT[kvh * 64:(kvh + 1) * 64, :, :], start=True, stop=True)
                E = Eb.next()
                P.I("act", "activation", [TpS], [E.T], out=E[:].rearrange("p g i -> p (g i)"), in_=pS[:, 0:384], func=AF.Exp, scale=0.125)
                if b != 0:
                    bnd = (b == -1 and m % SEGC == 0) or (b == 1 and m % SEGC == SEGC - 1)
                    mk = trif if bnd else tri
                    mi = 0 if b == -1 else 1
                    P.I("dve", "tensor_tensor", [E.T, mk.T], [E.T], out=E[:], in0=E[:], in1=mk[:, mi, :].unsqueeze(1).to_broadcast([128, 3, 128]), op=ALU.mult)
                Es.append(E)
            for g in range(3):
                for bi, b in enumerate(blks):
                    cv = cvb.at(m + b)
                    P.I("pe", "matmul", [Es[bi].T, cv.T], [TpO], O3[:, kvh * 3 + g, :], lhsT=Es[bi][:, g, :], rhs=cv[:, kvh, :],
                        start=(bi == 0), stop=(bi == len(blks) - 1))
        s = CUR["st"].next()
        P.I("dve", "tensor_tensor", [TpO, ESK.T], [s.T], out=s[:, 0:6], in0=O3[:, :, 64], in1=ESK[:], op=ALU.add)
        P.I("dve", "reciprocal", [s.T], [s.T], out=s[:, 0:6], in_=s[:, 0:6])
        P.I("dve", "tensor_tensor", [TpO, s.T], [mx.T], out=mx[:, 640:1024].rearrange("p (h d) -> p h d", h=6), in0=O3[:, :, 0:64],
                                              in1=s[:, 0:6].unsqueeze(2).to_broadcast([128, 6, 64]), op=ALU.mult)

    def out_stage(l, m, xc, Gb, dst, Tdst):
        mx = mix.at(m)
        for half in range(2):
            for k4 in range(4):
                kc = half * 4 + k4
                P.I("pe", "transpose", [mx.T, ident.T], [Tb1a], out=bankb(1)[:, k4 * 128:(k4 + 1) * 128], in_=mx[:, kc * 128:(kc + 1) * 128], identity=ident[:])
            P.I("act", "activation", [Tb1a], [mixT.T], out=mixT[:, half * 4:(half + 1) * 4, :].rearrange("p k i -> p (k i)"), in_=bankb(1)[:, 0:512], func=AF.Copy)
        post_stage(lambda kc: (mixT[:, kc, :], mixT.T), 8, wout_v, Twout, xc, Gb, m, dst, Tdst, gLpost, add_eng="dve")

    def phase_A1(l, src, Tsrc, dst, Tdst):
        P.I("pool", "memset", [], [Rf.T], Rf[:], 0.0)
        P.I("pool", "memset", [], [Sfb.T], Sfb[:], 0.0)
        for b_ in cvb.b:
            P.I("pool", "memset", [], [b_.T], b_[:], 1.0)
        xcs = {}
        Gb = None
        Gseg = {}
        LAG = 2
        hTs, rps = {}, {}

        class _SR:
            def __init__(self, bufs):
                self.b = bufs; self.n = len(bufs); self.i = -1

            def next(self):
                self.i += 1
                return self.b[self.i % self.n]

        xNr = _SR(xcr.b[0:2]); xLr = _SR(xcr.b[2:3])

        def attn_proj(n, hT, rp):
            pcq, Tpcq = proj(hT, win_v, Twin, 2048, 384, CUR["gen"])
            P.I("act", "activation", [Tpcq], [cqb.T], out=cqb[:], in_=pcq[:, 0:384], func=AF.Copy)
            c3 = pcq[:, 0:384].rearrange("p (h d) -> p h d", h=6)
            P.I("dve", "tensor_tensor", [Tpcq, rp.T], [a1.T], out=a1[:].rearrange("p h (t d) -> p h t d", t=2), in0=c3[:, :, 0:16].rearrange("p h (t d) -> p h t d", t=2),
                                                  in1=rp[:, 96:104].unsqueeze(1).unsqueeze(1).to_broadcast([128, 6, 2, 8]), op=ALU.mult)
            P.I("dve", "tensor_tensor", [Tpcq, rp.T], [a2.T], out=a2[:, :, 0:8], in0=c3[:, :, 8:16], in1=rp[:, 112:120].unsqueeze(1).to_broadcast([128, 6, 8]), op=ALU.mult)
            P.I("dve", "tensor_tensor", [Tpcq, rp.T], [a2.T], out=a2[:, :, 8:16], in0=c3[:, :, 0:8], in1=rp[:, 104:112].unsqueeze(1).to_broadcast([128, 6, 8]), op=ALU.mult)
            P.I("dve", "tensor_tensor", [a1.T, a2.T], [cqb.T], out=cqb[:].rearrange("p (h d) -> p h d", h=6)[:, :, 0:16], in0=a1[:], in1=a2[:], op=ALU.add)
            pck, Tpck = proj(hT, win_v, Twin, 2432, 256, CUR["gen"])
            cv = cvb.at(n)
            P.I("act", "activation", [Tpck], [ckb.T], out=ckb[:], in_=pck[:, 0:128], func=AF.Copy)
            P.I("act", "activation", [Tpck], [cv.T], out=cv[:, :, 0:64], in_=pck[:, 128:256].rearrange("p (h d) -> p h d", h=2), func=AF.Copy)
            k3 = pck[:, 0:128].rearrange("p (h d) -> p h d", h=2)
            P.I("dve", "tensor_tensor", [Tpck, rp.T], [a1.T], out=a1[:, 0:2, :].rearrange("p h (t d) -> p h t d", t=2), in0=k3[:, :, 0:16].rearrange("p h (t d) -> p h t d", t=2),
                                                  in1=rp[:, 96:104].unsqueeze(1).unsqueeze(1).to_broadcast([128, 2, 2, 8]), op=ALU.mult)
            P.I("dve", "tensor_tensor", [Tpck, rp.T], [a2.T], out=a2[:, 0:2, 0:8], in0=k3[:, :, 8:16], in1=rp[:, 112:120].unsqueeze(1).to_broadcast([128, 2, 8]), op=ALU.mult)
            P.I("dve", "tensor_tensor", [Tpck, rp.T], [a2.T], out=a2[:, 0:2, 8:16], in0=k3[:, :, 0:8], in1=rp[:, 104:112].unsqueeze(1).to_broadcast([128, 2, 8]), op=ALU.mult)
            P.I("dve", "tensor_tensor", [a1.T, a2.T], [ckb.T], out=ckb[:].rearrange("p (h d) -> p h d", h=2)[:, :, 0:16], in0=a1[:, 0:2, :], in1=a2[:, 0:2, :], op=ALU.add)
            for q in range(3):
                P.I("pe", "transpose", [cqb.T, ident.T], [Tb1b], out=bankb(1)[:, 512 + q * 128:512 + (q + 1) * 128], in_=cqb[:, q * 128:(q + 1) * 128], identity=ident[:])
            P.I("pe", "transpose", [ckb.T, ident.T], [Tb1b], out=bankb(1)[:, 896:1024], in_=ckb[:], identity=ident[:])
            cq_t = cqT.at(n); ck_t = ckT.at(n)
            P.I("act", "activation", [Tb1b], [cq_t.T], out=cq_t[:].rearrange("p a i -> p (a i)"), in_=bankb(1)[:, 512:896], func=AF.Copy)
            P.I("act", "activation", [Tb1b], [ck_t.T], out=ck_t[:], in_=bankb(1)[:, 896:1024], func=AF.Copy)


        xbs = {}

        def front_a(c):
            xc = load_x(src, Tsrc, c, xNr, dx[0:2])
            rps[c] = load_rope(c)
            xbs[c] = norm_front(xc)

        def front_b(c):
            hT = hTr.next()
            hTs[c] = hT
            norm_back(xbs.pop(c), c // SEGC, A1f, fm[0], lambda kc: hT[:, kc, :], hT.T)

        front_a(0)
        front_b(0)
        for n in range(NCH + LAG):
            if n < NCH:
                hT = hTs.pop(n)
                rp = rps.pop(n)
                mx = mix.at(n)
            use("N")
            if n < NCH:
                attn_proj(n, hT, rp)
            if n + 1 < NCH:
                front_a(n + 1)
                front_b(n + 1)
            if n < NCH:
                use("SC")
                if 'sgu' in DBG:
                    pb, Tpb = proj(hT, win_v, Twin, 0, 512, CUR["gen"])
                    P.I("act", "activation", [Tpb], [u_b.T], out=u_b[:], in_=pb[:, 0:256], func=AF.Gelu_apprx_tanh)
                    P.I("act", "activation", [Tpb], [vg.T], out=vg[:], in_=pb[:, 256:512], func=AF.Gelu_apprx_tanh)
                    P.I("dve", "tensor_tensor", [vg.T], [sq.T], out=sq[:], in0=vg[:], in1=vg[:], op=ALU.mult)
                    s = CUR["st"].next()
                    P.I("dve", "tensor_reduce", [sq.T], [s.T], out=s[:, 0:4], in_=sq[:].rearrange("p (g d) -> p g d", g=4), axis=AX.X, op=ALU.add)
                    P.I("dve", "tensor_scalar", [s.T], [s.T], out=s[:, 0:4], in0=s[:, 0:4], scalar1=1.0 / 64, scalar2=EPS, op0=ALU.mult, op1=ALU.add)
                    P.I("pool", "tensor_tensor", [s.T, cmh.T], [s.T], out=s[:, 4:8], in0=s[:, 0:4], in1=cmh[:, 0:4], op=ALU.pow)
                    P.I("dve", "tensor_tensor", [vg.T, s.T], [vn.T], out=vn[:].rearrange("p (g d) -> p g d", g=4), in0=vg[:].rearrange("p (g d) -> p g d", g=4),
                                                          in1=s[:, 4:8].unsqueeze(2).to_broadcast([128, 4, 64]), op=ALU.mult)
                    P.I("pool", "tensor_tensor", [vn.T, SN.T], [vn2.T], out=vn2[:], in0=vn[:], in1=SN[:], op=ALU.mult)
                    pg, Tpg = CUR["gen"].next()
                    for g in range(4):
                        P.I("pe", "matmul", [WsT.T, vn2.T], [Tpg], pg[:, g * 64:(g + 1) * 64], lhsT=WsT[:, g, :], rhs=vn2[:, g * 64:(g + 1) * 64], start=True, stop=True)
                    P.I("dve", "tensor_tensor", [Tpg, SGB.T], [gt_.T], out=gt_[:].rearrange("p (g d) -> p g d", g=4), in0=pg[:, 0:256].rearrange("p (g d) -> p g d", g=4),
                                                          in1=SGB[:].unsqueeze(2).to_broadcast([128, 4, 64]), op=ALU.add)
                    P.I("pool", "tensor_tensor", [gt_.T, u_b.T], [mx.T], out=mx[:, 0:256], in0=gt_[:], in1=u_b[:], op=ALU.mult)
                use("R")
                if 'ret' in DBG:
                    pq, Tpq = proj(hT, win_v, Twin, 512, 384, CUR["gen"])
                    rope64(pq, Tpq, rp, rsum)
                    P.I("act", "activation", [rsum.T], [qr.T], out=qr[:], in_=rsum[:], func=AF.Copy)
                    pk, Tpk = proj(hT, win_v, Twin, 896, 384, CUR["gen"])
                    rope64(pk, Tpk, rp, rsum)
                    P.I("act", "activation", [rsum.T], [kr.T], out=kr[:], in_=rsum[:], func=AF.Copy)
                    P.I("dve", "tensor_tensor", [rsum.T, KDF.T], [kdf.T], out=kdf[:].rearrange("p (h d) -> p h d", h=6), in0=rsum[:].rearrange("p (h d) -> p h d", h=6),
                                                          in1=KDF[:].unsqueeze(2).to_broadcast([128, 6, 64]), op=ALU.mult)
                    pv, Tpv = proj(hT, win_v, Twin, 1280, 384, CUR["gen"])
                    P.I("act", "activation", [Tpv], [vt.T], out=vt[:], in_=pv[:, 0:384], func=AF.Copy)
                    pgg, Tpgg = proj(hT, win_v, Twin, 1664, 384, CUR["gen"])
                    P.I("act", "activation", [Tpgg], [sg.T], out=sg[:], in_=pgg[:, 0:384], func=AF.Silu)
                    if 'r1' in DBG2:
                        for i, srcb in enumerate([qr, kr]):
                            for q in range(3):
                                P.I("pe", "transpose", [srcb.T, ident.T], [Tbank[4]], out=bankb(4)[:, (i * 3 + q) * 128:(i * 3 + q + 1) * 128],
                                                                                               in_=srcb[:, q * 128:(q + 1) * 128], identity=ident[:])
                        P.I("act", "activation", [Tbank[4]], [qkT.T], out=qkT[:].rearrange("p a i -> p (a i)"), in_=bankb(4)[:, 0:768], func=AF.Copy)
                        P.I("dve", "tensor_tensor", [qkT.T, QDF.T], [qdf.T], out=qdf[:], in0=qkT[:, 0:3, :], in1=QDF[:], op=ALU.mult)
                        P.I("dve", "tensor_tensor", [qkT.T, QDB.T], [qdb.T], out=qdb[:], in0=qkT[:, 0:3, :], in1=QDB[:], op=ALU.mult)
                    if 'r2' in DBG2:
                        for par in range(2):
                            pS, TpS = CUR["gen"].next()
                            for q_ in range(3):
                                P.I("pe", "matmul", [qkT.T], [TpS], pS[:, q_ * 128:(q_ + 1) * 128], lhsT=qkT[par * 64:(par + 1) * 64, 3 + q_, :],
                                    rhs=qkT[par * 64:(par + 1) * 64, q_, :], start=True, stop=True)
                            P.I("dve", "tensor_tensor", [TpS, DT.T], [PT.T], out=PT[:, par * 3:(par + 1) * 3, :], in0=pS[:, 0:384].rearrange("p (h i) -> p h i", h=3),
                                in1=DT[:, par * 3:(par + 1) * 3, :], op=ALU.mult)
                    if 'r3' in DBG2:
                        pY, TpY = CUR["gen"].next()
                        for h in range(6):
                            q_, par = h // 2, h % 2
                            sl = slice(par * 64, (par + 1) * 64)
                            P.I("pe", "matmul", [PT.T, vt.T], [TpY], pY[:, h * 64:(h + 1) * 64], lhsT=PT[:, par * 3 + q_, :], rhs=vt[:, h * 64:(h + 1) * 64], start=True, stop=False)
                            P.I("pe", "matmul", [qdf.T, Sfb.T], [TpY], pY[:, h * 64:(h + 1) * 64], lhsT=qdf[sl, q_, :], rhs=Sfb[sl, q_, :], start=False, stop=False)
                            P.I("pe", "matmul", [qdb.T, Tsb[n]], [TpY], pY[:, h * 64:(h + 1) * 64], lhsT=qdb[sl, q_, :], rhs=sbst_v[sl, n, q_, :], start=False, stop=True)
                    if 'r4' in DBG2:
                        kv_update(kdf, Rf, CDF, n, Sfb[:], Sfb.T, CUR["gen"], boundary=((n + 1) % SEGC == 0))
                    if 'r5' in DBG2:
                        s2 = CUR["st"].next()
                        Y3 = pY[:, 0:384].rearrange("p (h d) -> p h d", h=6)
                        P.I("dve", "tensor_reduce", [TpY], [s2.T], out=s2[:, 0:6], in_=Y3, axis=AX.X, op=ALU.add)
                        P.I("act", "activation", [TpY], [ysq.T], out=ysq[:], in_=pY[:, 0:384], func=AF.Square)
                        s3 = CUR["st"].next()
                        P.I("dve", "tensor_reduce", [ysq.T], [s3.T], out=s3[:, 0:6], in_=ysq[:].rearrange("p (h d) -> p h d", h=6), axis=AX.X, op=ALU.add)
                        P.I("dve", "tensor_scalar", [s2.T], [s2.T], out=s2[:, 0:6], in0=s2[:, 0:6], scalar1=1.0 / 64, scalar2=None, op0=ALU.mult)
                        s4 = CUR["st"].next()
                        P.I("dve", "tensor_tensor", [s2.T], [s4.T], out=s4[:, 0:6], in0=s2[:, 0:6], in1=s2[:, 0:6], op=ALU.mult)
                        P.I("dve", "scalar_tensor_tensor", [s3.T, s4.T], [s3.T], out=s3[:, 0:6], in0=s3[:, 0:6], scalar=1.0 / 64, in1=s4[:, 0:6], op0=ALU.mult, op1=ALU.subtract)
                        P.I("dve", "tensor_scalar", [s3.T], [s3.T], out=s3[:, 0:6], in0=s3[:, 0:6], scalar1=EPS, scalar2=None, op0=ALU.add)
                        P.I("pool", "tensor_tensor", [s3.T, cmh.T], [s4.T], out=s4[:, 0:6], in0=s3[:, 0:6], in1=cmh[:, 0:6], op=ALU.pow)
                        P.I("dve", "tensor_tensor", [TpY, s2.T], [yc_.T], out=yc_[:].rearrange("p (h d) -> p h d", h=6), in0=Y3, in1=s2[:, 0:6].unsqueeze(2).to_broadcast([128, 6, 64]),
                                                              op=ALU.subtract)
                        P.I("dve", "tensor_tensor", [yc_.T, s4.T], [yn_.T], out=yn_[:].rearrange("p (h d) -> p h d", h=6), in0=yc_[:].rearrange("p (h d) -> p h d", h=6),
                                                              in1=s4[:, 0:6].unsqueeze(2).to_broadcast([128, 6, 64]), op=ALU.mult)
                        P.I("pool", "tensor_tensor", [yn_.T, sg.T], [mx.T], out=mx[:, 256:640], in0=yn_[:], in1=sg[:], op=ALU.mult)
            m = n - LAG
            if m >= 0:
                use("L")
                if 'attn' in DBG:
                    attention(l, m)
                if 'out' in DBG:
                    xr = load_x(src, Tsrc, m, xLr, dx[2:3])
                    out_stage(l, m, xr, Gseg[m // SEGC], dst, Tdst)
            use(None)
            P.flush()
            mf = n - (LAG - 1)
            if 0 <= mf < NCH and mf % SEGC == 0:
                Gseg[mf // SEGC] = load_G(l, 0, mf // SEGC)

    class SubRing:
        def __init__(self, bufs):
            self.b = bufs; self.n = len(bufs); self.i = -1

        def next(self):
            self.i += 1
            return self.b[self.i % self.n]

    def phase_B(l, src, Tsrc, dst, Tdst):
        P.dma("sp", CW[:], cw_in[l], [], [CW.T], ND("CW"))
        xN = SubRing(xcr.b[0:2]); xD = SubRing(xcr.b[2:3])
        dxN = dx[0:2]; dxD = dx[2:3]
        gB = Gen([1, 2, 3, 4, 5])
        gDn = Gen([6, 7])
        hbs = {}
        normed = [-1]
        Gseg = {}

        def do_norm(c):
            hb_alloc(c)
            nback(c, nfront(c))

        def hb_alloc(c):
            b, half = c // 2, c % 2
            if half == 0:
                hb = HB.next()
                hbs[b] = hb
                if b == 0:
                    P.I("pool", "memset", [], [hb.T], hb[:, :, 0:1], 0.0)
                if b == NB - 1:
                    P.I("pool", "memset", [], [hb.T], hb[:, :, 257:258], 0.0)

        def nfront(c):
            xc = load_x(src, Tsrc, c, xN, dxN)
            return norm_front(xc)

        def nfront_pre(c):
            hb_alloc(c)
            return nfront(c)

        def nf1(c):
            hb_alloc(c)
            xc = load_x(src, Tsrc, c, xN, dxN)
            return norm_front_1(xc)

        def down1(b, half):
            aT = actT.at(b)
            c = 2 * b + half
            seg = c // SEGC
            if seg not in Gseg:
                Gseg[seg] = load_G(l, 1, seg)
            xr = load_x(src, Tsrc, c, xD, dxD)
            return post_stage_1(lambda kc: (aT[:, kc, half * 128:(half + 1) * 128], aT.T), NFC, wdn_v, Twdn, xr, Gseg[seg], c, dst, Tdst, gDn)

        def nback(c, xb_):
            b, half = c // 2, c % 2
            seg = c // SEGC
            hb = hbs[b]
            o = 1 + half * 128
            norm_back(xb_, seg, A2f, fm[2], lambda kc: hb[:, kc, o:o + 128], hb.T)
            if half == 0 and b > 0:
                pv_ = hbs[b - 1]
                if c % SEGC == 0:
                    P.I("dve", "tensor_scalar", [hb.T, flag.T], [pv_.T], out=pv_[:, :, 257:258], in0=hb[:, :, 1:2], scalar1=flag[:, 0:1], scalar2=None, op0=ALU.mult)
                else:
                    P.I("pool", "tensor_copy", [hb.T], [pv_.T], out=pv_[:, :, 257:258], in_=hb[:, :, 1:2])
            normed[0] = c

        def halo_fwd(b):
            hb = hbs[b]; nx = hbs[b + 1]
            c = 2 * b + 1
            if (c + 1) % SEGC == 0:
                P.I("dve", "tensor_scalar", [hb.T, flag.T], [nx.T], out=nx[:, :, 0:1], in0=hb[:, :, 256:257], scalar1=flag[:, 0:1], scalar2=None, op0=ALU.mult)
            else:
                P.I("pool", "tensor_copy", [hb.T], [nx.T], out=nx[:, :, 0:1], in_=hb[:, :, 256:257])

        def up_pairs(b, f0, f1):
            hb = hbs[b]
            aT = actT.at(b)
            for fc in range(f0, f1):
                tl = []
                for which in range(2):
                    fcc = fc + which * NFC
                    pb, Tpb = gB.next()
                    for kc in range(8):
                        P.I("pe", "matmul", [Twup[kc], hb.T], [Tpb], pb[:, 0:258], lhsT=wup_v[:, kc, fcc * 128:(fcc + 1) * 128], rhs=hb[:, kc, :],
                            start=(kc == 0), stop=(kc == 7))
                    tl.append((fcc, pb, Tpb, T0.next(), T1.next(), T2.next()))
                for (fcc, pb, Tpb, t0, t1, t2) in tl:
                    P.I("act", "activation", [Tpb, CW.T], [t0.T], out=t0[:], in_=pb[:, 0:256], func=AF.Identity, scale=CW[:, fcc, 0:1], bias=CW[:, fcc, 3:4])
                for (fcc, pb, Tpb, t0, t1, t2) in tl:
                    P.I("dve", "scalar_tensor_tensor", [Tpb, CW.T, t0.T], [t1.T], out=t1[:], in0=pb[:, 1:257], scalar=CW[:, fcc, 1:2], in1=t0[:],
                        op0=ALU.mult, op1=ALU.add)
                for (fcc, pb, Tpb, t0, t1, t2) in tl:
                    P.I("dve", "scalar_tensor_tensor", [Tpb, CW.T, t1.T], [t2.T], out=t2[:], in0=pb[:, 2:258], scalar=CW[:, fcc, 2:3], in1=t1[:],
                        op0=ALU.mult, op1=ALU.add)
                g_ = gg.next()
                tg, tv = tl[0][5], tl[1][5]
                P.I("act", "activation", [tg.T], [g_.T], out=g_[:], in_=tg[:], func=AF.Gelu_apprx_tanh)
                P.I("pool", "tensor_tensor", [tv.T, g_.T], [aT.T], out=aT[:, fc, :], in0=tv[:], in1=g_[:], op=ALU.mult)

        def down(b):
            aT = actT.at(b)
            for half in range(2):
                c = 2 * b + half
                seg = c // SEGC
                if seg not in Gseg:
                    Gseg[seg] = load_G(l, 1, seg)
                xr = load_x(src, Tsrc, c, xD, dxD)
                post_stage(lambda kc: (aT[:, kc, half * 128:(half + 1) * 128], aT.T), NFC, wdn_v, Twdn, xr, Gseg[seg], c, dst, Tdst, gDn)

        do_norm(0); do_norm(1)
        if NCH > 2:
            do_norm(2)
            halo_fwd(0)
        for b in range(NB):
            c1, c2 = 2 * b + 3, 2 * b + 4
            n1 = nf1(c1) if c1 < NCH else None
            up_pairs(b, 0, 2)
            f1 = norm_front_2(n1) if n1 is not None else None
            up_pairs(b, 2, 4)
            d0 = down1(b - 1, 0) if b > 0 else None
            up_pairs(b, 4, 6)
            if d0 is not None:
                post_stage_2(d0)
            up_pairs(b, 6, 8)
            if f1 is not None:
                nback(c1, f1)
            d1 = down1(b - 1, 1) if b > 0 else None
            n2 = nf1(c2) if c2 < NCH else None
            up_pairs(b, 8, 10)
            if d1 is not None:
                post_stage_2(d1)
            f2 = norm_front_2(n2) if n2 is not None else None
            up_pairs(b, 10, 16)
            if f2 is not None:
                nback(c2, f2)
                halo_fwd(b + 1)
            up_pairs(b, 16, NFC)
        down(NB - 1)

    cur, Tcur = x_in, None
    for l in range(depth):
        P.barrier()
        load_A_weights(l)
        layer_setup(l)
        P.barrier()
        if stop_after == ("setup", l):
            break
        phase_A0(l, cur, Tcur)
        if stop_after == ("A0", l):
            break
        phase_A1(l, cur, Tcur, xa, Txa)
        if stop_after == ("A1", l):
            cur, Tcur = xa, Txa
            break
        P.barrier()
        load_B_weights(l)
        last = (l == depth - 1)
        dst, Tdst = (y_out, Ty) if last else (xb, Txb)
        phase_B(l, xa, Txa, dst, Tdst)
        cur, Tcur = dst, Tdst
    if cur is not y_out:
        for n in range(NCH):
            xc = load_x(cur, Tcur, n, xcr, dx)
            P.dma("sp", y_out[n * 128:(n + 1) * 128, :], xc[:], [xc.T], [Ty[n]], dcp[xcr.i % xcr.n])
    P.wait_all("sp", Ty)
    counts = P.emit()
    es.close()
    return nc, counts


def _rope_tables(pos, rot_dim, theta):
    half = rot_dim // 2
    freqs = np.exp(-math.log(theta) * np.arange(half, dtype=np.float32) * np.float32(2.0) / np.float32(rot_dim)).astype(np.float32)
    ang = pos.astype(np.float32)[:, None] * freqs[None, :]
    return np.cos(ang).astype(np.float32), np.sin(ang).astype(np.float32)


def core_tables(NCH, SEGC, is_prompt):
    n = np.arange(NCH)
    if is_prompt:
        base = n * 128
    else:
        base = (n % SEGC) * 128
    pos = (base[None, :] + np.arange(128)[:, None]).reshape(-1)
    rc_, rs_ = _rope_tables(pos, 64, 10000.0)
    ac_, as__ = _rope_tables(pos, 16, 500000.0)
    f = lambda a, d: a.reshape(128, NCH, d)
    rope = np.concatenate([f(rc_, 32), f(rs_, 32), f(-rs_, 32), f(ac_, 8), f(as__, 8), f(-as__, 8)], 2)
    return dict(rope=np.ascontiguousarray(rope.transpose(1, 0, 2)).astype(np.float32),
                flag=np.full((128, 1), 1.0 if is_prompt else 0.0, np.float32))


def const_inputs():
    j = np.arange(128, dtype=np.float32)[:, None]
    i = np.arange(128, dtype=np.float32)[None, :]
    dpos = np.maximum(i - j, 0); dneg = np.maximum(j - i, 0)
    mge = (i >= j).astype(np.float32); mlt = (i < j).astype(np.float32)
    io1 = np.broadcast_to(i + 1, (128, 128)); io2 = np.broadcast_to(128 - i, (128, 128))
    cm = np.stack([dpos, dneg, mge, mlt, io1, io2], 1).astype(np.float32)
    tri = np.stack([(j >= i).astype(np.float32) * np.ones((128, 128), np.float32), (j <= i).astype(np.float32) * np.ones((128, 128), np.float32)], 1)
    jv = np.concatenate([127 - j, j], 1).astype(np.float32)
    return dict(ident=np.eye(128, dtype=np.float32), cmats=np.ascontiguousarray(cm), tri=np.ascontiguousarray(tri), jv=jv)


def weight_inputs(w_ada, b_ada, norm_pre_mix, norm_post_mix, norm_pre_ffn, norm_post_ffn, w_in, sgu_norm, sgu_w, sgu_b,
                  ret_decay_fwd, ret_decay_bwd, attn_sink, w_out, w_up, conv_w, conv_b, w_down):
    L = w_in.shape[0]
    perm = np.arange(IN_DIM)
    hq = [0, 3, 1, 4, 2, 5]
    perm[2048:2432] = np.concatenate([2048 + h * 64 + np.arange(64) for h in hq])
    w_in_p = np.ascontiguousarray(w_in[:, :, perm])
    cw = np.concatenate([conv_w, conv_b[:, None, :]], 1)
    cw = np.ascontiguousarray(cw.reshape(L, 4, 2 * NFC, 128).transpose(0, 3, 2, 1))
    gfm = np.stack([norm_pre_mix, norm_pre_ffn], 1).reshape(L, 2, 8, 128).transpose(0, 3, 1, 2)
    gpost = np.concatenate([norm_post_mix, norm_post_ffn], 1)
    sgu_wT = np.ascontiguousarray(sgu_w.transpose(0, 3, 1, 2))
    sgu_bT = np.ascontiguousarray(sgu_b.transpose(0, 2, 1))
    dec6 = np.concatenate([ret_decay_fwd, ret_decay_bwd], 1)
    dec6 = np.broadcast_to(dec6[:, None, :], (L, 128, 12))
    decP = np.zeros((L, 128, 6), np.float32)
    decP[:, 0:64, 0:3] = ret_decay_fwd[:, None, 0::2]; decP[:, 64:128, 0:3] = ret_decay_fwd[:, None, 1::2]
    decP[:, 0:64, 3:6] = ret_decay_bwd[:, None, 0::2]; decP[:, 64:128, 3:6] = ret_decay_bwd[:, None, 1::2]
    sink6 = np.ascontiguousarray(np.broadcast_to(attn_sink[:, None, :], (L, 128, 6)))
    c = np.ascontiguousarray
    return dict(w_ada=c(w_ada), b_ada=c(b_ada), w_in=w_in_p, w_out=c(w_out), w_up=c(w_up), w_down=c(w_down), cw=cw,
                gfm=c(gfm.astype(np.float32)), gpost=c(gpost.astype(np.float32)), sgu_wT=sgu_wT, sgu_bT=sgu_bT, sgu_n=c(sgu_norm),
                dec18=c(np.concatenate([dec6, decP], 2).astype(np.float32)), sink6=sink6.astype(np.float32))


def core_c(c_rows):
    ns = c_rows.shape[0]
    return np.ascontiguousarray(c_rows.reshape(ns, 8, 128).transpose(2, 1, 0)).astype(np.float32)


_CACHE = {}


def kernel(x_prompt, x_sample, c_prompt, c_sample, w_ada, b_ada, norm_pre_mix, norm_post_mix, norm_pre_ffn, norm_post_ffn,
           w_in, sgu_norm, sgu_w, sgu_b, ret_decay_fwd, ret_decay_bwd, attn_sink, w_out, w_up, conv_w, conv_b, w_down):
    NCH, SEGC = 64, 16
    f = lambda a: np.asarray(a, dtype=np.float32)
    x_prompt, x_sample, c_prompt, c_sample = f(x_prompt), f(x_sample), f(c_prompt), f(c_sample)
    wts = weight_inputs(*[f(a) for a in (w_ada, b_ada, norm_pre_mix, norm_post_mix, norm_pre_ffn, norm_post_ffn, w_in, sgu_norm, sgu_w, sgu_b,
                                           ret_decay_fwd, ret_decay_bwd, attn_sink, w_out, w_up, conv_w, conv_b, w_down)])
    consts = const_inputs()
    if "nc" not in _CACHE:
        _CACHE["nc"] = build(NCH, SEGC)[0]
    nc = _CACHE["nc"]
    in_maps = []
    for core in range(8):
        if core < 2:
            xs = x_prompt[core]
            cr = np.repeat(c_prompt[core:core + 1], 4, 0)
            tb = core_tables(NCH, SEGC, True)
        else:
            k = min(core - 2, 3)
            xs = x_sample[4 * k:4 * k + 4].reshape(NCH * 128, D)
            cr = c_sample[4 * k:4 * k + 4]
            tb = core_tables(NCH, SEGC, False)
        m = dict(x=np.ascontiguousarray(xs), cT=core_c(cr))
        m.update(tb); m.update(consts); m.update(wts)
        in_maps.append(m)
    res = run_bass_kernel_spmd(nc, in_maps, core_ids=list(range(8)))
    r = res.results
    y_prompt = np.stack([r[0]["y"], r[1]["y"]], 0).astype(np.float32)
    y_sample = np.concatenate([r[2 + k]["y"].reshape(4, 2048, D) for k in range(4)], 0).astype(np.float32)
    return (y_prompt, y_sample)
```

```python
import math
import os
import numpy as np
DBG = os.environ.get("KDBG", "sgu,ret,ap,attn,out").split(",")
DBG2 = os.environ.get("KDBG2", "r1,r2,r3,r4,r5").split(",")
from contextlib import ExitStack
import concourse.bass as bass
import concourse.mybir as mybir
from concourse.bass_utils import run_bass_kernel_spmd

F32 = mybir.dt.float32
BF16 = mybir.dt.bfloat16
AF = mybir.ActivationFunctionType
ALU = mybir.AluOpType
AX = mybir.AxisListType

SAME_ENGINE_SYNC = bool(int(os.environ.get("KSES", "1")))
EPS = 1e-6
D = 1024
IN_DIM = 2688
DFF = 2816
NFC = 22
DEPTH = 2


class Tile:
    __slots__ = ("name", "w", "r", "rd", "excl")

    def __init__(self, name="", excl=False):
        self.name = name
        self.w = None
        self.r = {}
        self.rd = []
        self.excl = excl


class Op:
    __slots__ = ("eng", "fn", "deps", "inc", "val", "dsem")

    def __init__(self, eng, fn, dsem):
        self.eng = eng
        self.fn = fn
        self.dsem = dsem
        self.deps = set()
        self.inc = dsem is not None
        self.val = 0


class DSem:
    __slots__ = ("h", "cnt")

    def __init__(self, h):
        self.h = h
        self.cnt = 0


class Prog:
    ENGS = ("pe", "act", "dve", "pool", "sp")

    def __init__(self, nc, es):
        self.nc = nc
        self.es = es
        self.ops = {e: [] for e in self.ENGS}
        self.esem = {e: es.enter_context(nc.semaphore("s_" + e)) for e in self.ENGS}
        self.n = 0
        self.bar_idx = {}
        self.region = None
        self.cur = None
        self.streams = {}

    def sb(self, shape, dt, name=None):
        self.n += 1
        return self.es.enter_context(self.nc.sbuf_tensor(f"sb{self.n}_{name or ''}", list(shape), dt))

    def ps(self, shape, dt=F32, name=None):
        self.n += 1
        return self.es.enter_context(self.nc.psum_tensor(name or f"ps{self.n}", list(shape), dt))

    def dsem(self):
        self.n += 1
        return DSem(self.es.enter_context(self.nc.semaphore(f"ds{self.n}")))

    def set_stream(self, name):
        self.cur = None if name is None else self.streams.setdefault(name, [])

    def flush(self):
        self.cur = None
        lists = [v for v in self.streams.values() if v]
        if os.environ.get("KSEQ"):
            for li in lists:
                for r_ in li:
                    self._op(*r_)
            self.streams = {}
            return
        idx = [0] * len(lists)
        while True:
            best = None
            for i, li in enumerate(lists):
                if idx[i] < len(li):
                    f = (idx[i] + 1) / len(li)
                    if best is None or f < best[0]:
                        best = (f, i)
            if best is None:
                break
            i = best[1]
            self._op(*lists[i][idx[i]])
            idx[i] += 1
        self.streams = {}

    def op(self, eng, fn, reads=(), writes=(), dsem=None):
        if self.cur is not None:
            self.cur.append((eng, fn, list(reads), list(writes), dsem))
            return None
        return self._op(eng, fn, reads, writes, dsem)

    def _op(self, eng, fn, reads=(), writes=(), dsem=None):
        o = Op(eng, fn, dsem)
        deps = o.deps
        for t in reads:
            if t.w is not None:
                deps.add(t.w)
            if t.excl:
                for e_, o_ in t.r.items():
                    if e_ != eng:
                        deps.add(o_)
        for t in writes:
            if t.w is not None:
                deps.add(t.w)
            deps.update(t.r.values())
            deps.update(t.rd)
        for t in reads:
            if dsem is not None:
                t.rd.append(o)
            else:
                t.r[eng] = o
        for t in writes:
            t.w = o
            t.r = {}
            t.rd = []
        if dsem is not None:
            dsem.cnt += 16
            o.val = dsem.cnt
        self.ops[eng].append(o)
        return o

    def I(self, eng, meth, reads, writes, *a, **k):
        return self.op(eng, (meth, a, k), reads, writes)

    def dma(self, eng, out, in_, reads, writes, dsem=None, slow=False):
        if dsem is None:
            dsem = self.dsem()
        k = dict(out=out, in_=in_)
        if slow:
            k["allow_slow_non_contiguous"] = True
        return self.op(eng, ("dma_start", (), k), reads, writes, dsem)

    def barrier(self):
        lasts = []
        for e in self.ENGS:
            comp = [o for o in self.ops[e] if o.dsem is None and o.fn is not None]
            if comp:
                lasts.append(comp[-1])
        dmas = [o for e in self.ENGS for o in self.ops[e][self.bar_idx.get(e, 0):] if o.dsem is not None]
        for e in self.ENGS:
            self.bar_idx[e] = len(self.ops[e])
        for e in self.ENGS:
            o = Op(e, None, None)
            o.deps.update(lasts)
            o.deps.update(dmas)
            self.ops[e].append(o)

    def wait_all(self, eng, tiles):
        o = Op(eng, None, None)
        for t in tiles:
            if t.w is not None:
                o.deps.add(t.w)
        self.ops[eng].append(o)

    def emit(self):
        nc = self.nc
        for e in self.ENGS:
            for o in self.ops[e]:
                for d in o.deps:
                    if d.dsem is not None:
                        continue
                    if d.eng != o.eng or o.dsem is not None:
                        d.inc = True
                    elif SAME_ENGINE_SYNC and d.eng != "pe":
                        d.inc = True
        for e in self.ENGS:
            c = 0
            for o in self.ops[e]:
                if o.dsem is None:
                    if o.inc:
                        c += 1
                    o.val = c
        counts = {}
        with nc.Block() as block:
            def run(e):
                def body(h):
                    waited = {}
                    nw = 0
                    for o in self.ops[e]:
                        need = {}
                        for d in o.deps:
                            if d.dsem is not None:
                                key = d.dsem
                                sem = d.dsem.h
                            else:
                                if d.eng == e and o.dsem is None:
                                    if e == "pe" or not SAME_ENGINE_SYNC:
                                        continue
                                key = d.eng
                                sem = self.esem[d.eng]
                            if need.get(key, (None, 0))[1] < d.val:
                                need[key] = (sem, d.val)
                        for key, (sem, val) in need.items():
                            if waited.get(key, 0) < val:
                                h.wait_ge(sem, val)
                                waited[key] = val
                                nw += 1
                        if o.fn is None:
                            continue
                        meth, a, k = o.fn
                        inst = getattr(h, meth)(*a, **k)
                        if o.dsem is not None:
                            inst.then_inc(o.dsem.h, 16)
                        elif o.inc:
                            inst.then_inc(self.esem[e], 1)
                    counts[e] = (len(self.ops[e]), nw)
                return body
            block.tensor(run("pe"))
            block.scalar(run("act"))
            block.vector(run("dve"))
            block.gpsimd(run("pool"))
            block.sync(run("sp"))
        return counts


class Region:
    def __init__(self, big, start, limit):
        self.big = big
        self.off = start
        self.limit = limit

    def alloc(self, shape, dt):
        n = 1
        for s in shape[1:]:
            n *= s
        n16 = n * (2 if dt == F32 else 1)
        self.off = (self.off + 15) // 16 * 16
        ap = self.big[0:shape[0], self.off:self.off + n16]
        self.off += n16
        assert self.off <= self.limit, ("region overflow", self.off, self.limit)
        if dt == F32:
            ap = ap.bitcast(F32)
        if len(shape) == 3:
            ap = ap.rearrange("p (a b) -> p a b", a=shape[1])
        elif len(shape) == 4:
            ap = ap.rearrange("p (a b c) -> p a b c", a=shape[1], b=shape[2])
        return ap


class Buf:
    def __init__(self, P, shape, dt, name=None):
        if getattr(P, "region", None) is not None:
            self.t = P.region.alloc(shape, dt)
        else:
            self.t = P.sb(shape, dt, name)
        self.T = Tile(name or "")

    def __getitem__(self, k):
        return self.t[k]


class Ring:
    def __init__(self, P, n, shape, dt, name):
        self.b = [Buf(P, shape, dt, f"{name}{i}") for i in range(n)]
        self.n = n
        self.i = -1

    def next(self):
        self.i += 1
        return self.b[self.i % self.n]

    def at(self, k):
        return self.b[k % self.n]


def build(NCH, SEGC, depth=DEPTH, stop_after=None):
    NSEG = NCH // SEGC
    NT = NCH * 128
    NB = NCH // 2
    nc = bass.Bass("TRN2", target_bir_lowering=False)

    def din(name, shape):
        return nc.dram_tensor(name, list(shape), F32, kind="ExternalInput").ap()

    x_in = din("x", [NT, D])
    cT_in = din("cT", [128, 8, NSEG])
    flag_in = din("flag", [128, 1])
    rope_in = din("rope", [NCH, 128, 120])
    ident_in = din("ident", [128, 128])
    cm_in = din("cmats", [128, 6, 128])
    tri_in = din("tri", [128, 2, 128])
    jv_in = din("jv", [128, 2])
    w_ada = din("w_ada", [depth, D, 6 * D]); b_ada = din("b_ada", [depth, 6 * D])
    w_in = din("w_in", [depth, D, IN_DIM]); w_out = din("w_out", [depth, D, D])
    w_up = din("w_up", [depth, D, 2 * DFF]); w_down = din("w_down", [depth, DFF, D])
    cw_in = din("cw", [depth, 128, 2 * NFC, 4])
    gfm_in = din("gfm", [depth, 128, 2, 8])
    gpost_in = din("gpost", [depth, 2 * D])
    sguw_in = din("sgu_wT", [depth, 128, 4, 128]); sgub_in = din("sgu_bT", [depth, 128, 4])
    sgun_in = din("sgu_n", [depth, 256])
    dec18_in = din("dec18", [depth, 128, 18])
    sink_in = din("sink6", [depth, 128, 6])
    y_out = nc.dram_tensor("y", [NT, D], F32, kind="ExternalOutput").ap()
    xa = nc.dram_tensor("xa", [NT, D], F32, kind="Internal").ap()
    xb = nc.dram_tensor("xb", [NT, D], F32, kind="Internal").ap()
    modD = nc.dram_tensor("modD", [depth, NSEG, 6 * D], F32, kind="Internal").ap()
    gD = nc.dram_tensor("gD", [depth, 2, NSEG, D], F32, kind="Internal").ap()

    es = ExitStack()
    P = Prog(nc, es)
    Txa = [Tile(f"xa{i}") for i in range(NCH)]
    Txb = [Tile(f"xb{i}") for i in range(NCH)]
    Ty = [Tile(f"y{i}") for i in range(NCH)]
    TmodD = [None] * depth
    TgD = [None] * depth

    PS = P.ps([128, 4096], F32, "PSALL")
    PSb = PS.bitcast(BF16)
    Tbank = [Tile(f"bank{i}", excl=True) for i in range(8)]

    def bank(i):
        return PS[:, 512 * i:512 * (i + 1)]

    def bankb(i):
        return PSb[:, 1024 * i:1024 * (i + 1)]

    class Gen:
        def __init__(self, ids):
            self.ids = ids
            self.i = -1

        def next(self):
            self.i += 1
            b = self.ids[self.i % len(self.ids)]
            return bank(b), Tbank[b]

    ident = Buf(P, [128, 128], BF16, "ident")
    tri = Buf(P, [128, 2, 128], BF16, "tri")
    trif = Buf(P, [128, 2, 128], BF16, "trif")
    jv = Buf(P, [128, 2], F32, "jv")
    flag = Buf(P, [128, 1], F32, "flag")
    ropeR = Ring(P, 3, [128, 120], F32, "rope")
    drope = [P.dsem() for _ in range(3)]
    cmh = Buf(P, [128, 8], F32, "cmh")
    siluT = Buf(P, [128, 8, NSEG], BF16, "siluT")
    cTf = Buf(P, [128, 8, NSEG], F32, "cTf")

    dsems = []

    def DS():
        d = P.dsem()
        dsems.append(d)
        return d

    _nd = {}

    def ND(name):
        if name not in _nd:
            _nd[name] = DS()
        return _nd[name]
    for (b, src, eng) in [(ident, ident_in, "pool"), (tri, tri_in, "pool"), (jv, jv_in, "sp"),
                          (flag, flag_in, "sp"), (cTf, cT_in, "sp")]:
        P.dma(eng, b[:], src, [], [b.T], ND("c_" + b.T.name))
    P.I("pool", "memset", [], [cmh.T], cmh[:], -0.5)
    P.I("act", "activation", [cTf.T], [siluT.T], out=siluT[:], in_=cTf[:], func=AF.Silu)
    P.I("dve", "tensor_scalar", [tri.T, flag.T], [trif.T], out=trif[:], in0=tri[:], scalar1=flag[:, 0:1], scalar2=None, op0=ALU.mult)

    ARENA = 8 * 2 * DFF + NFC * D
    BIGN = ARENA + 21120
    W = P.sb([128, BIGN], BF16, "arena")
    win_v = W[:, 0:8 * IN_DIM].rearrange("p (k n) -> p k n", k=8)
    wout_v = W[:, 8 * IN_DIM:8 * IN_DIM + 8 * D].rearrange("p (k n) -> p k n", k=8)
    SB0 = 8 * IN_DIM + 8 * D
    sbst_v = W[:, SB0:SB0 + NCH * 192].rearrange("p (c q e) -> p c q e", c=NCH, q=3)
    wup_v = W[:, 0:8 * 2 * DFF].rearrange("p (k n) -> p k n", k=8)
    wdn_v = W[:, 8 * 2 * DFF:ARENA].rearrange("p (k n) -> p k n", k=NFC)
    Twin = [Tile(f"win{k}") for k in range(8)]
    Twout = [Tile(f"wout{k}") for k in range(8)]
    Tsb = [Tile(f"sbst{c}") for c in range(NCH)]
    Twup = [Tile(f"wup{k}") for k in range(8)]
    Twdn = [Tile(f"wdn{k}") for k in range(NFC)]
    A_tiles = Twin + Twout + Tsb
    B_tiles = Twup + Twdn
    dWin = [DS() for _ in range(8)]; dWout = [DS() for _ in range(8)]; dWup = [DS() for _ in range(8)]; dWdn = [DS() for _ in range(NFC)]

    regA = Region(W, SB0 + NCH * 192, BIGN)
    P.region = regA
    d18 = Buf(P, [128, 18], F32, "d18"); e18 = Buf(P, [128, 18], F32, "e18"); lg18 = Buf(P, [128, 18], F32, "lg18")
    DT = Buf(P, [128, 6, 128], F32, "DT")
    QDF = Buf(P, [128, 3, 128], F32, "QDF"); QDB = Buf(P, [128, 3, 128], F32, "QDB")
    KDF = Buf(P, [128, 6], F32, "KDF"); KDB = Buf(P, [128, 6], F32, "KDB")
    CDF = Buf(P, [128, 3], F32, "CDF"); CDB = Buf(P, [128, 3], F32, "CDB")
    ESK = Buf(P, [128, 6], F32, "ESK")
    SN = Buf(P, [128, 256], F32, "SN")
    WsT = Buf(P, [128, 4, 128], BF16, "WsT"); SGB = Buf(P, [128, 4], F32, "SGB")
    P.region = None
    CW = Buf(P, [128, 2 * NFC, 4], F32, "CW")
    gfm = Buf(P, [128, 2, 8], F32, "gfm")
    fm = [Buf(P, [128, NSEG, 8], F32, f"fm{i}") for i in range(4)]
    A1f = Buf(P, [128, NSEG, 8], F32, "A1f"); A2f = Buf(P, [128, NSEG, 8], F32, "A2f")
    Gt = Ring(P, 1, [128, D], F32, "Gt")
    dl = [DS() for _ in range(4)]
    dmb = [DS() for _ in range(2)]; dgp = [DS() for _ in range(2)]; dgb = [DS() for _ in range(2)]

    xcr = Ring(P, 3, [128, D], F32, "xc")
    st = Ring(P, 16, [128, 8], F32, "st")
    xn = Ring(P, 1, [128, D], BF16, "xn")
    xnew = Ring(P, 2, [128, D], F32, "xnew")
    P.region = regA
    hTr = Ring(P, 2, [128, 8, 128], BF16, "hT")
    gA = Gen([2, 3, 5, 6])
    st_rings = {None: st, "SC": Ring(P, 8, [128, 8], F32, "stSC"), "R": Ring(P, 8, [128, 8], F32, "stR"), "L": Ring(P, 8, [128, 8], F32, "stL"),
                "N": Ring(P, 8, [128, 8], F32, "stN"), "U": st}
    gens = {None: gA, "SC": Gen([2]), "R": Gen([3, 5]), "L": Gen([6]), "N": Gen([0]), "U": gA}
    gLpost = Gen([6, 7])
    _rg = P.region
    P.region = None
    st_rings["K"] = st
    gens["K"] = gA
    st_rings["NB"] = Ring(P, 8, [128, 8], F32, "stNB")
    gens["NB"] = Gen([0])
    P.region = _rg
    CUR = {"st": st, "gen": gA}
    Tb1a = Tbank[1]; Tb1b = Tbank[1]

    def use(name):
        P.set_stream(name)
        CUR["st"] = st_rings[name]
        CUR["gen"] = gens[name]

    u_b = Buf(P, [128, 256], BF16, "u"); vg = Buf(P, [128, 256], F32, "vg"); sq = Buf(P, [128, 256], F32, "sq")
    vn = Buf(P, [128, 256], F32, "vn"); vn2 = Buf(P, [128, 256], BF16, "vn2"); gt_ = Buf(P, [128, 256], F32, "gt")
    r1 = Buf(P, [128, 384], F32, "r1"); r2 = Buf(P, [128, 384], F32, "r2"); rsum = Buf(P, [128, 384], F32, "rsum")
    qr = Buf(P, [128, 384], BF16, "qr"); kr = Buf(P, [128, 384], BF16, "kr")
    kdf = Buf(P, [128, 384], BF16, "kdf"); kdb = Buf(P, [128, 384], BF16, "kdb")
    vt = Buf(P, [128, 384], BF16, "vt"); sg = Buf(P, [128, 384], F32, "sg")
    qkT = Buf(P, [128, 6, 128], BF16, "qkT")
    qdf = Buf(P, [128, 3, 128], BF16, "qdf"); qdb = Buf(P, [128, 3, 128], BF16, "qdb")
    PT = Buf(P, [128, 6, 128], BF16, "PT")
    Rf = Buf(P, [128, 3, 64], F32, "Rf"); Rb = Buf(P, [128, 3, 64], F32, "Rb"); Rt = Buf(P, [128, 3, 64], F32, "Rt")
    Sfb = Buf(P, [128, 3, 64], BF16, "Sfb")
    ysq = Buf(P, [128, 384], F32, "ysq"); yc_ = Buf(P, [128, 384], F32, "yc"); yn_ = Buf(P, [128, 384], F32, "yn")
    cqb = Buf(P, [128, 384], BF16, "cqb"); ckb = Buf(P, [128, 128], BF16, "ckb")
    a1 = Buf(P, [128, 6, 16], F32, "a1"); a2 = Buf(P, [128, 6, 16], F32, "a2")
    cqT = Ring(P, 3, [128, 3, 128], BF16, "cqT"); ckT = Ring(P, 4, [128, 128], BF16, "ckT")
    cvb = Ring(P, 4, [128, 2, 65], BF16, "cvb")
    Eb = Ring(P, 6, [128, 3, 128], BF16, "Eb")
    mix = Ring(P, 3, [128, D], BF16, "mix")
    mixT = Buf(P, [128, 8, 128], BF16, "mixT")
    xnA = Ring(P, 2, [128, D], BF16, "xnA")
    cm = Buf(P, [128, 6, 128], F32, "cmats")
    ta = Buf(P, [128, 128], F32, "ta"); tb = Buf(P, [128, 128], F32, "tb")
    wad = Ring(P, 2, [128, 8, 256], BF16, "wad")
    badb = Ring(P, 2, [NSEG, 256], F32, "badb")
    mblk = Ring(P, 2, [NSEG, 256], F32, "mblk")
    gpb = Ring(P, 2, [NSEG, 256], F32, "gpb")
    gblk = Ring(P, 2, [NSEG, 256], F32, "gblk")
    print("regA end", regA.off, BIGN)
    regB = Region(W, ARENA, BIGN)
    P.region = regB
    HB = Ring(P, 3, [128, 8, 258], BF16, "HB")
    T0 = Ring(P, 2, [128, 256], F32, "T0"); T1 = Ring(P, 2, [128, 256], F32, "T1"); T2 = Ring(P, 2, [128, 256], F32, "T2")
    gg = Ring(P, 1, [128, 256], F32, "gg")
    actT = Ring(P, 2, [128, NFC, 256], BF16, "actT")
    print("regB end", regB.off, BIGN)
    P.region = None
    dx = [DS() for _ in range(4)]
    dst_ = [DS() for _ in range(2)]
    dG = [DS() for _ in range(2)]
    dcp = [DS() for _ in range(4)]


    def rstd_from(ssb, n_inv, out_col):
        sbuf, c = ssb
        obuf, oc = out_col
        t = CUR["st"].next()
        P.I("dve", "tensor_scalar", [sbuf.T], [t.T], out=t[:, 0:1], in0=sbuf[:, c:c + 1], scalar1=n_inv, scalar2=EPS, op0=ALU.mult, op1=ALU.add)
        P.I("pool", "tensor_tensor", [t.T, cmh.T], [obuf.T], out=obuf[:, oc:oc + 1], in0=t[:, 0:1], in1=cmh[:, 0:1], op=ALU.pow)

    def layer_setup(l):
        P.dma("sp", cm[:], cm_in, [], [cm.T], ND("cm"))
        TmD = [Tile(f"modD{l}_{i}") for i in range(24)]
        TgDl = [Tile(f"gD{l}_{i}") for i in range(8)]
        for cb in range(24):
            c0 = cb * 256
            wb = wad.next(); bb = badb.next(); mb = mblk.next()
            P.dma("pool", wb[:], w_ada[l, :, c0:c0 + 256].rearrange("(k p) n -> p k n", p=128), [], [wb.T], dl[cb % 2])
            P.dma("sp", bb[:], b_ada[l:l + 1, c0:c0 + 256].partition_broadcast(NSEG), [], [bb.T], dl[2 + cb % 2])
            pb, Tpb = CUR["gen"].next()
            for kc in range(8):
                P.I("pe", "matmul", [siluT.T, wb.T], [Tpb], pb[0:NSEG, 0:256], lhsT=siluT[:, kc, :], rhs=wb[:, kc, :], start=(kc == 0), stop=(kc == 7))
            P.I("dve", "tensor_tensor", [Tpb, bb.T], [mb.T], out=mb[:], in0=pb[0:NSEG, 0:256], in1=bb[:], op=ALU.add)
            P.dma("sp", modD[l, :, c0:c0 + 256], mb[:], [mb.T], [TmD[cb]], dmb[cb % 2])
            part = cb // 4
            if part in (2, 5):
                gi = 0 if part == 2 else 1
                j = cb % 4
                gp = gpb.next(); gb_ = gblk.next()
                P.dma("sp", gp[:], gpost_in[l:l + 1, gi * D + j * 256:gi * D + (j + 1) * 256].partition_broadcast(NSEG), [], [gp.T], dgp[gpb.i % 2])
                P.I("dve", "tensor_tensor", [mb.T, gp.T], [gb_.T], out=gb_[:], in0=mb[:], in1=gp[:], op=ALU.mult)
                P.dma("sp", gD[l, gi, :, j * 256:(j + 1) * 256], gb_[:], [gb_.T], [TgDl[gi * 4 + j]], dgb[gblk.i % 2])
        TmodD[l] = TmD
        TgD[l] = TgDl
        for i, part in enumerate([0, 1, 3, 4]):
            for s_ in range(NSEG):
                P.dma("sp", fm[i][:, s_, :], modD[l, s_, part * D:(part + 1) * D].rearrange("(k p) -> p k", p=128), TmodD[l][part * 4:part * 4 + 4], [fm[i].T], ND(f"fm{i}"), slow=True)
        P.dma("sp", gfm[:], gfm_in[l], [], [gfm.T], ND("gfm"))
        for (Af, scb, gi) in [(A1f, fm[1], 0), (A2f, fm[3], 1)]:
            P.I("dve", "scalar_tensor_tensor", [scb.T, gfm.T], [Af.T],
                out=Af[:], in0=scb[:], scalar=1.0, in1=gfm[:, gi, :].unsqueeze(1).to_broadcast([128, NSEG, 8]),
                op0=ALU.add, op1=ALU.mult)
        P.dma("sp", d18[:], dec18_in[l], [], [d18.T], ND("d18"))
        P.dma("sp", ESK[:], sink_in[l], [], [ESK.T], ND("ESK"))
        P.dma("sp", SN[:], sgun_in[l:l + 1, :].partition_broadcast(128), [], [SN.T], ND("SN"))
        P.dma("pool", WsT[:], sguw_in[l], [], [WsT.T], ND("WsT"))
        P.dma("sp", SGB[:], sgub_in[l], [], [SGB.T], ND("SGB"))
        P.I("act", "activation", [d18.T], [e18.T], out=e18[:], in_=d18[:], func=AF.Exp, scale=-1.0)
        P.I("dve", "tensor_scalar", [e18.T], [e18.T], out=e18[:], in0=e18[:], scalar1=1.0, scalar2=None, op0=ALU.add)
        P.I("act", "activation", [e18.T], [lg18.T], out=lg18[:], in_=e18[:], func=AF.Ln)
        P.I("dve", "tensor_scalar", [lg18.T], [lg18.T], out=lg18[:], in0=lg18[:], scalar1=-1.0, scalar2=None, op0=ALU.mult)
        P.I("act", "activation", [ESK.T], [ESK.T], out=ESK[:], in_=ESK[:], func=AF.Exp)
        for h in range(6):
            P.I("act", "activation", [cm.T, lg18.T], [ta.T], out=ta[:], in_=cm[:, 0, :], func=AF.Exp, scale=lg18[:, h:h + 1])
            P.I("act", "activation", [cm.T, lg18.T], [tb.T], out=tb[:], in_=cm[:, 1, :], func=AF.Exp, scale=lg18[:, 6 + h:7 + h])
            P.I("dve", "scalar_tensor_tensor", [ta.T, cm.T], [ta.T], out=ta[:], in0=ta[:], scalar=0.125, in1=cm[:, 2, :], op0=ALU.mult, op1=ALU.mult)
            P.I("dve", "scalar_tensor_tensor", [tb.T, cm.T], [tb.T], out=tb[:], in0=tb[:], scalar=0.125, in1=cm[:, 3, :], op0=ALU.mult, op1=ALU.mult)
            P.I("dve", "tensor_tensor", [ta.T, tb.T], [DT.T], out=DT[:, (h % 2) * 3 + h // 2, :], in0=ta[:], in1=tb[:], op=ALU.add)
        for q in range(3):
            P.I("act", "activation", [cm.T, lg18.T], [QDF.T], out=QDF[:, q, :], in_=cm[:, 4, :], func=AF.Exp, scale=lg18[:, 12 + q:13 + q])
            P.I("act", "activation", [cm.T, lg18.T], [QDB.T], out=QDB[:, q, :], in_=cm[:, 5, :], func=AF.Exp, scale=lg18[:, 15 + q:16 + q])
        P.I("act", "activation", [lg18.T, jv.T], [KDF.T], out=KDF[:], in_=lg18[:, 0:6], func=AF.Exp, scale=jv[:, 0:1])
        P.I("act", "activation", [lg18.T, jv.T], [KDB.T], out=KDB[:], in_=lg18[:, 6:12], func=AF.Exp, scale=jv[:, 1:2])
        P.I("dve", "tensor_scalar", [KDF.T], [KDF.T], out=KDF[:], in0=KDF[:], scalar1=0.125, scalar2=None, op0=ALU.mult)
        P.I("dve", "tensor_scalar", [KDB.T], [KDB.T], out=KDB[:], in0=KDB[:], scalar1=0.125, scalar2=None, op0=ALU.mult)
        P.I("act", "activation", [lg18.T], [CDF.T], out=CDF[:], in_=lg18[:, 12:15], func=AF.Exp, scale=128.0)
        P.I("act", "activation", [lg18.T], [CDB.T], out=CDB[:], in_=lg18[:, 15:18], func=AF.Exp, scale=128.0)

    def load_A_weights(l):
        for k in range(8):
            P.dma("pool", win_v[:, k, :], w_in[l, k * 128:(k + 1) * 128, :], [], [Twin[k]] + (B_tiles if k == 0 else []), dWin[k])
        for k in range(8):
            P.dma("pool", wout_v[:, k, :], w_out[l, k * 128:(k + 1) * 128, :], [], [Twout[k]], dWout[k])

    def load_B_weights(l):
        for k in range(8):
            P.dma("pool", wup_v[:, k, :], w_up[l, k * 128:(k + 1) * 128, :], [], [Twup[k]] + (A_tiles if k == 0 else []), dWup[k])
        for k in range(NFC):
            P.dma("pool", wdn_v[:, k, :], w_down[l, k * 128:(k + 1) * 128, :], [], [Twdn[k]], dWdn[k])

    def load_x(src, Tsrc, n, ring, dlist):
        xc = ring.next()
        P.dma("sp", xc[:], src[n * 128:(n + 1) * 128, :], [Tsrc[n]] if Tsrc is not None else [], [xc.T], dlist[ring.i % ring.n])
        return xc

    def norm_front(xc):
        return norm_front_2(norm_front_1(xc))

    def norm_front_1(xc):
        s = CUR["st"].next()
        xb_ = (CUR.get("xn") or xn).next()
        P.I("act", "activation", [xc.T], [xb_.T, s.T], out=xb_[:], in_=xc[:], func=AF.Square, accum_out=s[:, 0:1])
        rstd_from((s, 0), 1.0 / D, (s, 1))
        return (xc, s, xb_)

    def norm_front_2(st3):
        xc, s, xb_ = st3
        P.I("dve", "tensor_scalar", [xc.T, s.T], [xb_.T], out=xb_[:], in0=xc[:], scalar1=s[:, 1:2], scalar2=None, op0=ALU.mult)
        return xb_

    def norm_T(xc, seg, Af, Bf, dst_fn, dstT):
        norm_back(norm_front(xc), seg, Af, Bf, dst_fn, dstT)

    def norm_back(xb_, seg, Af, Bf, dst_fn, dstT):
        for kc in range(8):
            P.I("pe", "transpose", [xb_.T, ident.T], [Tbank[0]], out=bankb(0)[:, kc * 128:(kc + 1) * 128], in_=xb_[:, kc * 128:(kc + 1) * 128], identity=ident[:])
        for kc in range(8):
            P.I("act", "activation", [Tbank[0], Af.T, Bf.T], [dstT], out=dst_fn(kc), in_=bankb(0)[:, kc * 128:(kc + 1) * 128], func=AF.Identity,
                                                              scale=Af[:, seg, kc:kc + 1], bias=Bf[:, seg, kc:kc + 1])

    def proj(hT, wv, Tw, c0, ncol, gen):
        pb, Tpb = gen.next()
        for kc in range(8):
            P.I("pe", "matmul", [hT.T, Tw[kc]], [Tpb], pb[:, 0:ncol], lhsT=hT[:, kc, :], rhs=wv[:, kc, c0:c0 + ncol], start=(kc == 0), stop=(kc == 7))
        return pb, Tpb

    def load_rope(n):
        rp = ropeR.next()
        P.dma("sp", rp[:], rope_in[n], [], [rp.T], drope[ropeR.i % 3])
        return rp

    def rope64(pb, Tpb, rp, out_f32):
        x3 = pb[:, 0:384].rearrange("p (a d) -> p a d", d=32)
        x4 = pb[:, 0:384].rearrange("p (h t d) -> p h t d", h=6, t=2)
        P.I("dve", "tensor_tensor", [Tpb, rp.T], [r1.T], out=r1[:].rearrange("p (a d) -> p a d", d=32), in0=x3,
                                              in1=rp[:, 0:32].unsqueeze(1).to_broadcast([128, 12, 32]), op=ALU.mult)
        r2v = r2[:].rearrange("p (h t d) -> p h t d", h=6, t=2)
        P.I("dve", "tensor_tensor", [Tpb, rp.T], [r2.T], out=r2v[:, :, 0, :], in0=x4[:, :, 1, :],
                                              in1=rp[:, 64:96].unsqueeze(1).to_broadcast([128, 6, 32]), op=ALU.mult)
        P.I("dve", "tensor_tensor", [Tpb, rp.T], [r2.T], out=r2v[:, :, 1, :], in0=x4[:, :, 0, :],
                                              in1=rp[:, 32:64].unsqueeze(1).to_broadcast([128, 6, 32]), op=ALU.mult)
        P.I("dve", "tensor_tensor", [r1.T, r2.T], [out_f32.T], out=out_f32[:], in0=r1[:], in1=r2[:], op=ALU.add)

    def kv_update(kd, R, CD, n, store_fn, store_T, gen, boundary):
        pb, Tpb = gen.next()
        for q in range(3):
            P.I("pe", "matmul", [kd.T, vt.T], [Tpb], pb[:, q * 128:(q + 1) * 128], lhsT=kd[:, q * 128:(q + 1) * 128], rhs=vt[:, q * 128:(q + 1) * 128],
                                                       start=True, stop=True)
        P.I("dve", "tensor_tensor", [R.T, CD.T], [Rt.T], out=Rt[:], in0=R[:], in1=CD[:].unsqueeze(2).to_broadcast([128, 3, 64]), op=ALU.mult)
        kv3 = pb[:, 0:384].rearrange("p (q e) -> p q e", q=3)
        P.I("dve", "tensor_tensor", [Rt.T, Tpb], [R.T], out=R[0:64], in0=Rt[0:64], in1=kv3[0:64, :, 0:64], op=ALU.add)
        P.I("dve", "tensor_tensor", [Rt.T, Tpb], [R.T], out=R[64:128], in0=Rt[64:128], in1=kv3[64:128, :, 64:128], op=ALU.add)
        if boundary:
            P.I("dve", "tensor_scalar", [R.T, flag.T], [R.T], out=R[:], in0=R[:], scalar1=flag[:, 0:1], scalar2=None, op0=ALU.mult)
        if store_fn is not None:
            P.I("act", "activation", [R.T], [store_T], out=store_fn, in_=R[:], func=AF.Copy)

    def post_stage(lhs_fn, K, wv, Tw, xc, Gb, n, dst, Tdst, gen, add_eng="pool"):
        post_stage_2(post_stage_1(lhs_fn, K, wv, Tw, xc, Gb, n, dst, Tdst, gen), add_eng)

    def post_stage_1(lhs_fn, K, wv, Tw, xc, Gb, n, dst, Tdst, gen):
        pbs = []
        for half in range(2):
            pb, Tpb = gen.next()
            for kc in range(K):
                lh, Tl = lhs_fn(kc)
                P.I("pe", "matmul", [Tl, Tw[kc]], [Tpb], pb[:, :], lhsT=lh, rhs=wv[:, kc, half * 512:(half + 1) * 512],
                                                                                      start=(kc == 0), stop=(kc == K - 1))
            pbs.append((pb, Tpb))
        s = CUR["st"].next()
        xo = xnew.next()
        for half in range(2):
            pb, Tpb = pbs[half]
            P.I("act", "activation", [Tpb], [xo.T, s.T], out=xo[:, 0:512], in_=pb[:, :], func=AF.Square, accum_out=s[:, half:half + 1])
        P.I("dve", "tensor_tensor", [s.T], [s.T], out=s[:, 2:3], in0=s[:, 0:1], in1=s[:, 1:2], op=ALU.add)
        rstd_from((s, 2), 1.0 / D, (s, 3))
        return (pbs, s, xo, xc, Gb, n, dst, Tdst, xnew.i % 2)

    def post_stage_2(state, add_eng="pool"):
        pbs, s, xo, xc, Gb, n, dst, Tdst, slot = state
        for half in range(2):
            pb, Tpb = pbs[half]
            P.I("dve", "scalar_tensor_tensor", [Tpb, s.T, Gb.T], [xo.T], out=xo[:, half * 512:(half + 1) * 512], in0=pb[:, :], scalar=s[:, 3:4],
                                                                                   in1=Gb[:, half * 512:(half + 1) * 512], op0=ALU.mult, op1=ALU.mult)
        P.I(add_eng, "tensor_tensor", [xc.T, xo.T], [xo.T], out=xo[:], in0=xo[:], in1=xc[:], op=ALU.add)
        P.dma("sp", dst[n * 128:(n + 1) * 128, :], xo[:], [xo.T], [Tdst[n]], dst_[slot])

    def load_G(l, gi, seg):
        Gb = Gt.next()
        P.dma("sp", Gb[:], gD[l, gi, seg:seg + 1, :].partition_broadcast(128), TgD[l][gi * 4:gi * 4 + 4], [Gb.T], dG[0])
        return Gb

    def phase_A0(l, src, Tsrc):
        P.I("pool", "memset", [], [Rb.T], Rb[:], 0.0)
        P.I("pool", "memset", [], [Tsb[NCH - 1]], sbst_v[:, NCH - 1], 0.0)
        hTs, rps = {}, {}

        xbs = {}

        def front_a(c):
            xc = load_x(src, Tsrc, c, xcr, dx)
            rps[c] = load_rope(c)
            xbs[c] = norm_front(xc)

        def front_b(c):
            hT = hTr.next()
            hTs[c] = hT
            norm_back(xbs.pop(c), c // SEGC, A1f, fm[0], lambda kc: hT[:, kc, :], hT.T)

        front_a(NCH - 1)
        front_b(NCH - 1)
        for n in range(NCH - 1, 0, -1):
            if n - 1 >= 1:
                use("N")
                front_a(n - 1)
                front_b(n - 1)
            use("K")
            hT = hTs.pop(n)
            rp = rps.pop(n)
            pk, Tpk = proj(hT, win_v, Twin, 896, 384, CUR["gen"])
            pv, Tpv = proj(hT, win_v, Twin, 1280, 384, CUR["gen"])
            rope64(pk, Tpk, rp, rsum)
            P.I("dve", "tensor_tensor", [rsum.T, KDB.T], [kdb.T], out=kdb[:].rearrange("p (h d) -> p h d", h=6), in0=rsum[:].rearrange("p (h d) -> p h d", h=6),
                in1=KDB[:].unsqueeze(2).to_broadcast([128, 6, 64]), op=ALU.mult)
            P.I("act", "activation", [Tpv], [vt.T], out=vt[:], in_=pv[:, 0:384], func=AF.Copy)
            kv_update(kdb, Rb, CDB, n, sbst_v[:, n - 1], Tsb[n - 1], CUR["gen"], boundary=(n % SEGC == 0))
            use(None)
            P.flush()

    def attention(l, m):
        mx = mix.at(m)
        pO, TpO = bank(7), Tbank[7]
        O3 = pO[:, 0:390].rearrange("p (h e) -> p h e", h=6)
        blks = [b for b in (-1, 0, 1) if 0 <= m + b < NCH]
        for kvh in range(2):
            Es = []
            for bi, b in enumerate(blks):
                pS, TpS = CUR["gen"].next()
                kT = ckT.at(m + b); qT = cqT.at(m)
                P.I("pe", "matmul", [kT.T, qT.T], [TpS], pS[:, 0:384], lhsT=kT[kvh * 64:(kvh + 1) * 64, :],
                    rhs=qT[kvh * 64:(kvh + 1) * 64, :, :], start=True, stop=True)
                E = Eb.next()
                P.I("act", "activation", [TpS], [E.T], out=E[:].rearrange("p g i -> p (g i)"), in_=pS[:, 0:384], func=AF.Exp, scale=0.125)
                if b != 0:
                    bnd = (b == -1 and m % SEGC == 0) or (b == 1 and m % SEGC == SEGC - 1)
                    mk = trif if bnd else tri
                    mi = 0 if b == -1 else 1
                    P.I("dve", "tensor_tensor", [E.T, mk.T], [E.T], out=E[:], in0=E[:], in1=mk[:, mi, :].unsqueeze(1).to_broadcast([128, 3, 128]), op=ALU.mult)
                Es.append(E)
            for g in range(3):
                for bi, b in enumerate(blks):
                    cv = cvb.at(m + b)
                    P.I("pe", "matmul", [Es[bi].T, cv.T], [TpO], O3[:, kvh * 3 + g, :], lhsT=Es[bi][:, g, :], rhs=cv[:, kvh, :],
                        start=(bi == 0), stop=(bi == len(blks) - 1))
        s = CUR["st"].next()
        P.I("dve", "tensor_tensor", [TpO, ESK.T], [s.T], out=s[:, 0:6], in0=O3[:, :, 64], in1=ESK[:], op=ALU.add)
        P.I("dve", "reciprocal", [s.T], [s.T], out=s[:, 0:6], in_=s[:, 0:6])
        P.I("dve", "tensor_tensor", [TpO, s.T], [mx.T], out=mx[:, 640:1024].rearrange("p (h d) -> p h d", h=6), in0=O3[:, :, 0:64],
                                              in1=s[:, 0:6].unsqueeze(2).to_broadcast([128, 6, 64]), op=ALU.mult)

    def out_stage(l, m, xc, Gb, dst, Tdst):
        mx = mix.at(m)
        for half in range(2):
            bk, Tbk = (bankb(1), Tb1a) if half == 0 else (bankb(6), Tbank[6])
            for k4 in range(4):
                kc = half * 4 + k4
                P.I("pe", "transpose", [mx.T, ident.T], [Tbk], out=bk[:, k4 * 128:(k4 + 1) * 128], in_=mx[:, kc * 128:(kc + 1) * 128], identity=ident[:])
        for half in range(2):
            bk, Tbk = (bankb(1), Tb1a) if half == 0 else (bankb(6), Tbank[6])
            P.I("act", "activation", [Tbk], [mixT.T], out=mixT[:, half * 4:(half + 1) * 4, :].rearrange("p k i -> p (k i)"), in_=bk[:, 0:512], func=AF.Copy)
        post_stage(lambda kc: (mixT[:, kc, :], mixT.T), 8, wout_v, Twout, xc, Gb, m, dst, Tdst, gLpost, add_eng="dve")

    def phase_A1(l, src, Tsrc, dst, Tdst):
        P.I("pool", "memset", [], [Rf.T], Rf[:], 0.0)
        P.I("pool", "memset", [], [Sfb.T], Sfb[:], 0.0)
        for b_ in cvb.b:
            P.I("pool", "memset", [], [b_.T], b_[:], 1.0)
        xcs = {}
        Gb = None
        Gseg = {}
        LAG = 2
        hTs, rps = {}, {}

        class _SR:
            def __init__(self, bufs):
                self.b = bufs; self.n = len(bufs); self.i = -1

            def next(self):
                self.i += 1
                return self.b[self.i % self.n]

        xNr = _SR(xcr.b[0:2]); xLr = _SR(xcr.b[2:3])

        def attn_proj(n, hT, rp):
            pcq, Tpcq = proj(hT, win_v, Twin, 2048, 384, CUR["gen"])
            P.I("act", "activation", [Tpcq], [cqb.T], out=cqb[:], in_=pcq[:, 0:384], func=AF.Copy)
            c3 = pcq[:, 0:384].rearrange("p (h d) -> p h d", h=6)
            P.I("dve", "tensor_tensor", [Tpcq, rp.T], [a1.T], out=a1[:].rearrange("p h (t d) -> p h t d", t=2), in0=c3[:, :, 0:16].rearrange("p h (t d) -> p h t d", t=2),
                                                  in1=rp[:, 96:104].unsqueeze(1).unsqueeze(1).to_broadcast([128, 6, 2, 8]), op=ALU.mult)
            P.I("dve", "tensor_tensor", [Tpcq, rp.T], [a2.T], out=a2[:, :, 0:8], in0=c3[:, :, 8:16], in1=rp[:, 112:120].unsqueeze(1).to_broadcast([128, 6, 8]), op=ALU.mult)
            P.I("dve", "tensor_tensor", [Tpcq, rp.T], [a2.T], out=a2[:, :, 8:16], in0=c3[:, :, 0:8], in1=rp[:, 104:112].unsqueeze(1).to_broadcast([128, 6, 8]), op=ALU.mult)
            P.I("dve", "tensor_tensor", [a1.T, a2.T], [cqb.T], out=cqb[:].rearrange("p (h d) -> p h d", h=6)[:, :, 0:16], in0=a1[:], in1=a2[:], op=ALU.add)

        def attn_proj_k(n, hT, rp):
            pck, Tpck = proj(hT, win_v, Twin, 2432, 256, CUR["gen"])
            cv = cvb.at(n)
            P.I("act", "activation", [Tpck], [ckb.T], out=ckb[:], in_=pck[:, 0:128], func=AF.Copy)
            P.I("act", "activation", [Tpck], [cv.T], out=cv[:, :, 0:64], in_=pck[:, 128:256].rearrange("p (h d) -> p h d", h=2), func=AF.Copy)
            k3 = pck[:, 0:128].rearrange("p (h d) -> p h d", h=2)
            P.I("dve", "tensor_tensor", [Tpck, rp.T], [a1.T], out=a1[:, 0:2, :].rearrange("p h (t d) -> p h t d", t=2), in0=k3[:, :, 0:16].rearrange("p h (t d) -> p h t d", t=2),
                                                  in1=rp[:, 96:104].unsqueeze(1).unsqueeze(1).to_broadcast([128, 2, 2, 8]), op=ALU.mult)
            P.I("dve", "tensor_tensor", [Tpck, rp.T], [a2.T], out=a2[:, 0:2, 0:8], in0=k3[:, :, 8:16], in1=rp[:, 112:120].unsqueeze(1).to_broadcast([128, 2, 8]), op=ALU.mult)
            P.I("dve", "tensor_tensor", [Tpck, rp.T], [a2.T], out=a2[:, 0:2, 8:16], in0=k3[:, :, 0:8], in1=rp[:, 104:112].unsqueeze(1).to_broadcast([128, 2, 8]), op=ALU.mult)
            P.I("dve", "tensor_tensor", [a1.T, a2.T], [ckb.T], out=ckb[:].rearrange("p (h d) -> p h d", h=2)[:, :, 0:16], in0=a1[:, 0:2, :], in1=a2[:, 0:2, :], op=ALU.add)
            for q in range(3):
                P.I("pe", "transpose", [cqb.T, ident.T], [Tb1b], out=bankb(1)[:, 512 + q * 128:512 + (q + 1) * 128], in_=cqb[:, q * 128:(q + 1) * 128], identity=ident[:])
            P.I("pe", "transpose", [ckb.T, ident.T], [Tb1b], out=bankb(1)[:, 896:1024], in_=ckb[:], identity=ident[:])
            cq_t = cqT.at(n); ck_t = ckT.at(n)
            P.I("act", "activation", [Tb1b], [cq_t.T], out=cq_t[:].rearrange("p a i -> p (a i)"), in_=bankb(1)[:, 512:896], func=AF.Copy)
            P.I("act", "activation", [Tb1b], [ck_t.T], out=ck_t[:], in_=bankb(1)[:, 896:1024], func=AF.Copy)


        xbs = {}

        def front_a(c):
            xc = load_x(src, Tsrc, c, xNr, dx[0:2])
            rps[c] = load_rope(c)
            xbs[c] = norm_front(xc)

        def front_b(c):
            hT = hTr.next()
            hTs[c] = hT
            norm_back(xbs.pop(c), c // SEGC, A1f, fm[0], lambda kc: hT[:, kc, :], hT.T)

        CUR["xn"] = xnA
        front_a(0)
        front_b(0)
        if NCH > 1:
            front_a(1)
        for n in range(NCH + LAG):
            if n < NCH:
                hT = hTs.pop(n)
                rp = rps.pop(n)
                mx = mix.at(n)
            use("N")
            if n + 1 < NCH:
                front_b(n + 1)
            if n + 2 < NCH:
                front_a(n + 2)
            if n < NCH:
                attn_proj(n, hT, rp)
                attn_proj_k(n, hT, rp)
            if n < NCH:
                use("SC")
                if 'sgu' in DBG:
                    pb, Tpb = proj(hT, win_v, Twin, 0, 512, CUR["gen"])
                    P.I("act", "activation", [Tpb], [u_b.T], out=u_b[:], in_=pb[:, 0:256], func=AF.Gelu_apprx_tanh)
                    P.I("act", "activation", [Tpb], [vg.T], out=vg[:], in_=pb[:, 256:512], func=AF.Gelu_apprx_tanh)
                    P.I("dve", "tensor_tensor", [vg.T], [sq.T], out=sq[:], in0=vg[:], in1=vg[:], op=ALU.mult)
                    s = CUR["st"].next()
                    P.I("dve", "tensor_reduce", [sq.T], [s.T], out=s[:, 0:4], in_=sq[:].rearrange("p (g d) -> p g d", g=4), axis=AX.X, op=ALU.add)
                    P.I("dve", "tensor_scalar", [s.T], [s.T], out=s[:, 0:4], in0=s[:, 0:4], scalar1=1.0 / 64, scalar2=EPS, op0=ALU.mult, op1=ALU.add)
                    P.I("pool", "tensor_tensor", [s.T, cmh.T], [s.T], out=s[:, 4:8], in0=s[:, 0:4], in1=cmh[:, 0:4], op=ALU.pow)
                    P.I("dve", "tensor_tensor", [vg.T, s.T], [vn.T], out=vn[:].rearrange("p (g d) -> p g d", g=4), in0=vg[:].rearrange("p (g d) -> p g d", g=4),
                                                          in1=s[:, 4:8].unsqueeze(2).to_broadcast([128, 4, 64]), op=ALU.mult)
                    P.I("pool", "tensor_tensor", [vn.T, SN.T], [vn2.T], out=vn2[:], in0=vn[:], in1=SN[:], op=ALU.mult)
                    pg, Tpg = CUR["gen"].next()
                    for g in range(4):
                        P.I("pe", "matmul", [WsT.T, vn2.T], [Tpg], pg[:, g * 64:(g + 1) * 64], lhsT=WsT[:, g, :], rhs=vn2[:, g * 64:(g + 1) * 64], start=True, stop=True)
                    P.I("dve", "tensor_tensor", [Tpg, SGB.T], [gt_.T], out=gt_[:].rearrange("p (g d) -> p g d", g=4), in0=pg[:, 0:256].rearrange("p (g d) -> p g d", g=4),
                                                          in1=SGB[:].unsqueeze(2).to_broadcast([128, 4, 64]), op=ALU.add)
                    P.I("pool", "tensor_tensor", [gt_.T, u_b.T], [mx.T], out=mx[:, 0:256], in0=gt_[:], in1=u_b[:], op=ALU.mult)
                use("R")
                if 'ret' in DBG:
                    pq, Tpq = proj(hT, win_v, Twin, 512, 384, CUR["gen"])
                    rope64(pq, Tpq, rp, rsum)
                    P.I("act", "activation", [rsum.T], [qr.T], out=qr[:], in_=rsum[:], func=AF.Copy)
                    pk, Tpk = proj(hT, win_v, Twin, 896, 384, CUR["gen"])
                    rope64(pk, Tpk, rp, rsum)
                    P.I("act", "activation", [rsum.T], [kr.T], out=kr[:], in_=rsum[:], func=AF.Copy)
                    P.I("dve", "tensor_tensor", [rsum.T, KDF.T], [kdf.T], out=kdf[:].rearrange("p (h d) -> p h d", h=6), in0=rsum[:].rearrange("p (h d) -> p h d", h=6),
                                                          in1=KDF[:].unsqueeze(2).to_broadcast([128, 6, 64]), op=ALU.mult)
                    pv, Tpv = proj(hT, win_v, Twin, 1280, 384, CUR["gen"])
                    P.I("act", "activation", [Tpv], [vt.T], out=vt[:], in_=pv[:, 0:384], func=AF.Copy)
                    pgg, Tpgg = proj(hT, win_v, Twin, 1664, 384, CUR["gen"])
                    P.I("act", "activation", [Tpgg], [sg.T], out=sg[:], in_=pgg[:, 0:384], func=AF.Silu)
                    if 'r1' in DBG2:
                        for i, srcb in enumerate([qr, kr]):
                            for q in range(3):
                                P.I("pe", "transpose", [srcb.T, ident.T], [Tbank[4]], out=bankb(4)[:, (i * 3 + q) * 128:(i * 3 + q + 1) * 128],
                                                                                               in_=srcb[:, q * 128:(q + 1) * 128], identity=ident[:])
                        P.I("act", "activation", [Tbank[4]], [qkT.T], out=qkT[:].rearrange("p a i -> p (a i)"), in_=bankb(4)[:, 0:768], func=AF.Copy)
                        P.I("dve", "tensor_tensor", [qkT.T, QDF.T], [qdf.T], out=qdf[:], in0=qkT[:, 0:3, :], in1=QDF[:], op=ALU.mult)
                        P.I("dve", "tensor_tensor", [qkT.T, QDB.T], [qdb.T], out=qdb[:], in0=qkT[:, 0:3, :], in1=QDB[:], op=ALU.mult)
                    if 'r2' in DBG2:
                        for par in range(2):
                            pS, TpS = CUR["gen"].next()
                            for q_ in range(3):
                                P.I("pe", "matmul", [qkT.T], [TpS], pS[:, q_ * 128:(q_ + 1) * 128], lhsT=qkT[par * 64:(par + 1) * 64, 3 + q_, :],
                                    rhs=qkT[par * 64:(par + 1) * 64, q_, :], start=True, stop=True)
                            P.I("dve", "tensor_tensor", [TpS, DT.T], [PT.T], out=PT[:, par * 3:(par + 1) * 3, :], in0=pS[:, 0:384].rearrange("p (h i) -> p h i", h=3),
                                in1=DT[:, par * 3:(par + 1) * 3, :], op=ALU.mult)
                    if 'r3' in DBG2:
                        pY, TpY = CUR["gen"].next()
                        for h in range(6):
                            q_, par = h // 2, h % 2
                            sl = slice(par * 64, (par + 1) * 64)
                            P.I("pe", "matmul", [PT.T, vt.T], [TpY], pY[:, h * 64:(h + 1) * 64], lhsT=PT[:, par * 3 + q_, :], rhs=vt[:, h * 64:(h + 1) * 64], start=True, stop=False)
                            P.I("pe", "matmul", [qdf.T, Sfb.T], [TpY], pY[:, h * 64:(h + 1) * 64], lhsT=qdf[sl, q_, :], rhs=Sfb[sl, q_, :], start=False, stop=False)
                            P.I("pe", "matmul", [qdb.T, Tsb[n]], [TpY], pY[:, h * 64:(h + 1) * 64], lhsT=qdb[sl, q_, :], rhs=sbst_v[sl, n, q_, :], start=False, stop=True)
                    if 'r4' in DBG2:
                        kv_update(kdf, Rf, CDF, n, Sfb[:], Sfb.T, CUR["gen"], boundary=((n + 1) % SEGC == 0))
                    if 'r5' in DBG2:
                        s2 = CUR["st"].next()
                        Y3 = pY[:, 0:384].rearrange("p (h d) -> p h d", h=6)
                        P.I("dve", "tensor_reduce", [TpY], [s2.T], out=s2[:, 0:6], in_=Y3, axis=AX.X, op=ALU.add)
                        P.I("act", "activation", [TpY], [ysq.T], out=ysq[:], in_=pY[:, 0:384], func=AF.Square)
                        s3 = CUR["st"].next()
                        P.I("dve", "tensor_reduce", [ysq.T], [s3.T], out=s3[:, 0:6], in_=ysq[:].rearrange("p (h d) -> p h d", h=6), axis=AX.X, op=ALU.add)
                        P.I("dve", "tensor_scalar", [s2.T], [s2.T], out=s2[:, 0:6], in0=s2[:, 0:6], scalar1=1.0 / 64, scalar2=None, op0=ALU.mult)
                        s4 = CUR["st"].next()
                        P.I("dve", "tensor_tensor", [s2.T], [s4.T], out=s4[:, 0:6], in0=s2[:, 0:6], in1=s2[:, 0:6], op=ALU.mult)
                        P.I("dve", "scalar_tensor_tensor", [s3.T, s4.T], [s3.T], out=s3[:, 0:6], in0=s3[:, 0:6], scalar=1.0 / 64, in1=s4[:, 0:6], op0=ALU.mult, op1=ALU.subtract)
                        P.I("dve", "tensor_scalar", [s3.T], [s3.T], out=s3[:, 0:6], in0=s3[:, 0:6], scalar1=EPS, scalar2=None, op0=ALU.add)
                        P.I("pool", "tensor_tensor", [s3.T, cmh.T], [s4.T], out=s4[:, 0:6], in0=s3[:, 0:6], in1=cmh[:, 0:6], op=ALU.pow)
                        P.I("dve", "tensor_tensor", [TpY, s2.T], [yc_.T], out=yc_[:].rearrange("p (h d) -> p h d", h=6), in0=Y3, in1=s2[:, 0:6].unsqueeze(2).to_broadcast([128, 6, 64]),
                                                              op=ALU.subtract)
                        P.I("dve", "tensor_tensor", [yc_.T, s4.T], [yn_.T], out=yn_[:].rearrange("p (h d) -> p h d", h=6), in0=yc_[:].rearrange("p (h d) -> p h d", h=6),
                                                              in1=s4[:, 0:6].unsqueeze(2).to_broadcast([128, 6, 64]), op=ALU.mult)
                        P.I("pool", "tensor_tensor", [yn_.T, sg.T], [mx.T], out=mx[:, 256:640], in0=yn_[:], in1=sg[:], op=ALU.mult)
            m = n - LAG
            if m >= 0:
                use("L")
                if 'attn' in DBG:
                    attention(l, m)
                if 'out' in DBG:
                    xr = load_x(src, Tsrc, m, xLr, dx[2:3])
                    out_stage(l, m, xr, Gseg[m // SEGC], dst, Tdst)
            use(None)
            P.flush()
            mf = n - (LAG - 1)
            if 0 <= mf < NCH and mf % SEGC == 0:
                Gseg[mf // SEGC] = load_G(l, 0, mf // SEGC)
        CUR["xn"] = None

    class SubRing:
        def __init__(self, bufs):
            self.b = bufs; self.n = len(bufs); self.i = -1

        def next(self):
            self.i += 1
            return self.b[self.i % self.n]

    def phase_B(l, src, Tsrc, dst, Tdst):
        P.dma("sp", CW[:], cw_in[l], [], [CW.T], ND("CW"))
        xN = SubRing(xcr.b[0:2]); xD = SubRing(xcr.b[2:3])
        dxN = dx[0:2]; dxD = dx[2:3]
        gB = Gen([1, 2, 3, 4, 5])
        gDn = Gen([6, 7])
        hbs = {}
        normed = [-1]
        Gseg = {}

        def do_norm(c):
            hb_alloc(c)
            nback(c, nfront(c))

        def hb_alloc(c):
            b, half = c // 2, c % 2
            if half == 0:
                hb = HB.next()
                hbs[b] = hb
                if b == 0:
                    P.I("pool", "memset", [], [hb.T], hb[:, :, 0:1], 0.0)
                if b == NB - 1:
                    P.I("pool", "memset", [], [hb.T], hb[:, :, 257:258], 0.0)

        def nfront(c):
            xc = load_x(src, Tsrc, c, xN, dxN)
            return norm_front(xc)

        def nfront_pre(c):
            hb_alloc(c)
            return nfront(c)

        def nf1(c):
            hb_alloc(c)
            xc = load_x(src, Tsrc, c, xN, dxN)
            return norm_front_1(xc)

        def down1(b, half):
            aT = actT.at(b)
            c = 2 * b + half
            seg = c // SEGC
            if seg not in Gseg:
                Gseg[seg] = load_G(l, 1, seg)
            xr = load_x(src, Tsrc, c, xD, dxD)
            return post_stage_1(lambda kc: (aT[:, kc, half * 128:(half + 1) * 128], aT.T), NFC, wdn_v, Twdn, xr, Gseg[seg], c, dst, Tdst, gDn)

        def nback(c, xb_):
            b, half = c // 2, c % 2
            seg = c // SEGC
            hb = hbs[b]
            o = 1 + half * 128
            norm_back(xb_, seg, A2f, fm[2], lambda kc: hb[:, kc, o:o + 128], hb.T)
            if half == 0 and b > 0:
                pv_ = hbs[b - 1]
                if c % SEGC == 0:
                    P.I("dve", "tensor_scalar", [hb.T, flag.T], [pv_.T], out=pv_[:, :, 257:258], in0=hb[:, :, 1:2], scalar1=flag[:, 0:1], scalar2=None, op0=ALU.mult)
                else:
                    P.I("pool", "tensor_copy", [hb.T], [pv_.T], out=pv_[:, :, 257:258], in_=hb[:, :, 1:2])
            normed[0] = c

        def halo_fwd(b):
            hb = hbs[b]; nx = hbs[b + 1]
            c = 2 * b + 1
            if (c + 1) % SEGC == 0:
                P.I("dve", "tensor_scalar", [hb.T, flag.T], [nx.T], out=nx[:, :, 0:1], in0=hb[:, :, 256:257], scalar1=flag[:, 0:1], scalar2=None, op0=ALU.mult)
            else:
                P.I("pool", "tensor_copy", [hb.T], [nx.T], out=nx[:, :, 0:1], in_=hb[:, :, 256:257])

        def up_pairs(b, f0, f1):
            hb = hbs[b]
            aT = actT.at(b)
            for fc in range(f0, f1):
                tl = []
                for which in range(2):
                    fcc = fc + which * NFC
                    pb, Tpb = gB.next()
                    for kc in range(8):
                        P.I("pe", "matmul", [Twup[kc], hb.T], [Tpb], pb[:, 0:258], lhsT=wup_v[:, kc, fcc * 128:(fcc + 1) * 128], rhs=hb[:, kc, :],
                            start=(kc == 0), stop=(kc == 7))
                    tl.append((fcc, pb, Tpb, T0.next(), T1.next(), T2.next()))
                for (fcc, pb, Tpb, t0, t1, t2) in tl:
                    P.I("act", "activation", [Tpb, CW.T], [t0.T], out=t0[:], in_=pb[:, 0:256], func=AF.Identity, scale=CW[:, fcc, 0:1], bias=CW[:, fcc, 3:4])
                for (fcc, pb, Tpb, t0, t1, t2) in tl:
                    P.I("dve", "scalar_tensor_tensor", [Tpb, CW.T, t0.T], [t1.T], out=t1[:], in0=pb[:, 1:257], scalar=CW[:, fcc, 1:2], in1=t0[:],
                        op0=ALU.mult, op1=ALU.add)
                for (fcc, pb, Tpb, t0, t1, t2) in tl:
                    P.I("dve", "scalar_tensor_tensor", [Tpb, CW.T, t1.T], [t2.T], out=t2[:], in0=pb[:, 2:258], scalar=CW[:, fcc, 2:3], in1=t1[:],
                        op0=ALU.mult, op1=ALU.add)
                g_ = gg.next()
                tg, tv = tl[0][5], tl[1][5]
                P.I("act", "activation", [tg.T], [g_.T], out=g_[:], in_=tg[:], func=AF.Gelu_apprx_tanh)
                P.I("pool", "tensor_tensor", [tv.T, g_.T], [aT.T], out=aT[:, fc, :], in0=tv[:], in1=g_[:], op=ALU.mult)

        def down(b):
            aT = actT.at(b)
            for half in range(2):
                c = 2 * b + half
                seg = c // SEGC
                if seg not in Gseg:
                    Gseg[seg] = load_G(l, 1, seg)
                xr = load_x(src, Tsrc, c, xD, dxD)
                post_stage(lambda kc: (aT[:, kc, half * 128:(half + 1) * 128], aT.T), NFC, wdn_v, Twdn, xr, Gseg[seg], c, dst, Tdst, gDn)

        do_norm(0); do_norm(1)
        if NCH > 2:
            do_norm(2)
            halo_fwd(0)
        for b in range(NB):
            c1, c2 = 2 * b + 3, 2 * b + 4
            n1 = nf1(c1) if c1 < NCH else None
            up_pairs(b, 0, 2)
            f1 = norm_front_2(n1) if n1 is not None else None
            up_pairs(b, 2, 4)
            d0 = down1(b - 1, 0) if b > 0 else None
            up_pairs(b, 4, 6)
            if d0 is not None:
                post_stage_2(d0)
            up_pairs(b, 6, 8)
            if f1 is not None:
                nback(c1, f1)
            d1 = down1(b - 1, 1) if b > 0 else None
            n2 = nf1(c2) if c2 < NCH else None
            up_pairs(b, 8, 10)
            if d1 is not None:
                post_stage_2(d1)
            f2 = norm_front_2(n2) if n2 is not None else None
            up_pairs(b, 10, 16)
            if f2 is not None:
                nback(c2, f2)
                halo_fwd(b + 1)
            up_pairs(b, 16, NFC)
        down(NB - 1)

    cur, Tcur = x_in, None
    for l in range(depth):
        P.barrier()
        load_A_weights(l)
        layer_setup(l)
        P.barrier()
        if stop_after == ("setup", l):
            break
        phase_A0(l, cur, Tcur)
        if stop_after == ("A0", l):
            break
        phase_A1(l, cur, Tcur, xa, Txa)
        if stop_after == ("A1", l):
            cur, Tcur = xa, Txa
            break
        P.barrier()
        load_B_weights(l)
        last = (l == depth - 1)
        dst, Tdst = (y_out, Ty) if last else (xb, Txb)
        phase_B(l, xa, Txa, dst, Tdst)
        cur, Tcur = dst, Tdst
    if cur is not y_out:
        for n in range(NCH):
            xc = load_x(cur, Tcur, n, xcr, dx)
            P.dma("sp", y_out[n * 128:(n + 1) * 128, :], xc[:], [xc.T], [Ty[n]], dcp[xcr.i % xcr.n])
    P.wait_all("sp", Ty)
    counts = P.emit()
    es.close()
    return nc, counts


def _rope_tables(pos, rot_dim, theta):
    half = rot_dim // 2
    freqs = np.exp(-math.log(theta) * np.arange(half, dtype=np.float32) * np.float32(2.0) / np.float32(rot_dim)).astype(np.float32)
    ang = pos.astype(np.float32)[:, None] * freqs[None, :]
    return np.cos(ang).astype(np.float32), np.sin(ang).astype(np.float32)


def core_tables(NCH, SEGC, is_prompt):
    n = np.arange(NCH)
    if is_prompt:
        base = n * 128
    else:
        base = (n % SEGC) * 128
    pos = (base[None, :] + np.arange(128)[:, None]).reshape(-1)
    rc_, rs_ = _rope_tables(pos, 64, 10000.0)
    ac_, as__ = _rope_tables(pos, 16, 500000.0)
    f = lambda a, d: a.reshape(128, NCH, d)
    rope = np.concatenate([f(rc_, 32), f(rs_, 32), f(-rs_, 32), f(ac_, 8), f(as__, 8), f(-as__, 8)], 2)
    return dict(rope=np.ascontiguousarray(rope.transpose(1, 0, 2)).astype(np.float32),
                flag=np.full((128, 1), 1.0 if is_prompt else 0.0, np.float32))


def const_inputs():
    j = np.arange(128, dtype=np.float32)[:, None]
    i = np.arange(128, dtype=np.float32)[None, :]
    dpos = np.maximum(i - j, 0); dneg = np.maximum(j - i, 0)
    mge = (i >= j).astype(np.float32); mlt = (i < j).astype(np.float32)
    io1 = np.broadcast_to(i + 1, (128, 128)); io2 = np.broadcast_to(128 - i, (128, 128))
    cm = np.stack([dpos, dneg, mge, mlt, io1, io2], 1).astype(np.float32)
    tri = np.stack([(j >= i).astype(np.float32) * np.ones((128, 128), np.float32), (j <= i).astype(np.float32) * np.ones((128, 128), np.float32)], 1)
    jv = np.concatenate([127 - j, j], 1).astype(np.float32)
    return dict(ident=np.eye(128, dtype=np.float32), cmats=np.ascontiguousarray(cm), tri=np.ascontiguousarray(tri), jv=jv)


def weight_inputs(w_ada, b_ada, norm_pre_mix, norm_post_mix, norm_pre_ffn, norm_post_ffn, w_in, sgu_norm, sgu_w, sgu_b,
                  ret_decay_fwd, ret_decay_bwd, attn_sink, w_out, w_up, conv_w, conv_b, w_down):
    L = w_in.shape[0]
    perm = np.arange(IN_DIM)
    hq = [0, 3, 1, 4, 2, 5]
    perm[2048:2432] = np.concatenate([2048 + h * 64 + np.arange(64) for h in hq])
    w_in_p = np.ascontiguousarray(w_in[:, :, perm])
    cw = np.concatenate([conv_w, conv_b[:, None, :]], 1)
    cw = np.ascontiguousarray(cw.reshape(L, 4, 2 * NFC, 128).transpose(0, 3, 2, 1))
    gfm = np.stack([norm_pre_mix, norm_pre_ffn], 1).reshape(L, 2, 8, 128).transpose(0, 3, 1, 2)
    gpost = np.concatenate([norm_post_mix, norm_post_ffn], 1)
    sgu_wT = np.ascontiguousarray(sgu_w.transpose(0, 3, 1, 2))
    sgu_bT = np.ascontiguousarray(sgu_b.transpose(0, 2, 1))
    dec6 = np.concatenate([ret_decay_fwd, ret_decay_bwd], 1)
    dec6 = np.broadcast_to(dec6[:, None, :], (L, 128, 12))
    decP = np.zeros((L, 128, 6), np.float32)
    decP[:, 0:64, 0:3] = ret_decay_fwd[:, None, 0::2]; decP[:, 64:128, 0:3] = ret_decay_fwd[:, None, 1::2]
    decP[:, 0:64, 3:6] = ret_decay_bwd[:, None, 0::2]; decP[:, 64:128, 3:6] = ret_decay_bwd[:, None, 1::2]
    sink6 = np.ascontiguousarray(np.broadcast_to(attn_sink[:, None, :], (L, 128, 6)))
    c = np.ascontiguousarray
    return dict(w_ada=c(w_ada), b_ada=c(b_ada), w_in=w_in_p, w_out=c(w_out), w_up=c(w_up), w_down=c(w_down), cw=cw,
                gfm=c(gfm.astype(np.float32)), gpost=c(gpost.astype(np.float32)), sgu_wT=sgu_wT, sgu_bT=sgu_bT, sgu_n=c(sgu_norm),
                dec18=c(np.concatenate([dec6, decP], 2).astype(np.float32)), sink6=sink6.astype(np.float32))


def core_c(c_rows):
    ns = c_rows.shape[0]
    return np.ascontiguousarray(c_rows.reshape(ns, 8, 128).transpose(2, 1, 0)).astype(np.float32)


_CACHE = {}


def kernel(x_prompt, x_sample, c_prompt, c_sample, w_ada, b_ada, norm_pre_mix, norm_post_mix, norm_pre_ffn, norm_post_ffn,
           w_in, sgu_norm, sgu_w, sgu_b, ret_decay_fwd, ret_decay_bwd, attn_sink, w_out, w_up, conv_w, conv_b, w_down):
    NCH, SEGC = 64, 16
    f = lambda a: np.asarray(a, dtype=np.float32)
    x_prompt, x_sample, c_prompt, c_sample = f(x_prompt), f(x_sample), f(c_prompt), f(c_sample)
    wts = weight_inputs(*[f(a) for a in (w_ada, b_ada, norm_pre_mix, norm_post_mix, norm_pre_ffn, norm_post_ffn, w_in, sgu_norm, sgu_w, sgu_b,
                                           ret_decay_fwd, ret_decay_bwd, attn_sink, w_out, w_up, conv_w, conv_b, w_down)])
    consts = const_inputs()
    if "nc" not in _CACHE:
        _CACHE["nc"] = build(NCH, SEGC)[0]
    nc = _CACHE["nc"]
    in_maps = []
    for core in range(8):
        if core < 2:
            xs = x_prompt[core]
            cr = np.repeat(c_prompt[core:core + 1], 4, 0)
            tb = core_tables(NCH, SEGC, True)
        else:
            k = min(core - 2, 3)
            xs = x_sample[4 * k:4 * k + 4].reshape(NCH * 128, D)
            cr = c_sample[4 * k:4 * k + 4]
            tb = core_tables(NCH, SEGC, False)
        m = dict(x=np.ascontiguousarray(xs), cT=core_c(cr))
        m.update(tb); m.update(consts); m.update(wts)
        in_maps.append(m)
    res = run_bass_kernel_spmd(nc, in_maps, core_ids=list(range(8)))
    r = res.results
    y_prompt = np.stack([r[0]["y"], r[1]["y"]], 0).astype(np.float32)
    y_sample = np.concatenate([r[2 + k]["y"].reshape(4, 2048, D) for k in range(4)], 0).astype(np.float32)
    return (y_prompt, y_sample)
```

```python
import math
import os
import numpy as np
DBG = os.environ.get("KDBG", "sgu,ret,ap,attn,out").split(",")
DBG2 = os.environ.get("KDBG2", "r1,r2,r3,r4,r5").split(",")
from contextlib import ExitStack
import concourse.bass as bass
import concourse.mybir as mybir
from concourse.bass_utils import run_bass_kernel_spmd

F32 = mybir.dt.float32
BF16 = mybir.dt.bfloat16
AF = mybir.ActivationFunctionType
ALU = mybir.AluOpType
AX = mybir.AxisListType

SAME_ENGINE_SYNC = bool(int(os.environ.get("KSES", "1")))
EPS = 1e-6
D = 1024
IN_DIM = 2688
DFF = 2816
NFC = 22
DEPTH = 2


class Tile:
    __slots__ = ("name", "w", "r", "rd", "excl")

    def __init__(self, name="", excl=False):
        self.name = name
        self.w = None
        self.r = {}
        self.rd = []
        self.excl = excl


class Op:
    __slots__ = ("eng", "fn", "deps", "inc", "val", "dsem")

    def __init__(self, eng, fn, dsem):
        self.eng = eng
        self.fn = fn
        self.dsem = dsem
        self.deps = set()
        self.inc = dsem is not None
        self.val = 0


class DSem:
    __slots__ = ("h", "cnt")

    def __init__(self, h):
        self.h = h
        self.cnt = 0


class Prog:
    ENGS = ("pe", "act", "dve", "pool", "sp")

    def __init__(self, nc, es):
        self.nc = nc
        self.es = es
        self.ops = {e: [] for e in self.ENGS}
        self.esem = {e: es.enter_context(nc.semaphore("s_" + e)) for e in self.ENGS}
        self.n = 0
        self.bar_idx = {}
        self.region = None
        self.cur = None
        self.streams = {}

    def sb(self, shape, dt, name=None):
        self.n += 1
        return self.es.enter_context(self.nc.sbuf_tensor(f"sb{self.n}_{name or ''}", list(shape), dt))

    def ps(self, shape, dt=F32, name=None):
        self.n += 1
        return self.es.enter_context(self.nc.psum_tensor(name or f"ps{self.n}", list(shape), dt))

    def dsem(self):
        self.n += 1
        return DSem(self.es.enter_context(self.nc.semaphore(f"ds{self.n}")))

    def set_stream(self, name):
        self.cur = None if name is None else self.streams.setdefault(name, [])

    def flush(self):
        self.cur = None
        lists = [v for v in self.streams.values() if v]
        if os.environ.get("KSEQ"):
            for li in lists:
                for r_ in li:
                    self._op(*r_)
            self.streams = {}
            return
        idx = [0] * len(lists)
        while True:
            best = None
            for i, li in enumerate(lists):
                if idx[i] < len(li):
                    f = (idx[i] + 1) / len(li)
                    if best is None or f < best[0]:
                        best = (f, i)
            if best is None:
                break
            i = best[1]
            self._op(*lists[i][idx[i]])
            idx[i] += 1
        self.streams = {}

    def op(self, eng, fn, reads=(), writes=(), dsem=None):
        if self.cur is not None:
            self.cur.append((eng, fn, list(reads), list(writes), dsem))
            return None
        return self._op(eng, fn, reads, writes, dsem)

    def _op(self, eng, fn, reads=(), writes=(), dsem=None):
        o = Op(eng, fn, dsem)
        deps = o.deps
        for t in reads:
            if t.w is not None:
                deps.add(t.w)
            if t.excl:
                for e_, o_ in t.r.items():
                    if e_ != eng:
                        deps.add(o_)
        for t in writes:
            if t.w is not None:
                deps.add(t.w)
            deps.update(t.r.values())
            deps.update(t.rd)
        for t in reads:
            if dsem is not None:
                t.rd.append(o)
            else:
                t.r[eng] = o
        for t in writes:
            t.w = o
            t.r = {}
            t.rd = []
        if dsem is not None:
            dsem.cnt += 16
            o.val = dsem.cnt
        self.ops[eng].append(o)
        return o

    def I(self, eng, meth, reads, writes, *a, **k):
        return self.op(eng, (meth, a, k), reads, writes)

    def dma(self, eng, out, in_, reads, writes, dsem=None, slow=False):
        if dsem is None:
            dsem = self.dsem()
        k = dict(out=out, in_=in_)
        if slow:
            k["allow_slow_non_contiguous"] = True
        return self.op(eng, ("dma_start", (), k), reads, writes, dsem)

    def barrier(self):
        lasts = []
        for e in self.ENGS:
            comp = [o for o in self.ops[e] if o.dsem is None and o.fn is not None]
            if comp:
                lasts.append(comp[-1])
        dmas = [o for e in self.ENGS for o in self.ops[e][self.bar_idx.get(e, 0):] if o.dsem is not None]
        for e in self.ENGS:
            self.bar_idx[e] = len(self.ops[e])
        for e in self.ENGS:
            o = Op(e, None, None)
            o.deps.update(lasts)
            o.deps.update(dmas)
            self.ops[e].append(o)

    def wait_all(self, eng, tiles):
        o = Op(eng, None, None)
        for t in tiles:
            if t.w is not None:
                o.deps.add(t.w)
        self.ops[eng].append(o)

    def emit(self):
        nc = self.nc
        for e in self.ENGS:
            for o in self.ops[e]:
                for d in o.deps:
                    if d.dsem is not None:
                        continue
                    if d.eng != o.eng or o.dsem is not None:
                        d.inc = True
                    elif SAME_ENGINE_SYNC and d.eng != "pe":
                        d.inc = True
        for e in self.ENGS:
            c = 0
            for o in self.ops[e]:
                if o.dsem is None:
                    if o.inc:
                        c += 1
                    o.val = c
        counts = {}
        with nc.Block() as block:
            def run(e):
                def body(h):
                    waited = {}
                    nw = 0
                    for o in self.ops[e]:
                        need = {}
                        for d in o.deps:
                            if d.dsem is not None:
                                key = d.dsem
                                sem = d.dsem.h
                            else:
                                if d.eng == e and o.dsem is None:
                                    if e == "pe" or not SAME_ENGINE_SYNC:
                                        continue
                                key = d.eng
                                sem = self.esem[d.eng]
                            if need.get(key, (None, 0))[1] < d.val:
                                need[key] = (sem, d.val)
                        for key, (sem, val) in need.items():
                            if waited.get(key, 0) < val:
                                h.wait_ge(sem, val)
                                waited[key] = val
                                nw += 1
                        if o.fn is None:
                            continue
                        meth, a, k = o.fn
                        inst = getattr(h, meth)(*a, **k)
                        if o.dsem is not None:
                            inst.then_inc(o.dsem.h, 16)
                        elif o.inc:
                            inst.then_inc(self.esem[e], 1)
                    counts[e] = (len(self.ops[e]), nw)
                return body
            block.tensor(run("pe"))
            block.scalar(run("act"))
            block.vector(run("dve"))
            block.gpsimd(run("pool"))
            block.sync(run("sp"))
        return counts


class Region:
    def __init__(self, big, start, limit):
        self.big = big
        self.off = start
        self.limit = limit

    def alloc(self, shape, dt):
        n = 1
        for s in shape[1:]:
            n *= s
        n16 = n * (2 if dt == F32 else 1)
        self.off = (self.off + 15) // 16 * 16
        ap = self.big[0:shape[0], self.off:self.off + n16]
        self.off += n16
        assert self.off <= self.limit, ("region overflow", self.off, self.limit)
        if dt == F32:
            ap = ap.bitcast(F32)
        if len(shape) == 3:
            ap = ap.rearrange("p (a b) -> p a b", a=shape[1])
        elif len(shape) == 4:
            ap = ap.rearrange("p (a b c) -> p a b c", a=shape[1], b=shape[2])
        return ap


class Buf:
    def __init__(self, P, shape, dt, name=None):
        if getattr(P, "region", None) is not None:
            self.t = P.region.alloc(shape, dt)
        else:
            self.t = P.sb(shape, dt, name)
        self.T = Tile(name or "")

    def __getitem__(self, k):
        return self.t[k]


class Ring:
    def __init__(self, P, n, shape, dt, name):
        self.b = [Buf(P, shape, dt, f"{name}{i}") for i in range(n)]
        self.n = n
        self.i = -1

    def next(self):
        self.i += 1
        return self.b[self.i % self.n]

    def at(self, k):
        return self.b[k % self.n]


def build(NCH, SEGC, depth=DEPTH, stop_after=None):
    NSEG = NCH // SEGC
    NT = NCH * 128
    NB = NCH // 2
    nc = bass.Bass("TRN2", target_bir_lowering=False)

    def din(name, shape):
        return nc.dram_tensor(name, list(shape), F32, kind="ExternalInput").ap()

    x_in = din("x", [NT, D])
    cT_in = din("cT", [128, 8, NSEG])
    flag_in = din("flag", [128, 1])
    rope_in = din("rope", [NCH, 128, 120])
    ident_in = din("ident", [128, 128])
    cm_in = din("cmats", [128, 6, 128])
    tri_in = din("tri", [128, 2, 128])
    jv_in = din("jv", [128, 2])
    w_ada = din("w_ada", [depth, D, 6 * D]); b_ada = din("b_ada", [depth, 6 * D])
    w_in = din("w_in", [depth, D, IN_DIM]); w_out = din("w_out", [depth, D, D])
    w_up = din("w_up", [depth, D, 2 * DFF]); w_down = din("w_down", [depth, DFF, D])
    cw_in = din("cw", [depth, 128, 2 * NFC, 4])
    gfm_in = din("gfm", [depth, 128, 2, 8])
    gpost_in = din("gpost", [depth, 2 * D])
    sguw_in = din("sgu_wT", [depth, 128, 4, 128]); sgub_in = din("sgu_bT", [depth, 128, 4])
    sgun_in = din("sgu_n", [depth, 256])
    dec18_in = din("dec18", [depth, 128, 18])
    sink_in = din("sink6", [depth, 128, 6])
    y_out = nc.dram_tensor("y", [NT, D], F32, kind="ExternalOutput").ap()
    xa = nc.dram_tensor("xa", [NT, D], F32, kind="Internal").ap()
    xb = nc.dram_tensor("xb", [NT, D], F32, kind="Internal").ap()
    modD = nc.dram_tensor("modD", [depth, NSEG, 6 * D], F32, kind="Internal").ap()
    gD = nc.dram_tensor("gD", [depth, 2, NSEG, D], F32, kind="Internal").ap()

    es = ExitStack()
    P = Prog(nc, es)
    Txa = [Tile(f"xa{i}") for i in range(NCH)]
    Txb = [Tile(f"xb{i}") for i in range(NCH)]
    Ty = [Tile(f"y{i}") for i in range(NCH)]
    TmodD = [None] * depth
    TgD = [None] * depth

    PS = P.ps([128, 4096], F32, "PSALL")
    PSb = PS.bitcast(BF16)
    Tbank = [Tile(f"bank{i}", excl=True) for i in range(8)]

    def bank(i):
        return PS[:, 512 * i:512 * (i + 1)]

    def bankb(i):
        return PSb[:, 1024 * i:1024 * (i + 1)]

    class Gen:
        def __init__(self, ids):
            self.ids = ids
            self.i = -1

        def next(self):
            self.i += 1
            b = self.ids[self.i % len(self.ids)]
            return bank(b), Tbank[b]

    ident = Buf(P, [128, 128], BF16, "ident")
    tri = Buf(P, [128, 2, 128], BF16, "tri")
    trif = Buf(P, [128, 2, 128], BF16, "trif")
    jv = Buf(P, [128, 2], F32, "jv")
    flag = Buf(P, [128, 1], F32, "flag")
    ropeR = Ring(P, 3, [128, 120], F32, "rope")
    drope = [P.dsem() for _ in range(3)]
    cmh = Buf(P, [128, 8], F32, "cmh")
    siluT = Buf(P, [128, 8, NSEG], BF16, "siluT")
    cTf = Buf(P, [128, 8, NSEG], F32, "cTf")

    dsems = []

    def DS():
        d = P.dsem()
        dsems.append(d)
        return d

    _nd = {}

    def ND(name):
        if name not in _nd:
            _nd[name] = DS()
        return _nd[name]
    for (b, src, eng) in [(ident, ident_in, "pool"), (tri, tri_in, "pool"), (jv, jv_in, "sp"),
                          (flag, flag_in, "sp"), (cTf, cT_in, "sp")]:
        P.dma(eng, b[:], src, [], [b.T], ND("c_" + b.T.name))
    P.I("pool", "memset", [], [cmh.T], cmh[:], -0.5)
    P.I("act", "activation", [cTf.T], [siluT.T], out=siluT[:], in_=cTf[:], func=AF.Silu)
    P.I("dve", "tensor_scalar", [tri.T, flag.T], [trif.T], out=trif[:], in0=tri[:], scalar1=flag[:, 0:1], scalar2=None, op0=ALU.mult)

    ARENA = 8 * 2 * DFF + NFC * D
    BIGN = ARENA + 21120
    W = P.sb([128, BIGN], BF16, "arena")
    win_v = W[:, 0:8 * IN_DIM].rearrange("p (k n) -> p k n", k=8)
    wout_v = W[:, 8 * IN_DIM:8 * IN_DIM + 8 * D].rearrange("p (k n) -> p k n", k=8)
    SB0 = 8 * IN_DIM + 8 * D
    sbst_v = W[:, SB0:SB0 + NCH * 192].rearrange("p (c q e) -> p c q e", c=NCH, q=3)
    wup_v = W[:, 0:8 * 2 * DFF].rearrange("p (k n) -> p k n", k=8)
    wdn_v = W[:, 8 * 2 * DFF:ARENA].rearrange("p (k n) -> p k n", k=NFC)
    Twin = [Tile(f"win{k}") for k in range(8)]
    Twout = [Tile(f"wout{k}") for k in range(8)]
    Tsb = [Tile(f"sbst{c}") for c in range(NCH)]
    Twup = [Tile(f"wup{k}") for k in range(8)]
    Twdn = [Tile(f"wdn{k}") for k in range(NFC)]
    A_tiles = Twin + Twout + Tsb
    B_tiles = Twup + Twdn
    dWin = [DS() for _ in range(8)]; dWout = [DS() for _ in range(8)]; dWup = [DS() for _ in range(8)]; dWdn = [DS() for _ in range(NFC)]

    regA = Region(W, SB0 + NCH * 192, BIGN)
    P.region = regA
    d18 = Buf(P, [128, 18], F32, "d18"); e18 = Buf(P, [128, 18], F32, "e18"); lg18 = Buf(P, [128, 18], F32, "lg18")
    DT = Buf(P, [128, 6, 128], F32, "DT")
    QDF = Buf(P, [128, 3, 128], F32, "QDF"); QDB = Buf(P, [128, 3, 128], F32, "QDB")
    KDF = Buf(P, [128, 6], F32, "KDF"); KDB = Buf(P, [128, 6], F32, "KDB")
    CDF = Buf(P, [128, 3], F32, "CDF"); CDB = Buf(P, [128, 3], F32, "CDB")
    ESK = Buf(P, [128, 6], F32, "ESK")
    SN = Buf(P, [128, 256], F32, "SN")
    WsT = Buf(P, [128, 4, 128], BF16, "WsT"); SGB = Buf(P, [128, 4], F32, "SGB")
    P.region = None
    CW = Buf(P, [128, 2 * NFC, 4], F32, "CW")
    gfm = Buf(P, [128, 2, 8], F32, "gfm")
    fm = [Buf(P, [128, NSEG, 8], F32, f"fm{i}") for i in range(4)]
    A1f = Buf(P, [128, NSEG, 8], F32, "A1f"); A2f = Buf(P, [128, NSEG, 8], F32, "A2f")
    Gt = Ring(P, 1, [128, D], F32, "Gt")
    dl = [DS() for _ in range(4)]
    dmb = [DS() for _ in range(2)]; dgp = [DS() for _ in range(2)]; dgb = [DS() for _ in range(2)]

    xcr = Ring(P, 3, [128, D], F32, "xc")
    st = Ring(P, 16, [128, 8], F32, "st")
    xn = Ring(P, 1, [128, D], BF16, "xn")
    xnew = Ring(P, 2, [128, D], F32, "xnew")
    P.region = regA
    hTr = Ring(P, 2, [128, 8, 128], BF16, "hT")
    gA = Gen([2, 3, 5, 6])
    st_rings = {None: st, "SC": Ring(P, 8, [128, 8], F32, "stSC"), "R": Ring(P, 8, [128, 8], F32, "stR"), "L": Ring(P, 8, [128, 8], F32, "stL"),
                "N": Ring(P, 8, [128, 8], F32, "stN"), "U": st}
    gens = {None: gA, "SC": Gen([2]), "R": Gen([3, 5]), "L": Gen([6]), "N": Gen([0]), "U": gA}
    gLpost = Gen([6, 7])
    _rg = P.region
    P.region = None
    st_rings["K"] = st
    gens["K"] = gA
    st_rings["NB"] = Ring(P, 8, [128, 8], F32, "stNB")
    gens["NB"] = Gen([0])
    P.region = _rg
    CUR = {"st": st, "gen": gA}
    Tb1a = Tbank[1]; Tb1b = Tbank[1]

    def use(name):
        P.set_stream(name)
        CUR["st"] = st_rings[name]
        CUR["gen"] = gens[name]

    u_b = Buf(P, [128, 256], BF16, "u"); vg = Buf(P, [128, 256], F32, "vg"); sq = Buf(P, [128, 256], F32, "sq")
    vn = Buf(P, [128, 256], F32, "vn"); vn2 = Buf(P, [128, 256], BF16, "vn2"); gt_ = Buf(P, [128, 256], F32, "gt")
    r1 = Buf(P, [128, 384], F32, "r1"); r2 = Buf(P, [128, 384], F32, "r2"); rsum = Buf(P, [128, 384], F32, "rsum")
    qr = Buf(P, [128, 384], BF16, "qr"); kr = Buf(P, [128, 384], BF16, "kr")
    kdf = Buf(P, [128, 384], BF16, "kdf"); kdb = Buf(P, [128, 384], BF16, "kdb")
    vt = Buf(P, [128, 384], BF16, "vt"); sg = Buf(P, [128, 384], F32, "sg")
    qkT = Buf(P, [128, 6, 128], BF16, "qkT")
    qdf = Buf(P, [128, 3, 128], BF16, "qdf"); qdb = Buf(P, [128, 3, 128], BF16, "qdb")
    PT = Buf(P, [128, 6, 128], BF16, "PT")
    Rf = Buf(P, [128, 3, 64], F32, "Rf"); Rb = Buf(P, [128, 3, 64], F32, "Rb"); Rt = Buf(P, [128, 3, 64], F32, "Rt")
    Sfb = Buf(P, [128, 3, 64], BF16, "Sfb")
    ysq = Buf(P, [128, 384], F32, "ysq"); yc_ = Buf(P, [128, 384], F32, "yc"); yn_ = Buf(P, [128, 384], F32, "yn")
    cqb = Buf(P, [128, 384], BF16, "cqb"); ckb = Buf(P, [128, 128], BF16, "ckb")
    a1 = Buf(P, [128, 6, 16], F32, "a1"); a2 = Buf(P, [128, 6, 16], F32, "a2")
    cqT = Ring(P, 3, [128, 3, 128], BF16, "cqT"); ckT = Ring(P, 4, [128, 128], BF16, "ckT")
    cvb = Ring(P, 4, [128, 2, 65], BF16, "cvb")
    Eb = Ring(P, 6, [128, 3, 128], BF16, "Eb")
    mix = Ring(P, 3, [128, D], BF16, "mix")
    mixT = Buf(P, [128, 8, 128], BF16, "mixT")
    xnA = Ring(P, 2, [128, D], BF16, "xnA")
    ysb = Buf(P, [128, 384], F32, "ysb")
    cm = Buf(P, [128, 6, 128], F32, "cmats")
    ta = Buf(P, [128, 128], F32, "ta"); tb = Buf(P, [128, 128], F32, "tb")
    wad = Ring(P, 2, [128, 8, 256], BF16, "wad")
    badb = Ring(P, 2, [NSEG, 256], F32, "badb")
    mblk = Ring(P, 2, [NSEG, 256], F32, "mblk")
    gpb = Ring(P, 2, [NSEG, 256], F32, "gpb")
    gblk = Ring(P, 2, [NSEG, 256], F32, "gblk")
    print("regA end", regA.off, BIGN)
    regB = Region(W, ARENA, BIGN)
    P.region = regB
    HB = Ring(P, 3, [128, 8, 258], BF16, "HB")
    T0 = Ring(P, 2, [128, 256], F32, "T0"); T1 = Ring(P, 2, [128, 256], F32, "T1"); T2 = Ring(P, 2, [128, 256], F32, "T2")
    gg = Ring(P, 1, [128, 256], F32, "gg")
    actT = Ring(P, 2, [128, NFC, 256], BF16, "actT")
    print("regB end", regB.off, BIGN)
    P.region = None
    dx = [DS() for _ in range(4)]
    dst_ = [DS() for _ in range(2)]
    dG = [DS() for _ in range(2)]
    dcp = [DS() for _ in range(4)]


    def rstd_from(ssb, n_inv, out_col):
        sbuf, c = ssb
        obuf, oc = out_col
        t = CUR["st"].next()
        P.I("dve", "tensor_scalar", [sbuf.T], [t.T], out=t[:, 0:1], in0=sbuf[:, c:c + 1], scalar1=n_inv, scalar2=EPS, op0=ALU.mult, op1=ALU.add)
        P.I("pool", "tensor_tensor", [t.T, cmh.T], [obuf.T], out=obuf[:, oc:oc + 1], in0=t[:, 0:1], in1=cmh[:, 0:1], op=ALU.pow)

    def layer_setup(l):
        P.dma("sp", cm[:], cm_in, [], [cm.T], ND("cm"))
        TmD = [Tile(f"modD{l}_{i}") for i in range(24)]
        TgDl = [Tile(f"gD{l}_{i}") for i in range(8)]
        for cb in range(24):
            c0 = cb * 256
            wb = wad.next(); bb = badb.next(); mb = mblk.next()
            P.dma("pool", wb[:], w_ada[l, :, c0:c0 + 256].rearrange("(k p) n -> p k n", p=128), [], [wb.T], dl[cb % 2])
            P.dma("sp", bb[:], b_ada[l:l + 1, c0:c0 + 256].partition_broadcast(NSEG), [], [bb.T], dl[2 + cb % 2])
            pb, Tpb = CUR["gen"].next()
            for kc in range(8):
                P.I("pe", "matmul", [siluT.T, wb.T], [Tpb], pb[0:NSEG, 0:256], lhsT=siluT[:, kc, :], rhs=wb[:, kc, :], start=(kc == 0), stop=(kc == 7))
            P.I("dve", "tensor_tensor", [Tpb, bb.T], [mb.T], out=mb[:], in0=pb[0:NSEG, 0:256], in1=bb[:], op=ALU.add)
            P.dma("sp", modD[l, :, c0:c0 + 256], mb[:], [mb.T], [TmD[cb]], dmb[cb % 2])
            part = cb // 4
            if part in (2, 5):
                gi = 0 if part == 2 else 1
                j = cb % 4
                gp = gpb.next(); gb_ = gblk.next()
                P.dma("sp", gp[:], gpost_in[l:l + 1, gi * D + j * 256:gi * D + (j + 1) * 256].partition_broadcast(NSEG), [], [gp.T], dgp[gpb.i % 2])
                P.I("dve", "tensor_tensor", [mb.T, gp.T], [gb_.T], out=gb_[:], in0=mb[:], in1=gp[:], op=ALU.mult)
                P.dma("sp", gD[l, gi, :, j * 256:(j + 1) * 256], gb_[:], [gb_.T], [TgDl[gi * 4 + j]], dgb[gblk.i % 2])
        TmodD[l] = TmD
        TgD[l] = TgDl
        for i, part in enumerate([0, 1, 3, 4]):
            for s_ in range(NSEG):
                P.dma("sp", fm[i][:, s_, :], modD[l, s_, part * D:(part + 1) * D].rearrange("(k p) -> p k", p=128), TmodD[l][part * 4:part * 4 + 4], [fm[i].T], ND(f"fm{i}"), slow=True)
        P.dma("sp", gfm[:], gfm_in[l], [], [gfm.T], ND("gfm"))
        for (Af, scb, gi) in [(A1f, fm[1], 0), (A2f, fm[3], 1)]:
            P.I("dve", "scalar_tensor_tensor", [scb.T, gfm.T], [Af.T],
                out=Af[:], in0=scb[:], scalar=1.0, in1=gfm[:, gi, :].unsqueeze(1).to_broadcast([128, NSEG, 8]),
                op0=ALU.add, op1=ALU.mult)
        P.dma("sp", d18[:], dec18_in[l], [], [d18.T], ND("d18"))
        P.dma("sp", ESK[:], sink_in[l], [], [ESK.T], ND("ESK"))
        P.dma("sp", SN[:], sgun_in[l:l + 1, :].partition_broadcast(128), [], [SN.T], ND("SN"))
        P.dma("pool", WsT[:], sguw_in[l], [], [WsT.T], ND("WsT"))
        P.dma("sp", SGB[:], sgub_in[l], [], [SGB.T], ND("SGB"))
        P.I("act", "activation", [d18.T], [e18.T], out=e18[:], in_=d18[:], func=AF.Exp, scale=-1.0)
        P.I("dve", "tensor_scalar", [e18.T], [e18.T], out=e18[:], in0=e18[:], scalar1=1.0, scalar2=None, op0=ALU.add)
        P.I("act", "activation", [e18.T], [lg18.T], out=lg18[:], in_=e18[:], func=AF.Ln)
        P.I("dve", "tensor_scalar", [lg18.T], [lg18.T], out=lg18[:], in0=lg18[:], scalar1=-1.0, scalar2=None, op0=ALU.mult)
        P.I("act", "activation", [ESK.T], [ESK.T], out=ESK[:], in_=ESK[:], func=AF.Exp)
        for h in range(6):
            P.I("act", "activation", [cm.T, lg18.T], [ta.T], out=ta[:], in_=cm[:, 0, :], func=AF.Exp, scale=lg18[:, h:h + 1])
            P.I("act", "activation", [cm.T, lg18.T], [tb.T], out=tb[:], in_=cm[:, 1, :], func=AF.Exp, scale=lg18[:, 6 + h:7 + h])
            P.I("dve", "scalar_tensor_tensor", [ta.T, cm.T], [ta.T], out=ta[:], in0=ta[:], scalar=0.125, in1=cm[:, 2, :], op0=ALU.mult, op1=ALU.mult)
            P.I("dve", "scalar_tensor_tensor", [tb.T, cm.T], [tb.T], out=tb[:], in0=tb[:], scalar=0.125, in1=cm[:, 3, :], op0=ALU.mult, op1=ALU.mult)
            P.I("dve", "tensor_tensor", [ta.T, tb.T], [DT.T], out=DT[:, (h % 2) * 3 + h // 2, :], in0=ta[:], in1=tb[:], op=ALU.add)
        for q in range(3):
            P.I("act", "activation", [cm.T, lg18.T], [QDF.T], out=QDF[:, q, :], in_=cm[:, 4, :], func=AF.Exp, scale=lg18[:, 12 + q:13 + q])
            P.I("act", "activation", [cm.T, lg18.T], [QDB.T], out=QDB[:, q, :], in_=cm[:, 5, :], func=AF.Exp, scale=lg18[:, 15 + q:16 + q])
        P.I("act", "activation", [lg18.T, jv.T], [KDF.T], out=KDF[:], in_=lg18[:, 0:6], func=AF.Exp, scale=jv[:, 0:1])
        P.I("act", "activation", [lg18.T, jv.T], [KDB.T], out=KDB[:], in_=lg18[:, 6:12], func=AF.Exp, scale=jv[:, 1:2])
        P.I("dve", "tensor_scalar", [KDF.T], [KDF.T], out=KDF[:], in0=KDF[:], scalar1=0.125, scalar2=None, op0=ALU.mult)
        P.I("dve", "tensor_scalar", [KDB.T], [KDB.T], out=KDB[:], in0=KDB[:], scalar1=0.125, scalar2=None, op0=ALU.mult)
        P.I("act", "activation", [lg18.T], [CDF.T], out=CDF[:], in_=lg18[:, 12:15], func=AF.Exp, scale=128.0)
        P.I("act", "activation", [lg18.T], [CDB.T], out=CDB[:], in_=lg18[:, 15:18], func=AF.Exp, scale=128.0)

    def load_A_weights(l):
        for k in range(8):
            P.dma("pool", win_v[:, k, :], w_in[l, k * 128:(k + 1) * 128, :], [], [Twin[k]] + (B_tiles if k == 0 else []), dWin[k])
        for k in range(8):
            P.dma("pool", wout_v[:, k, :], w_out[l, k * 128:(k + 1) * 128, :], [], [Twout[k]], dWout[k])

    def load_B_weights(l):
        for k in range(8):
            P.dma("pool", wup_v[:, k, :], w_up[l, k * 128:(k + 1) * 128, :], [], [Twup[k]] + (A_tiles if k == 0 else []), dWup[k])
        for k in range(NFC):
            P.dma("pool", wdn_v[:, k, :], w_down[l, k * 128:(k + 1) * 128, :], [], [Twdn[k]], dWdn[k])

    def load_x(src, Tsrc, n, ring, dlist):
        xc = ring.next()
        P.dma("sp", xc[:], src[n * 128:(n + 1) * 128, :], [Tsrc[n]] if Tsrc is not None else [], [xc.T], dlist[ring.i % ring.n])
        return xc

    def norm_front(xc):
        return norm_front_2(norm_front_1(xc))

    def norm_front_1(xc):
        s = CUR["st"].next()
        xb_ = (CUR.get("xn") or xn).next()
        P.I("act", "activation", [xc.T], [xb_.T, s.T], out=xb_[:], in_=xc[:], func=AF.Square, accum_out=s[:, 0:1])
        rstd_from((s, 0), 1.0 / D, (s, 1))
        return (xc, s, xb_)

    def norm_front_2(st3):
        xc, s, xb_ = st3
        P.I("dve", "tensor_scalar", [xc.T, s.T], [xb_.T], out=xb_[:], in0=xc[:], scalar1=s[:, 1:2], scalar2=None, op0=ALU.mult)
        return xb_

    def norm_T(xc, seg, Af, Bf, dst_fn, dstT):
        norm_back(norm_front(xc), seg, Af, Bf, dst_fn, dstT)

    def norm_back(xb_, seg, Af, Bf, dst_fn, dstT):
        for kc in range(8):
            P.I("pe", "transpose", [xb_.T, ident.T], [Tbank[0]], out=bankb(0)[:, kc * 128:(kc + 1) * 128], in_=xb_[:, kc * 128:(kc + 1) * 128], identity=ident[:])
        for kc in range(8):
            P.I("act", "activation", [Tbank[0], Af.T, Bf.T], [dstT], out=dst_fn(kc), in_=bankb(0)[:, kc * 128:(kc + 1) * 128], func=AF.Identity,
                                                              scale=Af[:, seg, kc:kc + 1], bias=Bf[:, seg, kc:kc + 1])

    def proj(hT, wv, Tw, c0, ncol, gen):
        pb, Tpb = gen.next()
        for kc in range(8):
            P.I("pe", "matmul", [hT.T, Tw[kc]], [Tpb], pb[:, 0:ncol], lhsT=hT[:, kc, :], rhs=wv[:, kc, c0:c0 + ncol], start=(kc == 0), stop=(kc == 7))
        return pb, Tpb

    def load_rope(n):
        rp = ropeR.next()
        P.dma("sp", rp[:], rope_in[n], [], [rp.T], drope[ropeR.i % 3])
        return rp

    def rope64(pb, Tpb, rp, out_f32):
        x3 = pb[:, 0:384].rearrange("p (a d) -> p a d", d=32)
        x4 = pb[:, 0:384].rearrange("p (h t d) -> p h t d", h=6, t=2)
        P.I("dve", "tensor_tensor", [Tpb, rp.T], [r1.T], out=r1[:].rearrange("p (a d) -> p a d", d=32), in0=x3,
                                              in1=rp[:, 0:32].unsqueeze(1).to_broadcast([128, 12, 32]), op=ALU.mult)
        r2v = r2[:].rearrange("p (h t d) -> p h t d", h=6, t=2)
        P.I("dve", "tensor_tensor", [Tpb, rp.T], [r2.T], out=r2v[:, :, 0, :], in0=x4[:, :, 1, :],
                                              in1=rp[:, 64:96].unsqueeze(1).to_broadcast([128, 6, 32]), op=ALU.mult)
        P.I("dve", "tensor_tensor", [Tpb, rp.T], [r2.T], out=r2v[:, :, 1, :], in0=x4[:, :, 0, :],
                                              in1=rp[:, 32:64].unsqueeze(1).to_broadcast([128, 6, 32]), op=ALU.mult)
        P.I("dve", "tensor_tensor", [r1.T, r2.T], [out_f32.T], out=out_f32[:], in0=r1[:], in1=r2[:], op=ALU.add)

    def kv_update(kd, R, CD, n, store_fn, store_T, gen, boundary):
        pb, Tpb = gen.next()
        for q in range(3):
            P.I("pe", "matmul", [kd.T, vt.T], [Tpb], pb[:, q * 128:(q + 1) * 128], lhsT=kd[:, q * 128:(q + 1) * 128], rhs=vt[:, q * 128:(q + 1) * 128],
                                                       start=True, stop=True)
        P.I("dve", "tensor_tensor", [R.T, CD.T], [Rt.T], out=Rt[:], in0=R[:], in1=CD[:].unsqueeze(2).to_broadcast([128, 3, 64]), op=ALU.mult)
        kv3 = pb[:, 0:384].rearrange("p (q e) -> p q e", q=3)
        P.I("dve", "tensor_tensor", [Rt.T, Tpb], [R.T], out=R[0:64], in0=Rt[0:64], in1=kv3[0:64, :, 0:64], op=ALU.add)
        P.I("dve", "tensor_tensor", [Rt.T, Tpb], [R.T], out=R[64:128], in0=Rt[64:128], in1=kv3[64:128, :, 64:128], op=ALU.add)
        if boundary:
            P.I("dve", "tensor_scalar", [R.T, flag.T], [R.T], out=R[:], in0=R[:], scalar1=flag[:, 0:1], scalar2=None, op0=ALU.mult)
        if store_fn is not None:
            P.I("act", "activation", [R.T], [store_T], out=store_fn, in_=R[:], func=AF.Copy)

    def post_stage(lhs_fn, K, wv, Tw, xc, Gb, n, dst, Tdst, gen, add_eng="pool"):
        post_stage_2(post_stage_1(lhs_fn, K, wv, Tw, xc, Gb, n, dst, Tdst, gen), add_eng)

    def post_stage_1(lhs_fn, K, wv, Tw, xc, Gb, n, dst, Tdst, gen):
        pbs = []
        for half in range(2):
            pb, Tpb = gen.next()
            for kc in range(K):
                lh, Tl = lhs_fn(kc)
                P.I("pe", "matmul", [Tl, Tw[kc]], [Tpb], pb[:, :], lhsT=lh, rhs=wv[:, kc, half * 512:(half + 1) * 512],
                                                                                      start=(kc == 0), stop=(kc == K - 1))
            pbs.append((pb, Tpb))
        s = CUR["st"].next()
        xo = xnew.next()
        for half in range(2):
            pb, Tpb = pbs[half]
            P.I("act", "activation", [Tpb], [xo.T, s.T], out=xo[:, 0:512], in_=pb[:, :], func=AF.Square, accum_out=s[:, half:half + 1])
        P.I("dve", "tensor_tensor", [s.T], [s.T], out=s[:, 2:3], in0=s[:, 0:1], in1=s[:, 1:2], op=ALU.add)
        rstd_from((s, 2), 1.0 / D, (s, 3))
        return (pbs, s, xo, xc, Gb, n, dst, Tdst, xnew.i % 2)

    def post_stage_2(state, add_eng="pool"):
        pbs, s, xo, xc, Gb, n, dst, Tdst, slot = state
        for half in range(2):
            pb, Tpb = pbs[half]
            P.I("dve", "scalar_tensor_tensor", [Tpb, s.T, Gb.T], [xo.T], out=xo[:, half * 512:(half + 1) * 512], in0=pb[:, :], scalar=s[:, 3:4],
                                                                                   in1=Gb[:, half * 512:(half + 1) * 512], op0=ALU.mult, op1=ALU.mult)
        P.I(add_eng, "tensor_tensor", [xc.T, xo.T], [xo.T], out=xo[:], in0=xo[:], in1=xc[:], op=ALU.add)
        P.dma("sp", dst[n * 128:(n + 1) * 128, :], xo[:], [xo.T], [Tdst[n]], dst_[slot])

    def load_G(l, gi, seg):
        Gb = Gt.next()
        P.dma("sp", Gb[:], gD[l, gi, seg:seg + 1, :].partition_broadcast(128), TgD[l][gi * 4:gi * 4 + 4], [Gb.T], dG[0])
        return Gb

    def phase_A0(l, src, Tsrc):
        P.I("pool", "memset", [], [Rb.T], Rb[:], 0.0)
        P.I("pool", "memset", [], [Tsb[NCH - 1]], sbst_v[:, NCH - 1], 0.0)
        hTs, rps = {}, {}

        xbs = {}

        def front_a(c):
            xc = load_x(src, Tsrc, c, xcr, dx)
            rps[c] = load_rope(c)
            xbs[c] = norm_front(xc)

        def front_b(c):
            hT = hTr.next()
            hTs[c] = hT
            norm_back(xbs.pop(c), c // SEGC, A1f, fm[0], lambda kc: hT[:, kc, :], hT.T)

        front_a(NCH - 1)
        front_b(NCH - 1)
        for n in range(NCH - 1, 0, -1):
            if n - 1 >= 1:
                use("N")
                front_a(n - 1)
                front_b(n - 1)
            use("K")
            hT = hTs.pop(n)
            rp = rps.pop(n)
            pk, Tpk = proj(hT, win_v, Twin, 896, 384, CUR["gen"])
            pv, Tpv = proj(hT, win_v, Twin, 1280, 384, CUR["gen"])
            rope64(pk, Tpk, rp, rsum)
            P.I("dve", "tensor_tensor", [rsum.T, KDB.T], [kdb.T], out=kdb[:].rearrange("p (h d) -> p h d", h=6), in0=rsum[:].rearrange("p (h d) -> p h d", h=6),
                in1=KDB[:].unsqueeze(2).to_broadcast([128, 6, 64]), op=ALU.mult)
            P.I("act", "activation", [Tpv], [vt.T], out=vt[:], in_=pv[:, 0:384], func=AF.Copy)
            kv_update(kdb, Rb, CDB, n, sbst_v[:, n - 1], Tsb[n - 1], CUR["gen"], boundary=(n % SEGC == 0))
            use(None)
            P.flush()

    def attention(l, m):
        mx = mix.at(m)
        pO, TpO = bank(7), Tbank[7]
        O3 = pO[:, 0:390].rearrange("p (h e) -> p h e", h=6)
        blks = [b for b in (-1, 0, 1) if 0 <= m + b < NCH]
        for kvh in range(2):
            Es = []
            for bi, b in enumerate(blks):
                pS, TpS = CUR["gen"].next()
                kT = ckT.at(m + b); qT = cqT.at(m)
                P.I("pe", "matmul", [kT.T, qT.T], [TpS], pS[:, 0:384], lhsT=kT[kvh * 64:(kvh + 1) * 64, :],
                    rhs=qT[kvh * 64:(kvh + 1) * 64, :, :], start=True, stop=True)
                E = Eb.next()
                P.I("act", "activation", [TpS], [E.T], out=E[:].rearrange("p g i -> p (g i)"), in_=pS[:, 0:384], func=AF.Exp, scale=0.125)
                if b != 0:
                    bnd = (b == -1 and m % SEGC == 0) or (b == 1 and m % SEGC == SEGC - 1)
                    mk = trif if bnd else tri
                    mi = 0 if b == -1 else 1
                    P.I("dve", "tensor_tensor", [E.T, mk.T], [E.T], out=E[:], in0=E[:], in1=mk[:, mi, :].unsqueeze(1).to_broadcast([128, 3, 128]), op=ALU.mult)
                Es.append(E)
            for g in range(3):
                for bi, b in enumerate(blks):
                    cv = cvb.at(m + b)
                    P.I("pe", "matmul", [Es[bi].T, cv.T], [TpO], O3[:, kvh * 3 + g, :], lhsT=Es[bi][:, g, :], rhs=cv[:, kvh, :],
                        start=(bi == 0), stop=(bi == len(blks) - 1))
        s = CUR["st"].next()
        P.I("dve", "tensor_tensor", [TpO, ESK.T], [s.T], out=s[:, 0:6], in0=O3[:, :, 64], in1=ESK[:], op=ALU.add)
        P.I("dve", "reciprocal", [s.T], [s.T], out=s[:, 0:6], in_=s[:, 0:6])
        P.I("dve", "tensor_tensor", [TpO, s.T], [mx.T], out=mx[:, 640:1024].rearrange("p (h d) -> p h d", h=6), in0=O3[:, :, 0:64],
                                              in1=s[:, 0:6].unsqueeze(2).to_broadcast([128, 6, 64]), op=ALU.mult)

    def out_stage(l, m, xc, Gb, dst, Tdst):
        mx = mix.at(m)
        for half in range(2):
            bk, Tbk = (bankb(1), Tb1a) if half == 0 else (bankb(6), Tbank[6])
            for k4 in range(4):
                kc = half * 4 + k4
                P.I("pe", "transpose", [mx.T, ident.T], [Tbk], out=bk[:, k4 * 128:(k4 + 1) * 128], in_=mx[:, kc * 128:(kc + 1) * 128], identity=ident[:])
        for half in range(2):
            bk, Tbk = (bankb(1), Tb1a) if half == 0 else (bankb(6), Tbank[6])
            P.I("act", "activation", [Tbk], [mixT.T], out=mixT[:, half * 4:(half + 1) * 4, :].rearrange("p k i -> p (k i)"), in_=bk[:, 0:512], func=AF.Copy)
        post_stage(lambda kc: (mixT[:, kc, :], mixT.T), 8, wout_v, Twout, xc, Gb, m, dst, Tdst, gLpost, add_eng="dve")

    def phase_A1(l, src, Tsrc, dst, Tdst):
        P.I("pool", "memset", [], [Rf.T], Rf[:], 0.0)
        P.I("pool", "memset", [], [Sfb.T], Sfb[:], 0.0)
        for b_ in cvb.b:
            P.I("pool", "memset", [], [b_.T], b_[:], 1.0)
        xcs = {}
        Gb = None
        Gseg = {}
        LAG = 2
        hTs, rps = {}, {}

        class _SR:
            def __init__(self, bufs):
                self.b = bufs; self.n = len(bufs); self.i = -1

            def next(self):
                self.i += 1
                return self.b[self.i % self.n]

        xNr = _SR(xcr.b[0:2]); xLr = _SR(xcr.b[2:3])

        def attn_proj(n, hT, rp):
            pcq, Tpcq = proj(hT, win_v, Twin, 2048, 384, CUR["gen"])
            P.I("act", "activation", [Tpcq], [cqb.T], out=cqb[:], in_=pcq[:, 0:384], func=AF.Copy)
            c3 = pcq[:, 0:384].rearrange("p (h d) -> p h d", h=6)
            P.I("dve", "tensor_tensor", [Tpcq, rp.T], [a1.T], out=a1[:].rearrange("p h (t d) -> p h t d", t=2), in0=c3[:, :, 0:16].rearrange("p h (t d) -> p h t d", t=2),
                                                  in1=rp[:, 96:104].unsqueeze(1).unsqueeze(1).to_broadcast([128, 6, 2, 8]), op=ALU.mult)
            P.I("dve", "tensor_tensor", [Tpcq, rp.T], [a2.T], out=a2[:, :, 0:8], in0=c3[:, :, 8:16], in1=rp[:, 112:120].unsqueeze(1).to_broadcast([128, 6, 8]), op=ALU.mult)
            P.I("dve", "tensor_tensor", [Tpcq, rp.T], [a2.T], out=a2[:, :, 8:16], in0=c3[:, :, 0:8], in1=rp[:, 104:112].unsqueeze(1).to_broadcast([128, 6, 8]), op=ALU.mult)
            P.I("dve", "tensor_tensor", [a1.T, a2.T], [cqb.T], out=cqb[:].rearrange("p (h d) -> p h d", h=6)[:, :, 0:16], in0=a1[:], in1=a2[:], op=ALU.add)

        def attn_proj_k(n, hT, rp):
            pck, Tpck = proj(hT, win_v, Twin, 2432, 256, CUR["gen"])
            cv = cvb.at(n)
            P.I("act", "activation", [Tpck], [ckb.T], out=ckb[:], in_=pck[:, 0:128], func=AF.Copy)
            P.I("act", "activation", [Tpck], [cv.T], out=cv[:, :, 0:64], in_=pck[:, 128:256].rearrange("p (h d) -> p h d", h=2), func=AF.Copy)
            k3 = pck[:, 0:128].rearrange("p (h d) -> p h d", h=2)
            P.I("dve", "tensor_tensor", [Tpck, rp.T], [a1.T], out=a1[:, 0:2, :].rearrange("p h (t d) -> p h t d", t=2), in0=k3[:, :, 0:16].rearrange("p h (t d) -> p h t d", t=2),
                                                  in1=rp[:, 96:104].unsqueeze(1).unsqueeze(1).to_broadcast([128, 2, 2, 8]), op=ALU.mult)
            P.I("dve", "tensor_tensor", [Tpck, rp.T], [a2.T], out=a2[:, 0:2, 0:8], in0=k3[:, :, 8:16], in1=rp[:, 112:120].unsqueeze(1).to_broadcast([128, 2, 8]), op=ALU.mult)
            P.I("dve", "tensor_tensor", [Tpck, rp.T], [a2.T], out=a2[:, 0:2, 8:16], in0=k3[:, :, 0:8], in1=rp[:, 104:112].unsqueeze(1).to_broadcast([128, 2, 8]), op=ALU.mult)
            P.I("dve", "tensor_tensor", [a1.T, a2.T], [ckb.T], out=ckb[:].rearrange("p (h d) -> p h d", h=2)[:, :, 0:16], in0=a1[:, 0:2, :], in1=a2[:, 0:2, :], op=ALU.add)
            for q in range(3):
                P.I("pe", "transpose", [cqb.T, ident.T], [Tb1b], out=bankb(1)[:, 512 + q * 128:512 + (q + 1) * 128], in_=cqb[:, q * 128:(q + 1) * 128], identity=ident[:])
            P.I("pe", "transpose", [ckb.T, ident.T], [Tb1b], out=bankb(1)[:, 896:1024], in_=ckb[:], identity=ident[:])
            cq_t = cqT.at(n); ck_t = ckT.at(n)
            P.I("act", "activation", [Tb1b], [cq_t.T], out=cq_t[:].rearrange("p a i -> p (a i)"), in_=bankb(1)[:, 512:896], func=AF.Copy)
            P.I("act", "activation", [Tb1b], [ck_t.T], out=ck_t[:], in_=bankb(1)[:, 896:1024], func=AF.Copy)


        xbs = {}

        def front_a(c):
            xc = load_x(src, Tsrc, c, xNr, dx[0:2])
            rps[c] = load_rope(c)
            xbs[c] = norm_front(xc)

        def front_b(c):
            hT = hTr.next()
            hTs[c] = hT
            norm_back(xbs.pop(c), c // SEGC, A1f, fm[0], lambda kc: hT[:, kc, :], hT.T)

        CUR["xn"] = xnA
        front_a(0)
        front_b(0)
        if NCH > 1:
            front_a(1)
        for n in range(NCH + LAG):
            if n < NCH:
                hT = hTs.pop(n)
                rp = rps.pop(n)
                mx = mix.at(n)
            use("N")
            if n + 1 < NCH:
                front_b(n + 1)
            if n + 2 < NCH:
                front_a(n + 2)
            if n < NCH:
                attn_proj(n, hT, rp)
                attn_proj_k(n, hT, rp)
            if n < NCH:
                use("SC")
                if 'sgu' in DBG:
                    pb, Tpb = proj(hT, win_v, Twin, 0, 512, CUR["gen"])
                    P.I("act", "activation", [Tpb], [u_b.T], out=u_b[:], in_=pb[:, 0:256], func=AF.Gelu_apprx_tanh)
                    P.I("act", "activation", [Tpb], [vg.T], out=vg[:], in_=pb[:, 256:512], func=AF.Gelu_apprx_tanh)
                    P.I("dve", "tensor_tensor", [vg.T], [sq.T], out=sq[:], in0=vg[:], in1=vg[:], op=ALU.mult)
                    s = CUR["st"].next()
                    P.I("dve", "tensor_reduce", [sq.T], [s.T], out=s[:, 0:4], in_=sq[:].rearrange("p (g d) -> p g d", g=4), axis=AX.X, op=ALU.add)
                    P.I("dve", "tensor_scalar", [s.T], [s.T], out=s[:, 0:4], in0=s[:, 0:4], scalar1=1.0 / 64, scalar2=EPS, op0=ALU.mult, op1=ALU.add)
                    P.I("pool", "tensor_tensor", [s.T, cmh.T], [s.T], out=s[:, 4:8], in0=s[:, 0:4], in1=cmh[:, 0:4], op=ALU.pow)
                    P.I("dve", "tensor_tensor", [vg.T, s.T], [vn.T], out=vn[:].rearrange("p (g d) -> p g d", g=4), in0=vg[:].rearrange("p (g d) -> p g d", g=4),
                                                          in1=s[:, 4:8].unsqueeze(2).to_broadcast([128, 4, 64]), op=ALU.mult)
                    P.I("pool", "tensor_tensor", [vn.T, SN.T], [vn2.T], out=vn2[:], in0=vn[:], in1=SN[:], op=ALU.mult)
                    pg, Tpg = CUR["gen"].next()
                    for g in range(4):
                        P.I("pe", "matmul", [WsT.T, vn2.T], [Tpg], pg[:, g * 64:(g + 1) * 64], lhsT=WsT[:, g, :], rhs=vn2[:, g * 64:(g + 1) * 64], start=True, stop=True)
                    P.I("dve", "tensor_tensor", [Tpg, SGB.T], [gt_.T], out=gt_[:].rearrange("p (g d) -> p g d", g=4), in0=pg[:, 0:256].rearrange("p (g d) -> p g d", g=4),
                                                          in1=SGB[:].unsqueeze(2).to_broadcast([128, 4, 64]), op=ALU.add)
                    P.I("pool", "tensor_tensor", [gt_.T, u_b.T], [mx.T], out=mx[:, 0:256], in0=gt_[:], in1=u_b[:], op=ALU.mult)
                use("R")
                if 'ret' in DBG:
                    pq, Tpq = proj(hT, win_v, Twin, 512, 384, CUR["gen"])
                    rope64(pq, Tpq, rp, rsum)
                    P.I("act", "activation", [rsum.T], [qr.T], out=qr[:], in_=rsum[:], func=AF.Copy)
                    pk, Tpk = proj(hT, win_v, Twin, 896, 384, CUR["gen"])
                    rope64(pk, Tpk, rp, rsum)
                    P.I("act", "activation", [rsum.T], [kr.T], out=kr[:], in_=rsum[:], func=AF.Copy)
                    P.I("dve", "tensor_tensor", [rsum.T, KDF.T], [kdf.T], out=kdf[:].rearrange("p (h d) -> p h d", h=6), in0=rsum[:].rearrange("p (h d) -> p h d", h=6),
                                                          in1=KDF[:].unsqueeze(2).to_broadcast([128, 6, 64]), op=ALU.mult)
                    pv, Tpv = proj(hT, win_v, Twin, 1280, 384, CUR["gen"])
                    P.I("act", "activation", [Tpv], [vt.T], out=vt[:], in_=pv[:, 0:384], func=AF.Copy)
                    pgg, Tpgg = proj(hT, win_v, Twin, 1664, 384, CUR["gen"])
                    P.I("act", "activation", [Tpgg], [sg.T], out=sg[:], in_=pgg[:, 0:384], func=AF.Silu)
                    if 'r1' in DBG2:
                        for i, srcb in enumerate([qr, kr]):
                            for q in range(3):
                                P.I("pe", "transpose", [srcb.T, ident.T], [Tbank[4]], out=bankb(4)[:, (i * 3 + q) * 128:(i * 3 + q + 1) * 128],
                                                                                               in_=srcb[:, q * 128:(q + 1) * 128], identity=ident[:])
                        P.I("act", "activation", [Tbank[4]], [qkT.T], out=qkT[:].rearrange("p a i -> p (a i)"), in_=bankb(4)[:, 0:768], func=AF.Copy)
                        P.I("dve", "tensor_tensor", [qkT.T, QDF.T], [qdf.T], out=qdf[:], in0=qkT[:, 0:3, :], in1=QDF[:], op=ALU.mult)
                        P.I("dve", "tensor_tensor", [qkT.T, QDB.T], [qdb.T], out=qdb[:], in0=qkT[:, 0:3, :], in1=QDB[:], op=ALU.mult)
                    if 'r2' in DBG2:
                        for par in range(2):
                            pS, TpS = CUR["gen"].next()
                            for q_ in range(3):
                                P.I("pe", "matmul", [qkT.T], [TpS], pS[:, q_ * 128:(q_ + 1) * 128], lhsT=qkT[par * 64:(par + 1) * 64, 3 + q_, :],
                                    rhs=qkT[par * 64:(par + 1) * 64, q_, :], start=True, stop=True)
                            P.I("dve", "tensor_tensor", [TpS, DT.T], [PT.T], out=PT[:, par * 3:(par + 1) * 3, :], in0=pS[:, 0:384].rearrange("p (h i) -> p h i", h=3),
                                in1=DT[:, par * 3:(par + 1) * 3, :], op=ALU.mult)
                    if 'r3' in DBG2:
                        pY, TpY = CUR["gen"].next()
                        for h in range(6):
                            q_, par = h // 2, h % 2
                            sl = slice(par * 64, (par + 1) * 64)
                            P.I("pe", "matmul", [PT.T, vt.T], [TpY], pY[:, h * 64:(h + 1) * 64], lhsT=PT[:, par * 3 + q_, :], rhs=vt[:, h * 64:(h + 1) * 64], start=True, stop=False)
                            P.I("pe", "matmul", [qdf.T, Sfb.T], [TpY], pY[:, h * 64:(h + 1) * 64], lhsT=qdf[sl, q_, :], rhs=Sfb[sl, q_, :], start=False, stop=False)
                            P.I("pe", "matmul", [qdb.T, Tsb[n]], [TpY], pY[:, h * 64:(h + 1) * 64], lhsT=qdb[sl, q_, :], rhs=sbst_v[sl, n, q_, :], start=False, stop=True)
                        P.I("act", "activation", [TpY], [ysb.T], out=ysb[:], in_=pY[:, 0:384], func=AF.Copy)
                    if 'r4' in DBG2:
                        kv_update(kdf, Rf, CDF, n, Sfb[:], Sfb.T, CUR["gen"], boundary=((n + 1) % SEGC == 0))
                    if 'r5' in DBG2:
                        s2 = CUR["st"].next()
                        Y3 = ysb[:].rearrange("p (h d) -> p h d", h=6)
                        P.I("dve", "tensor_reduce", [ysb.T], [s2.T], out=s2[:, 0:6], in_=Y3, axis=AX.X, op=ALU.add)
                        P.I("act", "activation", [ysb.T], [ysq.T], out=ysq[:], in_=ysb[:], func=AF.Square)
                        s3 = CUR["st"].next()
                        P.I("dve", "tensor_reduce", [ysq.T], [s3.T], out=s3[:, 0:6], in_=ysq[:].rearrange("p (h d) -> p h d", h=6), axis=AX.X, op=ALU.add)
                        P.I("dve", "tensor_scalar", [s2.T], [s2.T], out=s2[:, 0:6], in0=s2[:, 0:6], scalar1=1.0 / 64, scalar2=None, op0=ALU.mult)
                        s4 = CUR["st"].next()
                        P.I("dve", "tensor_tensor", [s2.T], [s4.T], out=s4[:, 0:6], in0=s2[:, 0:6], in1=s2[:, 0:6], op=ALU.mult)
                        P.I("dve", "scalar_tensor_tensor", [s3.T, s4.T], [s3.T], out=s3[:, 0:6], in0=s3[:, 0:6], scalar=1.0 / 64, in1=s4[:, 0:6], op0=ALU.mult, op1=ALU.subtract)
                        P.I("dve", "tensor_scalar", [s3.T], [s3.T], out=s3[:, 0:6], in0=s3[:, 0:6], scalar1=EPS, scalar2=None, op0=ALU.add)
                        P.I("pool", "tensor_tensor", [s3.T, cmh.T], [s4.T], out=s4[:, 0:6], in0=s3[:, 0:6], in1=cmh[:, 0:6], op=ALU.pow)
                        P.I("dve", "tensor_tensor", [ysb.T, s2.T], [yc_.T], out=yc_[:].rearrange("p (h d) -> p h d", h=6), in0=Y3, in1=s2[:, 0:6].unsqueeze(2).to_broadcast([128, 6, 64]),
                                                              op=ALU.subtract)
                        P.I("dve", "tensor_tensor", [yc_.T, s4.T], [yn_.T], out=yn_[:].rearrange("p (h d) -> p h d", h=6), in0=yc_[:].rearrange("p (h d) -> p h d", h=6),
                                                              in1=s4[:, 0:6].unsqueeze(2).to_broadcast([128, 6, 64]), op=ALU.mult)
                        P.I("pool", "tensor_tensor", [yn_.T, sg.T], [mx.T], out=mx[:, 256:640], in0=yn_[:], in1=sg[:], op=ALU.mult)
            m = n - LAG
            if m >= 0:
                use("L")
                if 'attn' in DBG:
                    attention(l, m)
                if 'out' in DBG:
                    xr = load_x(src, Tsrc, m, xLr, dx[2:3])
                    out_stage(l, m, xr, Gseg[m // SEGC], dst, Tdst)
            use(None)
            P.flush()
            mf = n - (LAG - 1)
            if 0 <= mf < NCH and mf % SEGC == 0:
                Gseg[mf // SEGC] = load_G(l, 0, mf // SEGC)
        CUR["xn"] = None

    class SubRing:
        def __init__(self, bufs):
            self.b = bufs; self.n = len(bufs); self.i = -1

        def next(self):
            self.i += 1
            return self.b[self.i % self.n]

    def phase_B(l, src, Tsrc, dst, Tdst):
        P.dma("sp", CW[:], cw_in[l], [], [CW.T], ND("CW"))
        xN = SubRing(xcr.b[0:2]); xD = SubRing(xcr.b[2:3])
        dxN = dx[0:2]; dxD = dx[2:3]
        gB = Gen([1, 2, 3, 4, 5])
        gDn = Gen([6, 7])
        hbs = {}
        normed = [-1]
        Gseg = {}

        def do_norm(c):
            hb_alloc(c)
            nback(c, nfront(c))

        def hb_alloc(c):
            b, half = c // 2, c % 2
            if half == 0:
                hb = HB.next()
                hbs[b] = hb
                if b == 0:
                    P.I("pool", "memset", [], [hb.T], hb[:, :, 0:1], 0.0)
                if b == NB - 1:
                    P.I("pool", "memset", [], [hb.T], hb[:, :, 257:258], 0.0)

        def nfront(c):
            xc = load_x(src, Tsrc, c, xN, dxN)
            return norm_front(xc)

        def nfront_pre(c):
            hb_alloc(c)
            return nfront(c)

        def nf1(c):
            hb_alloc(c)
            xc = load_x(src, Tsrc, c, xN, dxN)
            return norm_front_1(xc)

        def down1(b, half):
            aT = actT.at(b)
            c = 2 * b + half
            seg = c // SEGC
            if seg not in Gseg:
                Gseg[seg] = load_G(l, 1, seg)
            xr = load_x(src, Tsrc, c, xD, dxD)
            return post_stage_1(lambda kc: (aT[:, kc, half * 128:(half + 1) * 128], aT.T), NFC, wdn_v, Twdn, xr, Gseg[seg], c, dst, Tdst, gDn)

        def nback(c, xb_):
            b, half = c // 2, c % 2
            seg = c // SEGC
            hb = hbs[b]
            o = 1 + half * 128
            norm_back(xb_, seg, A2f, fm[2], lambda kc: hb[:, kc, o:o + 128], hb.T)
            if half == 0 and b > 0:
                pv_ = hbs[b - 1]
                if c % SEGC == 0:
                    P.I("dve", "tensor_scalar", [hb.T, flag.T], [pv_.T], out=pv_[:, :, 257:258], in0=hb[:, :, 1:2], scalar1=flag[:, 0:1], scalar2=None, op0=ALU.mult)
                else:
                    P.I("pool", "tensor_copy", [hb.T], [pv_.T], out=pv_[:, :, 257:258], in_=hb[:, :, 1:2])
            normed[0] = c

        def halo_fwd(b):
            hb = hbs[b]; nx = hbs[b + 1]
            c = 2 * b + 1
            if (c + 1) % SEGC == 0:
                P.I("dve", "tensor_scalar", [hb.T, flag.T], [nx.T], out=nx[:, :, 0:1], in0=hb[:, :, 256:257], scalar1=flag[:, 0:1], scalar2=None, op0=ALU.mult)
            else:
                P.I("pool", "tensor_copy", [hb.T], [nx.T], out=nx[:, :, 0:1], in_=hb[:, :, 256:257])

        def up_pairs(b, f0, f1):
            hb = hbs[b]
            aT = actT.at(b)
            for fc in range(f0, f1):
                tl = []
                for which in range(2):
                    fcc = fc + which * NFC
                    pb, Tpb = gB.next()
                    for kc in range(8):
                        P.I("pe", "matmul", [Twup[kc], hb.T], [Tpb], pb[:, 0:258], lhsT=wup_v[:, kc, fcc * 128:(fcc + 1) * 128], rhs=hb[:, kc, :],
                            start=(kc == 0), stop=(kc == 7))
                    tl.append((fcc, pb, Tpb, T0.next(), T1.next(), T2.next()))
                for (fcc, pb, Tpb, t0, t1, t2) in tl:
                    P.I("act", "activation", [Tpb, CW.T], [t0.T], out=t0[:], in_=pb[:, 0:256], func=AF.Identity, scale=CW[:, fcc, 0:1], bias=CW[:, fcc, 3:4])
                for (fcc, pb, Tpb, t0, t1, t2) in tl:
                    P.I("dve", "scalar_tensor_tensor", [Tpb, CW.T, t0.T], [t1.T], out=t1[:], in0=pb[:, 1:257], scalar=CW[:, fcc, 1:2], in1=t0[:],
                        op0=ALU.mult, op1=ALU.add)
                for (fcc, pb, Tpb, t0, t1, t2) in tl:
                    P.I("dve", "scalar_tensor_tensor", [Tpb, CW.T, t1.T], [t2.T], out=t2[:], in0=pb[:, 2:258], scalar=CW[:, fcc, 2:3], in1=t1[:],
                        op0=ALU.mult, op1=ALU.add)
                g_ = gg.next()
                tg, tv = tl[0][5], tl[1][5]
                P.I("act", "activation", [tg.T], [g_.T], out=g_[:], in_=tg[:], func=AF.Gelu_apprx_tanh)
                P.I("pool", "tensor_tensor", [tv.T, g_.T], [aT.T], out=aT[:, fc, :], in0=tv[:], in1=g_[:], op=ALU.mult)

        def down(b):
            aT = actT.at(b)
            for half in range(2):
                c = 2 * b + half
                seg = c // SEGC
                if seg not in Gseg:
                    Gseg[seg] = load_G(l, 1, seg)
                xr = load_x(src, Tsrc, c, xD, dxD)
                post_stage(lambda kc: (aT[:, kc, half * 128:(half + 1) * 128], aT.T), NFC, wdn_v, Twdn, xr, Gseg[seg], c, dst, Tdst, gDn)

        do_norm(0); do_norm(1)
        if NCH > 2:
            do_norm(2)
            halo_fwd(0)
        for b in range(NB):
            c1, c2 = 2 * b + 3, 2 * b + 4
            n1 = nf1(c1) if c1 < NCH else None
            up_pairs(b, 0, 2)
            f1 = norm_front_2(n1) if n1 is not None else None
            up_pairs(b, 2, 4)
            d0 = down1(b - 1, 0) if b > 0 else None
            up_pairs(b, 4, 6)
            if d0 is not None:
                post_stage_2(d0)
            up_pairs(b, 6, 8)
            if f1 is not None:
                nback(c1, f1)
            d1 = down1(b - 1, 1) if b > 0 else None
            n2 = nf1(c2) if c2 < NCH else None
            up_pairs(b, 8, 10)
            if d1 is not None:
                post_stage_2(d1)
            f2 = norm_front_2(n2) if n2 is not None else None
            up_pairs(b, 10, 16)
            if f2 is not None:
                nback(c2, f2)
                halo_fwd(b + 1)
            up_pairs(b, 16, NFC)
        down(NB - 1)

    cur, Tcur = x_in, None
    for l in range(depth):
        P.barrier()
        load_A_weights(l)
        layer_setup(l)
        P.barrier()
        if stop_after == ("setup", l):
            break
        phase_A0(l, cur, Tcur)
        if stop_after == ("A0", l):
            break
        phase_A1(l, cur, Tcur, xa, Txa)
        if stop_after == ("A1", l):
            cur, Tcur = xa, Txa
            break
        P.barrier()
        load_B_weights(l)
        last = (l == depth - 1)
        dst, Tdst = (y_out, Ty) if last else (xb, Txb)
        phase_B(l, xa, Txa, dst, Tdst)
        cur, Tcur = dst, Tdst
    if cur is not y_out:
        for n in range(NCH):
            xc = load_x(cur, Tcur, n, xcr, dx)
            P.dma("sp", y_out[n * 128:(n + 1) * 128, :], xc[:], [xc.T], [Ty[n]], dcp[xcr.i % xcr.n])
    P.wait_all("sp", Ty)
    counts = P.emit()
    es.close()
    return nc, counts


def _rope_tables(pos, rot_dim, theta):
    half = rot_dim // 2
    freqs = np.exp(-math.log(theta) * np.arange(half, dtype=np.float32) * np.float32(2.0) / np.float32(rot_dim)).astype(np.float32)
    ang = pos.astype(np.float32)[:, None] * freqs[None, :]
    return np.cos(ang).astype(np.float32), np.sin(ang).astype(np.float32)


def core_tables(NCH, SEGC, is_prompt):
    n = np.arange(NCH)
    if is_prompt:
        base = n * 128
    else:
        base = (n % SEGC) * 128
    pos = (base[None, :] + np.arange(128)[:, None]).reshape(-1)
    rc_, rs_ = _rope_tables(pos, 64, 10000.0)
    ac_, as__ = _rope_tables(pos, 16, 500000.0)
    f = lambda a, d: a.reshape(128, NCH, d)
    rope = np.concatenate([f(rc_, 32), f(rs_, 32), f(-rs_, 32), f(ac_, 8), f(as__, 8), f(-as__, 8)], 2)
    return dict(rope=np.ascontiguousarray(rope.transpose(1, 0, 2)).astype(np.float32),
                flag=np.full((128, 1), 1.0 if is_prompt else 0.0, np.float32))


def const_inputs():
    j = np.arange(128, dtype=np.float32)[:, None]
    i = np.arange(128, dtype=np.float32)[None, :]
    dpos = np.maximum(i - j, 0); dneg = np.maximum(j - i, 0)
    mge = (i >= j).astype(np.float32); mlt = (i < j).astype(np.float32)
    io1 = np.broadcast_to(i + 1, (128, 128)); io2 = np.broadcast_to(128 - i, (128, 128))
    cm = np.stack([dpos, dneg, mge, mlt, io1, io2], 1).astype(np.float32)
    tri = np.stack([(j >= i).astype(np.float32) * np.ones((128, 128), np.float32), (j <= i).astype(np.float32) * np.ones((128, 128), np.float32)], 1)
    jv = np.concatenate([127 - j, j], 1).astype(np.float32)
    return dict(ident=np.eye(128, dtype=np.float32), cmats=np.ascontiguousarray(cm), tri=np.ascontiguousarray(tri), jv=jv)


def weight_inputs(w_ada, b_ada, norm_pre_mix, norm_post_mix, norm_pre_ffn, norm_post_ffn, w_in, sgu_norm, sgu_w, sgu_b,
                  ret_decay_fwd, ret_decay_bwd, attn_sink, w_out, w_up, conv_w, conv_b, w_down):
    L = w_in.shape[0]
    perm = np.arange(IN_DIM)
    hq = [0, 3, 1, 4, 2, 5]
    perm[2048:2432] = np.concatenate([2048 + h * 64 + np.arange(64) for h in hq])
    w_in_p = np.ascontiguousarray(w_in[:, :, perm])
    cw = np.concatenate([conv_w, conv_b[:, None, :]], 1)
    cw = np.ascontiguousarray(cw.reshape(L, 4, 2 * NFC, 128).transpose(0, 3, 2, 1))
    gfm = np.stack([norm_pre_mix, norm_pre_ffn], 1).reshape(L, 2, 8, 128).transpose(0, 3, 1, 2)
    gpost = np.concatenate([norm_post_mix, norm_post_ffn], 1)
    sgu_wT = np.ascontiguousarray(sgu_w.transpose(0, 3, 1, 2))
    sgu_bT = np.ascontiguousarray(sgu_b.transpose(0, 2, 1))
    dec6 = np.concatenate([ret_decay_fwd, ret_decay_bwd], 1)
    dec6 = np.broadcast_to(dec6[:, None, :], (L, 128, 12))
    decP = np.zeros((L, 128, 6), np.float32)
    decP[:, 0:64, 0:3] = ret_decay_fwd[:, None, 0::2]; decP[:, 64:128, 0:3] = ret_decay_fwd[:, None, 1::2]
    decP[:, 0:64, 3:6] = ret_decay_bwd[:, None, 0::2]; decP[:, 64:128, 3:6] = ret_decay_bwd[:, None, 1::2]
    sink6 = np.ascontiguousarray(np.broadcast_to(attn_sink[:, None, :], (L, 128, 6)))
    c = np.ascontiguousarray
    return dict(w_ada=c(w_ada), b_ada=c(b_ada), w_in=w_in_p, w_out=c(w_out), w_up=c(w_up), w_down=c(w_down), cw=cw,
                gfm=c(gfm.astype(np.float32)), gpost=c(gpost.astype(np.float32)), sgu_wT=sgu_wT, sgu_bT=sgu_bT, sgu_n=c(sgu_norm),
                dec18=c(np.concatenate([dec6, decP], 2).astype(np.float32)), sink6=sink6.astype(np.float32))


def core_c(c_rows):
    ns = c_rows.shape[0]
    return np.ascontiguousarray(c_rows.reshape(ns, 8, 128).transpose(2, 1, 0)).astype(np.float32)


_CACHE = {}


def kernel(x_prompt, x_sample, c_prompt, c_sample, w_ada, b_ada, norm_pre_mix, norm_post_mix, norm_pre_ffn, norm_post_ffn,
           w_in, sgu_norm, sgu_w, sgu_b, ret_decay_fwd, ret_decay_bwd, attn_sink, w_out, w_up, conv_w, conv_b, w_down):
    NCH, SEGC = 64, 16
    f = lambda a: np.asarray(a, dtype=np.float32)
    x_prompt, x_sample, c_prompt, c_sample = f(x_prompt), f(x_sample), f(c_prompt), f(c_sample)
    wts = weight_inputs(*[f(a) for a in (w_ada, b_ada, norm_pre_mix, norm_post_mix, norm_pre_ffn, norm_post_ffn, w_in, sgu_norm, sgu_w, sgu_b,
                                           ret_decay_fwd, ret_decay_bwd, attn_sink, w_out, w_up, conv_w, conv_b, w_down)])
    consts = const_inputs()
    if "nc" not in _CACHE:
        _CACHE["nc"] = build(NCH, SEGC)[0]
    nc = _CACHE["nc"]
    in_maps = []
    for core in range(8):
        if core < 2:
            xs = x_prompt[core]
            cr = np.repeat(c_prompt[core:core + 1], 4, 0)
            tb = core_tables(NCH, SEGC, True)
        else:
            k = min(core - 2, 3)
            xs = x_sample[4 * k:4 * k + 4].reshape(NCH * 128, D)
            cr = c_sample[4 * k:4 * k + 4]
            tb = core_tables(NCH, SEGC, False)
        m = dict(x=np.ascontiguousarray(xs), cT=core_c(cr))
        m.update(tb); m.update(consts); m.update(wts)
        in_maps.append(m)
    res = run_bass_kernel_spmd(nc, in_maps, core_ids=list(range(8)))
    r = res.results
    y_prompt = np.stack([r[0]["y"], r[1]["y"]], 0).astype(np.float32)
    y_sample = np.concatenate([r[2 + k]["y"].reshape(4, 2048, D) for k in range(4)], 0).astype(np.float32)
    return (y_prompt, y_sample)
```
